# Optimizing a Trainium2 kernel written in Bass

```python
import math
import jax
import jax.numpy as jnp
from jax import lax
import numpy as np

D_MODEL = 1024
BATCH = 8
SEQ = 4096
DEPTH = 2

CTX_LEN = 256
GRID_W = 64
D_FF = 4 * D_MODEL
N_BRANCH = 3
NORM_EPS = 1e-6

CONV_DIM = D_MODEL // 2
CONV_WIDTH = 31
LN_EPS = 1e-5

ATT_HEADS = 4
ATT_HEAD_DIM = 64
ATT_VAL_DIM = 2 * ATT_HEAD_DIM
ATT_DIM = ATT_HEADS * 2 * ATT_HEAD_DIM
Q_BLOCK = 128
ROPE_BASE = 10000.0
SUBLN_EPS = 1e-5

RWKV_HEADS = 8
RWKV_HEAD_DIM = 64
RWKV_DIM = RWKV_HEADS * RWKV_HEAD_DIM
DECAY_LORA = 64
ICLR_LORA = 64
GATE_LORA = 128
SHIFT_WIDTH = 3
RWKV_IN = 3 * RWKV_DIM + 2 * DECAY_LORA + 2 * ICLR_LORA + GATE_LORA
GN_EPS = 64e-5

IN_SPLITS = (2 * CONV_DIM, ATT_DIM, ATT_DIM, ATT_DIM, RWKV_IN, N_BRANCH * D_MODEL)
RWKV_SPLITS = (RWKV_DIM, RWKV_DIM, RWKV_DIM, DECAY_LORA, DECAY_LORA, ICLR_LORA, ICLR_LORA, GATE_LORA)
D_IN = 2 * CONV_DIM + 3 * ATT_DIM + RWKV_IN + N_BRANCH * D_MODEL

kernel_name = 'hybrid_conv_diffattn_rwkv7_prefix_dit'


def _split(x, sizes):
    cuts, acc = [], 0
    for s in sizes[:-1]:
        acc += s
        cuts.append(acc)
    return jnp.split(x, cuts, axis=-1)


def _rmsnorm(x, gain, eps=NORM_EPS):
    xf = x.astype(jnp.float32)
    y = xf * lax.rsqrt(jnp.mean(xf * xf, axis=-1, keepdims=True) + eps)
    return (y * gain.astype(jnp.float32)).astype(x.dtype)


def _layernorm(x, gain, bias, eps=LN_EPS):
    xf = x.astype(jnp.float32)
    mu = jnp.mean(xf, axis=-1, keepdims=True)
    var = jnp.mean(jnp.square(xf - mu), axis=-1, keepdims=True)
    y = (xf - mu) * lax.rsqrt(var + eps) * gain.astype(jnp.float32) + bias.astype(jnp.float32)
    return y.astype(x.dtype)


def _modulate(x, gain, shift, scale):
    return _rmsnorm(x, gain) * (1.0 + scale) + shift


def _depthwise_conv(x, w):
    return lax.conv_general_dilated(
        x, w[:, None, :].astype(x.dtype), window_strides=(1,), padding='SAME',
        dimension_numbers=('NWC', 'WIO', 'NWC'), feature_group_count=x.shape[-1])


def _sqrelu_mlp(h, w1, w2):
    return jnp.square(jax.nn.relu(h @ w1)) @ w2


def _conv_branch(z, dw_w, dw_b, ln_g, ln_b):
    a, b = jnp.split(z, 2, axis=-1)
    u = a * jax.nn.sigmoid(b)
    u = _depthwise_conv(u, dw_w) + dw_b
    return jax.nn.silu(_layernorm(u, ln_g, ln_b))


def _axial_rope(seq_len):
    t = jnp.arange(seq_len, dtype=jnp.int32)
    row = (t // GRID_W).astype(jnp.float32)
    col = (t % GRID_W).astype(jnp.float32)
    n_pairs = ATT_HEAD_DIM // 4
    inv_freq = ROPE_BASE ** (-jnp.arange(n_pairs, dtype=jnp.float32) / n_pairs)
    ang = jnp.concatenate([row[:, None] * inv_freq, col[:, None] * inv_freq], axis=-1)
    return jnp.cos(ang), jnp.sin(ang)


def _apply_rope(x, cos, sin):
    xp = x.astype(jnp.float32).reshape(x.shape[:-1] + (ATT_HEAD_DIM // 2, 2))
    x0, x1 = xp[..., 0], xp[..., 1]
    out = jnp.stack([x0 * cos - x1 * sin, x0 * sin + x1 * cos], axis=-1)
    return out.reshape(x.shape).astype(x.dtype)


def _qk_heads(u):
    b, t, _ = u.shape
    return u.reshape(b, t, ATT_HEADS, 2, ATT_HEAD_DIM).transpose(0, 2, 3, 1, 4)


def _v_heads(u):
    b, t, _ = u.shape
    return u.reshape(b, t, ATT_HEADS, ATT_VAL_DIM).transpose(0, 2, 1, 3)


def _diff_lambda(lq1, lk1, lq2, lk2, lambda_init):
    f = jnp.float32
    return (jnp.exp(jnp.sum(lq1.astype(f) * lk1.astype(f)))
            - jnp.exp(jnp.sum(lq2.astype(f) * lk2.astype(f))) + lambda_init)


def _diff_attend(q, k, v, lam):
    b, h, _, tq, dh = q.shape
    nblk = tq // Q_BLOCK
    qb = jnp.moveaxis(q.reshape(b, h, 2, nblk, Q_BLOCK, dh), 3, 0)
    scale = 1.0 / math.sqrt(dh)

    def block(qi):
        s = jnp.einsum('bhcqd,bhckd->bhcqk', qi, k).astype(jnp.float32) * scale
        p = jax.nn.softmax(s, axis=-1)
        a = p[:, :, 0] - lam * p[:, :, 1]
        return jnp.einsum('bhqk,bhkd->bhqd', a.astype(v.dtype), v)

    out = lax.map(block, qb)
    return jnp.moveaxis(out, 0, 2).reshape(b, h, tq, v.shape[-1])


def _att_post(o, subln_g, lambda_init):
    o = _rmsnorm(o, subln_g, SUBLN_EPS) * (1.0 - lambda_init)
    b, h, t, dv = o.shape
    return o.transpose(0, 2, 1, 3).reshape(b, t, h * dv)


def _rwkv_prep(z, lp):
    b, t, _ = z.shape
    r, k, v, wf, wb, af, ab, gd = _split(z, RWKV_SPLITS)

    def heads(u):
        return u.reshape(b, t, RWKV_HEADS, RWKV_HEAD_DIM)

    kkf = heads(k * lp['rwkv_kk']).astype(jnp.float32)
    kk = kkf * lax.rsqrt(jnp.maximum(jnp.sum(kkf * kkf, axis=-1, keepdims=True), 1e-24))
    g = jax.nn.sigmoid(gd) @ lp['rwkv_g2']
    dirs = []
    for d, (wd, ad) in enumerate(((wf, af), (wb, ab))):
        w_log = -jax.nn.softplus(-(lp['rwkv_w0'][d] + jnp.tanh(wd) @ lp['rwkv_w2'][d])) - 0.5
        decay = jnp.exp(-jnp.exp(w_log.astype(jnp.float32)))
        a = jax.nn.sigmoid(lp['rwkv_a0'][d] + ad @ lp['rwkv_a2'][d])
        kd = k * (1.0 + (a - 1.0) * lp['rwkv_ka'])
        dirs.append((heads(decay), heads(kd), heads(a)))
    return heads(r), heads(v), kk, g, dirs


def _rwkv_scan(r, w, k, v, kk, a, s0, reverse):
    xs = tuple(jnp.moveaxis(u.astype(jnp.float32), 1, 0) for u in (r, w, k, v, kk, a))

    def step(s, inp):
        r_t, w_t, k_t, v_t, kk_t, a_t = inp
        sa = jnp.einsum('bhij,bhj->bhi', s, -kk_t)
        s = (s * w_t[:, :, None, :] + sa[..., None] * (kk_t * a_t)[:, :, None, :]
             + v_t[..., None] * k_t[:, :, None, :])
        return s, jnp.einsum('bhij,bhj->bhi', s, r_t)

    s_fin, ys = lax.scan(step, s0, xs, reverse=reverse)
    return jnp.moveaxis(ys, 0, 1), s_fin


def _rwkv_out(ys, r, v, kds, g, lp):
    b, t = ys.shape[:2]
    mu = jnp.mean(ys, axis=-1, keepdims=True)
    var = jnp.mean(jnp.square(ys - mu), axis=-1, keepdims=True)
    yn = ((ys - mu) * lax.rsqrt(var + GN_EPS)).reshape(b, t, RWKV_DIM)
    yn = (yn * lp['rwkv_gn_g'] + lp['rwkv_gn_b']).astype(r.dtype)
    bonus = sum(jnp.sum(r * kd * lp['rwkv_rk'], axis=-1, keepdims=True) * v for kd in kds)
    return (yn + bonus.reshape(b, t, RWKV_DIM)) * g


def _rwkv_branch(zc, zl, lp, need_ctx):
    zc = _depthwise_conv(zc, lp['rwkv_shift'])
    zl = _depthwise_conv(zl, lp['rwkv_shift'])
    rc, vc, kkc, gc, dirs_c = _rwkv_prep(zc, lp)
    rl, vl, kkl, gl, dirs_l = _rwkv_prep(zl, lp)
    s0 = jnp.zeros((zl.shape[0], RWKV_HEADS, RWKV_HEAD_DIM, RWKV_HEAD_DIM), jnp.float32)
    ys_c, ys_l = [], []
    for d, reverse in enumerate((False, True)):
        wc, kc, ac = dirs_c[d]
        wl, kl, al = dirs_l[d]
        yc, s_ctx = _rwkv_scan(rc, wc, kc, vc, kkc, ac, s0, reverse)
        yl, _ = _rwkv_scan(rl, wl, kl, vl, kkl, al, s_ctx, reverse)
        ys_c.append(yc)
        ys_l.append(yl)
    y_l = _rwkv_out(ys_l[0] + ys_l[1], rl, vl, [dirs_l[0][1], dirs_l[1][1]], gl, lp)
    if not need_ctx:
        return None, y_l
    y_c = _rwkv_out(ys_c[0] + ys_c[1], rc, vc, [dirs_c[0][1], dirs_c[1][1]], gc, lp)
    return y_c, y_l


def _merge(y_conv, y_att, y_rwkv, gates, lp):
    g1, g2, g3 = jnp.split(jax.nn.sigmoid(gates), N_BRANCH, axis=-1)
    m = (g1 * (y_conv @ lp['p_conv']) + g2 * (y_att @ lp['p_att'])
         + g3 * (y_rwkv @ lp['p_rwkv']))
    return m @ lp['w_out']


def _token_mixer(hc, hl, lp, lambda_init, cos, sin, need_ctx):
    conv_c, q_c, k_c, v_c, rw_c, gate_c = _split(hc @ lp['w_in'], IN_SPLITS)
    conv_l, q_l, k_l, v_l, rw_l, gate_l = _split(hl @ lp['w_in'], IN_SPLITS)
    conv_args = (lp['conv_dw_w'], lp['conv_dw_b'], lp['conv_ln_g'], lp['conv_ln_b'])
    lam = _diff_lambda(lp['att_lq1'], lp['att_lk1'], lp['att_lq2'], lp['att_lk2'], lambda_init)

    kh_c, vh_c = _qk_heads(k_c), _v_heads(v_c)
    keys_l = jnp.concatenate([kh_c, _apply_rope(_qk_heads(k_l), cos, sin)], axis=3)
    vals_l = jnp.concatenate([vh_c, _v_heads(v_l)], axis=2)
    q_lat = _apply_rope(_qk_heads(q_l), cos, sin)
    y_att_l = _att_post(_diff_attend(q_lat, keys_l, vals_l, lam), lp['att_subln_g'], lambda_init)
    y_conv_l = _conv_branch(conv_l, *conv_args)
    y_rw_c, y_rw_l = _rwkv_branch(rw_c, rw_l, lp, need_ctx)
    out_l = _merge(y_conv_l, y_att_l, y_rw_l, gate_l, lp)
    if not need_ctx:
        return None, out_l
    y_conv_c = _conv_branch(conv_c, *conv_args)
    y_att_c = _att_post(_diff_attend(_qk_heads(q_c), kh_c, vh_c, lam), lp['att_subln_g'], lambda_init)
    out_c = _merge(y_conv_c, y_att_c, y_rw_c, gate_c, lp)
    return out_c, out_l


def setup_inputs(seed: int = 0) -> dict:
    key = jax.random.key(seed)
    keys = jax.random.split(key, 64)
    f32 = jnp.float32
    cnt = [0]

    def nxt():
        cnt[0] += 1
        return keys[cnt[0]]

    def nrm(shape, scale):
        return jax.random.normal(nxt(), shape, f32) * scale

    def gain(shape):
        return 1.0 + nrm(shape, 0.02)

    L, D = DEPTH, D_MODEL
    shift_base = jnp.array([0.25, 0.5, 0.25], f32)[None, :, None]
    return {
        'x': nrm((BATCH, SEQ, D), 1.0),
        'c': nrm((BATCH, D), 1.0),
        'ctx': nrm((BATCH, CTX_LEN, D), 1.0),
        'c_ctx': nrm((D,), 1.0),
        'mod_w': nrm((L, D, 6 * D), D ** -0.5),
        'mod_b': nrm((L, 6 * D), 0.01),
        'norm1_g': gain((L, D)),
        'norm2_g': gain((L, D)),
        'w_in': nrm((L, D, D_IN), D ** -0.5),
        'conv_dw_w': nrm((L, CONV_WIDTH, CONV_DIM), CONV_WIDTH ** -0.5),
        'conv_dw_b': nrm((L, CONV_DIM), 0.01),
        'conv_ln_g': gain((L, CONV_DIM)),
        'conv_ln_b': nrm((L, CONV_DIM), 0.01),
        'p_conv': nrm((L, CONV_DIM, D), CONV_DIM ** -0.5),
        'att_lq1': nrm((L, ATT_HEAD_DIM), 0.1),
        'att_lk1': nrm((L, ATT_HEAD_DIM), 0.1),
        'att_lq2': nrm((L, ATT_HEAD_DIM), 0.1),
        'att_lk2': nrm((L, ATT_HEAD_DIM), 0.1),
        'att_subln_g': gain((L, ATT_VAL_DIM)),
        'p_att': nrm((L, ATT_DIM, D), ATT_DIM ** -0.5),
        'rwkv_shift': shift_base + nrm((L, SHIFT_WIDTH, RWKV_IN), 0.05),
        'rwkv_w0': jax.random.uniform(nxt(), (L, 2, RWKV_DIM), f32, -6.0, -1.0),
        'rwkv_w2': nrm((L, 2, DECAY_LORA, RWKV_DIM), 0.3 * DECAY_LORA ** -0.5),
        'rwkv_a0': nrm((L, 2, RWKV_DIM), 0.1),
        'rwkv_a2': nrm((L, 2, ICLR_LORA, RWKV_DIM), 0.3 * ICLR_LORA ** -0.5),
        'rwkv_g2': nrm((L, GATE_LORA, RWKV_DIM), GATE_LORA ** -0.5),
        'rwkv_kk': 0.85 + nrm((L, RWKV_DIM), 0.05),
        'rwkv_ka': 1.0 + nrm((L, RWKV_DIM), 0.05),
        'rwkv_rk': nrm((L, RWKV_HEADS, RWKV_HEAD_DIM), 0.1),
        'rwkv_gn_g': gain((L, RWKV_DIM)),
        'rwkv_gn_b': nrm((L, RWKV_DIM), 0.01),
        'p_rwkv': nrm((L, RWKV_DIM, D), RWKV_DIM ** -0.5),
        'w_out': nrm((L, D, D), D ** -0.5),
        'mlp_w1': nrm((L, D, D_FF), D ** -0.5),
        'mlp_w2': nrm((L, D_FF, D), D_FF ** -0.5),
        'final_g': gain((D,)),
    }


def reference(x, c, ctx, c_ctx, mod_w, mod_b, norm1_g, norm2_g, w_in,
              conv_dw_w, conv_dw_b, conv_ln_g, conv_ln_b, p_conv,
              att_lq1, att_lk1, att_lq2, att_lk2, att_subln_g, p_att,
              rwkv_shift, rwkv_w0, rwkv_w2, rwkv_a0, rwkv_a2, rwkv_g2,
              rwkv_kk, rwkv_ka, rwkv_rk, rwkv_gn_g, rwkv_gn_b, p_rwkv,
              w_out, mlp_w1, mlp_w2, final_g):
    seq = x.shape[1]
    cos, sin = _axial_rope(seq)
    xl, xc = x, ctx
    silu_c = jax.nn.silu(c)
    silu_cc = jax.nn.silu(c_ctx)
    for l in range(DEPTH):
        need_ctx = l < DEPTH - 1
        lambda_init = 0.8 - 0.6 * math.exp(-0.3 * l)
        lp = dict(w_in=w_in[l], conv_dw_w=conv_dw_w[l], conv_dw_b=conv_dw_b[l],
                  conv_ln_g=conv_ln_g[l], conv_ln_b=conv_ln_b[l], p_conv=p_conv[l],
                  att_lq1=att_lq1[l], att_lk1=att_lk1[l], att_lq2=att_lq2[l], att_lk2=att_lk2[l],
                  att_subln_g=att_subln_g[l], p_att=p_att[l],
                  rwkv_shift=rwkv_shift[l], rwkv_w0=rwkv_w0[l], rwkv_w2=rwkv_w2[l],
                  rwkv_a0=rwkv_a0[l], rwkv_a2=rwkv_a2[l], rwkv_g2=rwkv_g2[l],
                  rwkv_kk=rwkv_kk[l], rwkv_ka=rwkv_ka[l], rwkv_rk=rwkv_rk[l],
                  rwkv_gn_g=rwkv_gn_g[l], rwkv_gn_b=rwkv_gn_b[l], p_rwkv=p_rwkv[l],
                  w_out=w_out[l])
        shl1, scl1, gl1, shl2, scl2, gl2 = jnp.split(
            (silu_c @ mod_w[l] + mod_b[l])[:, None, :], 6, axis=-1)
        shc1, scc1, gc1, shc2, scc2, gc2 = jnp.split(
            (silu_cc @ mod_w[l] + mod_b[l])[None, None, :], 6, axis=-1)
        hl = _modulate(xl, norm1_g[l], shl1, scl1)
        hc = _modulate(xc, norm1_g[l], shc1, scc1)
        oc, ol = _token_mixer(hc, hl, lp, lambda_init, cos, sin, need_ctx)
        xl = xl + gl1 * ol
        xl = xl + gl2 * _sqrelu_mlp(_modulate(xl, norm2_g[l], shl2, scl2), mlp_w1[l], mlp_w2[l])
        if need_ctx:
            xc = xc + gc1 * oc
            xc = xc + gc2 * _sqrelu_mlp(_modulate(xc, norm2_g[l], shc2, scc2), mlp_w1[l], mlp_w2[l])
    return _rmsnorm(xl, final_g)
```

```python
import math
import numpy as np
import concourse.bass as bass
import concourse.mybir as mybir
from concourse.bass_utils import run_bass_kernel_spmd

F32 = mybir.dt.float32
BF16 = mybir.dt.bfloat16
AF = mybir.ActivationFunctionType
ALU = mybir.AluOpType
AX = mybir.AxisListType

D = 1024
SEQ = 4096
CTX = 256
T = SEQ + CTX
DEPTH = 2
DIN = 7552
DFF = 4096
NCH = D // 128
BLKS = [(0, 256)] + [(256 + 512 * i, 512) for i in range(8)]
NORM_EPS = 1e-6
LN_EPS = 1e-5
SUBLN_EPS = 1e-5
GN_EPS = 64e-5


class Sched:
    def __init__(self, nc, n_dma=24):
        self.nc = nc
        self.eng = dict(pe=nc.tensor, dve=nc.vector, act=nc.scalar, pool=nc.gpsimd, sp=nc.sync)
        self.sem = {e: nc.alloc_semaphore('sem_' + e) for e in self.eng}
        self.cnt = {e: 0 for e in self.eng}
        self.dsem = [nc.alloc_semaphore('dsem%d' % i) for i in range(n_dma)]
        self.dval = [0] * n_dma
        self.drr = 0
        self.seen = {e: {} for e in self.eng}
        self.lastw = {}
        self.readers = {}
        self.nps = 0

    def _semh(self, key):
        return self.sem[key] if isinstance(key, str) else self.dsem[key]

    def _wait(self, e, key, val):
        if self.seen[e].get(key, 0) >= val:
            return
        self.eng[e].wait_ge(self._semh(key), val)
        self.seen[e][key] = val

    def _deps(self, e, reads, writes):
        need = {}
        for r in reads:
            tok = self.lastw.get(r)
            if tok is not None:
                need[tok[0]] = max(need.get(tok[0], 0), tok[1])
        for w in writes:
            tok = self.lastw.get(w)
            if tok is not None:
                need[tok[0]] = max(need.get(tok[0], 0), tok[1])
            for k, v in self.readers.get(w, {}).items():
                need[k] = max(need.get(k, 0), v)
        for k, v in need.items():
            self._wait(e, k, v)

    def _commit(self, tok, reads, writes):
        for w in writes:
            self.lastw[w] = tok
            self.readers[w] = {}
        for r in reads:
            if r in writes:
                continue
            d = self.readers.setdefault(r, {})
            d[tok[0]] = max(d.get(tok[0], 0), tok[1])

    def op(self, e, fn, reads=(), writes=()):
        if e == 'pe':
            self.seen[e]['pe'] = self.cnt['pe']
        self._deps(e, reads, writes)
        inst = fn(self.eng[e])
        self.cnt[e] += 1
        inst.then_inc(self.sem[e], 1)
        self._commit((e, self.cnt[e]), reads, writes)

    def dma(self, e, out, in_, reads=(), writes=(), **kw):
        self._deps(e, reads, writes)
        i = self.drr
        self.drr = (self.drr + 1) % len(self.dsem)
        if self.dval[i] > 0:
            self._wait(e, i, self.dval[i])
        self.dval[i] += 16
        self.eng[e].dma_start(out=out, in_=in_, **kw).then_inc(self.dsem[i], 16)
        self._commit((i, self.dval[i]), reads, writes)

    def barrier(self):
        for e in self.eng:
            for i, v in enumerate(self.dval):
                if v > 0:
                    self._wait(e, i, v)
            for k in self.eng:
                if k != e and self.cnt[k] > 0:
                    self._wait(e, k, self.cnt[k])

    def finish(self, e='sp'):
        for i, v in enumerate(self.dval):
            if v > 0:
                self._wait(e, i, v)
        for k in self.eng:
            if k != e and self.cnt[k] > 0:
                self._wait(e, k, self.cnt[k])


class Ctx:
    pass


class Phase:
    uid = 0

    def __init__(self, g):
        self.g = g
        self.guards = []

    def __enter__(self):
        return self

    def __call__(self, name, shape, dt=F32):
        Phase.uid += 1
        gd = self.g.nc.sbuf_tensor('%s_u%d' % (name, Phase.uid), list(shape), dt)
        t = gd.__enter__()
        self.guards.append(gd)
        return t

    def __exit__(self, *a):
        self.g.S.barrier()
        for gd in reversed(self.guards):
            gd.__exit__(None, None, None)
        return False


def mm_group(S, out, pairs, reads, writes, start=True, stop=True):
    n = len(pairs)

    def fn(pe):
        inst = None
        for i, (l, r) in enumerate(pairs):
            inst = pe.matmul(out, l, r, start=(start and i == 0), stop=(stop and i == n - 1))
        return inst
    S.op('pe', fn, reads, writes)


def build(dbg=(), upto='all'):
    nc = bass.Bass("TRN2", target_bir_lowering=False)
    S = Sched(nc)
    g = Ctx()
    g.nc, g.S = nc, S
    din = lambda name, shape, dt=F32: nc.dram_tensor(name, list(shape), dt, kind="ExternalInput").ap()
    dint = lambda name, shape, dt=F32: nc.dram_tensor(name, list(shape), dt, kind="Internal").ap()
    g.x = din('x', [SEQ, D]); g.ctx = din('ctx', [CTX, D])
    g.cvec = din('cvec', [128, NCH, 2])
    g.mod_w = din('mod_w', [DEPTH, D, 6 * D]); g.modb = din('modb', [DEPTH, 128, 48])
    g.n1g = din('n1g', [DEPTH, 128, NCH]); g.n2g = din('n2g', [DEPTH, 128, NCH]); g.fing = din('fing', [128, NCH])
    g.ident = din('ident', [128, 128])
    g.w_in = din('w_in', [DEPTH, D, DIN])
    g.convw = din('convw', [DEPTH, 128, 4, 31]); g.convp = din('convp', [DEPTH, 128, 3, 4])
    g.wqks = din('wqks', [DEPTH, D, 1024]); g.rope = din('rope', [2, 128, SEQ]); g.cmask = din('cmask', [3, 128, 128])
    g.attp = din('attp', [DEPTH, 4, 64]); g.subg = din('subg', [DEPTH, 128])
    g.rw_w2 = din('rw_w2', [DEPTH, 2, 64, 512]); g.rw_a2 = din('rw_a2', [DEPTH, 2, 64, 512]); g.rw_g2 = din('rw_g2', [DEPTH, 128, 512])
    g.rwp = din('rwp', [DEPTH, 128, P_OMKA]); g.smask = din('smask', [9, 128, 128])
    g.rS = dint('rS', [512, T], BF16); g.kkS = dint('kkS', [512, T], BF16); g.gS = dint('gS', [512, T], BF16); g.bonS = dint('bonS', [512, T], BF16)
    g.kdS = [dint('kdS%d' % d, [512, T], BF16) for d in range(2)]; g.bS = [dint('bS%d' % d, [512, T], BF16) for d in range(2)]
    g.wlS = [dint('wlS%d' % d, [512, T]) for d in range(2)]; g.vT = dint('vT', [T, 512], BF16); g.yfS = dint('yfS', [512, T])
    g.p_conv = din('p_conv', [DEPTH, 512, D]); g.p_att = din('p_att', [DEPTH, 512, D]); g.p_rwkv = din('p_rwkv', [DEPTH, 512, D])
    g.w_out = din('w_out', [DEPTH, D, D]); g.mlp_w1 = din('mlp_w1', [DEPTH, D, DFF]); g.mlp_w2 = din('mlp_w2', [DEPTH, DFF, D])
    g.out = nc.dram_tensor('out', [SEQ, D], F32, kind="ExternalOutput").ap()
    g.xT = dint('xT', [D, T]); g.hT = dint('hTd', [D, T], BF16)
    g.ycT = dint('ycT', [512, T], BF16); g.yaT = dint('yaT', [512, T], BF16); g.yrT = dint('yrT', [512, T], BF16)
    g.dbg = {}
    for name, shape, dt in dbg:
        g.dbg[name] = nc.dram_tensor('dbg_' + name, list(shape), dt, kind="ExternalOutput").ap()

    sb = lambda name, shape, dt=F32: nc.alloc_sbuf_tensor(name, list(shape), dt)
    g.ps = [nc.alloc_psum_tensor('ps%d' % i, [128, 512], F32) for i in range(8)]
    g.psi = 0

    g.identf = sb('identf', [128, 128]); g.identb = sb('identb', [128, 128], BF16)
    g.onesf = sb('onesf', [128, 128]); g.onesb = sb('onesb', [128, 128], BF16)
    g.epsv = sb('epsv', [128, 4])
    g.cs = sb('cs', [128, NCH, 2]); g.modv = sb('modv', [128, 48, 2]); g.gs = sb('gs', [128, 2, NCH, 2])
    S.dma('sp', g.identf[:], g.ident[:, :], writes=['identf'])
    S.op('dve', lambda e: e.tensor_copy(g.identb[:], g.identf[:]), ['identf'], ['identb'])
    S.op('dve', lambda e: e.memset(g.onesf[:], 1.0), [], ['onesf'])
    S.op('dve', lambda e: e.memset(g.onesb[:], 1.0), [], ['onesb'])
    for i, v in enumerate((NORM_EPS, LN_EPS, SUBLN_EPS, GN_EPS)):
        S.op('dve', lambda e, i=i, v=v: e.memset(g.epsv[:, i:i + 1], v), [], ['epsv'])

    import os
    if os.environ.get('SCAN_LIMIT'):
        g.scan_limit = int(os.environ['SCAN_LIMIT'])
    if upto.startswith('rwonly'):
        g.scan_limit = int(upto[6:] or 0)
        phase_rwkv_scan(g, 0)
        S.finish('sp')
        return nc
    phase_x0(g)
    for l in range(DEPTH):
        phase_mod(g, l)
        phase_h(g, l, 0)
        if upto == 'h':
            break
        phase_conv(g, l)
        if upto == 'conv':
            break
        phase_att(g, l)
        if upto == 'att':
            break
        phase_rwkv_prep(g, l)
        if upto == 'rwprep':
            break
        phase_rwkv_scan(g, l)
        if upto == 'rw':
            break
        phase_merge(g, l)
        phase_mlp(g, l)
        if upto == 'l0':
            break
    for nm in ('ycT', 'yaT', 'yrT', 'hT', 'rS', 'kkS', 'gS', 'bonS', 'vT', 'yfS'):
        if nm in g.dbg:
            S.dma('sp', g.dbg[nm][:, :], getattr(g, nm)[:, :], reads=[nm], writes=['dbg_' + nm])
    for nm, ap in (('kdS0', g.kdS[0]), ('bS0', g.bS[0]), ('wlS0', g.wlS[0])):
        if nm in g.dbg:
            S.dma('sp', g.dbg[nm][:, :], ap[:, :], reads=[nm], writes=['dbg_' + nm])
    S.finish('sp')
    return nc


def nextps(g):
    i = g.psi
    g.psi = (g.psi + 1) % 8
    return i


def phase_x0(g):
    S, nc = g.S, g.nc
    with Phase(g) as A:
        xin = [A('x0in%d' % i, [128, D]) for i in range(2)]
        xst = [A('x0st%d' % i, [128, NCH, 128]) for i in range(2)]
        xTv = g.xT.rearrange('(k p) t -> p k t', p=128)
        for ti in range(T // 128):
            b = ti % 2
            src = g.ctx[ti * 128:(ti + 1) * 128, :] if ti < 2 else g.x[(ti - 2) * 128:(ti - 1) * 128, :]
            S.dma('sp', xin[b][:], src, writes=[('x0in', b)])
            for half in range(2):
                pi = nextps(g)
                ps = g.ps[pi]

                def fn(pe, half=half, ps=ps, b=b):
                    inst = None
                    for j in range(4):
                        k = half * 4 + j
                        inst = pe.transpose(ps[:, j * 128:(j + 1) * 128], xin[b][:, k * 128:(k + 1) * 128], g.identf[:])
                    return inst
                S.op('pe', fn, [('x0in', b), 'identf'], [('ps', pi)])
                dst = xst[b][:, half * 4:(half + 1) * 4, :]
                src_ps = ps[:].rearrange('p (j t) -> p j t', j=4)
                if half == 0:
                    S.op('act', lambda e, dst=dst, s=src_ps: e.copy(dst, s), [('ps', pi)], [('x0st', b, half)])
                else:
                    S.op('dve', lambda e, dst=dst, s=src_ps: e.tensor_copy(dst, s), [('ps', pi)], [('x0st', b, half)])
            S.dma('sp', xTv[:, :, ti * 128:(ti + 1) * 128], xst[b][:], reads=[('x0st', b, 0), ('x0st', b, 1)], writes=['xT'])


def phase_mod(g, l):
    S, nc = g.S, g.nc
    with Phase(g) as A:
        modbs = A('modbs', [128, 48])
        mw = [A('mw%d' % i, [128, NCH, 512]) for i in range(2)]
        ng = A('ng', [128, 2, NCH])
        if l == 0:
            tmp = A('cs_tmp', [128, NCH, 2])
            S.dma('sp', tmp[:], g.cvec[:, :, :], writes=['cs_tmp'])
            S.op('act', lambda e: e.activation(g.cs[:], tmp[:], AF.Sigmoid), ['cs_tmp'], ['cs'])
            S.op('dve', lambda e: e.tensor_tensor(g.cs[:], g.cs[:], tmp[:], ALU.mult), ['cs', 'cs_tmp'], ['cs'])
        S.dma('sp', modbs[:], g.modb[l], writes=['modbs'])
        S.dma('sp', ng[:, 0, :], g.n1g[l], writes=['ng'])
        S.dma('sp', ng[:, 1, :], g.n2g[l], writes=['ng'])
        mwv = g.mod_w[l].rearrange('(k p) c -> p k c', p=128)
        for cg in range(12):
            b = cg % 2
            S.dma('sp', mw[b][:], mwv[:, :, cg * 512:(cg + 1) * 512], writes=[('mw', b)])
            pi = nextps(g)
            ps = g.ps[pi]
            for j in range(4):
                pairs = [(mw[b][:, k, j * 128:(j + 1) * 128], g.cs[:, k, :]) for k in range(NCH)]
                mm_group(S, ps[:, 2 * j:2 * j + 2], pairs, [('mw', b), 'cs'], [('ps', pi)])
            for j in range(4):
                jj = cg * 4 + j
                S.op('dve', lambda e, j=j, jj=jj, ps=ps: e.tensor_scalar(g.modv[:, jj, :], ps[:, 2 * j:2 * j + 2],
                     modbs[:, jj:jj + 1], None, ALU.add), [('ps', pi), 'modbs'], ['modv'])
        for n in range(2):
            sc = g.modv[:, (3 * n + 1) * 8:(3 * n + 2) * 8, :]
            S.op('dve', lambda e, n=n, sc=sc: e.tensor_scalar(g.gs[:, n], sc, 1.0, None, ALU.add), ['modv'], ['gs'])
            S.op('dve', lambda e, n=n: e.tensor_tensor(g.gs[:, n], g.gs[:, n],
                 ng[:, n, :].unsqueeze(2).broadcast_to([128, NCH, 2]), ALU.mult), ['gs', 'ng'], ['gs'])


def rms_stats(g, xb, n, sq, rstd, key_x, key_sq, key_rstd):
    S = g.S
    S.op('act', lambda e: e.activation(sq[:, :, :n], xb[:, :, :n], AF.Square), [key_x], [key_sq])
    pi = nextps(g)
    ps = g.ps[pi]
    mm_group(S, ps[:, :n], [(g.onesf[:], sq[:, k, :n]) for k in range(NCH)], [key_sq, 'onesf'], [('ps', pi)])
    S.op('act', lambda e: e.activation(rstd[:, :n], ps[:, :n], AF.Sqrt, bias=g.epsv[:, 0:1], scale=1.0 / D),
         [('ps', pi), 'epsv'], [key_rstd])
    S.op('dve', lambda e: e.reciprocal(rstd[:, :n], rstd[:, :n]), [key_rstd], [key_rstd])


def phase_h(g, l, n):
    S, nc = g.S, g.nc
    with Phase(g) as A:
        hx = [A('hx%d' % i, [128, NCH, 512]) for i in range(2)]
        hsq = A('hsq', [128, NCH, 512])
        hrs = [A('hrs%d' % i, [128, 512]) for i in range(2)]
        htmp = [A('htmp%d' % i, [128, 512]) for i in range(2)]
        hb = [A('hb%d' % i, [128, NCH, 512], BF16) for i in range(2)]
        xTv = g.xT.rearrange('(k p) t -> p k t', p=128)
        hTv = g.hT.rearrange('(k p) t -> p k t', p=128)
        for bi, (t0, nt) in enumerate(BLKS):
            b = bi % 2
            s = 1 if t0 < CTX else 0
            S.dma('sp', hx[b][:, :, :nt], xTv[:, :, t0:t0 + nt], reads=['xT'], writes=[('hx', b)])
            rms_stats(g, hx[b], nt, hsq, hrs[b], ('hx', b), 'hsq', ('hrs', b))
            for k in range(NCH):
                tb = k % 2
                S.op('dve', lambda e, k=k, tb=tb: e.tensor_tensor(htmp[tb][:, :nt], hx[b][:, k, :nt], hrs[b][:, :nt], ALU.mult),
                     [('hx', b), ('hrs', b)], [('htmp', tb)])
                S.op('act', lambda e, k=k, tb=tb: e.activation(hb[b][:, k, :nt], htmp[tb][:, :nt], AF.Identity,
                     bias=g.modv[:, (3 * n) * 8 + k, s:s + 1], scale=g.gs[:, n, k, s:s + 1]),
                     [('htmp', tb), 'modv', 'gs'], [('hb', b)])
            S.dma('sp', hTv[:, :, t0:t0 + nt], hb[b][:, :, :nt], reads=[('hb', b)], writes=['hT'])


class HLoader:
    def __init__(self, g, A):
        self.g = g
        self.t = [A('hblk%d' % i, [128, NCH, 512], BF16) for i in range(2)]
        self.i = 0

    def load(self, bi):
        g = self.g
        b = self.i
        self.i = (b + 1) % 2
        t0, nt = BLKS[bi]
        hTv = g.hT.rearrange('(k p) t -> p k t', p=128)
        g.S.dma('sp', self.t[b][:, :, :nt], hTv[:, :, t0:t0 + nt], reads=['hT'], writes=[('hblk', b)])
        return self.t[b], ('hblk', b)


def ucol(t):
    return t + 15 if t < CTX else t + 45


def phase_conv(g, l):
    S, nc = g.S, g.nc
    with Phase(g) as A:
        wcv = A('wcv', [128, NCH, 1024], BF16)
        uT = A('uT', [128, 4, T + 60], BF16)
        diag = A('diag', [128, 4, 31, 128], BF16)
        dww = A('dww', [128, 4, 31])
        cvp = A('cvp', [128, 3, 4])
        csg = [A('csg%d' % i, [128, 512]) for i in range(2)]
        cv = A('cv', [128, 4, 512]); cv2 = A('cv2', [128, 4, 512])
        cm = A('cm', [128, 512]); cmsq = A('cmsq', [128, 512]); crs = A('crs', [128, 512])
        ct = [A('ct%d' % i, [128, 512]) for i in range(2)]
        cyb = [A('cyb%d' % i, [128, 4, 512], BF16) for i in range(2)]
        HL = HLoader(g, A)
        S.op('pool', lambda e: e.memset(uT[:], 0.0), [], ['uT'])
        S.dma('pool', wcv[:], g.w_in[l][:, 0:1024].rearrange('(k p) c -> p k c', p=128), writes=['wcv'])
        S.dma('sp', dww[:], g.convw[l], writes=['dww'])
        S.dma('sp', cvp[:], g.convp[l], writes=['cvp'])
        for c in range(4):
            S.op('dve', lambda e, c=c: e.tensor_tensor(diag[:, c], g.identf[:].unsqueeze(1).broadcast_to([128, 31, 128]),
                 dww[:, c, :].unsqueeze(2).broadcast_to([128, 31, 128]), ALU.mult), ['identf', 'dww'], ['diag'])
        for bi, (t0, nt) in enumerate(BLKS):
            hb, hk = HL.load(bi)
            for c in range(4):
                pa, pb = nextps(g), nextps(g)
                mm_group(S, g.ps[pa][:, :nt], [(wcv[:, k, c * 128:(c + 1) * 128], hb[:, k, :nt]) for k in range(NCH)],
                         ['wcv', hk], [('ps', pa)])
                mm_group(S, g.ps[pb][:, :nt], [(wcv[:, k, 512 + c * 128:512 + (c + 1) * 128], hb[:, k, :nt]) for k in range(NCH)],
                         ['wcv', hk], [('ps', pb)])
                sb_ = c % 2
                S.op('act', lambda e, pb=pb, sb_=sb_: e.activation(csg[sb_][:, :nt], g.ps[pb][:, :nt], AF.Sigmoid),
                     [('ps', pb)], [('csg', sb_)])
                S.op('dve', lambda e, pa=pa, sb_=sb_, c=c: e.tensor_tensor(uT[:, c, ucol(t0):ucol(t0) + nt], g.ps[pa][:, :nt],
                     csg[sb_][:, :nt], ALU.mult), [('ps', pa), ('csg', sb_)], ['uT'])
        ycv = g.ycT.rearrange('(k p) t -> p k t', p=128)
        for bi, (t0, nt) in enumerate(BLKS):
            yb = cyb[bi % 2]
            ykey = ('cyb', bi % 2)
            for c in range(4):
                pi = nextps(g)
                base = ucol(t0) - 15
                mm_group(S, g.ps[pi][:, :nt], [(diag[:, c, k, :], uT[:, c, base + k:base + k + nt]) for k in range(31)],
                         ['diag', 'uT'], [('ps', pi)])
                S.op('act', lambda e, pi=pi, c=c: e.activation(cv[:, c, :nt], g.ps[pi][:, :nt], AF.Identity,
                     bias=cvp[:, 0, c:c + 1], scale=1.0), [('ps', pi), 'cvp'], [('cv', c)])
                S.op('act', lambda e, pi=pi, c=c: e.activation(cv2[:, c, :nt], g.ps[pi][:, :nt], AF.Square,
                     bias=cvp[:, 0, c:c + 1], scale=1.0), [('ps', pi), 'cvp'], [('cv2', c)])
            p1, p2 = nextps(g), nextps(g)
            mm_group(S, g.ps[p1][:, :nt], [(g.onesf[:], cv[:, c, :nt]) for c in range(4)], [('cv', c) for c in range(4)] + ['onesf'], [('ps', p1)])
            mm_group(S, g.ps[p2][:, :nt], [(g.onesf[:], cv2[:, c, :nt]) for c in range(4)], [('cv2', c) for c in range(4)] + ['onesf'], [('ps', p2)])
            S.op('act', lambda e: e.activation(cm[:, :nt], g.ps[p1][:, :nt], AF.Identity, scale=1.0 / 512), [('ps', p1)], ['cm'])
            S.op('dve', lambda e: e.tensor_tensor(cmsq[:, :nt], cm[:, :nt], cm[:, :nt], ALU.mult), ['cm'], ['cmsq'])
            S.op('dve', lambda e: e.scalar_tensor_tensor(crs[:, :nt], g.ps[p2][:, :nt], 1.0 / 512, cmsq[:, :nt], ALU.mult, ALU.subtract),
                 [('ps', p2), 'cmsq'], ['crs'])
            S.op('act', lambda e: e.activation(crs[:, :nt], crs[:, :nt], AF.Sqrt, bias=g.epsv[:, 1:2], scale=1.0), ['crs', 'epsv'], ['crs'])
            S.op('dve', lambda e: e.reciprocal(crs[:, :nt], crs[:, :nt]), ['crs'], ['crs'])
            for c in range(4):
                tb = c % 2
                S.op('dve', lambda e, c=c, tb=tb: e.tensor_tensor(ct[tb][:, :nt], cv[:, c, :nt], cm[:, :nt], ALU.subtract),
                     [('cv', c), 'cm'], [('ct', tb)])
                S.op('dve', lambda e, c=c, tb=tb: e.tensor_tensor(ct[tb][:, :nt], ct[tb][:, :nt], crs[:, :nt], ALU.mult),
                     [('ct', tb), 'crs'], [('ct', tb)])
                S.op('act', lambda e, c=c, tb=tb: e.activation(yb[:, c, :nt], ct[tb][:, :nt], AF.Silu,
                     bias=cvp[:, 2, c:c + 1], scale=cvp[:, 1, c:c + 1]), [('ct', tb), 'cvp'], [ykey])
            S.dma('sp', ycv[:, :, t0:t0 + nt], yb[:, :, :nt], reads=[ykey], writes=['ycT'])


def fm(v, nch):
    return np.ascontiguousarray(np.asarray(v, np.float32).reshape(nch, 128).T)


def host_shared(inp):
    m = {}
    m['mod_w'] = np.ascontiguousarray(inp['mod_w'], dtype=np.float32)
    m['modb'] = np.stack([fm(inp['mod_b'][l], 48) for l in range(DEPTH)])
    m['n1g'] = np.stack([fm(inp['norm1_g'][l], NCH) for l in range(DEPTH)])
    m['n2g'] = np.stack([fm(inp['norm2_g'][l], NCH) for l in range(DEPTH)])
    m['fing'] = fm(inp['final_g'], NCH)
    m['ident'] = np.eye(128, dtype=np.float32)
    m['w_in'] = np.ascontiguousarray(inp['w_in'], dtype=np.float32)
    sw = np.arange(1024) ^ 1
    m['wqks'] = np.ascontiguousarray(np.asarray(inp['w_in'])[:, :, 1024:2048][:, :, sw], dtype=np.float32)
    tt = np.arange(SEQ)
    inv = (10000.0 ** (-np.arange(16, dtype=np.float32) / 16)).astype(np.float32)
    ang = np.concatenate([(tt // 64).astype(np.float32)[:, None] * inv, (tt % 64).astype(np.float32)[:, None] * inv], axis=-1)
    pidx = (np.arange(128) % 64) // 2
    cosT = np.cos(ang)[:, pidx].T
    sinT = np.sin(ang)[:, pidx].T * np.where(np.arange(128) % 2 == 0, -1.0, 1.0)[:, None]
    m['rope'] = np.ascontiguousarray(np.stack([cosT, sinT]), dtype=np.float32)
    blk = (np.arange(128) // 64)
    bdm = (blk[:, None] == blk[None, :]).astype(np.float32)
    sel0 = np.repeat((blk == 0).astype(np.float32)[:, None], 128, 1)
    sel1 = np.repeat((blk == 1).astype(np.float32)[:, None], 128, 1)
    m['cmask'] = np.ascontiguousarray(np.stack([bdm, sel0, sel1]))
    m['attp'] = np.ascontiguousarray(np.stack([np.stack([inp[k][l] for k in ('att_lq1', 'att_lk1', 'att_lq2', 'att_lk2')]) for l in range(DEPTH)]), dtype=np.float32)
    m['subg'] = np.ascontiguousarray(inp['att_subln_g'], dtype=np.float32)
    for k in ('p_conv', 'p_att', 'p_rwkv', 'w_out', 'mlp_w1', 'mlp_w2'):
        m[k] = np.ascontiguousarray(inp[k], dtype=np.float32)
    m['rw_w2'] = np.ascontiguousarray(inp['rwkv_w2'], dtype=np.float32)
    m['rw_a2'] = np.ascontiguousarray(inp['rwkv_a2'], dtype=np.float32)
    m['rw_g2'] = np.ascontiguousarray(inp['rwkv_g2'], dtype=np.float32)
    rwp = []
    for l in range(DEPTH):
        cols = [np.asarray(inp['rwkv_shift'][l]).T.reshape(15, 128, 3).transpose(1, 0, 2).reshape(128, 45)]
        cols += [fm(inp['rwkv_w0'][l].reshape(-1), 8), fm(inp['rwkv_a0'][l].reshape(-1), 8)]
        cols += [fm(inp[k][l].reshape(-1), 4) for k in ('rwkv_kk', 'rwkv_ka', 'rwkv_rk', 'rwkv_gn_g', 'rwkv_gn_b')]
        rwp.append(np.concatenate(cols, axis=1))
    m['rwp'] = np.ascontiguousarray(np.stack(rwp), dtype=np.float32)
    ii = np.arange(128)
    lt = (ii[:, None] < ii[None, :]).astype(np.float32); le = (ii[:, None] <= ii[None, :]).astype(np.float32)
    seg = np.repeat((ii != 0).astype(np.float32)[None, :], 128, 0)
    blk = lambda n: (ii[:, None] // n == ii[None, :] // n)
    offm = lambda n: (blk(n) & ~blk(n // 2)).astype(np.float32)
    m['smask'] = np.ascontiguousarray(np.stack([lt, le, lt.T, le.T, seg, blk(16).astype(np.float32), offm(32), offm(64), offm(128)]))
    m['convw'] = np.stack([np.ascontiguousarray(np.asarray(inp['conv_dw_w'][l]).T.reshape(4, 128, 31).transpose(1, 0, 2)) for l in range(DEPTH)])
    m['convp'] = np.stack([np.stack([fm(inp[k][l], 4) for k in ('conv_dw_b', 'conv_ln_g', 'conv_ln_b')], axis=1) for l in range(DEPTH)])
    return m


def host_inputs(inp, b, shared=None):
    m = dict(shared if shared is not None else host_shared(inp))
    m['x'] = np.ascontiguousarray(inp['x'][b], dtype=np.float32)
    m['ctx'] = np.ascontiguousarray(inp['ctx'][b], dtype=np.float32)
    m['cvec'] = np.ascontiguousarray(np.stack([fm(inp['c'][b], NCH), fm(inp['c_ctx'], NCH)], axis=-1))
    return m


def phase_att(g, l):
    S, nc = g.S, g.nc
    lam_init = 0.8 - 0.6 * math.exp(-0.3 * l)
    need_ctx_q = l < DEPTH - 1
    with Phase(g) as A:
        qT = A('qT', [128, 4, T], BF16); kT = A('kT', [128, 4, T], BF16)
        vaug = A('vaug', [128, T // 128, 4, 129], BF16)
        nb = A('nb', [128, 2, 4]); neglam = A('neglam', [128, 1]); gsub = A('gsub', [128, 128])
        A1 = Phase(g)
        wq = A1('wq', [128, NCH, 512], BF16); wqs = A1('wqs', [128, NCH, 512], BF16)
        wk = A1('wk', [128, NCH, 512], BF16); wks = A1('wks', [128, NCH, 512], BF16)
        wv = A1('wv', [128, NCH, 512], BF16)
        cosT = A1('cosT', [128, SEQ]); sinT = A1('sinT', [128, SEQ])
        HL = HLoader(g, A1)
        rt = [A1('rt%d' % i, [128, 512]) for i in range(4)]
        bd = A1('bd', [128, 128], BF16); cmf = A1('cmf', [128, 3, 128])
        stat = A1('stat', [128, 2, 4, len(BLKS)]); stm = A1('stm', [128, 2, 4]); negb = A1('negb', [128, 4])
        lqk = A1('lqk', [128, 4, 64]); lam2 = A1('lam2', [128, 2])

        wsrc = g.w_in[l].rearrange('(k p) c -> p k c', p=128)
        ssrc = g.wqks[l].rearrange('(k p) c -> p k c', p=128)
        S.dma('pool', wq[:], wsrc[:, :, 1024:1536], writes=['wq'])
        S.dma('pool', wk[:], wsrc[:, :, 1536:2048], writes=['wk'])
        S.dma('pool', wv[:], wsrc[:, :, 2048:2560], writes=['wv'])
        S.dma('pool', wqs[:], ssrc[:, :, 0:512], writes=['wqs'])
        S.dma('pool', wks[:], ssrc[:, :, 512:1024], writes=['wks'])
        S.dma('sp', cosT[:], g.rope[0], writes=['cosT'])
        S.dma('sp', sinT[:], g.rope[1], writes=['sinT'])
        S.dma('sp', cmf[:], g.cmask.rearrange('m p c -> p m c'), writes=['cmf'])
        S.op('dve', lambda e: e.tensor_copy(bd[:], cmf[:, 0, :]), ['cmf'], ['bd'])
        S.dma('sp', lqk[:], g.attp[l:l + 1].broadcast_to([128, 4, 64]), writes=['lqk'])
        S.dma('sp', gsub[:], g.subg[l:l + 1, :].broadcast_to([128, 128]), writes=['gsub'])
        S.op('act', lambda e: e.mul(gsub[:], gsub[:], 1.0 - lam_init), ['gsub'], ['gsub'])
        S.op('dve', lambda e: e.tensor_tensor(lqk[:, 0, :], lqk[:, 0, :], lqk[:, 1, :], ALU.mult), ['lqk'], ['lqk'])
        S.op('dve', lambda e: e.tensor_tensor(lqk[:, 2, :], lqk[:, 2, :], lqk[:, 3, :], ALU.mult), ['lqk'], ['lqk'])
        S.op('dve', lambda e: e.reduce_sum(lam2[:, 0:1], lqk[:, 0, :], AX.X), ['lqk'], ['lam2'])
        S.op('dve', lambda e: e.reduce_sum(lam2[:, 1:2], lqk[:, 2, :], AX.X), ['lqk'], ['lam2'])
        S.op('act', lambda e: e.activation(lam2[:], lam2[:], AF.Exp), ['lam2'], ['lam2'])
        S.op('dve', lambda e: e.tensor_tensor(neglam[:], lam2[:, 1:2], lam2[:, 0:1], ALU.subtract), ['lam2'], ['neglam'])
        S.op('dve', lambda e: e.tensor_scalar(neglam[:], neglam[:], -lam_init, None, ALU.add), ['neglam'], ['neglam'])
        S.op('pool', lambda e: e.memset(vaug[:, :, :, 128:129], 1.0), [], ['vaug1'])

        for bi, (t0, nt) in enumerate(BLKS):
            hb, hk = HL.load(bi)
            lat = t0 >= CTX
            tl = t0 - CTX
            for (w, ws, dst, dk, wkey, wskey) in ((wq, wqs, qT, 'qT', 'wq', 'wqs'), (wk, wks, kT, 'kT', 'wk', 'wks')):
                for h in range(4):
                    pa = nextps(g)
                    mm_group(S, g.ps[pa][:, :nt], [(w[:, k, h * 128:(h + 1) * 128], hb[:, k, :nt]) for k in range(NCH)],
                             [wkey, hk], [('ps', pa)])
                    if not lat:
                        S.op('act', lambda e, pa=pa, h=h, dst=dst: e.copy(dst[:, h, t0:t0 + nt], g.ps[pa][:, :nt]), [('ps', pa)], [dk])
                        continue
                    pb = nextps(g)
                    mm_group(S, g.ps[pb][:, :nt], [(ws[:, k, h * 128:(h + 1) * 128], hb[:, k, :nt]) for k in range(NCH)],
                             [wskey, hk], [('ps', pb)])
                    r1, r2 = (0, 1) if h % 2 == 0 else (2, 3)
                    S.op('dve', lambda e, pa=pa, r1=r1: e.tensor_tensor(rt[r1][:, :nt], g.ps[pa][:, :nt], cosT[:, tl:tl + nt], ALU.mult),
                         [('ps', pa), 'cosT'], [('rt', r1)])
                    S.op('dve', lambda e, pb=pb, r2=r2: e.tensor_tensor(rt[r2][:, :nt], g.ps[pb][:, :nt], sinT[:, tl:tl + nt], ALU.mult),
                         [('ps', pb), 'sinT'], [('rt', r2)])
                    S.op('pool', lambda e, r1=r1, r2=r2, h=h, dst=dst: e.tensor_tensor(dst[:, h, t0:t0 + nt], rt[r1][:, :nt], rt[r2][:, :nt], ALU.add),
                         [('rt', r1), ('rt', r2)], [dk])
            for tt in range(nt // 128):
                ti = t0 // 128 + tt
                pv = nextps(g)
                mm_group(S, g.ps[pv][:, :512], [(hb[:, k, tt * 128:(tt + 1) * 128], wv[:, k, :]) for k in range(NCH)],
                         ['wv', hk], [('ps', pv)])
                S.op('act', lambda e, pv=pv, ti=ti: e.copy(vaug[:, ti, :, 0:128], g.ps[pv][:, :].rearrange('p (h d) -> p h d', h=4)),
                     [('ps', pv)], ['vaug'])
        sqb = rt
        for qi, (src, sk) in enumerate(((qT, 'qT'), (kT, 'kT'))):
            for h in range(4):
                for bi, (t0, nt) in enumerate(BLKS):
                    r = (h * len(BLKS) + bi) % 4
                    sq = rt[r][:, 0:256].bitcast(BF16)
                    S.op('act', lambda e, sq=sq, h=h, src=src: e.activation(sq[:, :nt], src[:, h, t0:t0 + nt], AF.Square), [sk], [('rt', r)])
                    pi = nextps(g)
                    mm_group(S, g.ps[pi][:, :nt], [(bd[:], sq[:, :nt])], ['bd', ('rt', r)], [('ps', pi)])
                    S.op('dve', lambda e, pi=pi, h=h, bi=bi, qi=qi: e.reduce_max(stat[:, qi, h, bi:bi + 1], g.ps[pi][:, :nt], AX.X),
                         [('ps', pi)], ['stat'])
        S.op('dve', lambda e: e.reduce_max(stm[:], stat[:], AX.X), ['stat'], ['stm'])
        S.op('dve', lambda e: e.tensor_tensor(negb[:], stm[:, 0, :], stm[:, 1, :], ALU.mult), ['stm'], ['negb'])
        S.op('act', lambda e: e.activation(negb[:], negb[:], AF.Sqrt), ['negb'], ['negb'])
        for c in range(2):
            pi = nextps(g)
            mm_group(S, g.ps[pi][:, 0:4], [(cmf[:, 1 + c, :], negb[:])], ['cmf', 'negb'], [('ps', pi)])
            S.op('act', lambda e, pi=pi, c=c: e.mul(nb[:, c, :], g.ps[pi][:, 0:4], -1.02 * 0.125 / 64.0), [('ps', pi)], ['nb'])

        A1.__exit__(None, None, None)
        pT = [A('pT%d' % i, [128, 512], BF16) for i in range(3)]
        oc = [[A('oc%d_%d' % (c, q), [128, 129]) for q in range(4)] for c in range(2)]
        sm = A('sm', [128, 8]); o0 = A('o0', [128, 128]); aa = A('aa', [128, 128]); junk = A('junk', [128, 128])
        ytok = [A('ytok%d' % q, [128, 512], BF16) for q in range(4)]
        yab = [A('yab%d' % i, [128, 4, 512], BF16) for i in range(2)]
        yav = g.yaT.rearrange('(k p) t -> p k t', p=128)
        pti = [0]
        sbank = [0]

        def attend(q0, nq, kt0, nkt, yslot):
            nqs = nq // 128
            for h in range(4):
                for c in range(2):
                    for kk in range(nkt):
                        kt = kt0 + kk
                        sb_ = 4 + sbank[0]
                        sbank[0] = (sbank[0] + 1) % 3
                        mm_group(S, g.ps[sb_][:, :nq], [(kT[64 * c:64 * c + 64, h, kt * 128:(kt + 1) * 128], qT[64 * c:64 * c + 64, h, q0:q0 + nq])],
                                 ['kT', 'qT'], [('ps', sb_)])
                        pb_ = pti[0]
                        pti[0] = (pti[0] + 1) % 3
                        S.op('act', lambda e, sb_=sb_, pb_=pb_, c=c, h=h: e.activation(pT[pb_][:, :nq], g.ps[sb_][:, :nq], AF.Exp,
                             bias=nb[:, c, h:h + 1], scale=0.125), [('ps', sb_), 'nb'], [('pT', pb_)])
                        for qs in range(nqs):
                            mm_group(S, g.ps[qs][:, 0:129], [(pT[pb_][:, qs * 128:(qs + 1) * 128], vaug[:, kt, h, :])],
                                     [('pT', pb_), 'vaug', 'vaug1'], [('ps', qs)], start=(kk == 0), stop=(kk == nkt - 1))
                    for qs in range(nqs):
                        if qs % 2 == 0:
                            S.op('act', lambda e, qs=qs, c=c: e.copy(oc[c][qs][:], g.ps[qs][:, 0:129]), [('ps', qs)], [('oc', c, qs)])
                        else:
                            S.op('dve', lambda e, qs=qs, c=c: e.tensor_copy(oc[c][qs][:], g.ps[qs][:, 0:129]), [('ps', qs)], [('oc', c, qs)])
                for qs in range(nqs):
                    k0, k1 = ('oc', 0, qs), ('oc', 1, qs)
                    S.op('dve', lambda e, qs=qs: e.reciprocal(sm[:, 0:1], oc[0][qs][:, 128:129]), [k0], ['sm0'])
                    S.op('dve', lambda e, qs=qs: e.reciprocal(sm[:, 1:2], oc[1][qs][:, 128:129]), [k1], ['sm1'])
                    S.op('dve', lambda e: e.tensor_tensor(sm[:, 2:3], sm[:, 1:2], neglam[:], ALU.mult), ['sm1', 'neglam'], ['sm2'])
                    S.op('dve', lambda e, qs=qs: e.tensor_scalar(o0[:], oc[0][qs][:, 0:128], sm[:, 0:1], None, ALU.mult), [k0, 'sm0'], ['o0'])
                    S.op('dve', lambda e, qs=qs: e.scalar_tensor_tensor(aa[:], oc[1][qs][:, 0:128], sm[:, 2:3], o0[:], ALU.mult, ALU.add),
                         [k1, 'sm2', 'o0'], ['aa'])
                    S.op('act', lambda e: e.activation(junk[:], aa[:], AF.Square, accum_out=sm[:, 3:4]), ['aa'], ['junk', 'sm3'])
                    S.op('act', lambda e: e.activation(sm[:, 4:5], sm[:, 3:4], AF.Sqrt, bias=g.epsv[:, 2:3], scale=1.0 / 128), ['sm3', 'epsv'], ['sm4'])
                    S.op('dve', lambda e: e.reciprocal(sm[:, 5:6], sm[:, 4:5]), ['sm4'], ['sm5'])
                    S.op('dve', lambda e, qs=qs, h=h: e.scalar_tensor_tensor(ytok[qs][:, h * 128:(h + 1) * 128], aa[:], sm[:, 5:6], gsub[:], ALU.mult, ALU.mult),
                         ['aa', 'sm5', 'gsub'], [('ytok', qs)])
            yb = yab[yslot % 2]
            ykey = ('yab', yslot % 2)
            for qs in range(nqs):
                psb = g.ps[7][:, :].bitcast(BF16)

                def fn(pe, qs=qs, psb=psb):
                    inst = None
                    for h in range(4):
                        inst = pe.transpose(psb[:, h * 128:(h + 1) * 128], ytok[qs][:, h * 128:(h + 1) * 128], g.identb[:])
                    return inst
                S.op('pe', fn, [('ytok', qs), 'identb'], [('ps', 7)])
                S.op('act', lambda e, qs=qs, psb=psb, yb=yb: e.copy(yb[:, :, qs * 128:(qs + 1) * 128], psb[:, 0:512].rearrange('p (h q) -> p h q', h=4)),
                     [('ps', 7)], [ykey])
            S.dma('sp', yav[:, :, q0:q0 + nq], yb[:, :, :nq], reads=[ykey], writes=['yaT'])

        slot = 0
        if need_ctx_q:
            attend(0, CTX, 0, CTX // 128, slot)
            slot += 1
        for qb in range(SEQ // 512):
            attend(CTX + qb * 512, 512, 0, T // 128, slot)
            slot += 1


RW0 = 2560
P_SH, P_W0, P_A0, P_KK, P_KA, P_RK, P_GG, P_GB, P_OMKA, NRWP = 0, 45, 53, 61, 65, 69, 73, 77, 81, 85
DECAY_C = -math.exp(-0.5)


def phase_rwkv_prep(g, l):
    S, nc = g.S, g.nc
    with Phase(g) as A:
        wrw = A('wrw', [128, NCH, 1920], BF16)
        w2b = A('w2b', [128, 512], BF16); a2b = A('a2b', [128, 512], BF16); g2b = A('g2b', [128, 512], BF16)
        rwp = A('rwp', [128, NRWP])
        bdf = A('bdf', [128, 128])
        hbx = [A('hbx%d' % i, [128, NCH, 514], BF16) for i in range(2)]
        zx = [A('zx%d' % i, [128, 514]) for i in range(3)]
        zc = A('zc', [128, 15, 512])
        tw = A('tw', [128, 512], BF16); ab = A('ab', [128, 512], BF16); sg = A('sg', [128, 512], BF16)
        kkt = A('kkt', [128, 4, 512]); kds = A('kds', [128, 4, 512])
        t1 = [A('rt1_%d' % i, [128, 512]) for i in range(3)]
        ob = {n: [A('ob_%s%d' % (n, i), [128, 4, 512], BF16) for i in range(1)] * 2 for n in ('r', 'kk', 'kd0', 'kd1', 'b0', 'b1', 'g', 'bon')}
        owl = {d: [A('owl%d_%d' % (d, i), [128, 4, 512]) for i in range(1)] * 2 for d in range(2)}
        vtile = [A('vtile%d' % i, [128, 512], BF16) for i in range(2)]

        wsrc = g.w_in[l].rearrange('(k p) c -> p k c', p=128)
        S.dma('pool', wrw[:], wsrc[:, :, RW0:RW0 + 1920], writes=['wrw'])
        S.dma('pool', w2b[:], g.rw_w2[l].rearrange('d m c -> (d m) c'), writes=['w2b'])
        S.dma('pool', a2b[:], g.rw_a2[l].rearrange('d m c -> (d m) c'), writes=['a2b'])
        S.dma('pool', g2b[:], g.rw_g2[l], writes=['g2b'])
        S.dma('sp', rwp[:, 0:P_OMKA], g.rwp[l], writes=['rwp'])
        S.dma('sp', bdf[:], g.cmask[0], writes=['bdf'])
        S.op('dve', lambda e: e.tensor_scalar(rwp[:, P_OMKA:P_OMKA + 4], rwp[:, P_KA:P_KA + 4], -1.0, 1.0, ALU.mult, ALU.add), ['rwp'], ['rwp'])
        hTv = g.hT.rearrange('(k p) t -> p k t', p=128)
        fmv = lambda ap: ap.rearrange('(k p) t -> p k t', p=128)
        for bi, (t0, nt) in enumerate(BLKS):
            b = bi % 2
            hb = hbx[b]
            hk = ('hbx', b)
            s0, s1 = (0, CTX) if t0 < CTX else (CTX, T)
            lo, hi = max(s0, t0 - 1), min(s1, t0 + nt + 1)
            if lo == t0:
                S.op('pool', lambda e, hb=hb: e.memset(hb[:, :, 0:1], 0.0), [], [hk])
            if hi == t0 + nt:
                S.op('pool', lambda e, hb=hb: e.memset(hb[:, :, nt + 1:nt + 2], 0.0), [], [hk])
            S.dma('sp', hb[:, :, 1 - (t0 - lo):1 + (hi - t0)], hTv[:, :, lo:hi], reads=['hT'], writes=[hk])
            ph = nextps(g)
            for ch in range(15):
                pm = nextps(g)
                if pm == ph:
                    pm = nextps(g)
                wsl = lambda k, ch=ch: wrw[:, k, ch * 128:(ch + 1) * 128]
                mm_group(S, g.ps[pm][:, :nt], [(wsl(k), hb[:, k, 1:1 + nt]) for k in range(NCH)], ['wrw', hk], [('ps', pm)])
                mm_group(S, g.ps[ph][:, 2 * ch:2 * ch + 1], [(wsl(k), hb[:, k, 0:1]) for k in range(NCH)], ['wrw', hk], [('ps', ph)])
                mm_group(S, g.ps[ph][:, 2 * ch + 1:2 * ch + 2], [(wsl(k), hb[:, k, nt + 1:nt + 2]) for k in range(NCH)], ['wrw', hk], [('ps', ph)])
                z = zx[ch % 3]
                zk = ('zx', ch % 3)
                S.op('act', lambda e, z=z, pm=pm: e.copy(z[:, 1:1 + nt], g.ps[pm][:, :nt]), [('ps', pm)], [zk])
                S.op('act', lambda e, z=z, ch=ch: e.copy(z[:, 0:1], g.ps[ph][:, 2 * ch:2 * ch + 1]), [('ps', ph)], [zk])
                S.op('act', lambda e, z=z, ch=ch: e.copy(z[:, nt + 1:nt + 2], g.ps[ph][:, 2 * ch + 1:2 * ch + 2]), [('ps', ph)], [zk])
                sh = lambda j, ch=ch: rwp[:, P_SH + ch * 3 + j:P_SH + ch * 3 + j + 1]
                S.op('dve', lambda e, z=z, ch=ch, sh=sh: e.tensor_scalar(zc[:, ch, :nt], z[:, 1:1 + nt], sh(1), None, ALU.mult), [zk, 'rwp'], [('zc', ch)])
                S.op('dve', lambda e, z=z, ch=ch, sh=sh: e.scalar_tensor_tensor(zc[:, ch, :nt], z[:, 0:nt], sh(0), zc[:, ch, :nt], ALU.mult, ALU.add),
                     [zk, 'rwp', ('zc', ch)], [('zc', ch)])
                S.op('dve', lambda e, z=z, ch=ch, sh=sh: e.scalar_tensor_tensor(zc[:, ch, :nt], z[:, 2:nt + 2], sh(2), zc[:, ch, :nt], ALU.mult, ALU.add),
                     [zk, 'rwp', ('zc', ch)], [('zc', ch)])
            o = {n: ob[n][0] for n in ob}
            okey = {n: ('ob', n, 0) for n in ob}
            S.op('act', lambda e: e.copy(o['r'][:, :, :nt], zc[:, 0:4, :nt]), [('zc', c) for c in range(4)], [okey['r']])
            S.dma('sp', fmv(g.rS)[:, :, t0:t0 + nt], o['r'][:, :, :nt], reads=[okey['r']], writes=['rS'])
            S.op('act', lambda e: e.activation(tw[:, :nt], zc[:, 12, :nt], AF.Tanh), [('zc', 12)], ['tw'])
            S.op('act', lambda e: e.copy(ab[:, :nt], zc[:, 13, :nt]), [('zc', 13)], ['ab'])
            S.op('act', lambda e: e.activation(sg[:, :nt], zc[:, 14, :nt], AF.Sigmoid), [('zc', 14)], ['sg'])
            for c in range(4):
                ti = c % 3
                S.op('act', lambda e, c=c: e.activation(kkt[:, c, :nt], zc[:, 4 + c, :nt], AF.Identity, scale=rwp[:, P_KK + c:P_KK + c + 1]),
                     [('zc', 4 + c), 'rwp'], [('kkt', c)])
                S.op('act', lambda e, c=c, ti=ti: e.activation(t1[ti][:, :nt], kkt[:, c, :nt], AF.Square), [('kkt', c)], [('t1', ti)])
                pi = nextps(g)
                mm_group(S, g.ps[pi][:, :nt], [(bdf[:], t1[ti][:, :nt])], ['bdf', ('t1', ti)], [('ps', pi)])
                S.op('dve', lambda e, pi=pi, ti=ti: e.tensor_scalar(t1[ti][:, :nt], g.ps[pi][:, :nt], 1e-24, None, ALU.max), [('ps', pi)], [('t1', ti)])
                S.op('act', lambda e, ti=ti: e.activation(t1[ti][:, :nt], t1[ti][:, :nt], AF.Sqrt), [('t1', ti)], [('t1', ti)])
                S.op('dve', lambda e, ti=ti: e.reciprocal(t1[ti][:, :nt], t1[ti][:, :nt]), [('t1', ti)], [('t1', ti)])
                S.op('dve', lambda e, c=c, ti=ti: e.tensor_tensor(kkt[:, c, :nt], kkt[:, c, :nt], t1[ti][:, :nt], ALU.mult), [('kkt', c), ('t1', ti)], [('kkt', c)])
            S.op('act', lambda e: e.copy(o['kk'][:, :, :nt], kkt[:, :, :nt]), [('kkt', c) for c in range(4)], [okey['kk']])
            S.dma('sp', fmv(g.kkS)[:, :, t0:t0 + nt], o['kk'][:, :, :nt], reads=[okey['kk']], writes=['kkS'])
            for d in range(2):
                kdn, bn = 'kd%d' % d, 'b%d' % d
                for c in range(4):
                    pu, pa = nextps(g), nextps(g)
                    mm_group(S, g.ps[pu][:, :nt], [(w2b[64 * d:64 * d + 64, c * 128:(c + 1) * 128], tw[64 * d:64 * d + 64, :nt])], ['w2b', 'tw'], [('ps', pu)])
                    mm_group(S, g.ps[pa][:, :nt], [(a2b[64 * d:64 * d + 64, c * 128:(c + 1) * 128], ab[64 * d:64 * d + 64, :nt])], ['a2b', 'ab'], [('ps', pa)])
                    wl = owl[d][0]
                    wk_ = ('owl', d, 0)
                    S.op('act', lambda e, pu=pu, c=c, d=d, wl=wl: e.activation(wl[:, c, :nt], g.ps[pu][:, :nt], AF.Sigmoid,
                         bias=rwp[:, P_W0 + d * 4 + c:P_W0 + d * 4 + c + 1], scale=1.0), [('ps', pu), 'rwp'], [wk_])
                    S.op('pool', lambda e, c=c, wl=wl: e.tensor_scalar(wl[:, c, :nt], wl[:, c, :nt], DECAY_C, None, ALU.mult), [wk_], [wk_])
                    ta, tb = t1[0], t1[1]
                    S.op('act', lambda e, pa=pa, c=c, d=d: e.activation(ta[:, :nt], g.ps[pa][:, :nt], AF.Sigmoid,
                         bias=rwp[:, P_A0 + d * 4 + c:P_A0 + d * 4 + c + 1], scale=1.0), [('ps', pa), 'rwp'], [('t1', 0)])
                    S.op('pool', lambda e, c=c, bn=bn: e.tensor_tensor(o[bn][:, c, :nt], kkt[:, c, :nt], ta[:, :nt], ALU.mult),
                         [('kkt', c), ('t1', 0)], [okey[bn]])
                    S.op('dve', lambda e, c=c: e.tensor_scalar(tb[:, :nt], ta[:, :nt], rwp[:, P_KA + c:P_KA + c + 1], rwp[:, P_OMKA + c:P_OMKA + c + 1], ALU.mult, ALU.add),
                         [('t1', 0), 'rwp'], [('t1', 1)])
                    S.op('dve', lambda e, c=c: e.tensor_tensor(tb[:, :nt], tb[:, :nt], zc[:, 4 + c, :nt], ALU.mult), [('t1', 1), ('zc', 4 + c)], [('t1', 1)])
                    S.op('act', lambda e, c=c, kdn=kdn: e.copy(o[kdn][:, c, :nt], tb[:, :nt]), [('t1', 1)], [okey[kdn]])
                    if d == 0:
                        S.op('pool', lambda e, c=c: e.tensor_copy(kds[:, c, :nt], tb[:, :nt]), [('t1', 1)], [('kds', c)])
                    else:
                        S.op('pool', lambda e, c=c: e.tensor_tensor(kds[:, c, :nt], kds[:, c, :nt], tb[:, :nt], ALU.add), [('t1', 1), ('kds', c)], [('kds', c)])
                S.dma('sp', fmv(g.wlS[d])[:, :, t0:t0 + nt], owl[d][0][:, :, :nt], reads=[('owl', d, 0)], writes=['wlS%d' % d])
                S.dma('sp', fmv(g.kdS[d])[:, :, t0:t0 + nt], o[kdn][:, :, :nt], reads=[okey[kdn]], writes=['kdS%d' % d])
                S.dma('sp', fmv(g.bS[d])[:, :, t0:t0 + nt], o[bn][:, :, :nt], reads=[okey[bn]], writes=['bS%d' % d])
            for c in range(4):
                pg = nextps(g)
                mm_group(S, g.ps[pg][:, :nt], [(g2b[:, c * 128:(c + 1) * 128], sg[:, :nt])], ['g2b', 'sg'], [('ps', pg)])
                S.op('act', lambda e, pg=pg, c=c: e.copy(o['g'][:, c, :nt], g.ps[pg][:, :nt]), [('ps', pg)], [okey['g']])
            S.dma('sp', fmv(g.gS)[:, :, t0:t0 + nt], o['g'][:, :, :nt], reads=[okey['g']], writes=['gS'])
            for c in range(4):
                tc_ = t1[2]
                S.op('dve', lambda e, c=c: e.scalar_tensor_tensor(tc_[:, :nt], zc[:, c, :nt], rwp[:, P_RK + c:P_RK + c + 1], kds[:, c, :nt], ALU.mult, ALU.mult),
                     [('zc', c), 'rwp', ('kds', c)], [('t1', 2)])
                pi = nextps(g)
                mm_group(S, g.ps[pi][:, :nt], [(bdf[:], tc_[:, :nt])], ['bdf', ('t1', 2)], [('ps', pi)])
                S.op('dve', lambda e, pi=pi, c=c: e.tensor_tensor(o['bon'][:, c, :nt], g.ps[pi][:, :nt], zc[:, 8 + c, :nt], ALU.mult),
                     [('ps', pi), ('zc', 8 + c)], [okey['bon']])
            S.dma('sp', fmv(g.bonS)[:, :, t0:t0 + nt], o['bon'][:, :, :nt], reads=[okey['bon']], writes=['bonS'])
            for tt in range(nt // 128):
                pv = nextps(g)

                def fn(pe, pv=pv, tt=tt):
                    inst = None
                    for c in range(4):
                        inst = pe.transpose(g.ps[pv][:, c * 128:(c + 1) * 128], zc[:, 8 + c, tt * 128:(tt + 1) * 128], g.identf[:])
                    return inst
                S.op('pe', fn, [('zc', 8 + c) for c in range(4)] + ['identf'], [('ps', pv)])
                vb = (t0 // 128 + tt) % 2
                S.op('act', lambda e, pv=pv, vb=vb: e.copy(vtile[vb][:], g.ps[pv][:, :]), [('ps', pv)], [('vtile', vb)])
                S.dma('sp', g.vT[t0 + tt * 128:t0 + (tt + 1) * 128, :], vtile[vb][:], reads=[('vtile', vb)], writes=['vT'])


def phase_rwkv_scan(g, l):
    S, nc = g.S, g.nc
    import os
    STAGE = int(os.environ.get('SCAN_STAGE', '99'))
    NCK = T // 128
    with Phase(g) as A:
        smf = A('smf', [128, 9, 128])
        msk = [A('msk%d' % i, [128, 128], BF16) for i in range(4)]
        NTF = [A('NTF%d' % q, [128, 4, 128], BF16) for q in range(2)]
        NF = [A('NF%d' % q, [128, 4, 128], BF16) for q in range(2)]
        NoT = [[A('NoT%d_%d' % (q, i), [128, 4, 128], BF16) for i in range(3)] for q in range(2)]
        MT = [[A('MT%d_%d' % (q, i), [128, 4, 128], BF16) for i in range(2)] for q in range(2)]
        Wb = [A('Wb%d' % q, [128, 4, 128], BF16) for q in range(2)]
        m4 = [A('m4_%d' % d, [128, 4, 128], BF16) for d in range(2)]
        mnt = [A('mnt%d' % d, [128, 128], BF16) for d in range(2)]
        bdf = A('bdf', [128, 128]); rwp = A('rwp', [128, P_OMKA])
        ld = {n: [A('ld_%s%d' % (n, i), [128, 4, 128], BF16) for i in range(2)] for n in ('r', 'kk', 'kd', 'b')}
        cwl = [A('cwl%d' % i, [128, 4, 128]) for i in range(2)]
        cv = [A('cv%d' % i, [128, 512], BF16) for i in range(2)]
        cum = A('cum', [128, 4, 128]); cumx = A('cumx', [128, 4, 128]); ep = A('ep', [128, 4, 128]); en = A('en', [128, 4, 128])
        AR = A('AR', [128, 4, 256], BF16); kt = A('kt', [128, 4, 128], BF16); bt = A('bt', [128, 4, 128], BF16)
        gam = A('gam', [128, 4, 1])
        AtT = A('AtT', [128, 8, 64], BF16); BtT = A('BtT', [128, 8, 64], BF16); KtT = A('KtT', [128, 8, 64], BF16)
        AB = [A('AB%d' % h, [128, 4, 128], BF16) for h in range(8)]
        X = [[A('X%d_%d' % (q, i), [128, 4, 128], BF16) for i in range(2)] for q in range(2)]
        XT = [[A('XT%d_%d' % (q, i), [128, 4, 128], BF16) for i in range(2)] for q in range(2)]
        M = [[A('M%d_%d' % (q, i), [128, 4, 128], BF16) for i in range(2)] for q in range(2)]
        G2 = A('G2', [128, 8, 64], BF16); U = A('U', [128, 8, 64]); P1 = A('P1', [128, 4, 128]); Et = A('Et', [128, 8, 64], BF16)
        ST = A('ST', [128, 4, 64]); STb = [A('STb%d' % i, [128, 4, 64], BF16) for i in range(2)]; stt = A('stt', [128, 4, 64])
        ysb = [A('ysb%d' % i, [128, 4, 128]) for i in range(2)]
        yf = A('yf', [128, 4, 128]); cbon = A('cbon', [128, 4, 128], BF16); cg = A('cg', [128, 4, 128], BF16)
        ysq = A('ysq', [128, 4, 128]); gmean = A('gmean', [128, 4, 128]); grs = A('grs', [128, 4, 128]); gt_ = A('gt_', [128, 4, 128])
        yob = [A('yob%d' % i, [128, 4, 128], BF16) for i in range(2)]

        S.dma('sp', smf[:], g.smask[0:9].rearrange('m p c -> p m c'), writes=['smf'])
        for i in range(4):
            S.op('dve', lambda e, i=i: e.tensor_copy(msk[i][:], smf[:, 5 + i, :]), ['smf'], [('msk', i)])
        S.dma('sp', bdf[:], g.cmask[0], writes=['bdf'])
        S.dma('sp', rwp[:], g.rwp[l], writes=['rwp'])
        for d in range(2):
            for j in range(4):
                S.op('dve', lambda e, d=d, j=j: e.tensor_copy(m4[d][:, j, :], smf[:, 2 * d + (j % 2), :]), ['smf'], [('m4', d)])
        S.op('dve', lambda e: e.tensor_copy(mnt[0][:], smf[:, 2, :]), ['smf'], [('mnt', 0)])
        S.op('dve', lambda e: e.tensor_copy(mnt[1][:], smf[:, 0, :]), ['smf'], [('mnt', 1)])
        fmc = lambda ap, c0: ap.rearrange('(k p) t -> p k t', p=128)[:, :, c0:c0 + 128]
        f2 = lambda t: t[:].rearrange('p a b -> p (a b)')
        segb = smf[:, 4, :].unsqueeze(1).broadcast_to([128, 4, 128])
        it = [0]
        for d in range(2):
            order = list(range(NCK)) if d == 0 else [1, 0] + list(range(NCK - 1, 1, -1))
            if getattr(g, 'scan_limit', None):
                order = order[:g.scan_limit]
            S.op('pool', lambda e: e.memset(ST[:], 0.0), [], ['ST'])
            sbi = 0
            S.op('pool', lambda e: e.memset(STb[0][:], 0.0), [], [('STb', 0)])
            for ci in order:
                c0 = ci * 128
                b = it[0] % 2
                it[0] += 1
                cr, ckk, ckd, cb = ld['r'][b], ld['kk'][b], ld['kd'][b], ld['b'][b]
                lk = lambda n: ('ld', n, b)
                S.dma('sp', cr[:], fmc(g.rS, c0), reads=['rS'], writes=[lk('r')])
                S.dma('sp', ckk[:], fmc(g.kkS, c0), reads=['kkS'], writes=[lk('kk')])
                S.dma('sp', ckd[:], fmc(g.kdS[d], c0), reads=['kdS%d' % d], writes=[lk('kd')])
                S.dma('sp', cb[:], fmc(g.bS[d], c0), reads=['bS%d' % d], writes=[lk('b')])
                S.dma('sp', cwl[b][:], fmc(g.wlS[d], c0), reads=['wlS%d' % d], writes=[('cwl', b)])
                S.dma('sp', cv[b][:], g.vT[c0:c0 + 128, :], reads=['vT'], writes=[('cv', b)])
                vk = ('cv', b)
                cvb = cv[b]
                S.op('pool', lambda e: e.tensor_copy(cumx[:], segb), ['smf'], ['cumx'])
                S.op('dve', lambda e, b=b: e.tensor_tensor_scan(f2(cum), f2(cumx), f2(cwl[b]), 0.0, ALU.mult, ALU.add), ['cumx', ('cwl', b)], ['cum'])
                if d == 0:
                    S.op('dve', lambda e, b=b: e.tensor_tensor(cumx[:], cum[:], cwl[b][:], ALU.subtract), ['cum', ('cwl', b)], ['cumx'])
                else:
                    S.op('dve', lambda e: e.tensor_tensor(cumx[:], cum[:, :, 127:128].broadcast_to([128, 4, 128]), cum[:], ALU.subtract), ['cum'], ['cumx'])
                    S.op('dve', lambda e, b=b: e.tensor_tensor(cum[:], cumx[:], cwl[b][:], ALU.add), ['cumx', ('cwl', b)], ['cum'])
                S.op('act', lambda e: e.activation(ep[:], cum[:], AF.Exp), ['cum'], ['ep'])
                S.op('act', lambda e: e.activation(en[:], cum[:], AF.Exp, scale=-1.0), ['cum'], ['en'])
                gcol = 127 if d == 0 else 0
                S.op('act', lambda e: e.copy(gam[:], ep[:, :, gcol:gcol + 1]), ['ep'], ['gam'])
                S.op('dve', lambda e, cr=cr: e.tensor_tensor(AR[:, :, 128:256], cr[:], ep[:], ALU.mult), [lk('r'), 'ep'], ['AR'])
                S.op('act', lambda e: e.activation(ep[:], cumx[:], AF.Exp), ['cumx', 'AR', 'gam'], ['ep'])
                S.op('dve', lambda e, ckk=ckk: e.scalar_tensor_tensor(AR[:, :, 0:128], ckk[:], -1.0, ep[:], ALU.mult, ALU.mult), [lk('kk'), 'ep'], ['AR'])
                S.op('pool', lambda e, ckd=ckd: e.tensor_tensor(kt[:], ckd[:], en[:], ALU.mult), [lk('kd'), 'en'], ['kt'])
                S.op('pool', lambda e, cb=cb: e.tensor_tensor(bt[:], cb[:], en[:], ALU.mult), [lk('b'), 'en'], ['bt'])
                if STAGE <= 1:
                    continue
                for (srcf, dst, dk, sk) in ((lambda hp: AR[:, hp, 0:128], AtT, 'AtT', 'AR'), (lambda hp: bt[:, hp, :], BtT, 'BtT', 'bt'), (lambda hp: kt[:, hp, :], KtT, 'KtT', 'kt')):
                    pi = nextps(g)
                    psb = g.ps[pi][:, :].bitcast(BF16)

                    def fn(pe, srcf=srcf, psb=psb):
                        inst = None
                        for hp in range(4):
                            inst = pe.transpose(psb[:, hp * 128:(hp + 1) * 128], srcf(hp), g.identb[:])
                        return inst
                    S.op('pe', fn, [sk, 'identb'], [('ps', pi)])
                    S.op('act', lambda e, dst=dst, psb=psb: e.copy(dst[:].rearrange('p h j -> p (h j)'), psb[:, 0:512]), [('ps', pi)], [dk])
                if STAGE <= 2:
                    continue
                ABH = int(os.environ.get('AB_H', '8')); ABM = int(os.environ.get('AB_MODE', '9'))
                for h in range(ABH):
                    hp, hb = h // 2, h % 2
                    sl = slice(hb * 64, hb * 64 + 64)
                    pi = nextps(g)
                    mm_group(S, g.ps[pi][:, 0:256], [(bt[sl, hp, :], AR[sl, hp, :])], ['bt', 'AR'], [('ps', pi)])
                    if ABM <= 1:
                        continue
                    mm_group(S, g.ps[pi][:, 256:512], [(kt[sl, hp, :], AR[sl, hp, :])], ['kt', 'AR'], [('ps', pi)])
                    if ABM <= 2:
                        continue
                    S.op('dve', lambda e, h=h, pi=pi: e.tensor_tensor(f2(AB[h]), g.ps[pi][:, :], f2(m4[d]), ALU.mult), [('ps', pi), ('m4', d)], [('AB', h)])
                SUB = int(os.environ.get('SCAN_SUB', '9'))
                if SUB <= 0:
                    continue
                b4 = lambda t: t[:].unsqueeze(1).broadcast_to([128, 4, 128])
                for q in range(2):
                    pi = nextps(g)
                    for j in range(4):
                        h = 2 * j + q
                        hp, hb = j, q
                        sl = slice(hb * 64, hb * 64 + 64)
                        mm_group(S, g.ps[pi][:, j * 128:(j + 1) * 128], [(AR[sl, hp, 0:128], bt[sl, hp, :])], ['bt', 'AR'], [('ps', pi)])
                    S.op('dve', lambda e, q=q, pi=pi: e.tensor_tensor(NTF[q][:], g.ps[pi][:, :].rearrange('p (j s) -> p j s', j=4), b4(mnt[d]), ALU.mult),
                         [('ps', pi), ('mnt', d)], [('NTF', q)])
                    for j in range(4):
                        h = 2 * j + q
                        S.op('pool', lambda e, q=q, j=j, h=h: e.tensor_copy(NF[q][:, j, :], AB[h][:, 0, :]), [('AB', h)], [('NF', q)])
                    S.op('pool', lambda e, q=q: e.tensor_tensor(X[q][0][:], NF[q][:], b4(msk[0]), ALU.mult), [('NF', q), ('msk', 0)], [('X', q, 0)])
                    S.op('dve', lambda e, q=q: e.tensor_tensor(XT[q][0][:], NTF[q][:], b4(msk[0]), ALU.mult), [('NTF', q), ('msk', 0)], [('XT', q, 0)])
                    S.op('pool', lambda e, q=q: e.tensor_tensor(M[q][0][:], X[q][0][:], b4(g.identb), ALU.add), [('X', q, 0), 'identb'], [('M', q, 0)])
                    S.op('pool', lambda e, q=q: e.tensor_tensor(MT[q][0][:], XT[q][0][:], b4(g.identb), ALU.add), [('XT', q, 0), 'identb'], [('MT', q, 0)])
                    for i in range(3):
                        S.op('pool', lambda e, q=q, i=i: e.tensor_tensor(NoT[q][i][:], NTF[q][:], b4(msk[1 + i]), ALU.mult), [('NTF', q), ('msk', 1 + i)], [('NoT', q, i)])
                if d == 0 and ci == 0:
                    for nm, tl, kk_ in (('d_XT0', NTF[0], ('NTF', 0)), ('d_X0', NF[0], ('NF', 0)), ('d_Mi', M[0][0], ('M', 0, 0))):
                        if nm in g.dbg:
                            S.dma('sp', g.dbg[nm][:, :], tl[:].rearrange('p a b -> p (a b)'), reads=[kk_], writes=['dbg' + nm])
                def mm4(q, lh, rh, rk):
                    pi = nextps(g)
                    for j in range(4):
                        mm_group(S, g.ps[pi][:, j * 128:(j + 1) * 128], [(lh[:, j, :], rh[:, j, :])], rk, [('ps', pi)])
                    return pi
                cur = 0
                for k in range(1, 4):
                    nx = 1 - cur
                    for q in range(2):
                        Xp, XTp, Mp, MTp = X[q][cur], XT[q][cur], M[q][cur], MT[q][cur]
                        Xn, XTn, Mn, MTn = X[q][nx], XT[q][nx], M[q][nx], MT[q][nx]
                        kX, kXT, kM, kMT = ('X', q, cur), ('XT', q, cur), ('M', q, cur), ('MT', q, cur)
                        nX, nXT, nM, nMT = ('X', q, nx), ('XT', q, nx), ('M', q, nx), ('MT', q, nx)
                        if k < 3:
                            pi = mm4(q, XTp, Xp, [kX, kXT])
                            S.op('act', lambda e, Xn=Xn, pi=pi: e.copy(f2(Xn), g.ps[pi][:, :]), [('ps', pi)], [nX])
                        pi = mm4(q, Xp, XTp, [kX, kXT])
                        S.op('act', lambda e, XTn=XTn, pi=pi: e.copy(f2(XTn), g.ps[pi][:, :]), [('ps', pi)], [nXT])
                        pi = mm4(q, XTn, Mp, [nXT, kM])
                        S.op('dve', lambda e, Mn=Mn, Mp=Mp, pi=pi: e.tensor_tensor(f2(Mn), g.ps[pi][:, :], f2(Mp), ALU.add), [('ps', pi), kM], [nM])
                        pi = mm4(q, Mp, XTn, [nXT, kM])
                        S.op('dve', lambda e, MTn=MTn, MTp=MTp, pi=pi: e.tensor_tensor(f2(MTn), g.ps[pi][:, :], f2(MTp), ALU.add), [('ps', pi), kMT], [nMT])
                    cur = nx
                for i in range(3):
                    nx = 1 - cur
                    for q in range(2):
                        Dp, DTp, Dn, DTn = M[q][cur], MT[q][cur], M[q][nx], MT[q][nx]
                        kD, kDT, nD, nDT = ('M', q, cur), ('MT', q, cur), ('M', q, nx), ('MT', q, nx)
                        pi = mm4(q, NoT[q][i], Dp, [('NoT', q, i), kD])
                        S.op('act', lambda e, q=q, pi=pi: e.copy(f2(Wb[q]), g.ps[pi][:, :]), [('ps', pi)], [('Wb', q)])
                        pi = mm4(q, DTp, Wb[q], [kDT, ('Wb', q)])
                        S.op('dve', lambda e, Dn=Dn, Dp=Dp, pi=pi: e.tensor_tensor(f2(Dn), g.ps[pi][:, :], f2(Dp), ALU.add), [('ps', pi), kD], [nD])
                        if i < 2:
                            pi = mm4(q, Wb[q], DTp, [kDT, ('Wb', q)])
                            S.op('dve', lambda e, DTn=DTn, DTp=DTp, pi=pi: e.tensor_tensor(f2(DTn), g.ps[pi][:, :], f2(DTp), ALU.add), [('ps', pi), kDT], [nDT])
                    cur = nx
                Mf = [M[q][cur] for q in range(2)]
                kMf = [('M', q, cur) for q in range(2)]
                if STAGE <= 4:
                    continue
                pi = nextps(g)
                for h in range(8):
                    mm_group(S, g.ps[pi][:, h * 64:(h + 1) * 64], [(AB[h][:, 2, :], cvb[:, h * 64:(h + 1) * 64])], [('AB', h), vk], [('ps', pi)])
                S.op('act', lambda e, pi=pi: e.copy(G2[:].rearrange('p h i -> p (h i)'), g.ps[pi][:, :]), [('ps', pi)], ['G2'])
                pi = nextps(g)
                for h in range(8):
                    mm_group(S, g.ps[pi][:, h * 64:(h + 1) * 64], [(Mf[h % 2][:, h // 2, :], G2[:, h, :])], [kMf[h % 2], 'G2'], [('ps', pi)])
                S.op('act', lambda e, pi=pi: e.copy(U[:].rearrange('p h i -> p (h i)'), g.ps[pi][:, :]), [('ps', pi)], ['U'])
                pi = nextps(g)
                for h in range(8):
                    hp, hb = h // 2, h % 2
                    mm_group(S, g.ps[pi][hb * 64:hb * 64 + 64, hp * 128:(hp + 1) * 128], [(AtT[:, h, :], Mf[h % 2][:, h // 2, :])], [kMf[h % 2], 'AtT'], [('ps', pi)])
                S.op('dve', lambda e, pi=pi: e.tensor_copy(f2(P1), g.ps[pi][:, :]), [('ps', pi)], ['P1'])
                if STAGE <= 5:
                    continue
                for hb in range(2):
                    pe_ = nextps(g)
                    sl = slice(hb * 64, hb * 64 + 64)
                    for hp in range(4):
                        mm_group(S, g.ps[pe_][:, hp * 64:(hp + 1) * 64], [(P1[sl, hp, :], ST[sl, hp, :])], ['P1', 'ST'], [('ps', pe_)])
                    S.op('dve', lambda e, pe_=pe_, hb=hb: e.tensor_tensor(Et[:].rearrange('p (hp hb) i -> p hp hb i', hb=2)[:, :, hb, :],
                         g.ps[pe_][:, 0:256].rearrange('p (hp i) -> p hp i', hp=4), U[:].rearrange('p (hp hb) i -> p hp hb i', hb=2)[:, :, hb, :], ALU.add),
                         [('ps', pe_), 'U'], ['Et'])
                if STAGE <= 6:
                    continue
                py2, pss = [nextps(g), nextps(g)], nextps(g)
                for h in range(8):
                    hp, hb = h // 2, h % 2
                    sl = slice(hb * 64, hb * 64 + 64)
                    mm_group(S, g.ps[pss][sl, hp * 64:(hp + 1) * 64], [(KtT[:, h, :], cvb[:, h * 64:(h + 1) * 64]), (BtT[:, h, :], Et[:, h, :])],
                             ['KtT', 'BtT', 'Et', vk], [('ps', pss)])
                for h in range(8):
                    hp, hb = h // 2, h % 2
                    sl = slice(hb * 64, hb * 64 + 64)
                    mm_group(S, g.ps[py2[hb]][sl, hp * 128:(hp + 1) * 128],
                             [(STb[sbi][sl, hp, :], AR[sl, hp, 128:256]), (Et[:, h, :], AB[h][:, 1, :]), (cvb[:, h * 64:(h + 1) * 64], AB[h][:, 3, :])],
                             [('STb', sbi), 'AR', 'Et', ('AB', h), vk], [('ps', py2[hb])])
                S.op('dve', lambda e, pss=pss: e.tensor_tensor(stt[:].rearrange('p a i -> p (a i)'), g.ps[pss][:, 0:256], ST[:].rearrange('p a i -> p (a i)'), ALU.add),
                     [('ps', pss), 'ST'], ['stt'])
                S.op('dve', lambda e: e.tensor_tensor(ST[:], stt[:], gam[:].broadcast_to([128, 4, 64]), ALU.mult), ['stt', 'gam'], ['ST'])
                sbi = 1 - sbi
                S.op('act', lambda e, sbi=sbi: e.copy(STb[sbi][:], ST[:]), ['ST'], [('STb', sbi)])
                if STAGE <= 7:
                    continue
                if d == 0 and ci == 0:
                    dm = {'d_AR': AR, 'd_AB0': AB[0], 'd_AB1': AB[1], 'd_M0': Mf[0], 'd_U': U, 'd_P1': P1, 'd_Et': Et, 'd_ST': ST, 'd_kt': kt, 'd_bt': bt,
                          'd_AtT': AtT, 'd_G2': G2}
                    for nm, tl in dm.items():
                        if nm in g.dbg:
                            S.dma('sp', g.dbg[nm][:, :], tl[:].rearrange('p a b -> p (a b)'), reads=['AR', ('AB', 0), ('AB', 1), kMf[0], 'U', 'P1', 'Et', 'ST', 'kt', 'bt', 'AtT', 'G2'], writes=['dbg' + nm])
                if d == 0:
                    yb = ysb[ci % 2]
                    for hb in range(2):
                        S.op('act', lambda e, yb=yb, hb=hb: e.copy(f2(yb)[hb * 64:hb * 64 + 64, :], g.ps[py2[hb]][hb * 64:hb * 64 + 64, :]), [('ps', py2[hb])], [('ysb', ci % 2)])
                    S.dma('sp', fmc(g.yfS, c0), yb[:], reads=[('ysb', ci % 2)], writes=['yfS'])
                    continue
                S.dma('sp', yf[:], fmc(g.yfS, c0), reads=['yfS'], writes=['yf'])
                S.dma('sp', cbon[:], fmc(g.bonS, c0), reads=['bonS'], writes=['cbon'])
                S.dma('sp', cg[:], fmc(g.gS, c0), reads=['gS'], writes=['cg'])
                ys = ysb[0]
                for hb in range(2):
                    S.op('dve', lambda e, hb=hb: e.tensor_tensor(f2(ys)[hb * 64:hb * 64 + 64, :], g.ps[py2[hb]][hb * 64:hb * 64 + 64, :], f2(yf)[hb * 64:hb * 64 + 64, :], ALU.add),
                         [('ps', py2[hb]), 'yf'], ['ys'])
                S.op('act', lambda e: e.activation(ysq[:], ys[:], AF.Square), ['ys'], ['ysq'])
                p1, p2 = nextps(g), nextps(g)
                mm_group(S, g.ps[p1][:, :], [(bdf[:], f2(ys))], ['bdf', 'ys'], [('ps', p1)])
                mm_group(S, g.ps[p2][:, :], [(bdf[:], f2(ysq))], ['bdf', 'ysq'], [('ps', p2)])
                S.op('act', lambda e, p1=p1: e.activation(f2(gmean), g.ps[p1][:, :], AF.Identity, scale=1.0 / 64), [('ps', p1)], ['gmean'])
                S.op('pool', lambda e: e.tensor_tensor(ysq[:], gmean[:], gmean[:], ALU.mult), ['gmean'], ['ysq'])
                S.op('dve', lambda e, p2=p2: e.scalar_tensor_tensor(f2(grs), g.ps[p2][:, :], 1.0 / 64, f2(ysq), ALU.mult, ALU.subtract), [('ps', p2), 'ysq'], ['grs'])
                S.op('act', lambda e: e.activation(grs[:], grs[:], AF.Sqrt, bias=g.epsv[:, 3:4], scale=1.0), ['grs', 'epsv'], ['grs'])
                S.op('dve', lambda e: e.reciprocal(grs[:], grs[:]), ['grs'], ['grs'])
                S.op('pool', lambda e: e.tensor_tensor(gt_[:], ys[:], gmean[:], ALU.subtract), ['ys', 'gmean'], ['gt_'])
                S.op('dve', lambda e: e.tensor_tensor(gt_[:], gt_[:], grs[:], ALU.mult), ['gt_', 'grs'], ['gt_'])
                S.op('pool', lambda e: e.tensor_tensor(gt_[:], gt_[:], rwp[:, P_GG:P_GG + 4].unsqueeze(2).broadcast_to([128, 4, 128]), ALU.mult), ['gt_', 'rwp'], ['gt_'])
                S.op('pool', lambda e: e.tensor_tensor(gt_[:], gt_[:], rwp[:, P_GB:P_GB + 4].unsqueeze(2).broadcast_to([128, 4, 128]), ALU.add), ['gt_', 'rwp'], ['gt_'])
                S.op('dve', lambda e: e.tensor_tensor(gt_[:], gt_[:], cbon[:], ALU.add), ['gt_', 'cbon'], ['gt_'])
                yo = yob[ci % 2]
                S.op('pool', lambda e, yo=yo: e.tensor_tensor(yo[:], gt_[:], cg[:], ALU.mult), ['gt_', 'cg'], [('yob', ci % 2)])
                S.dma('sp', fmc(g.yrT, c0), yo[:], reads=[('yob', ci % 2)], writes=['yrT'])


def phase_merge(g, l):
    S, nc = g.S, g.nc
    last = (l == DEPTH - 1)
    with Phase(g) as A:
        wg = A('wg', [128, NCH, 3072], BF16)
        wp = [A('wp%d' % i, [128, 4, 1024], BF16) for i in range(3)]
        wo = A('wo', [128, NCH, 1024], BF16)
        HL = HLoader(g, A)
        yb = [[A('my%d_%d' % (i, j), [128, 4, 512], BF16) for j in range(2)] for i in range(3)]
        xb = [A('mx%d' % i, [128, NCH, 512]) for i in range(2)]
        sig = [A('msig%d' % i, [128, 512]) for i in range(2)]
        tm = [A('mtm%d' % i, [128, 512]) for i in range(2)]
        macc = A('macc', [128, 512])
        mT = A('mT', [128, NCH, 512], BF16)
        wsrc = g.w_in[l].rearrange('(k p) c -> p k c', p=128)
        for i in range(2):
            S.dma('pool', wg[:, :, i * 1536:(i + 1) * 1536], wsrc[:, :, 4480 + i * 1536:4480 + (i + 1) * 1536], writes=['wg'])
        for i, nm in enumerate(('p_conv', 'p_att', 'p_rwkv')):
            S.dma('pool', wp[i][:], getattr(g, nm)[l].rearrange('(k p) c -> p k c', p=128), writes=[('wp', i)])
        S.dma('pool', wo[:], g.w_out[l].rearrange('(k p) c -> p k c', p=128), writes=['wo'])
        xTv = g.xT.rearrange('(k p) t -> p k t', p=128)
        ysrc = [g.ycT.rearrange('(k p) t -> p k t', p=128), g.yaT.rearrange('(k p) t -> p k t', p=128), g.yrT.rearrange('(k p) t -> p k t', p=128)]
        ynm = ['ycT', 'yaT', 'yrT']
        for bi, (t0, nt) in enumerate(BLKS):
            if last and t0 < CTX:
                continue
            b = bi % 2
            s = 1 if t0 < CTX else 0
            hb, hk = HL.load(bi)
            for i in range(3):
                S.dma('sp', yb[i][b][:, :, :nt], ysrc[i][:, :, t0:t0 + nt], reads=[ynm[i]], writes=[('my', i, b)])
            S.dma('sp', xb[b][:, :, :nt], xTv[:, :, t0:t0 + nt], reads=['xT'], writes=[('mx', b)])
            for oc in range(NCH):
                for i in range(3):
                    pg, pp = nextps(g), nextps(g)
                    c0 = i * 1024 + oc * 128
                    mm_group(S, g.ps[pg][:, :nt], [(wg[:, k, c0:c0 + 128], hb[:, k, :nt]) for k in range(NCH)], ['wg', hk], [('ps', pg)])
                    mm_group(S, g.ps[pp][:, :nt], [(wp[i][:, k, oc * 128:(oc + 1) * 128], yb[i][b][:, k, :nt]) for k in range(4)], [('wp', i), ('my', i, b)], [('ps', pp)])
                    sb_ = i % 2
                    S.op('act', lambda e, pg=pg, sb_=sb_: e.activation(sig[sb_][:, :nt], g.ps[pg][:, :nt], AF.Sigmoid), [('ps', pg)], [('msig', sb_)])
                    if i == 0:
                        S.op('dve', lambda e, pp=pp, sb_=sb_: e.tensor_tensor(macc[:, :nt], g.ps[pp][:, :nt], sig[sb_][:, :nt], ALU.mult), [('ps', pp), ('msig', sb_)], ['macc'])
                    else:
                        S.op('dve', lambda e, pp=pp, sb_=sb_: e.tensor_tensor(tm[sb_][:, :nt], g.ps[pp][:, :nt], sig[sb_][:, :nt], ALU.mult), [('ps', pp), ('msig', sb_)], [('mtm', sb_)])
                        if i == 1:
                            S.op('pool', lambda e, sb_=sb_: e.tensor_tensor(macc[:, :nt], macc[:, :nt], tm[sb_][:, :nt], ALU.add), ['macc', ('mtm', sb_)], ['macc'])
                        else:
                            S.op('pool', lambda e, sb_=sb_, oc=oc: e.tensor_tensor(mT[:, oc, :nt], macc[:, :nt], tm[sb_][:, :nt], ALU.add), ['macc', ('mtm', sb_)], [('mT', oc)])
            for oc in range(NCH):
                po = nextps(g)
                mm_group(S, g.ps[po][:, :nt], [(wo[:, k, oc * 128:(oc + 1) * 128], mT[:, k, :nt]) for k in range(NCH)], ['wo'] + [('mT', k) for k in range(NCH)], [('ps', po)])
                S.op('dve', lambda e, po=po, oc=oc: e.scalar_tensor_tensor(xb[b][:, oc, :nt], g.ps[po][:, :nt], g.modv[:, 2 * 8 + oc, s:s + 1], xb[b][:, oc, :nt], ALU.mult, ALU.add),
                     [('ps', po), 'modv', ('mx', b)], [('mx', b)])
            S.dma('sp', xTv[:, :, t0:t0 + nt], xb[b][:, :, :nt], reads=[('mx', b)], writes=['xT'])


def phase_mlp(g, l):
    S, nc = g.S, g.nc
    last = (l == DEPTH - 1)
    with Phase(g) as A:
        w1 = A('w1', [128, NCH, DFF], BF16)
        w2 = A('w2', [128, DFF // 128, D], BF16)
        xb = [A('fx%d' % i, [128, NCH, 512]) for i in range(1)] * 2
        rs = A('frs', [128, 512]); tmp = [A('ftmp%d' % i, [128, 512]) for i in range(2)]
        h2 = A('fh2', [128, NCH, 512], BF16)
        act = A('fact', [128, DFF // 128, 512], BF16)
        sq = act[:, 0:16, :].bitcast(F32).rearrange('p (a two) b -> p a (two b)', two=2)
        fg = A('fg', [128, NCH])
        osb = A('fosb', [128, D])
        w1src = g.mlp_w1[l].rearrange('(k p) c -> p k c', p=128)
        for i in range(4):
            S.dma('pool', w1[:, :, i * 1024:(i + 1) * 1024], w1src[:, :, i * 1024:(i + 1) * 1024], writes=['w1'])
        w2src = g.mlp_w2[l].rearrange('(k p) c -> p k c', p=128)
        for i in range(4):
            S.dma('pool', w2[:, i * 8:(i + 1) * 8, :], w2src[:, i * 8:(i + 1) * 8, :], writes=['w2'])
        S.dma('sp', fg[:], g.fing[:, :], writes=['fg'])
        xTv = g.xT.rearrange('(k p) t -> p k t', p=128)
        for bi, (t0, nt) in enumerate(BLKS):
            if last and t0 < CTX:
                continue
            b = bi % 2
            s = 1 if t0 < CTX else 0
            x = xb[b]
            xk = ('fx', 0)
            S.dma('sp', x[:, :, :nt], xTv[:, :, t0:t0 + nt], reads=['xT'], writes=[xk])
            rms_stats(g, x, nt, sq, rs, xk, 'fact', 'frs')
            for k in range(NCH):
                tb = k % 2
                S.op('dve', lambda e, k=k, tb=tb: e.tensor_tensor(tmp[tb][:, :nt], x[:, k, :nt], rs[:, :nt], ALU.mult), [xk, 'frs'], [('ftmp', tb)])
                S.op('act', lambda e, k=k, tb=tb: e.activation(h2[:, k, :nt], tmp[tb][:, :nt], AF.Identity,
                     bias=g.modv[:, 3 * 8 + k, s:s + 1], scale=g.gs[:, 1, k, s:s + 1]), [('ftmp', tb), 'modv', 'gs'], ['fh2'])
            for fc in range(DFF // 128):
                pf = nextps(g)
                tb = fc % 2
                mm_group(S, g.ps[pf][:, :nt], [(w1[:, k, fc * 128:(fc + 1) * 128], h2[:, k, :nt]) for k in range(NCH)], ['w1', 'fh2'], [('ps', pf)])
                S.op('dve', lambda e, pf=pf, tb=tb: e.tensor_scalar(tmp[tb][:, :nt], g.ps[pf][:, :nt], 0.0, None, ALU.max), [('ps', pf)], [('ftmp', tb)])
                S.op('act', lambda e, fc=fc, tb=tb: e.activation(act[:, fc, :nt], tmp[tb][:, :nt], AF.Square), [('ftmp', tb)], ['fact'])
            for oc in range(NCH):
                po = nextps(g)
                mm_group(S, g.ps[po][:, :nt], [(w2[:, fc, oc * 128:(oc + 1) * 128], act[:, fc, :nt]) for fc in range(DFF // 128)],
                         ['w2', 'fact'], [('ps', po)])
                S.op('dve', lambda e, po=po, oc=oc: e.scalar_tensor_tensor(x[:, oc, :nt], g.ps[po][:, :nt], g.modv[:, 5 * 8 + oc, s:s + 1], x[:, oc, :nt], ALU.mult, ALU.add),
                     [('ps', po), 'modv', xk], [xk])
            if not last:
                S.dma('sp', xTv[:, :, t0:t0 + nt], x[:, :, :nt], reads=[xk], writes=['xT'])
                continue
            rms_stats(g, x, nt, sq, rs, xk, 'fact', 'frs')
            for k in range(NCH):
                S.op('dve', lambda e, k=k: e.tensor_tensor(sq[:, k, :nt], x[:, k, :nt], rs[:, :nt], ALU.mult), [xk, 'frs', 'fact'], ['fact'])
                S.op('act', lambda e, k=k: e.activation(sq[:, k, :nt], sq[:, k, :nt], AF.Identity, scale=fg[:, k:k + 1]), ['fact', 'fg'], ['fact'])
            for tt in range(nt // 128):
                for half in range(2):
                    pi = nextps(g)

                    def fn(pe, pi=pi, half=half, tt=tt):
                        inst = None
                        for j in range(4):
                            inst = pe.transpose(g.ps[pi][:, j * 128:(j + 1) * 128], sq[:, half * 4 + j, tt * 128:(tt + 1) * 128], g.identf[:])
                        return inst
                    S.op('pe', fn, ['fact', 'identf'], [('ps', pi)])
                    S.op('act', lambda e, pi=pi, half=half: e.copy(osb[:, half * 512:(half + 1) * 512], g.ps[pi][:, :]), [('ps', pi)], ['fosb'])
                r0 = t0 - CTX + tt * 128
                S.dma('sp', g.out[r0:r0 + 128, :], osb[:], reads=['fosb'], writes=['out'])


_CACHE = {}


def kernel(**inputs):
    inp = {k: np.asarray(v) for k, v in inputs.items()}
    if 'nc' not in _CACHE:
        _CACHE['nc'] = build()
    nc = _CACHE['nc']
    shared = host_shared(inp)
    B = inp['x'].shape[0]
    in_maps = [host_inputs(inp, b, shared) for b in range(B)]
    res = run_bass_kernel_spmd(nc, in_maps, core_ids=list(range(B)))
    return np.stack([np.asarray(res.results[b]['out']) for b in range(B)]).astype(np.float32)
```

```python
import math
import numpy as np
import concourse.bass as bass
import concourse.mybir as mybir
from concourse.bass_utils import run_bass_kernel_spmd

F32 = mybir.dt.float32
BF16 = mybir.dt.bfloat16
AF = mybir.ActivationFunctionType
ALU = mybir.AluOpType
AX = mybir.AxisListType

D = 1024
SEQ = 4096
CTX = 256
T = SEQ + CTX
DEPTH = 2
DIN = 7552
DFF = 4096
NCH = D // 128
BLKS = [(0, 256)] + [(256 + 512 * i, 512) for i in range(8)]
NORM_EPS = 1e-6
LN_EPS = 1e-5
SUBLN_EPS = 1e-5
GN_EPS = 64e-5


class Sched:
    def __init__(self, nc, n_dma=24):
        self.nc = nc
        self.eng = dict(pe=nc.tensor, dve=nc.vector, act=nc.scalar, pool=nc.gpsimd, sp=nc.sync)
        self.sem = {e: nc.alloc_semaphore('sem_' + e) for e in self.eng}
        self.cnt = {e: 0 for e in self.eng}
        self.dsem = [nc.alloc_semaphore('dsem%d' % i) for i in range(n_dma)]
        self.dval = [0] * n_dma
        self.drr = 0
        self.seen = {e: {} for e in self.eng}
        self.lastw = {}
        self.readers = {}
        self.nps = 0

    def _semh(self, key):
        return self.sem[key] if isinstance(key, str) else self.dsem[key]

    def _wait(self, e, key, val):
        if self.seen[e].get(key, 0) >= val:
            return
        self.eng[e].wait_ge(self._semh(key), val)
        self.seen[e][key] = val

    def _deps(self, e, reads, writes):
        need = {}
        for r in reads:
            tok = self.lastw.get(r)
            if tok is not None:
                need[tok[0]] = max(need.get(tok[0], 0), tok[1])
        for w in writes:
            tok = self.lastw.get(w)
            if tok is not None:
                need[tok[0]] = max(need.get(tok[0], 0), tok[1])
            for k, v in self.readers.get(w, {}).items():
                need[k] = max(need.get(k, 0), v)
        for k, v in need.items():
            self._wait(e, k, v)

    def _commit(self, tok, reads, writes):
        for w in writes:
            self.lastw[w] = tok
            self.readers[w] = {}
        for r in reads:
            if r in writes:
                continue
            d = self.readers.setdefault(r, {})
            d[tok[0]] = max(d.get(tok[0], 0), tok[1])

    def op(self, e, fn, reads=(), writes=()):
        if e == 'pe':
            self.seen[e]['pe'] = self.cnt['pe']
        self._deps(e, reads, writes)
        inst = fn(self.eng[e])
        self.cnt[e] += 1
        inst.then_inc(self.sem[e], 1)
        self._commit((e, self.cnt[e]), reads, writes)

    def dma(self, e, out, in_, reads=(), writes=(), **kw):
        self._deps(e, reads, writes)
        i = self.drr
        self.drr = (self.drr + 1) % len(self.dsem)
        if self.dval[i] > 0:
            self._wait(e, i, self.dval[i])
        self.dval[i] += 16
        self.eng[e].dma_start(out=out, in_=in_, **kw).then_inc(self.dsem[i], 16)
        self._commit((i, self.dval[i]), reads, writes)

    def barrier(self):
        for e in self.eng:
            for i, v in enumerate(self.dval):
                if v > 0:
                    self._wait(e, i, v)
            for k in self.eng:
                if k != e and self.cnt[k] > 0:
                    self._wait(e, k, self.cnt[k])

    def finish(self, e='sp'):
        for i, v in enumerate(self.dval):
            if v > 0:
                self._wait(e, i, v)
        for k in self.eng:
            if k != e and self.cnt[k] > 0:
                self._wait(e, k, self.cnt[k])


class Ctx:
    pass


class Phase:
    uid = 0

    def __init__(self, g):
        self.g = g
        self.guards = []

    def __enter__(self):
        return self

    def __call__(self, name, shape, dt=F32):
        Phase.uid += 1
        gd = self.g.nc.sbuf_tensor('%s_u%d' % (name, Phase.uid), list(shape), dt)
        t = gd.__enter__()
        self.guards.append(gd)
        return t

    def __exit__(self, *a):
        self.g.S.barrier()
        for gd in reversed(self.guards):
            gd.__exit__(None, None, None)
        return False


def mm_group(S, out, pairs, reads, writes, start=True, stop=True):
    n = len(pairs)

    def fn(pe):
        inst = None
        for i, (l, r) in enumerate(pairs):
            inst = pe.matmul(out, l, r, start=(start and i == 0), stop=(stop and i == n - 1))
        return inst
    S.op('pe', fn, reads, writes)


def build(dbg=(), upto='all'):
    nc = bass.Bass("TRN2", target_bir_lowering=False)
    S = Sched(nc)
    g = Ctx()
    g.nc, g.S = nc, S
    din = lambda name, shape, dt=F32: nc.dram_tensor(name, list(shape), dt, kind="ExternalInput").ap()
    dint = lambda name, shape, dt=F32: nc.dram_tensor(name, list(shape), dt, kind="Internal").ap()
    g.x = din('x', [SEQ, D]); g.ctx = din('ctx', [CTX, D])
    g.cvec = din('cvec', [128, NCH, 2])
    g.mod_w = din('mod_w', [DEPTH, D, 6 * D]); g.modb = din('modb', [DEPTH, 128, 48])
    g.n1g = din('n1g', [DEPTH, 128, NCH]); g.n2g = din('n2g', [DEPTH, 128, NCH]); g.fing = din('fing', [128, NCH])
    g.ident = din('ident', [128, 128])
    g.w_in = din('w_in', [DEPTH, D, DIN])
    g.convw = din('convw', [DEPTH, 128, 4, 31]); g.convp = din('convp', [DEPTH, 128, 3, 4])
    g.wqks = din('wqks', [DEPTH, D, 1024]); g.rope = din('rope', [2, 128, SEQ]); g.cmask = din('cmask', [3, 128, 128])
    g.attp = din('attp', [DEPTH, 4, 64]); g.subg = din('subg', [DEPTH, 128])
    g.rw_w2 = din('rw_w2', [DEPTH, 2, 64, 512]); g.rw_a2 = din('rw_a2', [DEPTH, 2, 64, 512]); g.rw_g2 = din('rw_g2', [DEPTH, 128, 512])
    g.rwp = din('rwp', [DEPTH, 128, P_OMKA]); g.smask = din('smask', [9, 128, 128])
    g.rS = dint('rS', [512, T], BF16); g.kkS = dint('kkS', [512, T], BF16); g.gS = dint('gS', [512, T], BF16); g.bonS = dint('bonS', [512, T], BF16)
    g.kdS = [dint('kdS%d' % d, [512, T], BF16) for d in range(2)]; g.bS = [dint('bS%d' % d, [512, T], BF16) for d in range(2)]
    g.wlS = [dint('wlS%d' % d, [512, T]) for d in range(2)]; g.vT = dint('vT', [T, 512], BF16); g.yfS = dint('yfS', [512, T])
    g.p_conv = din('p_conv', [DEPTH, 512, D]); g.p_att = din('p_att', [DEPTH, 512, D]); g.p_rwkv = din('p_rwkv', [DEPTH, 512, D])
    g.w_out = din('w_out', [DEPTH, D, D]); g.mlp_w1 = din('mlp_w1', [DEPTH, D, DFF]); g.mlp_w2 = din('mlp_w2', [DEPTH, DFF, D])
    g.out = nc.dram_tensor('out', [SEQ, D], F32, kind="ExternalOutput").ap()
    g.xT = dint('xT', [D, T]); g.hT = dint('hTd', [D, T], BF16)
    g.ycT = dint('ycT', [512, T], BF16); g.yaT = dint('yaT', [512, T], BF16); g.yrT = dint('yrT', [512, T], BF16)
    g.dbg = {}
    for name, shape, dt in dbg:
        g.dbg[name] = nc.dram_tensor('dbg_' + name, list(shape), dt, kind="ExternalOutput").ap()

    sb = lambda name, shape, dt=F32: nc.alloc_sbuf_tensor(name, list(shape), dt)
    g.ps = [nc.alloc_psum_tensor('ps%d' % i, [128, 512], F32) for i in range(8)]
    g.psi = 0

    g.identf = sb('identf', [128, 128]); g.identb = sb('identb', [128, 128], BF16)
    g.onesf = sb('onesf', [128, 128]); g.onesb = sb('onesb', [128, 128], BF16)
    g.epsv = sb('epsv', [128, 4])
    g.cs = sb('cs', [128, NCH, 2]); g.modv = sb('modv', [128, 48, 2]); g.gs = sb('gs', [128, 2, NCH, 2])
    S.dma('sp', g.identf[:], g.ident[:, :], writes=['identf'])
    S.op('dve', lambda e: e.tensor_copy(g.identb[:], g.identf[:]), ['identf'], ['identb'])
    S.op('dve', lambda e: e.memset(g.onesf[:], 1.0), [], ['onesf'])
    S.op('dve', lambda e: e.memset(g.onesb[:], 1.0), [], ['onesb'])
    for i, v in enumerate((NORM_EPS, LN_EPS, SUBLN_EPS, GN_EPS)):
        S.op('dve', lambda e, i=i, v=v: e.memset(g.epsv[:, i:i + 1], v), [], ['epsv'])

    import os
    if os.environ.get('SCAN_LIMIT'):
        g.scan_limit = int(os.environ['SCAN_LIMIT'])
    if upto.startswith('rwonly'):
        g.scan_limit = int(upto[6:] or 0)
        phase_rwkv_scan(g, 0)
        S.finish('sp')
        return nc
    phase_x0(g)
    for l in range(DEPTH):
        phase_mod(g, l)
        phase_h(g, l, 0)
        if upto == 'h':
            break
        phase_conv(g, l)
        if upto == 'conv':
            break
        phase_att(g, l)
        if upto == 'att':
            break
        phase_rwkv_prep(g, l)
        if upto == 'rwprep':
            break
        phase_rwkv_scan(g, l)
        if upto == 'rw':
            break
        phase_merge(g, l)
        phase_mlp(g, l)
        if upto == 'l0':
            break
    for nm in ('ycT', 'yaT', 'yrT', 'hT', 'rS', 'kkS', 'gS', 'bonS', 'vT', 'yfS'):
        if nm in g.dbg:
            S.dma('sp', g.dbg[nm][:, :], getattr(g, nm)[:, :], reads=[nm], writes=['dbg_' + nm])
    for nm, ap in (('kdS0', g.kdS[0]), ('bS0', g.bS[0]), ('wlS0', g.wlS[0])):
        if nm in g.dbg:
            S.dma('sp', g.dbg[nm][:, :], ap[:, :], reads=[nm], writes=['dbg_' + nm])
    S.finish('sp')
    return nc


def nextps(g):
    i = g.psi
    g.psi = (g.psi + 1) % 8
    return i


def phase_x0(g):
    S, nc = g.S, g.nc
    with Phase(g) as A:
        xin = [A('x0in%d' % i, [128, D]) for i in range(2)]
        xst = [A('x0st%d' % i, [128, NCH, 128]) for i in range(2)]
        xTv = g.xT.rearrange('(k p) t -> p k t', p=128)
        for ti in range(T // 128):
            b = ti % 2
            src = g.ctx[ti * 128:(ti + 1) * 128, :] if ti < 2 else g.x[(ti - 2) * 128:(ti - 1) * 128, :]
            S.dma('sp', xin[b][:], src, writes=[('x0in', b)])
            for half in range(2):
                pi = nextps(g)
                ps = g.ps[pi]

                def fn(pe, half=half, ps=ps, b=b):
                    inst = None
                    for j in range(4):
                        k = half * 4 + j
                        inst = pe.transpose(ps[:, j * 128:(j + 1) * 128], xin[b][:, k * 128:(k + 1) * 128], g.identf[:])
                    return inst
                S.op('pe', fn, [('x0in', b), 'identf'], [('ps', pi)])
                dst = xst[b][:, half * 4:(half + 1) * 4, :]
                src_ps = ps[:].rearrange('p (j t) -> p j t', j=4)
                if half == 0:
                    S.op('act', lambda e, dst=dst, s=src_ps: e.copy(dst, s), [('ps', pi)], [('x0st', b, half)])
                else:
                    S.op('dve', lambda e, dst=dst, s=src_ps: e.tensor_copy(dst, s), [('ps', pi)], [('x0st', b, half)])
            S.dma('sp', xTv[:, :, ti * 128:(ti + 1) * 128], xst[b][:], reads=[('x0st', b, 0), ('x0st', b, 1)], writes=['xT'])


def phase_mod(g, l):
    S, nc = g.S, g.nc
    with Phase(g) as A:
        modbs = A('modbs', [128, 48])
        mw = [A('mw%d' % i, [128, NCH, 512]) for i in range(2)]
        ng = A('ng', [128, 2, NCH])
        if l == 0:
            tmp = A('cs_tmp', [128, NCH, 2])
            S.dma('sp', tmp[:], g.cvec[:, :, :], writes=['cs_tmp'])
            S.op('act', lambda e: e.activation(g.cs[:], tmp[:], AF.Sigmoid), ['cs_tmp'], ['cs'])
            S.op('dve', lambda e: e.tensor_tensor(g.cs[:], g.cs[:], tmp[:], ALU.mult), ['cs', 'cs_tmp'], ['cs'])
        S.dma('sp', modbs[:], g.modb[l], writes=['modbs'])
        S.dma('sp', ng[:, 0, :], g.n1g[l], writes=['ng'])
        S.dma('sp', ng[:, 1, :], g.n2g[l], writes=['ng'])
        mwv = g.mod_w[l].rearrange('(k p) c -> p k c', p=128)
        for cg in range(12):
            b = cg % 2
            S.dma('sp', mw[b][:], mwv[:, :, cg * 512:(cg + 1) * 512], writes=[('mw', b)])
            pi = nextps(g)
            ps = g.ps[pi]
            for j in range(4):
                pairs = [(mw[b][:, k, j * 128:(j + 1) * 128], g.cs[:, k, :]) for k in range(NCH)]
                mm_group(S, ps[:, 2 * j:2 * j + 2], pairs, [('mw', b), 'cs'], [('ps', pi)])
            for j in range(4):
                jj = cg * 4 + j
                S.op('dve', lambda e, j=j, jj=jj, ps=ps: e.tensor_scalar(g.modv[:, jj, :], ps[:, 2 * j:2 * j + 2],
                     modbs[:, jj:jj + 1], None, ALU.add), [('ps', pi), 'modbs'], ['modv'])
        for n in range(2):
            sc = g.modv[:, (3 * n + 1) * 8:(3 * n + 2) * 8, :]
            S.op('dve', lambda e, n=n, sc=sc: e.tensor_scalar(g.gs[:, n], sc, 1.0, None, ALU.add), ['modv'], ['gs'])
            S.op('dve', lambda e, n=n: e.tensor_tensor(g.gs[:, n], g.gs[:, n],
                 ng[:, n, :].unsqueeze(2).broadcast_to([128, NCH, 2]), ALU.mult), ['gs', 'ng'], ['gs'])


def rms_stats(g, xb, n, sq, rstd, key_x, key_sq, key_rstd):
    S = g.S
    S.op('act', lambda e: e.activation(sq[:, :, :n], xb[:, :, :n], AF.Square), [key_x], [key_sq])
    pi = nextps(g)
    ps = g.ps[pi]
    mm_group(S, ps[:, :n], [(g.onesf[:], sq[:, k, :n]) for k in range(NCH)], [key_sq, 'onesf'], [('ps', pi)])
    S.op('act', lambda e: e.activation(rstd[:, :n], ps[:, :n], AF.Sqrt, bias=g.epsv[:, 0:1], scale=1.0 / D),
         [('ps', pi), 'epsv'], [key_rstd])
    S.op('dve', lambda e: e.reciprocal(rstd[:, :n], rstd[:, :n]), [key_rstd], [key_rstd])


def phase_h(g, l, n):
    S, nc = g.S, g.nc
    with Phase(g) as A:
        hx = [A('hx%d' % i, [128, NCH, 512]) for i in range(2)]
        hsq = A('hsq', [128, NCH, 512])
        hrs = [A('hrs%d' % i, [128, 512]) for i in range(2)]
        htmp = [A('htmp%d' % i, [128, 512]) for i in range(2)]
        hb = [A('hb%d' % i, [128, NCH, 512], BF16) for i in range(2)]
        xTv = g.xT.rearrange('(k p) t -> p k t', p=128)
        hTv = g.hT.rearrange('(k p) t -> p k t', p=128)
        for bi, (t0, nt) in enumerate(BLKS):
            b = bi % 2
            s = 1 if t0 < CTX else 0
            S.dma('sp', hx[b][:, :, :nt], xTv[:, :, t0:t0 + nt], reads=['xT'], writes=[('hx', b)])
            rms_stats(g, hx[b], nt, hsq, hrs[b], ('hx', b), 'hsq', ('hrs', b))
            for k in range(NCH):
                tb = k % 2
                S.op('dve', lambda e, k=k, tb=tb: e.tensor_tensor(htmp[tb][:, :nt], hx[b][:, k, :nt], hrs[b][:, :nt], ALU.mult),
                     [('hx', b), ('hrs', b)], [('htmp', tb)])
                S.op('act', lambda e, k=k, tb=tb: e.activation(hb[b][:, k, :nt], htmp[tb][:, :nt], AF.Identity,
                     bias=g.modv[:, (3 * n) * 8 + k, s:s + 1], scale=g.gs[:, n, k, s:s + 1]),
                     [('htmp', tb), 'modv', 'gs'], [('hb', b)])
            S.dma('sp', hTv[:, :, t0:t0 + nt], hb[b][:, :, :nt], reads=[('hb', b)], writes=['hT'])


class HLoader:
    def __init__(self, g, A):
        self.g = g
        self.t = [A('hblk%d' % i, [128, NCH, 512], BF16) for i in range(2)]
        self.i = 0

    def load(self, bi):
        g = self.g
        b = self.i
        self.i = (b + 1) % 2
        t0, nt = BLKS[bi]
        hTv = g.hT.rearrange('(k p) t -> p k t', p=128)
        g.S.dma('sp', self.t[b][:, :, :nt], hTv[:, :, t0:t0 + nt], reads=['hT'], writes=[('hblk', b)])
        return self.t[b], ('hblk', b)


def ucol(t):
    return t + 15 if t < CTX else t + 45


def phase_conv(g, l):
    S, nc = g.S, g.nc
    with Phase(g) as A:
        wcv = A('wcv', [128, NCH, 1024], BF16)
        uT = A('uT', [128, 4, T + 60], BF16)
        diag = A('diag', [128, 4, 31, 128], BF16)
        dww = A('dww', [128, 4, 31])
        cvp = A('cvp', [128, 3, 4])
        csg = [A('csg%d' % i, [128, 512]) for i in range(2)]
        cv = A('cv', [128, 4, 512]); cv2 = A('cv2', [128, 4, 512])
        cm = A('cm', [128, 512]); cmsq = A('cmsq', [128, 512]); crs = A('crs', [128, 512])
        ct = [A('ct%d' % i, [128, 512]) for i in range(2)]
        cyb = [A('cyb%d' % i, [128, 4, 512], BF16) for i in range(2)]
        HL = HLoader(g, A)
        S.op('pool', lambda e: e.memset(uT[:], 0.0), [], ['uT'])
        S.dma('pool', wcv[:], g.w_in[l][:, 0:1024].rearrange('(k p) c -> p k c', p=128), writes=['wcv'])
        S.dma('sp', dww[:], g.convw[l], writes=['dww'])
        S.dma('sp', cvp[:], g.convp[l], writes=['cvp'])
        for c in range(4):
            S.op('dve', lambda e, c=c: e.tensor_tensor(diag[:, c], g.identf[:].unsqueeze(1).broadcast_to([128, 31, 128]),
                 dww[:, c, :].unsqueeze(2).broadcast_to([128, 31, 128]), ALU.mult), ['identf', 'dww'], ['diag'])
        for bi, (t0, nt) in enumerate(BLKS):
            hb, hk = HL.load(bi)
            for c in range(4):
                pa, pb = nextps(g), nextps(g)
                mm_group(S, g.ps[pa][:, :nt], [(wcv[:, k, c * 128:(c + 1) * 128], hb[:, k, :nt]) for k in range(NCH)],
                         ['wcv', hk], [('ps', pa)])
                mm_group(S, g.ps[pb][:, :nt], [(wcv[:, k, 512 + c * 128:512 + (c + 1) * 128], hb[:, k, :nt]) for k in range(NCH)],
                         ['wcv', hk], [('ps', pb)])
                sb_ = c % 2
                S.op('act', lambda e, pb=pb, sb_=sb_: e.activation(csg[sb_][:, :nt], g.ps[pb][:, :nt], AF.Sigmoid),
                     [('ps', pb)], [('csg', sb_)])
                S.op('dve', lambda e, pa=pa, sb_=sb_, c=c: e.tensor_tensor(uT[:, c, ucol(t0):ucol(t0) + nt], g.ps[pa][:, :nt],
                     csg[sb_][:, :nt], ALU.mult), [('ps', pa), ('csg', sb_)], ['uT'])
        ycv = g.ycT.rearrange('(k p) t -> p k t', p=128)
        for bi, (t0, nt) in enumerate(BLKS):
            yb = cyb[bi % 2]
            ykey = ('cyb', bi % 2)
            for c in range(4):
                pi = nextps(g)
                base = ucol(t0) - 15
                mm_group(S, g.ps[pi][:, :nt], [(diag[:, c, k, :], uT[:, c, base + k:base + k + nt]) for k in range(31)],
                         ['diag', 'uT'], [('ps', pi)])
                S.op('act', lambda e, pi=pi, c=c: e.activation(cv[:, c, :nt], g.ps[pi][:, :nt], AF.Identity,
                     bias=cvp[:, 0, c:c + 1], scale=1.0), [('ps', pi), 'cvp'], [('cv', c)])
                S.op('act', lambda e, pi=pi, c=c: e.activation(cv2[:, c, :nt], g.ps[pi][:, :nt], AF.Square,
                     bias=cvp[:, 0, c:c + 1], scale=1.0), [('ps', pi), 'cvp'], [('cv2', c)])
            p1, p2 = nextps(g), nextps(g)
            mm_group(S, g.ps[p1][:, :nt], [(g.onesf[:], cv[:, c, :nt]) for c in range(4)], [('cv', c) for c in range(4)] + ['onesf'], [('ps', p1)])
            mm_group(S, g.ps[p2][:, :nt], [(g.onesf[:], cv2[:, c, :nt]) for c in range(4)], [('cv2', c) for c in range(4)] + ['onesf'], [('ps', p2)])
            S.op('act', lambda e: e.activation(cm[:, :nt], g.ps[p1][:, :nt], AF.Identity, scale=1.0 / 512), [('ps', p1)], ['cm'])
            S.op('dve', lambda e: e.tensor_tensor(cmsq[:, :nt], cm[:, :nt], cm[:, :nt], ALU.mult), ['cm'], ['cmsq'])
            S.op('dve', lambda e: e.scalar_tensor_tensor(crs[:, :nt], g.ps[p2][:, :nt], 1.0 / 512, cmsq[:, :nt], ALU.mult, ALU.subtract),
                 [('ps', p2), 'cmsq'], ['crs'])
            S.op('act', lambda e: e.activation(crs[:, :nt], crs[:, :nt], AF.Sqrt, bias=g.epsv[:, 1:2], scale=1.0), ['crs', 'epsv'], ['crs'])
            S.op('dve', lambda e: e.reciprocal(crs[:, :nt], crs[:, :nt]), ['crs'], ['crs'])
            for c in range(4):
                tb = c % 2
                S.op('dve', lambda e, c=c, tb=tb: e.tensor_tensor(ct[tb][:, :nt], cv[:, c, :nt], cm[:, :nt], ALU.subtract),
                     [('cv', c), 'cm'], [('ct', tb)])
                S.op('dve', lambda e, c=c, tb=tb: e.tensor_tensor(ct[tb][:, :nt], ct[tb][:, :nt], crs[:, :nt], ALU.mult),
                     [('ct', tb), 'crs'], [('ct', tb)])
                S.op('act', lambda e, c=c, tb=tb: e.activation(yb[:, c, :nt], ct[tb][:, :nt], AF.Silu,
                     bias=cvp[:, 2, c:c + 1], scale=cvp[:, 1, c:c + 1]), [('ct', tb), 'cvp'], [ykey])
            S.dma('sp', ycv[:, :, t0:t0 + nt], yb[:, :, :nt], reads=[ykey], writes=['ycT'])


def fm(v, nch):
    return np.ascontiguousarray(np.asarray(v, np.float32).reshape(nch, 128).T)


def host_shared(inp):
    m = {}
    m['mod_w'] = np.ascontiguousarray(inp['mod_w'], dtype=np.float32)
    m['modb'] = np.stack([fm(inp['mod_b'][l], 48) for l in range(DEPTH)])
    m['n1g'] = np.stack([fm(inp['norm1_g'][l], NCH) for l in range(DEPTH)])
    m['n2g'] = np.stack([fm(inp['norm2_g'][l], NCH) for l in range(DEPTH)])
    m['fing'] = fm(inp['final_g'], NCH)
    m['ident'] = np.eye(128, dtype=np.float32)
    m['w_in'] = np.ascontiguousarray(inp['w_in'], dtype=np.float32)
    sw = np.arange(1024) ^ 1
    m['wqks'] = np.ascontiguousarray(np.asarray(inp['w_in'])[:, :, 1024:2048][:, :, sw], dtype=np.float32)
    tt = np.arange(SEQ)
    inv = (10000.0 ** (-np.arange(16, dtype=np.float32) / 16)).astype(np.float32)
    ang = np.concatenate([(tt // 64).astype(np.float32)[:, None] * inv, (tt % 64).astype(np.float32)[:, None] * inv], axis=-1)
    pidx = (np.arange(128) % 64) // 2
    cosT = np.cos(ang)[:, pidx].T
    sinT = np.sin(ang)[:, pidx].T * np.where(np.arange(128) % 2 == 0, -1.0, 1.0)[:, None]
    m['rope'] = np.ascontiguousarray(np.stack([cosT, sinT]), dtype=np.float32)
    blk = (np.arange(128) // 64)
    bdm = (blk[:, None] == blk[None, :]).astype(np.float32)
    sel0 = np.repeat((blk == 0).astype(np.float32)[:, None], 128, 1)
    sel1 = np.repeat((blk == 1).astype(np.float32)[:, None], 128, 1)
    m['cmask'] = np.ascontiguousarray(np.stack([bdm, sel0, sel1]))
    m['attp'] = np.ascontiguousarray(np.stack([np.stack([inp[k][l] for k in ('att_lq1', 'att_lk1', 'att_lq2', 'att_lk2')]) for l in range(DEPTH)]), dtype=np.float32)
    m['subg'] = np.ascontiguousarray(inp['att_subln_g'], dtype=np.float32)
    for k in ('p_conv', 'p_att', 'p_rwkv', 'w_out', 'mlp_w1', 'mlp_w2'):
        m[k] = np.ascontiguousarray(inp[k], dtype=np.float32)
    m['rw_w2'] = np.ascontiguousarray(inp['rwkv_w2'], dtype=np.float32)
    m['rw_a2'] = np.ascontiguousarray(inp['rwkv_a2'], dtype=np.float32)
    m['rw_g2'] = np.ascontiguousarray(inp['rwkv_g2'], dtype=np.float32)
    rwp = []
    for l in range(DEPTH):
        cols = [np.asarray(inp['rwkv_shift'][l]).T.reshape(15, 128, 3).transpose(1, 0, 2).reshape(128, 45)]
        cols += [fm(inp['rwkv_w0'][l].reshape(-1), 8), fm(inp['rwkv_a0'][l].reshape(-1), 8)]
        cols += [fm(inp[k][l].reshape(-1), 4) for k in ('rwkv_kk', 'rwkv_ka', 'rwkv_rk', 'rwkv_gn_g', 'rwkv_gn_b')]
        rwp.append(np.concatenate(cols, axis=1))
    m['rwp'] = np.ascontiguousarray(np.stack(rwp), dtype=np.float32)
    ii = np.arange(128)
    lt = (ii[:, None] < ii[None, :]).astype(np.float32); le = (ii[:, None] <= ii[None, :]).astype(np.float32)
    seg = np.repeat((ii != 0).astype(np.float32)[None, :], 128, 0)
    blk = lambda n: (ii[:, None] // n == ii[None, :] // n)
    offm = lambda n: (blk(n) & ~blk(n // 2)).astype(np.float32)
    m['smask'] = np.ascontiguousarray(np.stack([lt, le, lt.T, le.T, seg, blk(16).astype(np.float32), offm(32), offm(64), offm(128)]))
    m['convw'] = np.stack([np.ascontiguousarray(np.asarray(inp['conv_dw_w'][l]).T.reshape(4, 128, 31).transpose(1, 0, 2)) for l in range(DEPTH)])
    m['convp'] = np.stack([np.stack([fm(inp[k][l], 4) for k in ('conv_dw_b', 'conv_ln_g', 'conv_ln_b')], axis=1) for l in range(DEPTH)])
    return m


def host_inputs(inp, b, shared=None):
    m = dict(shared if shared is not None else host_shared(inp))
    m['x'] = np.ascontiguousarray(inp['x'][b], dtype=np.float32)
    m['ctx'] = np.ascontiguousarray(inp['ctx'][b], dtype=np.float32)
    m['cvec'] = np.ascontiguousarray(np.stack([fm(inp['c'][b], NCH), fm(inp['c_ctx'], NCH)], axis=-1))
    return m


def phase_att(g, l):
    S, nc = g.S, g.nc
    lam_init = 0.8 - 0.6 * math.exp(-0.3 * l)
    need_ctx_q = l < DEPTH - 1
    with Phase(g) as A:
        qT = A('qT', [128, 4, T], BF16); kT = A('kT', [128, 4, T], BF16)
        vaug = A('vaug', [128, T // 128, 4, 129], BF16)
        nb = A('nb', [128, 2, 4]); neglam = A('neglam', [128, 1]); gsub = A('gsub', [128, 128])
        A1 = Phase(g)
        wq = A1('wq', [128, NCH, 512], BF16); wqs = A1('wqs', [128, NCH, 512], BF16)
        wk = A1('wk', [128, NCH, 512], BF16); wks = A1('wks', [128, NCH, 512], BF16)
        wv = A1('wv', [128, NCH, 512], BF16)
        cosT = A1('cosT', [128, SEQ]); sinT = A1('sinT', [128, SEQ])
        HL = HLoader(g, A1)
        rt = [A1('rt%d' % i, [128, 512]) for i in range(4)]
        bd = A1('bd', [128, 128], BF16); cmf = A1('cmf', [128, 3, 128])
        stat = A1('stat', [128, 2, 4, len(BLKS)]); stm = A1('stm', [128, 2, 4]); negb = A1('negb', [128, 4])
        lqk = A1('lqk', [128, 4, 64]); lam2 = A1('lam2', [128, 2])

        wsrc = g.w_in[l].rearrange('(k p) c -> p k c', p=128)
        ssrc = g.wqks[l].rearrange('(k p) c -> p k c', p=128)
        S.dma('pool', wq[:], wsrc[:, :, 1024:1536], writes=['wq'])
        S.dma('pool', wk[:], wsrc[:, :, 1536:2048], writes=['wk'])
        S.dma('pool', wv[:], wsrc[:, :, 2048:2560], writes=['wv'])
        S.dma('pool', wqs[:], ssrc[:, :, 0:512], writes=['wqs'])
        S.dma('pool', wks[:], ssrc[:, :, 512:1024], writes=['wks'])
        S.dma('sp', cosT[:], g.rope[0], writes=['cosT'])
        S.dma('sp', sinT[:], g.rope[1], writes=['sinT'])
        S.dma('sp', cmf[:], g.cmask.rearrange('m p c -> p m c'), writes=['cmf'])
        S.op('dve', lambda e: e.tensor_copy(bd[:], cmf[:, 0, :]), ['cmf'], ['bd'])
        S.dma('sp', lqk[:], g.attp[l:l + 1].broadcast_to([128, 4, 64]), writes=['lqk'])
        S.dma('sp', gsub[:], g.subg[l:l + 1, :].broadcast_to([128, 128]), writes=['gsub'])
        S.op('act', lambda e: e.mul(gsub[:], gsub[:], 1.0 - lam_init), ['gsub'], ['gsub'])
        S.op('dve', lambda e: e.tensor_tensor(lqk[:, 0, :], lqk[:, 0, :], lqk[:, 1, :], ALU.mult), ['lqk'], ['lqk'])
        S.op('dve', lambda e: e.tensor_tensor(lqk[:, 2, :], lqk[:, 2, :], lqk[:, 3, :], ALU.mult), ['lqk'], ['lqk'])
        S.op('dve', lambda e: e.reduce_sum(lam2[:, 0:1], lqk[:, 0, :], AX.X), ['lqk'], ['lam2'])
        S.op('dve', lambda e: e.reduce_sum(lam2[:, 1:2], lqk[:, 2, :], AX.X), ['lqk'], ['lam2'])
        S.op('act', lambda e: e.activation(lam2[:], lam2[:], AF.Exp), ['lam2'], ['lam2'])
        S.op('dve', lambda e: e.tensor_tensor(neglam[:], lam2[:, 1:2], lam2[:, 0:1], ALU.subtract), ['lam2'], ['neglam'])
        S.op('dve', lambda e: e.tensor_scalar(neglam[:], neglam[:], -lam_init, None, ALU.add), ['neglam'], ['neglam'])
        S.op('pool', lambda e: e.memset(vaug[:, :, :, 128:129], 1.0), [], ['vaug1'])

        for bi, (t0, nt) in enumerate(BLKS):
            hb, hk = HL.load(bi)
            lat = t0 >= CTX
            tl = t0 - CTX
            for (w, ws, dst, dk, wkey, wskey) in ((wq, wqs, qT, 'qT', 'wq', 'wqs'), (wk, wks, kT, 'kT', 'wk', 'wks')):
                for h in range(4):
                    pa = nextps(g)
                    mm_group(S, g.ps[pa][:, :nt], [(w[:, k, h * 128:(h + 1) * 128], hb[:, k, :nt]) for k in range(NCH)],
                             [wkey, hk], [('ps', pa)])
                    if not lat:
                        S.op('act', lambda e, pa=pa, h=h, dst=dst: e.copy(dst[:, h, t0:t0 + nt], g.ps[pa][:, :nt]), [('ps', pa)], [dk])
                        continue
                    pb = nextps(g)
                    mm_group(S, g.ps[pb][:, :nt], [(ws[:, k, h * 128:(h + 1) * 128], hb[:, k, :nt]) for k in range(NCH)],
                             [wskey, hk], [('ps', pb)])
                    r1, r2 = (0, 1) if h % 2 == 0 else (2, 3)
                    S.op('dve', lambda e, pa=pa, r1=r1: e.tensor_tensor(rt[r1][:, :nt], g.ps[pa][:, :nt], cosT[:, tl:tl + nt], ALU.mult),
                         [('ps', pa), 'cosT'], [('rt', r1)])
                    S.op('dve', lambda e, pb=pb, r2=r2: e.tensor_tensor(rt[r2][:, :nt], g.ps[pb][:, :nt], sinT[:, tl:tl + nt], ALU.mult),
                         [('ps', pb), 'sinT'], [('rt', r2)])
                    S.op('pool', lambda e, r1=r1, r2=r2, h=h, dst=dst: e.tensor_tensor(dst[:, h, t0:t0 + nt], rt[r1][:, :nt], rt[r2][:, :nt], ALU.add),
                         [('rt', r1), ('rt', r2)], [dk])
            for tt in range(nt // 128):
                ti = t0 // 128 + tt
                pv = nextps(g)
                mm_group(S, g.ps[pv][:, :512], [(hb[:, k, tt * 128:(tt + 1) * 128], wv[:, k, :]) for k in range(NCH)],
                         ['wv', hk], [('ps', pv)])
                S.op('act', lambda e, pv=pv, ti=ti: e.copy(vaug[:, ti, :, 0:128], g.ps[pv][:, :].rearrange('p (h d) -> p h d', h=4)),
                     [('ps', pv)], ['vaug'])
        sqb = rt
        for qi, (src, sk) in enumerate(((qT, 'qT'), (kT, 'kT'))):
            for h in range(4):
                for bi, (t0, nt) in enumerate(BLKS):
                    r = (h * len(BLKS) + bi) % 4
                    sq = rt[r][:, 0:256].bitcast(BF16)
                    S.op('act', lambda e, sq=sq, h=h, src=src: e.activation(sq[:, :nt], src[:, h, t0:t0 + nt], AF.Square), [sk], [('rt', r)])
                    pi = nextps(g)
                    mm_group(S, g.ps[pi][:, :nt], [(bd[:], sq[:, :nt])], ['bd', ('rt', r)], [('ps', pi)])
                    S.op('dve', lambda e, pi=pi, h=h, bi=bi, qi=qi: e.reduce_max(stat[:, qi, h, bi:bi + 1], g.ps[pi][:, :nt], AX.X),
                         [('ps', pi)], ['stat'])
        S.op('dve', lambda e: e.reduce_max(stm[:], stat[:], AX.X), ['stat'], ['stm'])
        S.op('dve', lambda e: e.tensor_tensor(negb[:], stm[:, 0, :], stm[:, 1, :], ALU.mult), ['stm'], ['negb'])
        S.op('act', lambda e: e.activation(negb[:], negb[:], AF.Sqrt), ['negb'], ['negb'])
        for c in range(2):
            pi = nextps(g)
            mm_group(S, g.ps[pi][:, 0:4], [(cmf[:, 1 + c, :], negb[:])], ['cmf', 'negb'], [('ps', pi)])
            S.op('act', lambda e, pi=pi, c=c: e.mul(nb[:, c, :], g.ps[pi][:, 0:4], -1.02 * 0.125 / 64.0), [('ps', pi)], ['nb'])

        A1.__exit__(None, None, None)
        pT = [A('pT%d' % i, [128, 512], BF16) for i in range(6)]
        oc = [[A('oc%d_%d' % (c, q), [128, 129]) for q in range(4)] for c in range(2)]
        sm = A('sm', [128, 8]); o0 = A('o0', [128, 128]); aa = A('aa', [128, 128]); junk = A('junk', [128, 128])
        ytok = [A('ytok%d' % q, [128, 512], BF16) for q in range(4)]
        yab = [A('yab%d' % i, [128, 4, 512], BF16) for i in range(2)]
        yav = g.yaT.rearrange('(k p) t -> p k t', p=128)
        pti = [0]
        sbank = [0]

        def attend(q0, nq, kt0, nkt, yslot):
            nqs = nq // 128
            items = [(h, c, kk) for h in range(4) for c in range(2) for kk in range(nkt)]
            DPF = 2
            slots = {}

            def front(i):
                h, c, kk = items[i]
                kt = kt0 + kk
                sb_ = 4 + sbank[0]
                sbank[0] = (sbank[0] + 1) % 4
                mm_group(S, g.ps[sb_][:, :nq], [(kT[64 * c:64 * c + 64, h, kt * 128:(kt + 1) * 128], qT[64 * c:64 * c + 64, h, q0:q0 + nq])],
                         ['kT', 'qT'], [('ps', sb_)])
                pb_ = pti[0]
                pti[0] = (pti[0] + 1) % 6
                S.op('act', lambda e: e.activation(pT[pb_][:, :nq], g.ps[sb_][:, :nq], AF.Exp,
                     bias=nb[:, c, h:h + 1], scale=0.125), [('ps', sb_), 'nb'], [('pT', pb_)])
                slots[i] = pb_

            def back(i):
                h, c, kk = items[i]
                kt = kt0 + kk
                pb_ = slots.pop(i)
                for qs in range(nqs):
                    mm_group(S, g.ps[qs][:, 0:129], [(pT[pb_][:, qs * 128:(qs + 1) * 128], vaug[:, kt, h, :])],
                             [('pT', pb_), 'vaug', 'vaug1'], [('ps', qs)], start=(kk == 0), stop=(kk == nkt - 1))
                if kk != nkt - 1:
                    return
                for qs in range(nqs):
                    S.op('dve', lambda e, qs=qs: e.tensor_copy(oc[c][qs][:], g.ps[qs][:, 0:129]), [('ps', qs)], [('oc', c, qs)])
                if c != 1:
                    return
                for qs in range(nqs):
                    k0, k1 = ('oc', 0, qs), ('oc', 1, qs)
                    S.op('dve', lambda e, qs=qs: e.reciprocal(sm[:, 0:1], oc[0][qs][:, 128:129]), [k0], ['sm0'])
                    S.op('dve', lambda e, qs=qs: e.reciprocal(sm[:, 1:2], oc[1][qs][:, 128:129]), [k1], ['sm1'])
                    S.op('dve', lambda e: e.tensor_tensor(sm[:, 2:3], sm[:, 1:2], neglam[:], ALU.mult), ['sm1', 'neglam'], ['sm2'])
                    S.op('dve', lambda e, qs=qs: e.tensor_scalar(o0[:], oc[0][qs][:, 0:128], sm[:, 0:1], None, ALU.mult), [k0, 'sm0'], ['o0'])
                    S.op('dve', lambda e, qs=qs: e.scalar_tensor_tensor(aa[:], oc[1][qs][:, 0:128], sm[:, 2:3], o0[:], ALU.mult, ALU.add),
                         [k1, 'sm2', 'o0'], ['aa'])
                    S.op('dve', lambda e: e.scalar_tensor_tensor(junk[:], aa[:], 1.0, aa[:], ALU.mult, ALU.mult, accum_out=sm[:, 3:4]), ['aa'], ['junk', 'sm3'])
                    S.op('act', lambda e: e.activation(sm[:, 4:5], sm[:, 3:4], AF.Sqrt, bias=g.epsv[:, 2:3], scale=1.0 / 128), ['sm3', 'epsv'], ['sm4'])
                    S.op('dve', lambda e: e.reciprocal(sm[:, 5:6], sm[:, 4:5]), ['sm4'], ['sm5'])
                    S.op('dve', lambda e, qs=qs: e.scalar_tensor_tensor(ytok[qs][:, h * 128:(h + 1) * 128], aa[:], sm[:, 5:6], gsub[:], ALU.mult, ALU.mult),
                         ['aa', 'sm5', 'gsub'], [('ytok', qs)])

            for i in range(len(items) + DPF):
                if i < len(items):
                    front(i)
                if i - DPF >= 0:
                    back(i - DPF)
            yb = yab[yslot % 2]
            ykey = ('yab', yslot % 2)
            for qs in range(nqs):
                tb_ = 4 + sbank[0]
                sbank[0] = (sbank[0] + 1) % 4
                psb = g.ps[tb_][:, :].bitcast(BF16)

                def fn(pe, qs=qs, psb=psb):
                    inst = None
                    for h in range(4):
                        inst = pe.transpose(psb[:, h * 128:(h + 1) * 128], ytok[qs][:, h * 128:(h + 1) * 128], g.identb[:])
                    return inst
                S.op('pe', fn, [('ytok', qs), 'identb'], [('ps', tb_)])
                S.op('dve', lambda e, qs=qs, psb=psb, yb=yb: e.tensor_copy(yb[:, :, qs * 128:(qs + 1) * 128], psb[:, 0:512].rearrange('p (h q) -> p h q', h=4)),
                     [('ps', tb_)], [ykey])
            S.dma('sp', yav[:, :, q0:q0 + nq], yb[:, :, :nq], reads=[ykey], writes=['yaT'])

        slot = 0
        if need_ctx_q:
            attend(0, CTX, 0, CTX // 128, slot)
            slot += 1
        for qb in range(SEQ // 512):
            attend(CTX + qb * 512, 512, 0, T // 128, slot)
            slot += 1


RW0 = 2560
P_SH, P_W0, P_A0, P_KK, P_KA, P_RK, P_GG, P_GB, P_OMKA, NRWP = 0, 45, 53, 61, 65, 69, 73, 77, 81, 85
DECAY_C = -math.exp(-0.5)


def phase_rwkv_prep(g, l):
    S, nc = g.S, g.nc
    with Phase(g) as A:
        wrw = A('wrw', [128, NCH, 1920], BF16)
        w2b = A('w2b', [128, 512], BF16); a2b = A('a2b', [128, 512], BF16); g2b = A('g2b', [128, 512], BF16)
        rwp = A('rwp', [128, NRWP])
        bdf = A('bdf', [128, 128])
        hbx = [A('hbx%d' % i, [128, NCH, 514], BF16) for i in range(2)]
        zx = [A('zx%d' % i, [128, 514]) for i in range(3)]
        zc = A('zc', [128, 15, 512])
        tw = A('tw', [128, 512], BF16); ab = A('ab', [128, 512], BF16); sg = A('sg', [128, 512], BF16)
        kkt = A('kkt', [128, 4, 512]); kds = A('kds', [128, 4, 512])
        t1 = [A('rt1_%d' % i, [128, 512]) for i in range(3)]
        ob = {n: [A('ob_%s%d' % (n, i), [128, 4, 512], BF16) for i in range(1)] * 2 for n in ('r', 'kk', 'kd0', 'kd1', 'b0', 'b1', 'g', 'bon')}
        owl = {d: [A('owl%d_%d' % (d, i), [128, 4, 512]) for i in range(1)] * 2 for d in range(2)}
        vtile = [A('vtile%d' % i, [128, 512], BF16) for i in range(2)]

        wsrc = g.w_in[l].rearrange('(k p) c -> p k c', p=128)
        S.dma('pool', wrw[:], wsrc[:, :, RW0:RW0 + 1920], writes=['wrw'])
        S.dma('pool', w2b[:], g.rw_w2[l].rearrange('d m c -> (d m) c'), writes=['w2b'])
        S.dma('pool', a2b[:], g.rw_a2[l].rearrange('d m c -> (d m) c'), writes=['a2b'])
        S.dma('pool', g2b[:], g.rw_g2[l], writes=['g2b'])
        S.dma('sp', rwp[:, 0:P_OMKA], g.rwp[l], writes=['rwp'])
        S.dma('sp', bdf[:], g.cmask[0], writes=['bdf'])
        S.op('dve', lambda e: e.tensor_scalar(rwp[:, P_OMKA:P_OMKA + 4], rwp[:, P_KA:P_KA + 4], -1.0, 1.0, ALU.mult, ALU.add), ['rwp'], ['rwp'])
        hTv = g.hT.rearrange('(k p) t -> p k t', p=128)
        fmv = lambda ap: ap.rearrange('(k p) t -> p k t', p=128)
        for bi, (t0, nt) in enumerate(BLKS):
            b = bi % 2
            hb = hbx[b]
            hk = ('hbx', b)
            s0, s1 = (0, CTX) if t0 < CTX else (CTX, T)
            lo, hi = max(s0, t0 - 1), min(s1, t0 + nt + 1)
            if lo == t0:
                S.op('pool', lambda e, hb=hb: e.memset(hb[:, :, 0:1], 0.0), [], [hk])
            if hi == t0 + nt:
                S.op('pool', lambda e, hb=hb: e.memset(hb[:, :, nt + 1:nt + 2], 0.0), [], [hk])
            S.dma('sp', hb[:, :, 1 - (t0 - lo):1 + (hi - t0)], hTv[:, :, lo:hi], reads=['hT'], writes=[hk])
            ph = nextps(g)
            for ch in range(15):
                pm = nextps(g)
                if pm == ph:
                    pm = nextps(g)
                wsl = lambda k, ch=ch: wrw[:, k, ch * 128:(ch + 1) * 128]
                mm_group(S, g.ps[pm][:, :nt], [(wsl(k), hb[:, k, 1:1 + nt]) for k in range(NCH)], ['wrw', hk], [('ps', pm)])
                mm_group(S, g.ps[ph][:, 2 * ch:2 * ch + 1], [(wsl(k), hb[:, k, 0:1]) for k in range(NCH)], ['wrw', hk], [('ps', ph)])
                mm_group(S, g.ps[ph][:, 2 * ch + 1:2 * ch + 2], [(wsl(k), hb[:, k, nt + 1:nt + 2]) for k in range(NCH)], ['wrw', hk], [('ps', ph)])
                z = zx[ch % 3]
                zk = ('zx', ch % 3)
                S.op('act', lambda e, z=z, pm=pm: e.copy(z[:, 1:1 + nt], g.ps[pm][:, :nt]), [('ps', pm)], [zk])
                S.op('act', lambda e, z=z, ch=ch: e.copy(z[:, 0:1], g.ps[ph][:, 2 * ch:2 * ch + 1]), [('ps', ph)], [zk])
                S.op('act', lambda e, z=z, ch=ch: e.copy(z[:, nt + 1:nt + 2], g.ps[ph][:, 2 * ch + 1:2 * ch + 2]), [('ps', ph)], [zk])
                sh = lambda j, ch=ch: rwp[:, P_SH + ch * 3 + j:P_SH + ch * 3 + j + 1]
                S.op('dve', lambda e, z=z, ch=ch, sh=sh: e.tensor_scalar(zc[:, ch, :nt], z[:, 1:1 + nt], sh(1), None, ALU.mult), [zk, 'rwp'], [('zc', ch)])
                S.op('dve', lambda e, z=z, ch=ch, sh=sh: e.scalar_tensor_tensor(zc[:, ch, :nt], z[:, 0:nt], sh(0), zc[:, ch, :nt], ALU.mult, ALU.add),
                     [zk, 'rwp', ('zc', ch)], [('zc', ch)])
                S.op('dve', lambda e, z=z, ch=ch, sh=sh: e.scalar_tensor_tensor(zc[:, ch, :nt], z[:, 2:nt + 2], sh(2), zc[:, ch, :nt], ALU.mult, ALU.add),
                     [zk, 'rwp', ('zc', ch)], [('zc', ch)])
            o = {n: ob[n][0] for n in ob}
            okey = {n: ('ob', n, 0) for n in ob}
            S.op('act', lambda e: e.copy(o['r'][:, :, :nt], zc[:, 0:4, :nt]), [('zc', c) for c in range(4)], [okey['r']])
            S.dma('sp', fmv(g.rS)[:, :, t0:t0 + nt], o['r'][:, :, :nt], reads=[okey['r']], writes=['rS'])
            S.op('act', lambda e: e.activation(tw[:, :nt], zc[:, 12, :nt], AF.Tanh), [('zc', 12)], ['tw'])
            S.op('act', lambda e: e.copy(ab[:, :nt], zc[:, 13, :nt]), [('zc', 13)], ['ab'])
            S.op('act', lambda e: e.activation(sg[:, :nt], zc[:, 14, :nt], AF.Sigmoid), [('zc', 14)], ['sg'])
            for c in range(4):
                ti = c % 3
                S.op('act', lambda e, c=c: e.activation(kkt[:, c, :nt], zc[:, 4 + c, :nt], AF.Identity, scale=rwp[:, P_KK + c:P_KK + c + 1]),
                     [('zc', 4 + c), 'rwp'], [('kkt', c)])
                S.op('act', lambda e, c=c, ti=ti: e.activation(t1[ti][:, :nt], kkt[:, c, :nt], AF.Square), [('kkt', c)], [('t1', ti)])
                pi = nextps(g)
                mm_group(S, g.ps[pi][:, :nt], [(bdf[:], t1[ti][:, :nt])], ['bdf', ('t1', ti)], [('ps', pi)])
                S.op('dve', lambda e, pi=pi, ti=ti: e.tensor_scalar(t1[ti][:, :nt], g.ps[pi][:, :nt], 1e-24, None, ALU.max), [('ps', pi)], [('t1', ti)])
                S.op('act', lambda e, ti=ti: e.activation(t1[ti][:, :nt], t1[ti][:, :nt], AF.Sqrt), [('t1', ti)], [('t1', ti)])
                S.op('dve', lambda e, ti=ti: e.reciprocal(t1[ti][:, :nt], t1[ti][:, :nt]), [('t1', ti)], [('t1', ti)])
                S.op('dve', lambda e, c=c, ti=ti: e.tensor_tensor(kkt[:, c, :nt], kkt[:, c, :nt], t1[ti][:, :nt], ALU.mult), [('kkt', c), ('t1', ti)], [('kkt', c)])
            S.op('act', lambda e: e.copy(o['kk'][:, :, :nt], kkt[:, :, :nt]), [('kkt', c) for c in range(4)], [okey['kk']])
            S.dma('sp', fmv(g.kkS)[:, :, t0:t0 + nt], o['kk'][:, :, :nt], reads=[okey['kk']], writes=['kkS'])
            for d in range(2):
                kdn, bn = 'kd%d' % d, 'b%d' % d
                for c in range(4):
                    pu, pa = nextps(g), nextps(g)
                    mm_group(S, g.ps[pu][:, :nt], [(w2b[64 * d:64 * d + 64, c * 128:(c + 1) * 128], tw[64 * d:64 * d + 64, :nt])], ['w2b', 'tw'], [('ps', pu)])
                    mm_group(S, g.ps[pa][:, :nt], [(a2b[64 * d:64 * d + 64, c * 128:(c + 1) * 128], ab[64 * d:64 * d + 64, :nt])], ['a2b', 'ab'], [('ps', pa)])
                    wl = owl[d][0]
                    wk_ = ('owl', d, 0)
                    S.op('act', lambda e, pu=pu, c=c, d=d, wl=wl: e.activation(wl[:, c, :nt], g.ps[pu][:, :nt], AF.Sigmoid,
                         bias=rwp[:, P_W0 + d * 4 + c:P_W0 + d * 4 + c + 1], scale=1.0), [('ps', pu), 'rwp'], [wk_])
                    S.op('pool', lambda e, c=c, wl=wl: e.tensor_scalar(wl[:, c, :nt], wl[:, c, :nt], DECAY_C, None, ALU.mult), [wk_], [wk_])
                    ta, tb = t1[0], t1[1]
                    S.op('act', lambda e, pa=pa, c=c, d=d: e.activation(ta[:, :nt], g.ps[pa][:, :nt], AF.Sigmoid,
                         bias=rwp[:, P_A0 + d * 4 + c:P_A0 + d * 4 + c + 1], scale=1.0), [('ps', pa), 'rwp'], [('t1', 0)])
                    S.op('pool', lambda e, c=c, bn=bn: e.tensor_tensor(o[bn][:, c, :nt], kkt[:, c, :nt], ta[:, :nt], ALU.mult),
                         [('kkt', c), ('t1', 0)], [okey[bn]])
                    S.op('dve', lambda e, c=c: e.tensor_scalar(tb[:, :nt], ta[:, :nt], rwp[:, P_KA + c:P_KA + c + 1], rwp[:, P_OMKA + c:P_OMKA + c + 1], ALU.mult, ALU.add),
                         [('t1', 0), 'rwp'], [('t1', 1)])
                    S.op('dve', lambda e, c=c: e.tensor_tensor(tb[:, :nt], tb[:, :nt], zc[:, 4 + c, :nt], ALU.mult), [('t1', 1), ('zc', 4 + c)], [('t1', 1)])
                    S.op('act', lambda e, c=c, kdn=kdn: e.copy(o[kdn][:, c, :nt], tb[:, :nt]), [('t1', 1)], [okey[kdn]])
                    if d == 0:
                        S.op('pool', lambda e, c=c: e.tensor_copy(kds[:, c, :nt], tb[:, :nt]), [('t1', 1)], [('kds', c)])
                    else:
                        S.op('pool', lambda e, c=c: e.tensor_tensor(kds[:, c, :nt], kds[:, c, :nt], tb[:, :nt], ALU.add), [('t1', 1), ('kds', c)], [('kds', c)])
                S.dma('sp', fmv(g.wlS[d])[:, :, t0:t0 + nt], owl[d][0][:, :, :nt], reads=[('owl', d, 0)], writes=['wlS%d' % d])
                S.dma('sp', fmv(g.kdS[d])[:, :, t0:t0 + nt], o[kdn][:, :, :nt], reads=[okey[kdn]], writes=['kdS%d' % d])
                S.dma('sp', fmv(g.bS[d])[:, :, t0:t0 + nt], o[bn][:, :, :nt], reads=[okey[bn]], writes=['bS%d' % d])
            for c in range(4):
                pg = nextps(g)
                mm_group(S, g.ps[pg][:, :nt], [(g2b[:, c * 128:(c + 1) * 128], sg[:, :nt])], ['g2b', 'sg'], [('ps', pg)])
                S.op('act', lambda e, pg=pg, c=c: e.copy(o['g'][:, c, :nt], g.ps[pg][:, :nt]), [('ps', pg)], [okey['g']])
            S.dma('sp', fmv(g.gS)[:, :, t0:t0 + nt], o['g'][:, :, :nt], reads=[okey['g']], writes=['gS'])
            for c in range(4):
                tc_ = t1[2]
                S.op('dve', lambda e, c=c: e.scalar_tensor_tensor(tc_[:, :nt], zc[:, c, :nt], rwp[:, P_RK + c:P_RK + c + 1], kds[:, c, :nt], ALU.mult, ALU.mult),
                     [('zc', c), 'rwp', ('kds', c)], [('t1', 2)])
                pi = nextps(g)
                mm_group(S, g.ps[pi][:, :nt], [(bdf[:], tc_[:, :nt])], ['bdf', ('t1', 2)], [('ps', pi)])
                S.op('dve', lambda e, pi=pi, c=c: e.tensor_tensor(o['bon'][:, c, :nt], g.ps[pi][:, :nt], zc[:, 8 + c, :nt], ALU.mult),
                     [('ps', pi), ('zc', 8 + c)], [okey['bon']])
            S.dma('sp', fmv(g.bonS)[:, :, t0:t0 + nt], o['bon'][:, :, :nt], reads=[okey['bon']], writes=['bonS'])
            for tt in range(nt // 128):
                pv = nextps(g)

                def fn(pe, pv=pv, tt=tt):
                    inst = None
                    for c in range(4):
                        inst = pe.transpose(g.ps[pv][:, c * 128:(c + 1) * 128], zc[:, 8 + c, tt * 128:(tt + 1) * 128], g.identf[:])
                    return inst
                S.op('pe', fn, [('zc', 8 + c) for c in range(4)] + ['identf'], [('ps', pv)])
                vb = (t0 // 128 + tt) % 2
                S.op('act', lambda e, pv=pv, vb=vb: e.copy(vtile[vb][:], g.ps[pv][:, :]), [('ps', pv)], [('vtile', vb)])
                S.dma('sp', g.vT[t0 + tt * 128:t0 + (tt + 1) * 128, :], vtile[vb][:], reads=[('vtile', vb)], writes=['vT'])


def phase_rwkv_scan(g, l):
    S, nc = g.S, g.nc
    import os
    STAGE = int(os.environ.get('SCAN_STAGE', '99'))
    NCK = T // 128
    with Phase(g) as A:
        smf = A('smf', [128, 9, 128])
        msk = [A('msk%d' % i, [128, 128], BF16) for i in range(4)]
        NTF = [A('NTF%d' % q, [128, 4, 128], BF16) for q in range(2)]
        NF = [A('NF%d' % q, [128, 4, 128], BF16) for q in range(2)]
        NoT = [[A('NoT%d_%d' % (q, i), [128, 4, 128], BF16) for i in range(3)] for q in range(2)]
        MT = [[A('MT%d_%d' % (q, i), [128, 4, 128], BF16) for i in range(2)] for q in range(2)]
        Wb = [A('Wb%d' % q, [128, 4, 128], BF16) for q in range(2)]
        m4 = [A('m4_%d' % d, [128, 4, 128], BF16) for d in range(2)]
        mnt = [A('mnt%d' % d, [128, 128], BF16) for d in range(2)]
        bdf = A('bdf', [128, 128]); rwp = A('rwp', [128, P_OMKA])
        ld = {n: [A('ld_%s%d' % (n, i), [128, 4, 128], BF16) for i in range(2)] for n in ('r', 'kk', 'kd', 'b')}
        cwl = [A('cwl%d' % i, [128, 4, 128]) for i in range(2)]
        cv = [A('cv%d' % i, [128, 512], BF16) for i in range(2)]
        cum = A('cum', [128, 4, 128]); cumx = A('cumx', [128, 4, 128]); ep = A('ep', [128, 4, 128]); en = A('en', [128, 4, 128])
        AR = A('AR', [128, 4, 256], BF16); kt = A('kt', [128, 4, 128], BF16); bt = A('bt', [128, 4, 128], BF16)
        gam = A('gam', [128, 4, 1])
        AtT = A('AtT', [128, 8, 64], BF16); BtT = A('BtT', [128, 8, 64], BF16); KtT = A('KtT', [128, 8, 64], BF16)
        AB = [A('AB%d' % h, [128, 4, 128], BF16) for h in range(8)]
        X = [[A('X%d_%d' % (q, i), [128, 4, 128], BF16) for i in range(2)] for q in range(2)]
        XT = [[A('XT%d_%d' % (q, i), [128, 4, 128], BF16) for i in range(2)] for q in range(2)]
        M = [[A('M%d_%d' % (q, i), [128, 4, 128], BF16) for i in range(2)] for q in range(2)]
        G2 = A('G2', [128, 8, 64], BF16); U = A('U', [128, 8, 64]); P1 = A('P1', [128, 4, 128]); Et = A('Et', [128, 8, 64], BF16)
        ST = A('ST', [128, 4, 64]); STb = [A('STb%d' % i, [128, 4, 64], BF16) for i in range(2)]; stt = A('stt', [128, 4, 64])
        ysb = [A('ysb%d' % i, [128, 4, 128]) for i in range(2)]
        yf = A('yf', [128, 4, 128]); cbon = A('cbon', [128, 4, 128], BF16); cg = A('cg', [128, 4, 128], BF16)
        ysq = A('ysq', [128, 4, 128]); gmean = A('gmean', [128, 4, 128]); grs = A('grs', [128, 4, 128]); gt_ = A('gt_', [128, 4, 128])
        yob = [A('yob%d' % i, [128, 4, 128], BF16) for i in range(2)]

        S.dma('sp', smf[:], g.smask[0:9].rearrange('m p c -> p m c'), writes=['smf'])
        for i in range(4):
            S.op('dve', lambda e, i=i: e.tensor_copy(msk[i][:], smf[:, 5 + i, :]), ['smf'], [('msk', i)])
        S.dma('sp', bdf[:], g.cmask[0], writes=['bdf'])
        S.dma('sp', rwp[:], g.rwp[l], writes=['rwp'])
        for d in range(2):
            for j in range(4):
                S.op('dve', lambda e, d=d, j=j: e.tensor_copy(m4[d][:, j, :], smf[:, 2 * d + (j % 2), :]), ['smf'], [('m4', d)])
        S.op('dve', lambda e: e.tensor_copy(mnt[0][:], smf[:, 2, :]), ['smf'], [('mnt', 0)])
        S.op('dve', lambda e: e.tensor_copy(mnt[1][:], smf[:, 0, :]), ['smf'], [('mnt', 1)])
        fmc = lambda ap, c0: ap.rearrange('(k p) t -> p k t', p=128)[:, :, c0:c0 + 128]
        f2 = lambda t: t[:].rearrange('p a b -> p (a b)')
        segb = smf[:, 4, :].unsqueeze(1).broadcast_to([128, 4, 128])
        it = [0]
        for d in range(2):
            order = list(range(NCK)) if d == 0 else [1, 0] + list(range(NCK - 1, 1, -1))
            if getattr(g, 'scan_limit', None):
                order = order[:g.scan_limit]
            S.op('pool', lambda e: e.memset(ST[:], 0.0), [], ['ST'])
            sbi = 0
            S.op('pool', lambda e: e.memset(STb[0][:], 0.0), [], [('STb', 0)])
            for ci in order:
                c0 = ci * 128
                b = it[0] % 2
                it[0] += 1
                cr, ckk, ckd, cb = ld['r'][b], ld['kk'][b], ld['kd'][b], ld['b'][b]
                lk = lambda n: ('ld', n, b)
                S.dma('sp', cr[:], fmc(g.rS, c0), reads=['rS'], writes=[lk('r')])
                S.dma('sp', ckk[:], fmc(g.kkS, c0), reads=['kkS'], writes=[lk('kk')])
                S.dma('sp', ckd[:], fmc(g.kdS[d], c0), reads=['kdS%d' % d], writes=[lk('kd')])
                S.dma('sp', cb[:], fmc(g.bS[d], c0), reads=['bS%d' % d], writes=[lk('b')])
                S.dma('sp', cwl[b][:], fmc(g.wlS[d], c0), reads=['wlS%d' % d], writes=[('cwl', b)])
                S.dma('sp', cv[b][:], g.vT[c0:c0 + 128, :], reads=['vT'], writes=[('cv', b)])
                vk = ('cv', b)
                cvb = cv[b]
                S.op('pool', lambda e: e.tensor_copy(cumx[:], segb), ['smf'], ['cumx'])
                S.op('dve', lambda e, b=b: e.tensor_tensor_scan(f2(cum), f2(cumx), f2(cwl[b]), 0.0, ALU.mult, ALU.add), ['cumx', ('cwl', b)], ['cum'])
                if d == 0:
                    S.op('dve', lambda e, b=b: e.tensor_tensor(cumx[:], cum[:], cwl[b][:], ALU.subtract), ['cum', ('cwl', b)], ['cumx'])
                else:
                    S.op('dve', lambda e: e.tensor_tensor(cumx[:], cum[:, :, 127:128].broadcast_to([128, 4, 128]), cum[:], ALU.subtract), ['cum'], ['cumx'])
                    S.op('dve', lambda e, b=b: e.tensor_tensor(cum[:], cumx[:], cwl[b][:], ALU.add), ['cumx', ('cwl', b)], ['cum'])
                S.op('act', lambda e: e.activation(ep[:], cum[:], AF.Exp), ['cum'], ['ep'])
                S.op('act', lambda e: e.activation(en[:], cum[:], AF.Exp, scale=-1.0), ['cum'], ['en'])
                gcol = 127 if d == 0 else 0
                S.op('act', lambda e: e.copy(gam[:], ep[:, :, gcol:gcol + 1]), ['ep'], ['gam'])
                S.op('dve', lambda e, cr=cr: e.tensor_tensor(AR[:, :, 128:256], cr[:], ep[:], ALU.mult), [lk('r'), 'ep'], ['AR'])
                S.op('act', lambda e: e.activation(ep[:], cumx[:], AF.Exp), ['cumx', 'AR', 'gam'], ['ep'])
                S.op('dve', lambda e, ckk=ckk: e.scalar_tensor_tensor(AR[:, :, 0:128], ckk[:], -1.0, ep[:], ALU.mult, ALU.mult), [lk('kk'), 'ep'], ['AR'])
                S.op('pool', lambda e, ckd=ckd: e.tensor_tensor(kt[:], ckd[:], en[:], ALU.mult), [lk('kd'), 'en'], ['kt'])
                S.op('pool', lambda e, cb=cb: e.tensor_tensor(bt[:], cb[:], en[:], ALU.mult), [lk('b'), 'en'], ['bt'])
                if STAGE <= 1:
                    continue
                for (srcf, dst, dk, sk) in ((lambda hp: AR[:, hp, 0:128], AtT, 'AtT', 'AR'), (lambda hp: bt[:, hp, :], BtT, 'BtT', 'bt'), (lambda hp: kt[:, hp, :], KtT, 'KtT', 'kt')):
                    pi = nextps(g)
                    psb = g.ps[pi][:, :].bitcast(BF16)

                    def fn(pe, srcf=srcf, psb=psb):
                        inst = None
                        for hp in range(4):
                            inst = pe.transpose(psb[:, hp * 128:(hp + 1) * 128], srcf(hp), g.identb[:])
                        return inst
                    S.op('pe', fn, [sk, 'identb'], [('ps', pi)])
                    S.op('act', lambda e, dst=dst, psb=psb: e.copy(dst[:].rearrange('p h j -> p (h j)'), psb[:, 0:512]), [('ps', pi)], [dk])
                if STAGE <= 2:
                    continue
                ABH = int(os.environ.get('AB_H', '8')); ABM = int(os.environ.get('AB_MODE', '9'))
                for h in range(ABH):
                    hp, hb = h // 2, h % 2
                    sl = slice(hb * 64, hb * 64 + 64)
                    pi = nextps(g)
                    mm_group(S, g.ps[pi][:, 0:256], [(bt[sl, hp, :], AR[sl, hp, :])], ['bt', 'AR'], [('ps', pi)])
                    if ABM <= 1:
                        continue
                    mm_group(S, g.ps[pi][:, 256:512], [(kt[sl, hp, :], AR[sl, hp, :])], ['kt', 'AR'], [('ps', pi)])
                    if ABM <= 2:
                        continue
                    S.op('dve', lambda e, h=h, pi=pi: e.tensor_tensor(f2(AB[h]), g.ps[pi][:, :], f2(m4[d]), ALU.mult), [('ps', pi), ('m4', d)], [('AB', h)])
                SUB = int(os.environ.get('SCAN_SUB', '9'))
                if SUB <= 0:
                    continue
                b4 = lambda t: t[:].unsqueeze(1).broadcast_to([128, 4, 128])
                for q in range(2):
                    pi = nextps(g)
                    for j in range(4):
                        h = 2 * j + q
                        hp, hb = j, q
                        sl = slice(hb * 64, hb * 64 + 64)
                        mm_group(S, g.ps[pi][:, j * 128:(j + 1) * 128], [(AR[sl, hp, 0:128], bt[sl, hp, :])], ['bt', 'AR'], [('ps', pi)])
                    S.op('dve', lambda e, q=q, pi=pi: e.tensor_tensor(NTF[q][:], g.ps[pi][:, :].rearrange('p (j s) -> p j s', j=4), b4(mnt[d]), ALU.mult),
                         [('ps', pi), ('mnt', d)], [('NTF', q)])
                    for j in range(4):
                        h = 2 * j + q
                        S.op('pool', lambda e, q=q, j=j, h=h: e.tensor_copy(NF[q][:, j, :], AB[h][:, 0, :]), [('AB', h)], [('NF', q)])
                    S.op('pool', lambda e, q=q: e.tensor_tensor(X[q][0][:], NF[q][:], b4(msk[0]), ALU.mult), [('NF', q), ('msk', 0)], [('X', q, 0)])
                    S.op('dve', lambda e, q=q: e.tensor_tensor(XT[q][0][:], NTF[q][:], b4(msk[0]), ALU.mult), [('NTF', q), ('msk', 0)], [('XT', q, 0)])
                    S.op('pool', lambda e, q=q: e.tensor_tensor(M[q][0][:], X[q][0][:], b4(g.identb), ALU.add), [('X', q, 0), 'identb'], [('M', q, 0)])
                    S.op('pool', lambda e, q=q: e.tensor_tensor(MT[q][0][:], XT[q][0][:], b4(g.identb), ALU.add), [('XT', q, 0), 'identb'], [('MT', q, 0)])
                    for i in range(3):
                        S.op('pool', lambda e, q=q, i=i: e.tensor_tensor(NoT[q][i][:], NTF[q][:], b4(msk[1 + i]), ALU.mult), [('NTF', q), ('msk', 1 + i)], [('NoT', q, i)])
                if d == 0 and ci == 0:
                    for nm, tl, kk_ in (('d_XT0', NTF[0], ('NTF', 0)), ('d_X0', NF[0], ('NF', 0)), ('d_Mi', M[0][0], ('M', 0, 0))):
                        if nm in g.dbg:
                            S.dma('sp', g.dbg[nm][:, :], tl[:].rearrange('p a b -> p (a b)'), reads=[kk_], writes=['dbg' + nm])
                def mm4(q, lh, rh, rk):
                    pi = nextps(g)
                    for j in range(4):
                        mm_group(S, g.ps[pi][:, j * 128:(j + 1) * 128], [(lh[:, j, :], rh[:, j, :])], rk, [('ps', pi)])
                    return pi
                cur = 0
                for k in range(1, 4):
                    nx = 1 - cur
                    for q in range(2):
                        Xp, XTp, Mp, MTp = X[q][cur], XT[q][cur], M[q][cur], MT[q][cur]
                        Xn, XTn, Mn, MTn = X[q][nx], XT[q][nx], M[q][nx], MT[q][nx]
                        kX, kXT, kM, kMT = ('X', q, cur), ('XT', q, cur), ('M', q, cur), ('MT', q, cur)
                        nX, nXT, nM, nMT = ('X', q, nx), ('XT', q, nx), ('M', q, nx), ('MT', q, nx)
                        if k < 3:
                            pi = mm4(q, XTp, Xp, [kX, kXT])
                            S.op('act', lambda e, Xn=Xn, pi=pi: e.copy(f2(Xn), g.ps[pi][:, :]), [('ps', pi)], [nX])
                        pi = mm4(q, Xp, XTp, [kX, kXT])
                        S.op('act', lambda e, XTn=XTn, pi=pi: e.copy(f2(XTn), g.ps[pi][:, :]), [('ps', pi)], [nXT])
                        pi = mm4(q, XTn, Mp, [nXT, kM])
                        S.op('dve', lambda e, Mn=Mn, Mp=Mp, pi=pi: e.tensor_tensor(f2(Mn), g.ps[pi][:, :], f2(Mp), ALU.add), [('ps', pi), kM], [nM])
                        pi = mm4(q, Mp, XTn, [nXT, kM])
                        S.op('dve', lambda e, MTn=MTn, MTp=MTp, pi=pi: e.tensor_tensor(f2(MTn), g.ps[pi][:, :], f2(MTp), ALU.add), [('ps', pi), kMT], [nMT])
                    cur = nx
                for i in range(3):
                    nx = 1 - cur
                    for q in range(2):
                        Dp, DTp, Dn, DTn = M[q][cur], MT[q][cur], M[q][nx], MT[q][nx]
                        kD, kDT, nD, nDT = ('M', q, cur), ('MT', q, cur), ('M', q, nx), ('MT', q, nx)
                        pi = mm4(q, NoT[q][i], Dp, [('NoT', q, i), kD])
                        S.op('act', lambda e, q=q, pi=pi: e.copy(f2(Wb[q]), g.ps[pi][:, :]), [('ps', pi)], [('Wb', q)])
                        pi = mm4(q, DTp, Wb[q], [kDT, ('Wb', q)])
                        S.op('dve', lambda e, Dn=Dn, Dp=Dp, pi=pi: e.tensor_tensor(f2(Dn), g.ps[pi][:, :], f2(Dp), ALU.add), [('ps', pi), kD], [nD])
                        if i < 2:
                            pi = mm4(q, Wb[q], DTp, [kDT, ('Wb', q)])
                            S.op('dve', lambda e, DTn=DTn, DTp=DTp, pi=pi: e.tensor_tensor(f2(DTn), g.ps[pi][:, :], f2(DTp), ALU.add), [('ps', pi), kDT], [nDT])
                    cur = nx
                Mf = [M[q][cur] for q in range(2)]
                kMf = [('M', q, cur) for q in range(2)]
                if STAGE <= 4:
                    continue
                pi = nextps(g)
                for h in range(8):
                    mm_group(S, g.ps[pi][:, h * 64:(h + 1) * 64], [(AB[h][:, 2, :], cvb[:, h * 64:(h + 1) * 64])], [('AB', h), vk], [('ps', pi)])
                S.op('act', lambda e, pi=pi: e.copy(G2[:].rearrange('p h i -> p (h i)'), g.ps[pi][:, :]), [('ps', pi)], ['G2'])
                pi = nextps(g)
                for h in range(8):
                    mm_group(S, g.ps[pi][:, h * 64:(h + 1) * 64], [(Mf[h % 2][:, h // 2, :], G2[:, h, :])], [kMf[h % 2], 'G2'], [('ps', pi)])
                S.op('act', lambda e, pi=pi: e.copy(U[:].rearrange('p h i -> p (h i)'), g.ps[pi][:, :]), [('ps', pi)], ['U'])
                pi = nextps(g)
                for h in range(8):
                    hp, hb = h // 2, h % 2
                    mm_group(S, g.ps[pi][hb * 64:hb * 64 + 64, hp * 128:(hp + 1) * 128], [(AtT[:, h, :], Mf[h % 2][:, h // 2, :])], [kMf[h % 2], 'AtT'], [('ps', pi)])
                S.op('dve', lambda e, pi=pi: e.tensor_copy(f2(P1), g.ps[pi][:, :]), [('ps', pi)], ['P1'])
                if STAGE <= 5:
                    continue
                for hb in range(2):
                    pe_ = nextps(g)
                    sl = slice(hb * 64, hb * 64 + 64)
                    for hp in range(4):
                        mm_group(S, g.ps[pe_][:, hp * 64:(hp + 1) * 64], [(P1[sl, hp, :], ST[sl, hp, :])], ['P1', 'ST'], [('ps', pe_)])
                    S.op('dve', lambda e, pe_=pe_, hb=hb: e.tensor_tensor(Et[:].rearrange('p (hp hb) i -> p hp hb i', hb=2)[:, :, hb, :],
                         g.ps[pe_][:, 0:256].rearrange('p (hp i) -> p hp i', hp=4), U[:].rearrange('p (hp hb) i -> p hp hb i', hb=2)[:, :, hb, :], ALU.add),
                         [('ps', pe_), 'U'], ['Et'])
                if STAGE <= 6:
                    continue
                py2, pss = [nextps(g), nextps(g)], nextps(g)
                for h in range(8):
                    hp, hb = h // 2, h % 2
                    sl = slice(hb * 64, hb * 64 + 64)
                    mm_group(S, g.ps[pss][sl, hp * 64:(hp + 1) * 64], [(KtT[:, h, :], cvb[:, h * 64:(h + 1) * 64]), (BtT[:, h, :], Et[:, h, :])],
                             ['KtT', 'BtT', 'Et', vk], [('ps', pss)])
                for h in range(8):
                    hp, hb = h // 2, h % 2
                    sl = slice(hb * 64, hb * 64 + 64)
                    mm_group(S, g.ps[py2[hb]][sl, hp * 128:(hp + 1) * 128],
                             [(STb[sbi][sl, hp, :], AR[sl, hp, 128:256]), (Et[:, h, :], AB[h][:, 1, :]), (cvb[:, h * 64:(h + 1) * 64], AB[h][:, 3, :])],
                             [('STb', sbi), 'AR', 'Et', ('AB', h), vk], [('ps', py2[hb])])
                S.op('dve', lambda e, pss=pss: e.tensor_tensor(stt[:].rearrange('p a i -> p (a i)'), g.ps[pss][:, 0:256], ST[:].rearrange('p a i -> p (a i)'), ALU.add),
                     [('ps', pss), 'ST'], ['stt'])
                S.op('dve', lambda e: e.tensor_tensor(ST[:], stt[:], gam[:].broadcast_to([128, 4, 64]), ALU.mult), ['stt', 'gam'], ['ST'])
                sbi = 1 - sbi
                S.op('act', lambda e, sbi=sbi: e.copy(STb[sbi][:], ST[:]), ['ST'], [('STb', sbi)])
                if STAGE <= 7:
                    continue
                if d == 0 and ci == 0:
                    dm = {'d_AR': AR, 'd_AB0': AB[0], 'd_AB1': AB[1], 'd_M0': Mf[0], 'd_U': U, 'd_P1': P1, 'd_Et': Et, 'd_ST': ST, 'd_kt': kt, 'd_bt': bt,
                          'd_AtT': AtT, 'd_G2': G2}
                    for nm, tl in dm.items():
                        if nm in g.dbg:
                            S.dma('sp', g.dbg[nm][:, :], tl[:].rearrange('p a b -> p (a b)'), reads=['AR', ('AB', 0), ('AB', 1), kMf[0], 'U', 'P1', 'Et', 'ST', 'kt', 'bt', 'AtT', 'G2'], writes=['dbg' + nm])
                if d == 0:
                    yb = ysb[ci % 2]
                    for hb in range(2):
                        S.op('act', lambda e, yb=yb, hb=hb: e.copy(f2(yb)[hb * 64:hb * 64 + 64, :], g.ps[py2[hb]][hb * 64:hb * 64 + 64, :]), [('ps', py2[hb])], [('ysb', ci % 2)])
                    S.dma('sp', fmc(g.yfS, c0), yb[:], reads=[('ysb', ci % 2)], writes=['yfS'])
                    continue
                S.dma('sp', yf[:], fmc(g.yfS, c0), reads=['yfS'], writes=['yf'])
                S.dma('sp', cbon[:], fmc(g.bonS, c0), reads=['bonS'], writes=['cbon'])
                S.dma('sp', cg[:], fmc(g.gS, c0), reads=['gS'], writes=['cg'])
                ys = ysb[0]
                for hb in range(2):
                    S.op('dve', lambda e, hb=hb: e.tensor_tensor(f2(ys)[hb * 64:hb * 64 + 64, :], g.ps[py2[hb]][hb * 64:hb * 64 + 64, :], f2(yf)[hb * 64:hb * 64 + 64, :], ALU.add),
                         [('ps', py2[hb]), 'yf'], ['ys'])
                S.op('act', lambda e: e.activation(ysq[:], ys[:], AF.Square), ['ys'], ['ysq'])
                p1, p2 = nextps(g), nextps(g)
                mm_group(S, g.ps[p1][:, :], [(bdf[:], f2(ys))], ['bdf', 'ys'], [('ps', p1)])
                mm_group(S, g.ps[p2][:, :], [(bdf[:], f2(ysq))], ['bdf', 'ysq'], [('ps', p2)])
                S.op('act', lambda e, p1=p1: e.activation(f2(gmean), g.ps[p1][:, :], AF.Identity, scale=1.0 / 64), [('ps', p1)], ['gmean'])
                S.op('pool', lambda e: e.tensor_tensor(ysq[:], gmean[:], gmean[:], ALU.mult), ['gmean'], ['ysq'])
                S.op('dve', lambda e, p2=p2: e.scalar_tensor_tensor(f2(grs), g.ps[p2][:, :], 1.0 / 64, f2(ysq), ALU.mult, ALU.subtract), [('ps', p2), 'ysq'], ['grs'])
                S.op('act', lambda e: e.activation(grs[:], grs[:], AF.Sqrt, bias=g.epsv[:, 3:4], scale=1.0), ['grs', 'epsv'], ['grs'])
                S.op('dve', lambda e: e.reciprocal(grs[:], grs[:]), ['grs'], ['grs'])
                S.op('pool', lambda e: e.tensor_tensor(gt_[:], ys[:], gmean[:], ALU.subtract), ['ys', 'gmean'], ['gt_'])
                S.op('dve', lambda e: e.tensor_tensor(gt_[:], gt_[:], grs[:], ALU.mult), ['gt_', 'grs'], ['gt_'])
                S.op('pool', lambda e: e.tensor_tensor(gt_[:], gt_[:], rwp[:, P_GG:P_GG + 4].unsqueeze(2).broadcast_to([128, 4, 128]), ALU.mult), ['gt_', 'rwp'], ['gt_'])
                S.op('pool', lambda e: e.tensor_tensor(gt_[:], gt_[:], rwp[:, P_GB:P_GB + 4].unsqueeze(2).broadcast_to([128, 4, 128]), ALU.add), ['gt_', 'rwp'], ['gt_'])
                S.op('dve', lambda e: e.tensor_tensor(gt_[:], gt_[:], cbon[:], ALU.add), ['gt_', 'cbon'], ['gt_'])
                yo = yob[ci % 2]
                S.op('pool', lambda e, yo=yo: e.tensor_tensor(yo[:], gt_[:], cg[:], ALU.mult), ['gt_', 'cg'], [('yob', ci % 2)])
                S.dma('sp', fmc(g.yrT, c0), yo[:], reads=[('yob', ci % 2)], writes=['yrT'])


def phase_merge(g, l):
    S, nc = g.S, g.nc
    last = (l == DEPTH - 1)
    with Phase(g) as A:
        wg = A('wg', [128, NCH, 3072], BF16)
        wp = [A('wp%d' % i, [128, 4, 1024], BF16) for i in range(3)]
        wo = A('wo', [128, NCH, 1024], BF16)
        HL = HLoader(g, A)
        yb = [[A('my%d_%d' % (i, j), [128, 4, 512], BF16) for j in range(2)] for i in range(3)]
        xb = [A('mx%d' % i, [128, NCH, 512]) for i in range(2)]
        sig = [A('msig%d' % i, [128, 512]) for i in range(2)]
        tm = [A('mtm%d' % i, [128, 512]) for i in range(2)]
        macc = A('macc', [128, 512])
        mT = A('mT', [128, NCH, 512], BF16)
        wsrc = g.w_in[l].rearrange('(k p) c -> p k c', p=128)
        for i in range(2):
            S.dma('pool', wg[:, :, i * 1536:(i + 1) * 1536], wsrc[:, :, 4480 + i * 1536:4480 + (i + 1) * 1536], writes=['wg'])
        for i, nm in enumerate(('p_conv', 'p_att', 'p_rwkv')):
            S.dma('pool', wp[i][:], getattr(g, nm)[l].rearrange('(k p) c -> p k c', p=128), writes=[('wp', i)])
        S.dma('pool', wo[:], g.w_out[l].rearrange('(k p) c -> p k c', p=128), writes=['wo'])
        xTv = g.xT.rearrange('(k p) t -> p k t', p=128)
        ysrc = [g.ycT.rearrange('(k p) t -> p k t', p=128), g.yaT.rearrange('(k p) t -> p k t', p=128), g.yrT.rearrange('(k p) t -> p k t', p=128)]
        ynm = ['ycT', 'yaT', 'yrT']
        for bi, (t0, nt) in enumerate(BLKS):
            if last and t0 < CTX:
                continue
            b = bi % 2
            s = 1 if t0 < CTX else 0
            hb, hk = HL.load(bi)
            for i in range(3):
                S.dma('sp', yb[i][b][:, :, :nt], ysrc[i][:, :, t0:t0 + nt], reads=[ynm[i]], writes=[('my', i, b)])
            S.dma('sp', xb[b][:, :, :nt], xTv[:, :, t0:t0 + nt], reads=['xT'], writes=[('mx', b)])
            for oc in range(NCH):
                for i in range(3):
                    pg, pp = nextps(g), nextps(g)
                    c0 = i * 1024 + oc * 128
                    mm_group(S, g.ps[pg][:, :nt], [(wg[:, k, c0:c0 + 128], hb[:, k, :nt]) for k in range(NCH)], ['wg', hk], [('ps', pg)])
                    mm_group(S, g.ps[pp][:, :nt], [(wp[i][:, k, oc * 128:(oc + 1) * 128], yb[i][b][:, k, :nt]) for k in range(4)], [('wp', i), ('my', i, b)], [('ps', pp)])
                    sb_ = i % 2
                    S.op('act', lambda e, pg=pg, sb_=sb_: e.activation(sig[sb_][:, :nt], g.ps[pg][:, :nt], AF.Sigmoid), [('ps', pg)], [('msig', sb_)])
                    if i == 0:
                        S.op('dve', lambda e, pp=pp, sb_=sb_: e.tensor_tensor(macc[:, :nt], g.ps[pp][:, :nt], sig[sb_][:, :nt], ALU.mult), [('ps', pp), ('msig', sb_)], ['macc'])
                    else:
                        S.op('dve', lambda e, pp=pp, sb_=sb_: e.tensor_tensor(tm[sb_][:, :nt], g.ps[pp][:, :nt], sig[sb_][:, :nt], ALU.mult), [('ps', pp), ('msig', sb_)], [('mtm', sb_)])
                        if i == 1:
                            S.op('pool', lambda e, sb_=sb_: e.tensor_tensor(macc[:, :nt], macc[:, :nt], tm[sb_][:, :nt], ALU.add), ['macc', ('mtm', sb_)], ['macc'])
                        else:
                            S.op('pool', lambda e, sb_=sb_, oc=oc: e.tensor_tensor(mT[:, oc, :nt], macc[:, :nt], tm[sb_][:, :nt], ALU.add), ['macc', ('mtm', sb_)], [('mT', oc)])
            for oc in range(NCH):
                po = nextps(g)
                mm_group(S, g.ps[po][:, :nt], [(wo[:, k, oc * 128:(oc + 1) * 128], mT[:, k, :nt]) for k in range(NCH)], ['wo'] + [('mT', k) for k in range(NCH)], [('ps', po)])
                S.op('dve', lambda e, po=po, oc=oc: e.scalar_tensor_tensor(xb[b][:, oc, :nt], g.ps[po][:, :nt], g.modv[:, 2 * 8 + oc, s:s + 1], xb[b][:, oc, :nt], ALU.mult, ALU.add),
                     [('ps', po), 'modv', ('mx', b)], [('mx', b)])
            S.dma('sp', xTv[:, :, t0:t0 + nt], xb[b][:, :, :nt], reads=[('mx', b)], writes=['xT'])


def phase_mlp(g, l):
    S, nc = g.S, g.nc
    last = (l == DEPTH - 1)
    with Phase(g) as A:
        w1 = A('w1', [128, NCH, DFF], BF16)
        w2 = A('w2', [128, DFF // 128, D], BF16)
        xb = [A('fx%d' % i, [128, NCH, 512]) for i in range(1)] * 2
        rs = A('frs', [128, 512]); tmp = [A('ftmp%d' % i, [128, 512]) for i in range(2)]
        h2 = A('fh2', [128, NCH, 512], BF16)
        act = A('fact', [128, DFF // 128, 512], BF16)
        sq = act[:, 0:16, :].bitcast(F32).rearrange('p (a two) b -> p a (two b)', two=2)
        fg = A('fg', [128, NCH])
        osb = A('fosb', [128, D])
        w1src = g.mlp_w1[l].rearrange('(k p) c -> p k c', p=128)
        for i in range(4):
            S.dma('pool', w1[:, :, i * 1024:(i + 1) * 1024], w1src[:, :, i * 1024:(i + 1) * 1024], writes=['w1'])
        w2src = g.mlp_w2[l].rearrange('(k p) c -> p k c', p=128)
        for i in range(4):
            S.dma('pool', w2[:, i * 8:(i + 1) * 8, :], w2src[:, i * 8:(i + 1) * 8, :], writes=['w2'])
        S.dma('sp', fg[:], g.fing[:, :], writes=['fg'])
        xTv = g.xT.rearrange('(k p) t -> p k t', p=128)
        for bi, (t0, nt) in enumerate(BLKS):
            if last and t0 < CTX:
                continue
            b = bi % 2
            s = 1 if t0 < CTX else 0
            x = xb[b]
            xk = ('fx', 0)
            S.dma('sp', x[:, :, :nt], xTv[:, :, t0:t0 + nt], reads=['xT'], writes=[xk])
            rms_stats(g, x, nt, sq, rs, xk, 'fact', 'frs')
            for k in range(NCH):
                tb = k % 2
                S.op('dve', lambda e, k=k, tb=tb: e.tensor_tensor(tmp[tb][:, :nt], x[:, k, :nt], rs[:, :nt], ALU.mult), [xk, 'frs'], [('ftmp', tb)])
                S.op('act', lambda e, k=k, tb=tb: e.activation(h2[:, k, :nt], tmp[tb][:, :nt], AF.Identity,
                     bias=g.modv[:, 3 * 8 + k, s:s + 1], scale=g.gs[:, 1, k, s:s + 1]), [('ftmp', tb), 'modv', 'gs'], ['fh2'])
            for fc in range(DFF // 128):
                pf = nextps(g)
                tb = fc % 2
                mm_group(S, g.ps[pf][:, :nt], [(w1[:, k, fc * 128:(fc + 1) * 128], h2[:, k, :nt]) for k in range(NCH)], ['w1', 'fh2'], [('ps', pf)])
                S.op('dve', lambda e, pf=pf, tb=tb: e.tensor_scalar(tmp[tb][:, :nt], g.ps[pf][:, :nt], 0.0, None, ALU.max), [('ps', pf)], [('ftmp', tb)])
                S.op('act', lambda e, fc=fc, tb=tb: e.activation(act[:, fc, :nt], tmp[tb][:, :nt], AF.Square), [('ftmp', tb)], ['fact'])
            for oc in range(NCH):
                po = nextps(g)
                mm_group(S, g.ps[po][:, :nt], [(w2[:, fc, oc * 128:(oc + 1) * 128], act[:, fc, :nt]) for fc in range(DFF // 128)],
                         ['w2', 'fact'], [('ps', po)])
                S.op('dve', lambda e, po=po, oc=oc: e.scalar_tensor_tensor(x[:, oc, :nt], g.ps[po][:, :nt], g.modv[:, 5 * 8 + oc, s:s + 1], x[:, oc, :nt], ALU.mult, ALU.add),
                     [('ps', po), 'modv', xk], [xk])
            if not last:
                S.dma('sp', xTv[:, :, t0:t0 + nt], x[:, :, :nt], reads=[xk], writes=['xT'])
                continue
            rms_stats(g, x, nt, sq, rs, xk, 'fact', 'frs')
            for k in range(NCH):
                S.op('dve', lambda e, k=k: e.tensor_tensor(sq[:, k, :nt], x[:, k, :nt], rs[:, :nt], ALU.mult), [xk, 'frs', 'fact'], ['fact'])
                S.op('act', lambda e, k=k: e.activation(sq[:, k, :nt], sq[:, k, :nt], AF.Identity, scale=fg[:, k:k + 1]), ['fact', 'fg'], ['fact'])
            for tt in range(nt // 128):
                for half in range(2):
                    pi = nextps(g)

                    def fn(pe, pi=pi, half=half, tt=tt):
                        inst = None
                        for j in range(4):
                            inst = pe.transpose(g.ps[pi][:, j * 128:(j + 1) * 128], sq[:, half * 4 + j, tt * 128:(tt + 1) * 128], g.identf[:])
                        return inst
                    S.op('pe', fn, ['fact', 'identf'], [('ps', pi)])
                    S.op('act', lambda e, pi=pi, half=half: e.copy(osb[:, half * 512:(half + 1) * 512], g.ps[pi][:, :]), [('ps', pi)], ['fosb'])
                r0 = t0 - CTX + tt * 128
                S.dma('sp', g.out[r0:r0 + 128, :], osb[:], reads=['fosb'], writes=['out'])


_CACHE = {}


def kernel(**inputs):
    inp = {k: np.asarray(v) for k, v in inputs.items()}
    if 'nc' not in _CACHE:
        _CACHE['nc'] = build()
    nc = _CACHE['nc']
    shared = host_shared(inp)
    B = inp['x'].shape[0]
    in_maps = [host_inputs(inp, b, shared) for b in range(B)]
    res = run_bass_kernel_spmd(nc, in_maps, core_ids=list(range(B)))
    return np.stack([np.asarray(res.results[b]['out']) for b in range(B)]).astype(np.float32)
```

```python
import math
import numpy as np
import concourse.bass as bass
import concourse.mybir as mybir
from concourse.bass_utils import run_bass_kernel_spmd

F32 = mybir.dt.float32
BF16 = mybir.dt.bfloat16
AF = mybir.ActivationFunctionType
ALU = mybir.AluOpType
AX = mybir.AxisListType

D = 1024
SEQ = 4096
CTX = 256
T = SEQ + CTX
DEPTH = 2
DIN = 7552
DFF = 4096
NCH = D // 128
BLKS = [(0, 256)] + [(256 + 512 * i, 512) for i in range(8)]
NORM_EPS = 1e-6
LN_EPS = 1e-5
SUBLN_EPS = 1e-5
GN_EPS = 64e-5


class Sched:
    def __init__(self, nc, n_dma=24):
        self.nc = nc
        self.eng = dict(pe=nc.tensor, dve=nc.vector, act=nc.scalar, pool=nc.gpsimd, sp=nc.sync)
        self.sem = {e: nc.alloc_semaphore('sem_' + e) for e in self.eng}
        self.cnt = {e: 0 for e in self.eng}
        self.dsem = [nc.alloc_semaphore('dsem%d' % i) for i in range(n_dma)]
        self.dval = [0] * n_dma
        self.drr = 0
        self.seen = {e: {} for e in self.eng}
        self.lastw = {}
        self.readers = {}
        self.nps = 0

    def _semh(self, key):
        return self.sem[key] if isinstance(key, str) else self.dsem[key]

    def _wait(self, e, key, val):
        if self.seen[e].get(key, 0) >= val:
            return
        self.eng[e].wait_ge(self._semh(key), val)
        self.seen[e][key] = val

    def _deps(self, e, reads, writes):
        need = {}
        for r in reads:
            tok = self.lastw.get(r)
            if tok is not None:
                need[tok[0]] = max(need.get(tok[0], 0), tok[1])
        for w in writes:
            tok = self.lastw.get(w)
            if tok is not None:
                need[tok[0]] = max(need.get(tok[0], 0), tok[1])
            for k, v in self.readers.get(w, {}).items():
                need[k] = max(need.get(k, 0), v)
        for k, v in need.items():
            self._wait(e, k, v)

    def _commit(self, tok, reads, writes):
        for w in writes:
            self.lastw[w] = tok
            self.readers[w] = {}
        for r in reads:
            if r in writes:
                continue
            d = self.readers.setdefault(r, {})
            d[tok[0]] = max(d.get(tok[0], 0), tok[1])

    def op(self, e, fn, reads=(), writes=()):
        if e == 'pe':
            self.seen[e]['pe'] = self.cnt['pe']
        self._deps(e, reads, writes)
        inst = fn(self.eng[e])
        self.cnt[e] += 1
        inst.then_inc(self.sem[e], 1)
        self._commit((e, self.cnt[e]), reads, writes)

    def dma(self, e, out, in_, reads=(), writes=(), **kw):
        self._deps(e, reads, writes)
        i = self.drr
        self.drr = (self.drr + 1) % len(self.dsem)
        if self.dval[i] > 0:
            self._wait(e, i, self.dval[i])
        self.dval[i] += 16
        self.eng[e].dma_start(out=out, in_=in_, **kw).then_inc(self.dsem[i], 16)
        self._commit((i, self.dval[i]), reads, writes)

    def barrier(self):
        for e in self.eng:
            for i, v in enumerate(self.dval):
                if v > 0:
                    self._wait(e, i, v)
            for k in self.eng:
                if k != e and self.cnt[k] > 0:
                    self._wait(e, k, self.cnt[k])

    def finish(self, e='sp'):
        for i, v in enumerate(self.dval):
            if v > 0:
                self._wait(e, i, v)
        for k in self.eng:
            if k != e and self.cnt[k] > 0:
                self._wait(e, k, self.cnt[k])


class Ctx:
    pass


class Phase:
    uid = 0

    def __init__(self, g):
        self.g = g
        self.guards = []

    def __enter__(self):
        return self

    def __call__(self, name, shape, dt=F32):
        Phase.uid += 1
        gd = self.g.nc.sbuf_tensor('%s_u%d' % (name, Phase.uid), list(shape), dt)
        t = gd.__enter__()
        self.guards.append(gd)
        return t

    def __exit__(self, *a):
        self.g.S.barrier()
        for gd in reversed(self.guards):
            gd.__exit__(None, None, None)
        return False


def mm_group(S, out, pairs, reads, writes, start=True, stop=True):
    n = len(pairs)

    def fn(pe):
        inst = None
        for i, (l, r) in enumerate(pairs):
            inst = pe.matmul(out, l, r, start=(start and i == 0), stop=(stop and i == n - 1))
        return inst
    S.op('pe', fn, reads, writes)


def build(dbg=(), upto='all'):
    nc = bass.Bass("TRN2", target_bir_lowering=False)
    S = Sched(nc)
    g = Ctx()
    g.nc, g.S = nc, S
    din = lambda name, shape, dt=F32: nc.dram_tensor(name, list(shape), dt, kind="ExternalInput").ap()
    dint = lambda name, shape, dt=F32: nc.dram_tensor(name, list(shape), dt, kind="Internal").ap()
    g.x = din('x', [SEQ, D]); g.ctx = din('ctx', [CTX, D])
    g.cvec = din('cvec', [128, NCH, 2])
    g.mod_w = din('mod_w', [DEPTH, D, 6 * D]); g.modb = din('modb', [DEPTH, 128, 48])
    g.n1g = din('n1g', [DEPTH, 128, NCH]); g.n2g = din('n2g', [DEPTH, 128, NCH]); g.fing = din('fing', [128, NCH])
    g.ident = din('ident', [128, 128])
    g.w_in = din('w_in', [DEPTH, D, DIN])
    g.convw = din('convw', [DEPTH, 128, 4, 31]); g.convp = din('convp', [DEPTH, 128, 3, 4])
    g.wqks = din('wqks', [DEPTH, D, 1024]); g.rope = din('rope', [2, 128, SEQ]); g.cmask = din('cmask', [3, 128, 128])
    g.attp = din('attp', [DEPTH, 4, 64]); g.subg = din('subg', [DEPTH, 128])
    g.rw_w2 = din('rw_w2', [DEPTH, 2, 64, 512]); g.rw_a2 = din('rw_a2', [DEPTH, 2, 64, 512]); g.rw_g2 = din('rw_g2', [DEPTH, 128, 512])
    g.rwp = din('rwp', [DEPTH, 128, P_OMKA]); g.smask = din('smask', [9, 128, 128])
    g.rS = dint('rS', [512, T], BF16); g.kkS = dint('kkS', [512, T], BF16); g.gS = dint('gS', [512, T], BF16); g.bonS = dint('bonS', [512, T], BF16)
    g.kdS = [dint('kdS%d' % d, [512, T], BF16) for d in range(2)]; g.bS = [dint('bS%d' % d, [512, T], BF16) for d in range(2)]
    g.wlS = [dint('wlS%d' % d, [512, T]) for d in range(2)]; g.vT = dint('vT', [T, 512], BF16); g.yfS = dint('yfS', [512, T]); g.ybS = dint('ybS', [512, T])
    g.p_conv = din('p_conv', [DEPTH, 512, D]); g.p_att = din('p_att', [DEPTH, 512, D]); g.p_rwkv = din('p_rwkv', [DEPTH, 512, D])
    g.w_out = din('w_out', [DEPTH, D, D]); g.mlp_w1 = din('mlp_w1', [DEPTH, D, DFF]); g.mlp_w2 = din('mlp_w2', [DEPTH, DFF, D])
    g.out = nc.dram_tensor('out', [SEQ, D], F32, kind="ExternalOutput").ap()
    g.xT = dint('xT', [D, T]); g.hT = dint('hTd', [D, T], BF16)
    g.ycT = dint('ycT', [512, T], BF16); g.yaT = dint('yaT', [512, T], BF16); g.yrT = dint('yrT', [512, T], BF16)
    g.dbg = {}
    for name, shape, dt in dbg:
        g.dbg[name] = nc.dram_tensor('dbg_' + name, list(shape), dt, kind="ExternalOutput").ap()

    sb = lambda name, shape, dt=F32: nc.alloc_sbuf_tensor(name, list(shape), dt)
    g.ps = [nc.alloc_psum_tensor('ps%d' % i, [128, 512], F32) for i in range(8)]
    g.psi = 0

    g.identf = sb('identf', [128, 128]); g.identb = sb('identb', [128, 128], BF16)
    g.onesf = sb('onesf', [128, 128]); g.onesb = sb('onesb', [128, 128], BF16)
    g.epsv = sb('epsv', [128, 4])
    g.cs = sb('cs', [128, NCH, 2]); g.modv = sb('modv', [128, 48, 2]); g.gs = sb('gs', [128, 2, NCH, 2])
    S.dma('sp', g.identf[:], g.ident[:, :], writes=['identf'])
    S.op('dve', lambda e: e.tensor_copy(g.identb[:], g.identf[:]), ['identf'], ['identb'])
    S.op('dve', lambda e: e.memset(g.onesf[:], 1.0), [], ['onesf'])
    S.op('dve', lambda e: e.memset(g.onesb[:], 1.0), [], ['onesb'])
    for i, v in enumerate((NORM_EPS, LN_EPS, SUBLN_EPS, GN_EPS)):
        S.op('dve', lambda e, i=i, v=v: e.memset(g.epsv[:, i:i + 1], v), [], ['epsv'])

    import os
    if os.environ.get('SCAN_LIMIT'):
        g.scan_limit = int(os.environ['SCAN_LIMIT'])
    if upto.startswith('rwonly'):
        g.scan_limit = int(upto[6:] or 0)
        phase_rwkv_scan(g, 0)
        S.finish('sp')
        return nc
    phase_x0(g)
    for l in range(DEPTH):
        phase_mod(g, l)
        phase_h(g, l, 0)
        if upto == 'h':
            break
        phase_conv(g, l)
        if upto == 'conv':
            break
        phase_att(g, l)
        if upto == 'att':
            break
        phase_rwkv_prep(g, l)
        if upto == 'rwprep':
            break
        phase_rwkv_scan(g, l)
        if upto == 'rw':
            break
        phase_merge(g, l)
        phase_mlp(g, l)
        if upto == 'l0':
            break
    for nm in ('ycT', 'yaT', 'yrT', 'hT', 'rS', 'kkS', 'gS', 'bonS', 'vT', 'yfS'):
        if nm in g.dbg:
            S.dma('sp', g.dbg[nm][:, :], getattr(g, nm)[:, :], reads=[nm], writes=['dbg_' + nm])
    for nm, ap in (('kdS0', g.kdS[0]), ('bS0', g.bS[0]), ('wlS0', g.wlS[0])):
        if nm in g.dbg:
            S.dma('sp', g.dbg[nm][:, :], ap[:, :], reads=[nm], writes=['dbg_' + nm])
    S.finish('sp')
    return nc


def nextps(g):
    i = g.psi
    g.psi = (g.psi + 1) % 8
    return i


def phase_x0(g):
    S, nc = g.S, g.nc
    with Phase(g) as A:
        xin = [A('x0in%d' % i, [128, D]) for i in range(2)]
        xst = [A('x0st%d' % i, [128, NCH, 128]) for i in range(2)]
        xTv = g.xT.rearrange('(k p) t -> p k t', p=128)
        for ti in range(T // 128):
            b = ti % 2
            src = g.ctx[ti * 128:(ti + 1) * 128, :] if ti < 2 else g.x[(ti - 2) * 128:(ti - 1) * 128, :]
            S.dma('sp', xin[b][:], src, writes=[('x0in', b)])
            for half in range(2):
                pi = nextps(g)
                ps = g.ps[pi]

                def fn(pe, half=half, ps=ps, b=b):
                    inst = None
                    for j in range(4):
                        k = half * 4 + j
                        inst = pe.transpose(ps[:, j * 128:(j + 1) * 128], xin[b][:, k * 128:(k + 1) * 128], g.identf[:])
                    return inst
                S.op('pe', fn, [('x0in', b), 'identf'], [('ps', pi)])
                dst = xst[b][:, half * 4:(half + 1) * 4, :]
                src_ps = ps[:].rearrange('p (j t) -> p j t', j=4)
                if half == 0:
                    S.op('act', lambda e, dst=dst, s=src_ps: e.copy(dst, s), [('ps', pi)], [('x0st', b, half)])
                else:
                    S.op('dve', lambda e, dst=dst, s=src_ps: e.tensor_copy(dst, s), [('ps', pi)], [('x0st', b, half)])
            S.dma('sp', xTv[:, :, ti * 128:(ti + 1) * 128], xst[b][:], reads=[('x0st', b, 0), ('x0st', b, 1)], writes=['xT'])


def phase_mod(g, l):
    S, nc = g.S, g.nc
    with Phase(g) as A:
        modbs = A('modbs', [128, 48])
        mw = [A('mw%d' % i, [128, NCH, 512]) for i in range(2)]
        ng = A('ng', [128, 2, NCH])
        if l == 0:
            tmp = A('cs_tmp', [128, NCH, 2])
            S.dma('sp', tmp[:], g.cvec[:, :, :], writes=['cs_tmp'])
            S.op('act', lambda e: e.activation(g.cs[:], tmp[:], AF.Sigmoid), ['cs_tmp'], ['cs'])
            S.op('dve', lambda e: e.tensor_tensor(g.cs[:], g.cs[:], tmp[:], ALU.mult), ['cs', 'cs_tmp'], ['cs'])
        S.dma('sp', modbs[:], g.modb[l], writes=['modbs'])
        S.dma('sp', ng[:, 0, :], g.n1g[l], writes=['ng'])
        S.dma('sp', ng[:, 1, :], g.n2g[l], writes=['ng'])
        mwv = g.mod_w[l].rearrange('(k p) c -> p k c', p=128)
        for cg in range(12):
            b = cg % 2
            S.dma('sp', mw[b][:], mwv[:, :, cg * 512:(cg + 1) * 512], writes=[('mw', b)])
            pi = nextps(g)
            ps = g.ps[pi]
            for j in range(4):
                pairs = [(mw[b][:, k, j * 128:(j + 1) * 128], g.cs[:, k, :]) for k in range(NCH)]
                mm_group(S, ps[:, 2 * j:2 * j + 2], pairs, [('mw', b), 'cs'], [('ps', pi)])
            for j in range(4):
                jj = cg * 4 + j
                S.op('dve', lambda e, j=j, jj=jj, ps=ps: e.tensor_scalar(g.modv[:, jj, :], ps[:, 2 * j:2 * j + 2],
                     modbs[:, jj:jj + 1], None, ALU.add), [('ps', pi), 'modbs'], ['modv'])
        for n in range(2):
            sc = g.modv[:, (3 * n + 1) * 8:(3 * n + 2) * 8, :]
            S.op('dve', lambda e, n=n, sc=sc: e.tensor_scalar(g.gs[:, n], sc, 1.0, None, ALU.add), ['modv'], ['gs'])
            S.op('dve', lambda e, n=n: e.tensor_tensor(g.gs[:, n], g.gs[:, n],
                 ng[:, n, :].unsqueeze(2).broadcast_to([128, NCH, 2]), ALU.mult), ['gs', 'ng'], ['gs'])


def rms_stats(g, xb, n, sq, rstd, key_x, key_sq, key_rstd):
    S = g.S
    S.op('act', lambda e: e.activation(sq[:, :, :n], xb[:, :, :n], AF.Square), [key_x], [key_sq])
    pi = nextps(g)
    ps = g.ps[pi]
    mm_group(S, ps[:, :n], [(g.onesf[:], sq[:, k, :n]) for k in range(NCH)], [key_sq, 'onesf'], [('ps', pi)])
    S.op('act', lambda e: e.activation(rstd[:, :n], ps[:, :n], AF.Sqrt, bias=g.epsv[:, 0:1], scale=1.0 / D),
         [('ps', pi), 'epsv'], [key_rstd])
    S.op('dve', lambda e: e.reciprocal(rstd[:, :n], rstd[:, :n]), [key_rstd], [key_rstd])


def phase_h(g, l, n):
    S, nc = g.S, g.nc
    with Phase(g) as A:
        hx = [A('hx%d' % i, [128, NCH, 512]) for i in range(2)]
        hsq = A('hsq', [128, NCH, 512])
        hrs = [A('hrs%d' % i, [128, 512]) for i in range(2)]
        htmp = [A('htmp%d' % i, [128, 512]) for i in range(2)]
        hb = [A('hb%d' % i, [128, NCH, 512], BF16) for i in range(2)]
        xTv = g.xT.rearrange('(k p) t -> p k t', p=128)
        hTv = g.hT.rearrange('(k p) t -> p k t', p=128)
        for bi, (t0, nt) in enumerate(BLKS):
            b = bi % 2
            s = 1 if t0 < CTX else 0
            S.dma('sp', hx[b][:, :, :nt], xTv[:, :, t0:t0 + nt], reads=['xT'], writes=[('hx', b)])
            rms_stats(g, hx[b], nt, hsq, hrs[b], ('hx', b), 'hsq', ('hrs', b))
            for k in range(NCH):
                tb = k % 2
                S.op('dve', lambda e, k=k, tb=tb: e.tensor_tensor(htmp[tb][:, :nt], hx[b][:, k, :nt], hrs[b][:, :nt], ALU.mult),
                     [('hx', b), ('hrs', b)], [('htmp', tb)])
                S.op('act', lambda e, k=k, tb=tb: e.activation(hb[b][:, k, :nt], htmp[tb][:, :nt], AF.Identity,
                     bias=g.modv[:, (3 * n) * 8 + k, s:s + 1], scale=g.gs[:, n, k, s:s + 1]),
                     [('htmp', tb), 'modv', 'gs'], [('hb', b)])
            S.dma('sp', hTv[:, :, t0:t0 + nt], hb[b][:, :, :nt], reads=[('hb', b)], writes=['hT'])


class HLoader:
    def __init__(self, g, A):
        self.g = g
        self.t = [A('hblk%d' % i, [128, NCH, 512], BF16) for i in range(2)]
        self.i = 0

    def load(self, bi):
        g = self.g
        b = self.i
        self.i = (b + 1) % 2
        t0, nt = BLKS[bi]
        hTv = g.hT.rearrange('(k p) t -> p k t', p=128)
        g.S.dma('sp', self.t[b][:, :, :nt], hTv[:, :, t0:t0 + nt], reads=['hT'], writes=[('hblk', b)])
        return self.t[b], ('hblk', b)


def ucol(t):
    return t + 15 if t < CTX else t + 45


def phase_conv(g, l):
    S, nc = g.S, g.nc
    with Phase(g) as A:
        wcv = A('wcv', [128, NCH, 1024], BF16)
        uT = A('uT', [128, 4, T + 60], BF16)
        diag = A('diag', [128, 4, 31, 128], BF16)
        dww = A('dww', [128, 4, 31])
        cvp = A('cvp', [128, 3, 4])
        csg = [A('csg%d' % i, [128, 512]) for i in range(2)]
        cv = A('cv', [128, 4, 512]); cv2 = A('cv2', [128, 4, 512])
        cm = A('cm', [128, 512]); cmsq = A('cmsq', [128, 512]); crs = A('crs', [128, 512])
        ct = [A('ct%d' % i, [128, 512]) for i in range(2)]
        cyb = [A('cyb%d' % i, [128, 4, 512], BF16) for i in range(2)]
        HL = HLoader(g, A)
        S.op('pool', lambda e: e.memset(uT[:], 0.0), [], ['uT'])
        S.dma('pool', wcv[:], g.w_in[l][:, 0:1024].rearrange('(k p) c -> p k c', p=128), writes=['wcv'])
        S.dma('sp', dww[:], g.convw[l], writes=['dww'])
        S.dma('sp', cvp[:], g.convp[l], writes=['cvp'])
        for c in range(4):
            S.op('dve', lambda e, c=c: e.tensor_tensor(diag[:, c], g.identf[:].unsqueeze(1).broadcast_to([128, 31, 128]),
                 dww[:, c, :].unsqueeze(2).broadcast_to([128, 31, 128]), ALU.mult), ['identf', 'dww'], ['diag'])
        for bi, (t0, nt) in enumerate(BLKS):
            hb, hk = HL.load(bi)
            for c in range(4):
                pa, pb = nextps(g), nextps(g)
                mm_group(S, g.ps[pa][:, :nt], [(wcv[:, k, c * 128:(c + 1) * 128], hb[:, k, :nt]) for k in range(NCH)],
                         ['wcv', hk], [('ps', pa)])
                mm_group(S, g.ps[pb][:, :nt], [(wcv[:, k, 512 + c * 128:512 + (c + 1) * 128], hb[:, k, :nt]) for k in range(NCH)],
                         ['wcv', hk], [('ps', pb)])
                sb_ = c % 2
                S.op('act', lambda e, pb=pb, sb_=sb_: e.activation(csg[sb_][:, :nt], g.ps[pb][:, :nt], AF.Sigmoid),
                     [('ps', pb)], [('csg', sb_)])
                S.op('dve', lambda e, pa=pa, sb_=sb_, c=c: e.tensor_tensor(uT[:, c, ucol(t0):ucol(t0) + nt], g.ps[pa][:, :nt],
                     csg[sb_][:, :nt], ALU.mult), [('ps', pa), ('csg', sb_)], ['uT'])
        ycv = g.ycT.rearrange('(k p) t -> p k t', p=128)
        for bi, (t0, nt) in enumerate(BLKS):
            yb = cyb[bi % 2]
            ykey = ('cyb', bi % 2)
            for c in range(4):
                pi = nextps(g)
                base = ucol(t0) - 15
                mm_group(S, g.ps[pi][:, :nt], [(diag[:, c, k, :], uT[:, c, base + k:base + k + nt]) for k in range(31)],
                         ['diag', 'uT'], [('ps', pi)])
                S.op('act', lambda e, pi=pi, c=c: e.activation(cv[:, c, :nt], g.ps[pi][:, :nt], AF.Identity,
                     bias=cvp[:, 0, c:c + 1], scale=1.0), [('ps', pi), 'cvp'], [('cv', c)])
                S.op('act', lambda e, pi=pi, c=c: e.activation(cv2[:, c, :nt], g.ps[pi][:, :nt], AF.Square,
                     bias=cvp[:, 0, c:c + 1], scale=1.0), [('ps', pi), 'cvp'], [('cv2', c)])
            p1, p2 = nextps(g), nextps(g)
            mm_group(S, g.ps[p1][:, :nt], [(g.onesf[:], cv[:, c, :nt]) for c in range(4)], [('cv', c) for c in range(4)] + ['onesf'], [('ps', p1)])
            mm_group(S, g.ps[p2][:, :nt], [(g.onesf[:], cv2[:, c, :nt]) for c in range(4)], [('cv2', c) for c in range(4)] + ['onesf'], [('ps', p2)])
            S.op('act', lambda e: e.activation(cm[:, :nt], g.ps[p1][:, :nt], AF.Identity, scale=1.0 / 512), [('ps', p1)], ['cm'])
            S.op('dve', lambda e: e.tensor_tensor(cmsq[:, :nt], cm[:, :nt], cm[:, :nt], ALU.mult), ['cm'], ['cmsq'])
            S.op('dve', lambda e: e.scalar_tensor_tensor(crs[:, :nt], g.ps[p2][:, :nt], 1.0 / 512, cmsq[:, :nt], ALU.mult, ALU.subtract),
                 [('ps', p2), 'cmsq'], ['crs'])
            S.op('act', lambda e: e.activation(crs[:, :nt], crs[:, :nt], AF.Sqrt, bias=g.epsv[:, 1:2], scale=1.0), ['crs', 'epsv'], ['crs'])
            S.op('dve', lambda e: e.reciprocal(crs[:, :nt], crs[:, :nt]), ['crs'], ['crs'])
            for c in range(4):
                tb = c % 2
                S.op('dve', lambda e, c=c, tb=tb: e.tensor_tensor(ct[tb][:, :nt], cv[:, c, :nt], cm[:, :nt], ALU.subtract),
                     [('cv', c), 'cm'], [('ct', tb)])
                S.op('dve', lambda e, c=c, tb=tb: e.tensor_tensor(ct[tb][:, :nt], ct[tb][:, :nt], crs[:, :nt], ALU.mult),
                     [('ct', tb), 'crs'], [('ct', tb)])
                S.op('act', lambda e, c=c, tb=tb: e.activation(yb[:, c, :nt], ct[tb][:, :nt], AF.Silu,
                     bias=cvp[:, 2, c:c + 1], scale=cvp[:, 1, c:c + 1]), [('ct', tb), 'cvp'], [ykey])
            S.dma('sp', ycv[:, :, t0:t0 + nt], yb[:, :, :nt], reads=[ykey], writes=['ycT'])


def fm(v, nch):
    return np.ascontiguousarray(np.asarray(v, np.float32).reshape(nch, 128).T)


def host_shared(inp):
    m = {}
    m['mod_w'] = np.ascontiguousarray(inp['mod_w'], dtype=np.float32)
    m['modb'] = np.stack([fm(inp['mod_b'][l], 48) for l in range(DEPTH)])
    m['n1g'] = np.stack([fm(inp['norm1_g'][l], NCH) for l in range(DEPTH)])
    m['n2g'] = np.stack([fm(inp['norm2_g'][l], NCH) for l in range(DEPTH)])
    m['fing'] = fm(inp['final_g'], NCH)
    m['ident'] = np.eye(128, dtype=np.float32)
    m['w_in'] = np.ascontiguousarray(inp['w_in'], dtype=np.float32)
    sw = np.arange(1024) ^ 1
    m['wqks'] = np.ascontiguousarray(np.asarray(inp['w_in'])[:, :, 1024:2048][:, :, sw], dtype=np.float32)
    tt = np.arange(SEQ)
    inv = (10000.0 ** (-np.arange(16, dtype=np.float32) / 16)).astype(np.float32)
    ang = np.concatenate([(tt // 64).astype(np.float32)[:, None] * inv, (tt % 64).astype(np.float32)[:, None] * inv], axis=-1)
    pidx = (np.arange(128) % 64) // 2
    cosT = np.cos(ang)[:, pidx].T
    sinT = np.sin(ang)[:, pidx].T * np.where(np.arange(128) % 2 == 0, -1.0, 1.0)[:, None]
    m['rope'] = np.ascontiguousarray(np.stack([cosT, sinT]), dtype=np.float32)
    blk = (np.arange(128) // 64)
    bdm = (blk[:, None] == blk[None, :]).astype(np.float32)
    sel0 = np.repeat((blk == 0).astype(np.float32)[:, None], 128, 1)
    sel1 = np.repeat((blk == 1).astype(np.float32)[:, None], 128, 1)
    m['cmask'] = np.ascontiguousarray(np.stack([bdm, sel0, sel1]))
    m['attp'] = np.ascontiguousarray(np.stack([np.stack([inp[k][l] for k in ('att_lq1', 'att_lk1', 'att_lq2', 'att_lk2')]) for l in range(DEPTH)]), dtype=np.float32)
    m['subg'] = np.ascontiguousarray(inp['att_subln_g'], dtype=np.float32)
    for k in ('p_conv', 'p_att', 'p_rwkv', 'w_out', 'mlp_w1', 'mlp_w2'):
        m[k] = np.ascontiguousarray(inp[k], dtype=np.float32)
    m['rw_w2'] = np.ascontiguousarray(inp['rwkv_w2'], dtype=np.float32)
    m['rw_a2'] = np.ascontiguousarray(inp['rwkv_a2'], dtype=np.float32)
    m['rw_g2'] = np.ascontiguousarray(inp['rwkv_g2'], dtype=np.float32)
    rwp = []
    for l in range(DEPTH):
        cols = [np.asarray(inp['rwkv_shift'][l]).T.reshape(15, 128, 3).transpose(1, 0, 2).reshape(128, 45)]
        cols += [fm(inp['rwkv_w0'][l].reshape(-1), 8), fm(inp['rwkv_a0'][l].reshape(-1), 8)]
        cols += [fm(inp[k][l].reshape(-1), 4) for k in ('rwkv_kk', 'rwkv_ka', 'rwkv_rk', 'rwkv_gn_g', 'rwkv_gn_b')]
        rwp.append(np.concatenate(cols, axis=1))
    m['rwp'] = np.ascontiguousarray(np.stack(rwp), dtype=np.float32)
    ii = np.arange(128)
    lt = (ii[:, None] < ii[None, :]).astype(np.float32); le = (ii[:, None] <= ii[None, :]).astype(np.float32)
    seg = np.repeat((ii != 0).astype(np.float32)[None, :], 128, 0)
    blk = lambda n: (ii[:, None] // n == ii[None, :] // n)
    offm = lambda n: (blk(n) & ~blk(n // 2)).astype(np.float32)
    m['smask'] = np.ascontiguousarray(np.stack([lt, le, lt.T, le.T, seg, blk(16).astype(np.float32), offm(32), offm(64), offm(128)]))
    m['convw'] = np.stack([np.ascontiguousarray(np.asarray(inp['conv_dw_w'][l]).T.reshape(4, 128, 31).transpose(1, 0, 2)) for l in range(DEPTH)])
    m['convp'] = np.stack([np.stack([fm(inp[k][l], 4) for k in ('conv_dw_b', 'conv_ln_g', 'conv_ln_b')], axis=1) for l in range(DEPTH)])
    return m


def host_inputs(inp, b, shared=None):
    m = dict(shared if shared is not None else host_shared(inp))
    m['x'] = np.ascontiguousarray(inp['x'][b], dtype=np.float32)
    m['ctx'] = np.ascontiguousarray(inp['ctx'][b], dtype=np.float32)
    m['cvec'] = np.ascontiguousarray(np.stack([fm(inp['c'][b], NCH), fm(inp['c_ctx'], NCH)], axis=-1))
    return m


def phase_att(g, l):
    S, nc = g.S, g.nc
    lam_init = 0.8 - 0.6 * math.exp(-0.3 * l)
    need_ctx_q = l < DEPTH - 1
    with Phase(g) as A:
        qT = A('qT', [128, 4, T], BF16); kT = A('kT', [128, 4, T], BF16)
        vaug = A('vaug', [128, T // 128, 4, 129], BF16)
        nb = A('nb', [128, 2, 4]); neglam = A('neglam', [128, 1]); gsub = A('gsub', [128, 128])
        A1 = Phase(g)
        wq = A1('wq', [128, NCH, 512], BF16); wqs = A1('wqs', [128, NCH, 512], BF16)
        wk = A1('wk', [128, NCH, 512], BF16); wks = A1('wks', [128, NCH, 512], BF16)
        wv = A1('wv', [128, NCH, 512], BF16)
        cosT = A1('cosT', [128, SEQ]); sinT = A1('sinT', [128, SEQ])
        HL = HLoader(g, A1)
        rt = [A1('rt%d' % i, [128, 512]) for i in range(4)]
        bd = A1('bd', [128, 128], BF16); cmf = A1('cmf', [128, 3, 128])
        stat = A1('stat', [128, 2, 4, len(BLKS)]); stm = A1('stm', [128, 2, 4]); negb = A1('negb', [128, 4])
        lqk = A1('lqk', [128, 4, 64]); lam2 = A1('lam2', [128, 2])

        wsrc = g.w_in[l].rearrange('(k p) c -> p k c', p=128)
        ssrc = g.wqks[l].rearrange('(k p) c -> p k c', p=128)
        S.dma('pool', wq[:], wsrc[:, :, 1024:1536], writes=['wq'])
        S.dma('pool', wk[:], wsrc[:, :, 1536:2048], writes=['wk'])
        S.dma('pool', wv[:], wsrc[:, :, 2048:2560], writes=['wv'])
        S.dma('pool', wqs[:], ssrc[:, :, 0:512], writes=['wqs'])
        S.dma('pool', wks[:], ssrc[:, :, 512:1024], writes=['wks'])
        S.dma('sp', cosT[:], g.rope[0], writes=['cosT'])
        S.dma('sp', sinT[:], g.rope[1], writes=['sinT'])
        S.dma('sp', cmf[:], g.cmask.rearrange('m p c -> p m c'), writes=['cmf'])
        S.op('dve', lambda e: e.tensor_copy(bd[:], cmf[:, 0, :]), ['cmf'], ['bd'])
        S.dma('sp', lqk[:], g.attp[l:l + 1].broadcast_to([128, 4, 64]), writes=['lqk'])
        S.dma('sp', gsub[:], g.subg[l:l + 1, :].broadcast_to([128, 128]), writes=['gsub'])
        S.op('act', lambda e: e.mul(gsub[:], gsub[:], 1.0 - lam_init), ['gsub'], ['gsub'])
        S.op('dve', lambda e: e.tensor_tensor(lqk[:, 0, :], lqk[:, 0, :], lqk[:, 1, :], ALU.mult), ['lqk'], ['lqk'])
        S.op('dve', lambda e: e.tensor_tensor(lqk[:, 2, :], lqk[:, 2, :], lqk[:, 3, :], ALU.mult), ['lqk'], ['lqk'])
        S.op('dve', lambda e: e.reduce_sum(lam2[:, 0:1], lqk[:, 0, :], AX.X), ['lqk'], ['lam2'])
        S.op('dve', lambda e: e.reduce_sum(lam2[:, 1:2], lqk[:, 2, :], AX.X), ['lqk'], ['lam2'])
        S.op('act', lambda e: e.activation(lam2[:], lam2[:], AF.Exp), ['lam2'], ['lam2'])
        S.op('dve', lambda e: e.tensor_tensor(neglam[:], lam2[:, 1:2], lam2[:, 0:1], ALU.subtract), ['lam2'], ['neglam'])
        S.op('dve', lambda e: e.tensor_scalar(neglam[:], neglam[:], -lam_init, None, ALU.add), ['neglam'], ['neglam'])
        S.op('pool', lambda e: e.memset(vaug[:, :, :, 128:129], 1.0), [], ['vaug1'])

        for bi, (t0, nt) in enumerate(BLKS):
            hb, hk = HL.load(bi)
            lat = t0 >= CTX
            tl = t0 - CTX
            for (w, ws, dst, dk, wkey, wskey) in ((wq, wqs, qT, 'qT', 'wq', 'wqs'), (wk, wks, kT, 'kT', 'wk', 'wks')):
                for h in range(4):
                    pa = nextps(g)
                    mm_group(S, g.ps[pa][:, :nt], [(w[:, k, h * 128:(h + 1) * 128], hb[:, k, :nt]) for k in range(NCH)],
                             [wkey, hk], [('ps', pa)])
                    if not lat:
                        S.op('act', lambda e, pa=pa, h=h, dst=dst: e.copy(dst[:, h, t0:t0 + nt], g.ps[pa][:, :nt]), [('ps', pa)], [dk])
                        continue
                    pb = nextps(g)
                    mm_group(S, g.ps[pb][:, :nt], [(ws[:, k, h * 128:(h + 1) * 128], hb[:, k, :nt]) for k in range(NCH)],
                             [wskey, hk], [('ps', pb)])
                    r1, r2 = (0, 1) if h % 2 == 0 else (2, 3)
                    S.op('dve', lambda e, pa=pa, r1=r1: e.tensor_tensor(rt[r1][:, :nt], g.ps[pa][:, :nt], cosT[:, tl:tl + nt], ALU.mult),
                         [('ps', pa), 'cosT'], [('rt', r1)])
                    S.op('dve', lambda e, pb=pb, r2=r2: e.tensor_tensor(rt[r2][:, :nt], g.ps[pb][:, :nt], sinT[:, tl:tl + nt], ALU.mult),
                         [('ps', pb), 'sinT'], [('rt', r2)])
                    S.op('pool', lambda e, r1=r1, r2=r2, h=h, dst=dst: e.tensor_tensor(dst[:, h, t0:t0 + nt], rt[r1][:, :nt], rt[r2][:, :nt], ALU.add),
                         [('rt', r1), ('rt', r2)], [dk])
            for tt in range(nt // 128):
                ti = t0 // 128 + tt
                pv = nextps(g)
                mm_group(S, g.ps[pv][:, :512], [(hb[:, k, tt * 128:(tt + 1) * 128], wv[:, k, :]) for k in range(NCH)],
                         ['wv', hk], [('ps', pv)])
                S.op('act', lambda e, pv=pv, ti=ti: e.copy(vaug[:, ti, :, 0:128], g.ps[pv][:, :].rearrange('p (h d) -> p h d', h=4)),
                     [('ps', pv)], ['vaug'])
        sqb = rt
        for qi, (src, sk) in enumerate(((qT, 'qT'), (kT, 'kT'))):
            for h in range(4):
                for bi, (t0, nt) in enumerate(BLKS):
                    r = (h * len(BLKS) + bi) % 4
                    sq = rt[r][:, 0:256].bitcast(BF16)
                    S.op('act', lambda e, sq=sq, h=h, src=src: e.activation(sq[:, :nt], src[:, h, t0:t0 + nt], AF.Square), [sk], [('rt', r)])
                    pi = nextps(g)
                    mm_group(S, g.ps[pi][:, :nt], [(bd[:], sq[:, :nt])], ['bd', ('rt', r)], [('ps', pi)])
                    S.op('dve', lambda e, pi=pi, h=h, bi=bi, qi=qi: e.reduce_max(stat[:, qi, h, bi:bi + 1], g.ps[pi][:, :nt], AX.X),
                         [('ps', pi)], ['stat'])
        S.op('dve', lambda e: e.reduce_max(stm[:], stat[:], AX.X), ['stat'], ['stm'])
        S.op('dve', lambda e: e.tensor_tensor(negb[:], stm[:, 0, :], stm[:, 1, :], ALU.mult), ['stm'], ['negb'])
        S.op('act', lambda e: e.activation(negb[:], negb[:], AF.Sqrt), ['negb'], ['negb'])
        for c in range(2):
            pi = nextps(g)
            mm_group(S, g.ps[pi][:, 0:4], [(cmf[:, 1 + c, :], negb[:])], ['cmf', 'negb'], [('ps', pi)])
            S.op('act', lambda e, pi=pi, c=c: e.mul(nb[:, c, :], g.ps[pi][:, 0:4], -1.02 * 0.125 / 64.0), [('ps', pi)], ['nb'])

        A1.__exit__(None, None, None)
        pT = [A('pT%d' % i, [128, 512], BF16) for i in range(6)]
        oc = [[A('oc%d_%d' % (c, q), [128, 129]) for q in range(4)] for c in range(2)]
        sm = A('sm', [128, 8]); o0 = A('o0', [128, 128]); aa = A('aa', [128, 128]); junk = A('junk', [128, 128])
        ytok = [A('ytok%d' % q, [128, 512], BF16) for q in range(4)]
        yab = [A('yab%d' % i, [128, 4, 512], BF16) for i in range(2)]
        yav = g.yaT.rearrange('(k p) t -> p k t', p=128)
        pti = [0]
        sbank = [0]

        def attend(q0, nq, kt0, nkt, yslot):
            nqs = nq // 128
            items = [(h, c, kk) for h in range(4) for c in range(2) for kk in range(nkt)]
            DPF = 2
            slots = {}

            def front(i):
                h, c, kk = items[i]
                kt = kt0 + kk
                sb_ = 4 + sbank[0]
                sbank[0] = (sbank[0] + 1) % 4
                mm_group(S, g.ps[sb_][:, :nq], [(kT[64 * c:64 * c + 64, h, kt * 128:(kt + 1) * 128], qT[64 * c:64 * c + 64, h, q0:q0 + nq])],
                         ['kT', 'qT'], [('ps', sb_)])
                pb_ = pti[0]
                pti[0] = (pti[0] + 1) % 6
                S.op('act', lambda e: e.activation(pT[pb_][:, :nq], g.ps[sb_][:, :nq], AF.Exp,
                     bias=nb[:, c, h:h + 1], scale=0.125), [('ps', sb_), 'nb'], [('pT', pb_)])
                slots[i] = pb_

            def back(i):
                h, c, kk = items[i]
                kt = kt0 + kk
                pb_ = slots.pop(i)
                for qs in range(nqs):
                    mm_group(S, g.ps[qs][:, 0:129], [(pT[pb_][:, qs * 128:(qs + 1) * 128], vaug[:, kt, h, :])],
                             [('pT', pb_), 'vaug', 'vaug1'], [('ps', qs)], start=(kk == 0), stop=(kk == nkt - 1))
                if kk != nkt - 1:
                    return
                for qs in range(nqs):
                    S.op('dve', lambda e, qs=qs: e.tensor_copy(oc[c][qs][:], g.ps[qs][:, 0:129]), [('ps', qs)], [('oc', c, qs)])
                if c != 1:
                    return
                for qs in range(nqs):
                    k0, k1 = ('oc', 0, qs), ('oc', 1, qs)
                    S.op('dve', lambda e, qs=qs: e.reciprocal(sm[:, 0:1], oc[0][qs][:, 128:129]), [k0], ['sm0'])
                    S.op('dve', lambda e, qs=qs: e.reciprocal(sm[:, 1:2], oc[1][qs][:, 128:129]), [k1], ['sm1'])
                    S.op('dve', lambda e: e.tensor_tensor(sm[:, 2:3], sm[:, 1:2], neglam[:], ALU.mult), ['sm1', 'neglam'], ['sm2'])
                    S.op('dve', lambda e, qs=qs: e.tensor_scalar(o0[:], oc[0][qs][:, 0:128], sm[:, 0:1], None, ALU.mult), [k0, 'sm0'], ['o0'])
                    S.op('dve', lambda e, qs=qs: e.scalar_tensor_tensor(aa[:], oc[1][qs][:, 0:128], sm[:, 2:3], o0[:], ALU.mult, ALU.add),
                         [k1, 'sm2', 'o0'], ['aa'])
                    S.op('dve', lambda e: e.scalar_tensor_tensor(junk[:], aa[:], 1.0, aa[:], ALU.mult, ALU.mult, accum_out=sm[:, 3:4]), ['aa'], ['junk', 'sm3'])
                    S.op('act', lambda e: e.activation(sm[:, 4:5], sm[:, 3:4], AF.Sqrt, bias=g.epsv[:, 2:3], scale=1.0 / 128), ['sm3', 'epsv'], ['sm4'])
                    S.op('dve', lambda e: e.reciprocal(sm[:, 5:6], sm[:, 4:5]), ['sm4'], ['sm5'])
                    S.op('dve', lambda e, qs=qs: e.scalar_tensor_tensor(ytok[qs][:, h * 128:(h + 1) * 128], aa[:], sm[:, 5:6], gsub[:], ALU.mult, ALU.mult),
                         ['aa', 'sm5', 'gsub'], [('ytok', qs)])

            for i in range(len(items) + DPF):
                if i < len(items):
                    front(i)
                if i - DPF >= 0:
                    back(i - DPF)
            yb = yab[yslot % 2]
            ykey = ('yab', yslot % 2)
            for qs in range(nqs):
                tb_ = 4 + sbank[0]
                sbank[0] = (sbank[0] + 1) % 4
                psb = g.ps[tb_][:, :].bitcast(BF16)

                def fn(pe, qs=qs, psb=psb):
                    inst = None
                    for h in range(4):
                        inst = pe.transpose(psb[:, h * 128:(h + 1) * 128], ytok[qs][:, h * 128:(h + 1) * 128], g.identb[:])
                    return inst
                S.op('pe', fn, [('ytok', qs), 'identb'], [('ps', tb_)])
                S.op('dve', lambda e, qs=qs, psb=psb, yb=yb: e.tensor_copy(yb[:, :, qs * 128:(qs + 1) * 128], psb[:, 0:512].rearrange('p (h q) -> p h q', h=4)),
                     [('ps', tb_)], [ykey])
            S.dma('sp', yav[:, :, q0:q0 + nq], yb[:, :, :nq], reads=[ykey], writes=['yaT'])

        slot = 0
        if need_ctx_q:
            attend(0, CTX, 0, CTX // 128, slot)
            slot += 1
        for qb in range(SEQ // 512):
            attend(CTX + qb * 512, 512, 0, T // 128, slot)
            slot += 1


RW0 = 2560
P_SH, P_W0, P_A0, P_KK, P_KA, P_RK, P_GG, P_GB, P_OMKA, NRWP = 0, 45, 53, 61, 65, 69, 73, 77, 81, 85
DECAY_C = -math.exp(-0.5)


def phase_rwkv_prep(g, l):
    S, nc = g.S, g.nc
    with Phase(g) as A:
        wrw = A('wrw', [128, NCH, 1920], BF16)
        w2b = A('w2b', [128, 512], BF16); a2b = A('a2b', [128, 512], BF16); g2b = A('g2b', [128, 512], BF16)
        rwp = A('rwp', [128, NRWP])
        bdf = A('bdf', [128, 128])
        hbx = [A('hbx%d' % i, [128, NCH, 514], BF16) for i in range(2)]
        zx = [A('zx%d' % i, [128, 514]) for i in range(3)]
        zc = A('zc', [128, 15, 512])
        tw = A('tw', [128, 512], BF16); ab = A('ab', [128, 512], BF16); sg = A('sg', [128, 512], BF16)
        kkt = A('kkt', [128, 4, 512]); kds = A('kds', [128, 4, 512])
        t1 = [A('rt1_%d' % i, [128, 512]) for i in range(3)]
        ob = {n: [A('ob_%s%d' % (n, i), [128, 4, 512], BF16) for i in range(1)] * 2 for n in ('r', 'kk', 'kd0', 'kd1', 'b0', 'b1', 'g', 'bon')}
        owl = {d: [A('owl%d_%d' % (d, i), [128, 4, 512]) for i in range(1)] * 2 for d in range(2)}
        vtile = [A('vtile%d' % i, [128, 512], BF16) for i in range(2)]

        wsrc = g.w_in[l].rearrange('(k p) c -> p k c', p=128)
        S.dma('pool', wrw[:], wsrc[:, :, RW0:RW0 + 1920], writes=['wrw'])
        S.dma('pool', w2b[:], g.rw_w2[l].rearrange('d m c -> (d m) c'), writes=['w2b'])
        S.dma('pool', a2b[:], g.rw_a2[l].rearrange('d m c -> (d m) c'), writes=['a2b'])
        S.dma('pool', g2b[:], g.rw_g2[l], writes=['g2b'])
        S.dma('sp', rwp[:, 0:P_OMKA], g.rwp[l], writes=['rwp'])
        S.dma('sp', bdf[:], g.cmask[0], writes=['bdf'])
        S.op('dve', lambda e: e.tensor_scalar(rwp[:, P_OMKA:P_OMKA + 4], rwp[:, P_KA:P_KA + 4], -1.0, 1.0, ALU.mult, ALU.add), ['rwp'], ['rwp'])
        hTv = g.hT.rearrange('(k p) t -> p k t', p=128)
        fmv = lambda ap: ap.rearrange('(k p) t -> p k t', p=128)
        for bi, (t0, nt) in enumerate(BLKS):
            b = bi % 2
            hb = hbx[b]
            hk = ('hbx', b)
            s0, s1 = (0, CTX) if t0 < CTX else (CTX, T)
            lo, hi = max(s0, t0 - 1), min(s1, t0 + nt + 1)
            if lo == t0:
                S.op('pool', lambda e, hb=hb: e.memset(hb[:, :, 0:1], 0.0), [], [hk])
            if hi == t0 + nt:
                S.op('pool', lambda e, hb=hb: e.memset(hb[:, :, nt + 1:nt + 2], 0.0), [], [hk])
            S.dma('sp', hb[:, :, 1 - (t0 - lo):1 + (hi - t0)], hTv[:, :, lo:hi], reads=['hT'], writes=[hk])
            ph = nextps(g)
            for ch in range(15):
                pm = nextps(g)
                if pm == ph:
                    pm = nextps(g)
                wsl = lambda k, ch=ch: wrw[:, k, ch * 128:(ch + 1) * 128]
                mm_group(S, g.ps[pm][:, :nt], [(wsl(k), hb[:, k, 1:1 + nt]) for k in range(NCH)], ['wrw', hk], [('ps', pm)])
                mm_group(S, g.ps[ph][:, 2 * ch:2 * ch + 1], [(wsl(k), hb[:, k, 0:1]) for k in range(NCH)], ['wrw', hk], [('ps', ph)])
                mm_group(S, g.ps[ph][:, 2 * ch + 1:2 * ch + 2], [(wsl(k), hb[:, k, nt + 1:nt + 2]) for k in range(NCH)], ['wrw', hk], [('ps', ph)])
                z = zx[ch % 3]
                zk = ('zx', ch % 3)
                S.op('act', lambda e, z=z, pm=pm: e.copy(z[:, 1:1 + nt], g.ps[pm][:, :nt]), [('ps', pm)], [zk])
                S.op('act', lambda e, z=z, ch=ch: e.copy(z[:, 0:1], g.ps[ph][:, 2 * ch:2 * ch + 1]), [('ps', ph)], [zk])
                S.op('act', lambda e, z=z, ch=ch: e.copy(z[:, nt + 1:nt + 2], g.ps[ph][:, 2 * ch + 1:2 * ch + 2]), [('ps', ph)], [zk])
                sh = lambda j, ch=ch: rwp[:, P_SH + ch * 3 + j:P_SH + ch * 3 + j + 1]
                S.op('dve', lambda e, z=z, ch=ch, sh=sh: e.tensor_scalar(zc[:, ch, :nt], z[:, 1:1 + nt], sh(1), None, ALU.mult), [zk, 'rwp'], [('zc', ch)])
                S.op('dve', lambda e, z=z, ch=ch, sh=sh: e.scalar_tensor_tensor(zc[:, ch, :nt], z[:, 0:nt], sh(0), zc[:, ch, :nt], ALU.mult, ALU.add),
                     [zk, 'rwp', ('zc', ch)], [('zc', ch)])
                S.op('dve', lambda e, z=z, ch=ch, sh=sh: e.scalar_tensor_tensor(zc[:, ch, :nt], z[:, 2:nt + 2], sh(2), zc[:, ch, :nt], ALU.mult, ALU.add),
                     [zk, 'rwp', ('zc', ch)], [('zc', ch)])
            o = {n: ob[n][0] for n in ob}
            okey = {n: ('ob', n, 0) for n in ob}
            S.op('act', lambda e: e.copy(o['r'][:, :, :nt], zc[:, 0:4, :nt]), [('zc', c) for c in range(4)], [okey['r']])
            S.dma('sp', fmv(g.rS)[:, :, t0:t0 + nt], o['r'][:, :, :nt], reads=[okey['r']], writes=['rS'])
            S.op('act', lambda e: e.activation(tw[:, :nt], zc[:, 12, :nt], AF.Tanh), [('zc', 12)], ['tw'])
            S.op('act', lambda e: e.copy(ab[:, :nt], zc[:, 13, :nt]), [('zc', 13)], ['ab'])
            S.op('act', lambda e: e.activation(sg[:, :nt], zc[:, 14, :nt], AF.Sigmoid), [('zc', 14)], ['sg'])
            for c in range(4):
                ti = c % 3
                S.op('act', lambda e, c=c: e.activation(kkt[:, c, :nt], zc[:, 4 + c, :nt], AF.Identity, scale=rwp[:, P_KK + c:P_KK + c + 1]),
                     [('zc', 4 + c), 'rwp'], [('kkt', c)])
                S.op('act', lambda e, c=c, ti=ti: e.activation(t1[ti][:, :nt], kkt[:, c, :nt], AF.Square), [('kkt', c)], [('t1', ti)])
                pi = nextps(g)
                mm_group(S, g.ps[pi][:, :nt], [(bdf[:], t1[ti][:, :nt])], ['bdf', ('t1', ti)], [('ps', pi)])
                S.op('dve', lambda e, pi=pi, ti=ti: e.tensor_scalar(t1[ti][:, :nt], g.ps[pi][:, :nt], 1e-24, None, ALU.max), [('ps', pi)], [('t1', ti)])
                S.op('act', lambda e, ti=ti: e.activation(t1[ti][:, :nt], t1[ti][:, :nt], AF.Sqrt), [('t1', ti)], [('t1', ti)])
                S.op('dve', lambda e, ti=ti: e.reciprocal(t1[ti][:, :nt], t1[ti][:, :nt]), [('t1', ti)], [('t1', ti)])
                S.op('dve', lambda e, c=c, ti=ti: e.tensor_tensor(kkt[:, c, :nt], kkt[:, c, :nt], t1[ti][:, :nt], ALU.mult), [('kkt', c), ('t1', ti)], [('kkt', c)])
            S.op('act', lambda e: e.copy(o['kk'][:, :, :nt], kkt[:, :, :nt]), [('kkt', c) for c in range(4)], [okey['kk']])
            S.dma('sp', fmv(g.kkS)[:, :, t0:t0 + nt], o['kk'][:, :, :nt], reads=[okey['kk']], writes=['kkS'])
            for d in range(2):
                kdn, bn = 'kd%d' % d, 'b%d' % d
                for c in range(4):
                    pu, pa = nextps(g), nextps(g)
                    mm_group(S, g.ps[pu][:, :nt], [(w2b[64 * d:64 * d + 64, c * 128:(c + 1) * 128], tw[64 * d:64 * d + 64, :nt])], ['w2b', 'tw'], [('ps', pu)])
                    mm_group(S, g.ps[pa][:, :nt], [(a2b[64 * d:64 * d + 64, c * 128:(c + 1) * 128], ab[64 * d:64 * d + 64, :nt])], ['a2b', 'ab'], [('ps', pa)])
                    wl = owl[d][0]
                    wk_ = ('owl', d, 0)
                    S.op('act', lambda e, pu=pu, c=c, d=d, wl=wl: e.activation(wl[:, c, :nt], g.ps[pu][:, :nt], AF.Sigmoid,
                         bias=rwp[:, P_W0 + d * 4 + c:P_W0 + d * 4 + c + 1], scale=1.0), [('ps', pu), 'rwp'], [wk_])
                    S.op('pool', lambda e, c=c, wl=wl: e.tensor_scalar(wl[:, c, :nt], wl[:, c, :nt], DECAY_C, None, ALU.mult), [wk_], [wk_])
                    ta, tb = t1[0], t1[1]
                    S.op('act', lambda e, pa=pa, c=c, d=d: e.activation(ta[:, :nt], g.ps[pa][:, :nt], AF.Sigmoid,
                         bias=rwp[:, P_A0 + d * 4 + c:P_A0 + d * 4 + c + 1], scale=1.0), [('ps', pa), 'rwp'], [('t1', 0)])
                    S.op('pool', lambda e, c=c, bn=bn: e.tensor_tensor(o[bn][:, c, :nt], kkt[:, c, :nt], ta[:, :nt], ALU.mult),
                         [('kkt', c), ('t1', 0)], [okey[bn]])
                    S.op('dve', lambda e, c=c: e.tensor_scalar(tb[:, :nt], ta[:, :nt], rwp[:, P_KA + c:P_KA + c + 1], rwp[:, P_OMKA + c:P_OMKA + c + 1], ALU.mult, ALU.add),
                         [('t1', 0), 'rwp'], [('t1', 1)])
                    S.op('dve', lambda e, c=c: e.tensor_tensor(tb[:, :nt], tb[:, :nt], zc[:, 4 + c, :nt], ALU.mult), [('t1', 1), ('zc', 4 + c)], [('t1', 1)])
                    S.op('act', lambda e, c=c, kdn=kdn: e.copy(o[kdn][:, c, :nt], tb[:, :nt]), [('t1', 1)], [okey[kdn]])
                    if d == 0:
                        S.op('pool', lambda e, c=c: e.tensor_copy(kds[:, c, :nt], tb[:, :nt]), [('t1', 1)], [('kds', c)])
                    else:
                        S.op('pool', lambda e, c=c: e.tensor_tensor(kds[:, c, :nt], kds[:, c, :nt], tb[:, :nt], ALU.add), [('t1', 1), ('kds', c)], [('kds', c)])
                S.dma('sp', fmv(g.wlS[d])[:, :, t0:t0 + nt], owl[d][0][:, :, :nt], reads=[('owl', d, 0)], writes=['wlS%d' % d])
                S.dma('sp', fmv(g.kdS[d])[:, :, t0:t0 + nt], o[kdn][:, :, :nt], reads=[okey[kdn]], writes=['kdS%d' % d])
                S.dma('sp', fmv(g.bS[d])[:, :, t0:t0 + nt], o[bn][:, :, :nt], reads=[okey[bn]], writes=['bS%d' % d])
            for c in range(4):
                pg = nextps(g)
                mm_group(S, g.ps[pg][:, :nt], [(g2b[:, c * 128:(c + 1) * 128], sg[:, :nt])], ['g2b', 'sg'], [('ps', pg)])
                S.op('act', lambda e, pg=pg, c=c: e.copy(o['g'][:, c, :nt], g.ps[pg][:, :nt]), [('ps', pg)], [okey['g']])
            S.dma('sp', fmv(g.gS)[:, :, t0:t0 + nt], o['g'][:, :, :nt], reads=[okey['g']], writes=['gS'])
            for c in range(4):
                tc_ = t1[2]
                S.op('dve', lambda e, c=c: e.scalar_tensor_tensor(tc_[:, :nt], zc[:, c, :nt], rwp[:, P_RK + c:P_RK + c + 1], kds[:, c, :nt], ALU.mult, ALU.mult),
                     [('zc', c), 'rwp', ('kds', c)], [('t1', 2)])
                pi = nextps(g)
                mm_group(S, g.ps[pi][:, :nt], [(bdf[:], tc_[:, :nt])], ['bdf', ('t1', 2)], [('ps', pi)])
                S.op('dve', lambda e, pi=pi, c=c: e.tensor_tensor(o['bon'][:, c, :nt], g.ps[pi][:, :nt], zc[:, 8 + c, :nt], ALU.mult),
                     [('ps', pi), ('zc', 8 + c)], [okey['bon']])
            S.dma('sp', fmv(g.bonS)[:, :, t0:t0 + nt], o['bon'][:, :, :nt], reads=[okey['bon']], writes=['bonS'])
            for tt in range(nt // 128):
                pv = nextps(g)

                def fn(pe, pv=pv, tt=tt):
                    inst = None
                    for c in range(4):
                        inst = pe.transpose(g.ps[pv][:, c * 128:(c + 1) * 128], zc[:, 8 + c, tt * 128:(tt + 1) * 128], g.identf[:])
                    return inst
                S.op('pe', fn, [('zc', 8 + c) for c in range(4)] + ['identf'], [('ps', pv)])
                vb = (t0 // 128 + tt) % 2
                S.op('act', lambda e, pv=pv, vb=vb: e.copy(vtile[vb][:], g.ps[pv][:, :]), [('ps', pv)], [('vtile', vb)])
                S.dma('sp', g.vT[t0 + tt * 128:t0 + (tt + 1) * 128, :], vtile[vb][:], reads=[('vtile', vb)], writes=['vT'])


class TagS:
    GL = {'identb', 'identf', 'rwp', 'bdf', 'smf', 'epsv', 'rS', 'kkS', 'kdS0', 'kdS1', 'bS0', 'bS1', 'wlS0', 'wlS1', 'vT', 'yfS', 'ybS', 'bonS', 'gS', 'yrT'}

    def __init__(self, S, d):
        self.S, self.d = S, d

    def t(self, k):
        if isinstance(k, tuple) and k[0] in ('ps', 'msk', 'm4', 'mnt'):
            return k
        if isinstance(k, str) and (k in self.GL or k.startswith('dbg')):
            return k
        return ('dir%d' % self.d, k)

    def op(self, e, fn, reads=(), writes=()):
        self.S.op(e, fn, [self.t(k) for k in reads], [self.t(k) for k in writes])

    def dma(self, e, out, in_, reads=(), writes=(), **kw):
        self.S.dma(e, out, in_, reads=[self.t(k) for k in reads], writes=[self.t(k) for k in writes], **kw)


def phase_rwkv_scan(g, l):
    S, nc = g.S, g.nc
    import os
    STAGE = int(os.environ.get('SCAN_STAGE', '99'))
    NCK = T // 128
    with Phase(g) as A:
        smf = A('smf', [128, 9, 128])
        msk = [A('msk%d' % i, [128, 128], BF16) for i in range(4)]
        m4 = [A('m4_%d' % d, [128, 4, 128], BF16) for d in range(2)]
        mnt = [A('mnt%d' % d, [128, 128], BF16) for d in range(2)]
        bdf = A('bdf', [128, 128]); rwp = A('rwp', [128, P_OMKA])
        rwp = A('rwp', [128, P_OMKA])
        S.dma('sp', smf[:], g.smask[0:9].rearrange('m p c -> p m c'), writes=['smf'])
        for i in range(4):
            S.op('dve', lambda e, i=i: e.tensor_copy(msk[i][:], smf[:, 5 + i, :]), ['smf'], [('msk', i)])
        S.dma('sp', bdf[:], g.cmask[0], writes=['bdf'])
        S.dma('sp', rwp[:], g.rwp[l], writes=['rwp'])
        for d in range(2):
            for j in range(4):
                S.op('dve', lambda e, d=d, j=j: e.tensor_copy(m4[d][:, j, :], smf[:, 2 * d + (j % 2), :]), ['smf'], [('m4', d)])
        S.op('dve', lambda e: e.tensor_copy(mnt[0][:], smf[:, 2, :]), ['smf'], [('mnt', 0)])
        S.op('dve', lambda e: e.tensor_copy(mnt[1][:], smf[:, 0, :]), ['smf'], [('mnt', 1)])
        fmc = lambda ap, c0: ap.rearrange('(k p) t -> p k t', p=128)[:, :, c0:c0 + 128]
        f2 = lambda t: t[:].rearrange('p a b -> p (a b)')
        segb = smf[:, 4, :].unsqueeze(1).broadcast_to([128, 4, 128])
        it = [0]

        def scan_dir(d, A=A):
            S = TagS(g.S, d)
            A0 = A
            psc = [0]

            def nextps(g_):
                i = 4 * d + psc[0]
                psc[0] = (psc[0] + 1) % 4
                return i
            A = lambda name, shape, dt=F32: A0('%s_d%d' % (name, d), shape, dt)
            NTF = [A('NTF%d' % q, [128, 4, 128], BF16) for q in range(2)]
            NF = [A('NF%d' % q, [128, 4, 128], BF16) for q in range(2)]
            NoT = [[A('NoT%d_%d' % (q, i), [128, 4, 128], BF16) for i in range(3)] for q in range(2)]
            MT = [[A('MT%d_%d' % (q, i), [128, 4, 128], BF16) for i in range(2)] for q in range(2)]
            Wb = [A('Wb%d' % q, [128, 4, 128], BF16) for q in range(2)]
            ld = {n: [A('ld_%s%d' % (n, i), [128, 4, 128], BF16) for i in range(2)] for n in ('r', 'kk', 'kd', 'b')}
            cwl = [A('cwl%d' % i, [128, 4, 128]) for i in range(2)]
            cv = [A('cv%d' % i, [128, 512], BF16) for i in range(2)]
            cum = A('cum', [128, 4, 128]); cumx = A('cumx', [128, 4, 128]); ep = A('ep', [128, 4, 128]); en = A('en', [128, 4, 128])
            AR = A('AR', [128, 4, 256], BF16); kt = A('kt', [128, 4, 128], BF16); bt = A('bt', [128, 4, 128], BF16)
            gam = A('gam', [128, 4, 1])
            AtT = A('AtT', [128, 8, 64], BF16); BtT = A('BtT', [128, 8, 64], BF16); KtT = A('KtT', [128, 8, 64], BF16)
            AB = [A('AB%d' % h, [128, 4, 128], BF16) for h in range(8)]
            X = [[A('X%d_%d' % (q, i), [128, 4, 128], BF16) for i in range(2)] for q in range(2)]
            XT = [[A('XT%d_%d' % (q, i), [128, 4, 128], BF16) for i in range(2)] for q in range(2)]
            M = [[A('M%d_%d' % (q, i), [128, 4, 128], BF16) for i in range(2)] for q in range(2)]
            G2 = A('G2', [128, 8, 64], BF16); U = A('U', [128, 8, 64]); P1 = A('P1', [128, 4, 128]); Et = A('Et', [128, 8, 64], BF16)
            ST = A('ST', [128, 4, 64]); STb = [A('STb%d' % i, [128, 4, 64], BF16) for i in range(2)]; stt = A('stt', [128, 4, 64])
            ysb = [A('ysb%d' % i, [128, 4, 128]) for i in range(2)]
            order = list(range(NCK)) if d == 0 else [1, 0] + list(range(NCK - 1, 1, -1))
            if getattr(g, 'scan_limit', None):
                order = order[:g.scan_limit]
            S.op('pool', lambda e: e.memset(ST[:], 0.0), [], ['ST'])
            sbi = 0
            S.op('pool', lambda e: e.memset(STb[0][:], 0.0), [], [('STb', 0)])
            for ci in order:
                c0 = ci * 128
                b = it[0] % 2
                it[0] += 1
                cr, ckk, ckd, cb = ld['r'][b], ld['kk'][b], ld['kd'][b], ld['b'][b]
                lk = lambda n: ('ld', n, b)
                S.dma('sp', cr[:], fmc(g.rS, c0), reads=['rS'], writes=[lk('r')])
                S.dma('sp', ckk[:], fmc(g.kkS, c0), reads=['kkS'], writes=[lk('kk')])
                S.dma('sp', ckd[:], fmc(g.kdS[d], c0), reads=['kdS%d' % d], writes=[lk('kd')])
                S.dma('sp', cb[:], fmc(g.bS[d], c0), reads=['bS%d' % d], writes=[lk('b')])
                S.dma('sp', cwl[b][:], fmc(g.wlS[d], c0), reads=['wlS%d' % d], writes=[('cwl', b)])
                S.dma('sp', cv[b][:], g.vT[c0:c0 + 128, :], reads=['vT'], writes=[('cv', b)])
                vk = ('cv', b)
                cvb = cv[b]
                yield
                S.op('pool', lambda e: e.tensor_copy(cumx[:], segb), ['smf'], ['cumx'])
                S.op('dve', lambda e, b=b: e.tensor_tensor_scan(f2(cum), f2(cumx), f2(cwl[b]), 0.0, ALU.mult, ALU.add), ['cumx', ('cwl', b)], ['cum'])
                if d == 0:
                    S.op('dve', lambda e, b=b: e.tensor_tensor(cumx[:], cum[:], cwl[b][:], ALU.subtract), ['cum', ('cwl', b)], ['cumx'])
                else:
                    S.op('dve', lambda e: e.tensor_tensor(cumx[:], cum[:, :, 127:128].broadcast_to([128, 4, 128]), cum[:], ALU.subtract), ['cum'], ['cumx'])
                    S.op('dve', lambda e, b=b: e.tensor_tensor(cum[:], cumx[:], cwl[b][:], ALU.add), ['cumx', ('cwl', b)], ['cum'])
                S.op('act', lambda e: e.activation(ep[:], cum[:], AF.Exp), ['cum'], ['ep'])
                S.op('act', lambda e: e.activation(en[:], cum[:], AF.Exp, scale=-1.0), ['cum'], ['en'])
                gcol = 127 if d == 0 else 0
                S.op('act', lambda e: e.copy(gam[:], ep[:, :, gcol:gcol + 1]), ['ep'], ['gam'])
                S.op('dve', lambda e, cr=cr: e.tensor_tensor(AR[:, :, 128:256], cr[:], ep[:], ALU.mult), [lk('r'), 'ep'], ['AR'])
                S.op('act', lambda e: e.activation(ep[:], cumx[:], AF.Exp), ['cumx', 'AR', 'gam'], ['ep'])
                S.op('dve', lambda e, ckk=ckk: e.scalar_tensor_tensor(AR[:, :, 0:128], ckk[:], -1.0, ep[:], ALU.mult, ALU.mult), [lk('kk'), 'ep'], ['AR'])
                S.op('pool', lambda e, ckd=ckd: e.tensor_tensor(kt[:], ckd[:], en[:], ALU.mult), [lk('kd'), 'en'], ['kt'])
                S.op('pool', lambda e, cb=cb: e.tensor_tensor(bt[:], cb[:], en[:], ALU.mult), [lk('b'), 'en'], ['bt'])
                if STAGE <= 1:
                    continue
                yield
                for (srcf, dst, dk, sk) in ((lambda hp: AR[:, hp, 0:128], AtT, 'AtT', 'AR'), (lambda hp: bt[:, hp, :], BtT, 'BtT', 'bt'), (lambda hp: kt[:, hp, :], KtT, 'KtT', 'kt')):
                    pi = nextps(g)
                    psb = g.ps[pi][:, :].bitcast(BF16)

                    def fn(pe, srcf=srcf, psb=psb):
                        inst = None
                        for hp in range(4):
                            inst = pe.transpose(psb[:, hp * 128:(hp + 1) * 128], srcf(hp), g.identb[:])
                        return inst
                    S.op('pe', fn, [sk, 'identb'], [('ps', pi)])
                    S.op('act', lambda e, dst=dst, psb=psb: e.copy(dst[:].rearrange('p h j -> p (h j)'), psb[:, 0:512]), [('ps', pi)], [dk])
                if STAGE <= 2:
                    continue
                yield
                ABH = int(os.environ.get('AB_H', '8')); ABM = int(os.environ.get('AB_MODE', '9'))
                for h in range(ABH):
                    hp, hb = h // 2, h % 2
                    sl = slice(hb * 64, hb * 64 + 64)
                    pi = nextps(g)
                    mm_group(S, g.ps[pi][:, 0:256], [(bt[sl, hp, :], AR[sl, hp, :])], ['bt', 'AR'], [('ps', pi)])
                    if ABM <= 1:
                        continue
                    mm_group(S, g.ps[pi][:, 256:512], [(kt[sl, hp, :], AR[sl, hp, :])], ['kt', 'AR'], [('ps', pi)])
                    if ABM <= 2:
                        continue
                    S.op('dve', lambda e, h=h, pi=pi: e.tensor_tensor(f2(AB[h]), g.ps[pi][:, :], f2(m4[d]), ALU.mult), [('ps', pi), ('m4', d)], [('AB', h)])
                SUB = int(os.environ.get('SCAN_SUB', '9'))
                if SUB <= 0:
                    continue
                b4 = lambda t: t[:].unsqueeze(1).broadcast_to([128, 4, 128])
                for q in range(2):
                    pi = nextps(g)
                    for j in range(4):
                        h = 2 * j + q
                        hp, hb = j, q
                        sl = slice(hb * 64, hb * 64 + 64)
                        mm_group(S, g.ps[pi][:, j * 128:(j + 1) * 128], [(AR[sl, hp, 0:128], bt[sl, hp, :])], ['bt', 'AR'], [('ps', pi)])
                    S.op('dve', lambda e, q=q, pi=pi: e.tensor_tensor(NTF[q][:], g.ps[pi][:, :].rearrange('p (j s) -> p j s', j=4), b4(mnt[d]), ALU.mult),
                         [('ps', pi), ('mnt', d)], [('NTF', q)])
                    for j in range(4):
                        h = 2 * j + q
                        S.op('pool', lambda e, q=q, j=j, h=h: e.tensor_copy(NF[q][:, j, :], AB[h][:, 0, :]), [('AB', h)], [('NF', q)])
                    S.op('pool', lambda e, q=q: e.tensor_tensor(X[q][0][:], NF[q][:], b4(msk[0]), ALU.mult), [('NF', q), ('msk', 0)], [('X', q, 0)])
                    S.op('dve', lambda e, q=q: e.tensor_tensor(XT[q][0][:], NTF[q][:], b4(msk[0]), ALU.mult), [('NTF', q), ('msk', 0)], [('XT', q, 0)])
                    S.op('pool', lambda e, q=q: e.tensor_tensor(M[q][0][:], X[q][0][:], b4(g.identb), ALU.add), [('X', q, 0), 'identb'], [('M', q, 0)])
                    S.op('pool', lambda e, q=q: e.tensor_tensor(MT[q][0][:], XT[q][0][:], b4(g.identb), ALU.add), [('XT', q, 0), 'identb'], [('MT', q, 0)])
                    for i in range(3):
                        S.op('pool', lambda e, q=q, i=i: e.tensor_tensor(NoT[q][i][:], NTF[q][:], b4(msk[1 + i]), ALU.mult), [('NTF', q), ('msk', 1 + i)], [('NoT', q, i)])
                if d == 0 and ci == 0:
                    for nm, tl, kk_ in (('d_XT0', NTF[0], ('NTF', 0)), ('d_X0', NF[0], ('NF', 0)), ('d_Mi', M[0][0], ('M', 0, 0))):
                        if nm in g.dbg:
                            S.dma('sp', g.dbg[nm][:, :], tl[:].rearrange('p a b -> p (a b)'), reads=[kk_], writes=['dbg' + nm])
                yield
                def mm4(q, lh, rh, rk):
                    pi = nextps(g)
                    for j in range(4):
                        mm_group(S, g.ps[pi][:, j * 128:(j + 1) * 128], [(lh[:, j, :], rh[:, j, :])], rk, [('ps', pi)])
                    return pi
                cur = 0
                for k in range(1, 4):
                    nx = 1 - cur
                    for q in range(2):
                        Xp, XTp = X[q][cur], XT[q][cur]
                        Xn, XTn = X[q][nx], XT[q][nx]
                        kX, kXT = ('X', q, cur), ('XT', q, cur)
                        nX, nXT = ('X', q, nx), ('XT', q, nx)
                        pi = mm4(q, Xp, XTp, [kX, kXT])
                        S.op('act', lambda e, XTn=XTn, pi=pi: e.copy(f2(XTn), g.ps[pi][:, :]), [('ps', pi)], [nXT])
                        if k < 3:
                            pi = mm4(q, XTp, Xp, [kX, kXT])
                            S.op('act', lambda e, Xn=Xn, pi=pi: e.copy(f2(Xn), g.ps[pi][:, :]), [('ps', pi)], [nX])
                    yield
                    for q in range(2):
                        Mp, MTp, Mn, MTn, XTn = M[q][cur], MT[q][cur], M[q][nx], MT[q][nx], XT[q][nx]
                        kM, kMT, nM, nMT, nXT = ('M', q, cur), ('MT', q, cur), ('M', q, nx), ('MT', q, nx), ('XT', q, nx)
                        pi = mm4(q, XTn, Mp, [nXT, kM])
                        S.op('dve', lambda e, Mn=Mn, Mp=Mp, pi=pi: e.tensor_tensor(f2(Mn), g.ps[pi][:, :], f2(Mp), ALU.add), [('ps', pi), kM], [nM])
                        pi = mm4(q, Mp, XTn, [nXT, kM])
                        S.op('dve', lambda e, MTn=MTn, MTp=MTp, pi=pi: e.tensor_tensor(f2(MTn), g.ps[pi][:, :], f2(MTp), ALU.add), [('ps', pi), kMT], [nMT])
                    cur = nx
                    yield
                for i in range(3):
                    nx = 1 - cur
                    for q in range(2):
                        pi = mm4(q, NoT[q][i], M[q][cur], [('NoT', q, i), ('M', q, cur)])
                        S.op('act', lambda e, q=q, pi=pi: e.copy(f2(Wb[q]), g.ps[pi][:, :]), [('ps', pi)], [('Wb', q)])
                    yield
                    for q in range(2):
                        Dp, DTp, Dn, DTn = M[q][cur], MT[q][cur], M[q][nx], MT[q][nx]
                        kD, kDT, nD, nDT = ('M', q, cur), ('MT', q, cur), ('M', q, nx), ('MT', q, nx)
                        pi = mm4(q, DTp, Wb[q], [kDT, ('Wb', q)])
                        S.op('dve', lambda e, Dn=Dn, Dp=Dp, pi=pi: e.tensor_tensor(f2(Dn), g.ps[pi][:, :], f2(Dp), ALU.add), [('ps', pi), kD], [nD])
                        if i < 2:
                            pi = mm4(q, Wb[q], DTp, [kDT, ('Wb', q)])
                            S.op('dve', lambda e, DTn=DTn, DTp=DTp, pi=pi: e.tensor_tensor(f2(DTn), g.ps[pi][:, :], f2(DTp), ALU.add), [('ps', pi), kDT], [nDT])
                    cur = nx
                    yield
                Mf = [M[q][cur] for q in range(2)]
                kMf = [('M', q, cur) for q in range(2)]
                if STAGE <= 4:
                    continue
                yield
                pi = nextps(g)
                for h in range(8):
                    mm_group(S, g.ps[pi][:, h * 64:(h + 1) * 64], [(AB[h][:, 2, :], cvb[:, h * 64:(h + 1) * 64])], [('AB', h), vk], [('ps', pi)])
                S.op('act', lambda e, pi=pi: e.copy(G2[:].rearrange('p h i -> p (h i)'), g.ps[pi][:, :]), [('ps', pi)], ['G2'])
                pi = nextps(g)
                for h in range(8):
                    mm_group(S, g.ps[pi][:, h * 64:(h + 1) * 64], [(Mf[h % 2][:, h // 2, :], G2[:, h, :])], [kMf[h % 2], 'G2'], [('ps', pi)])
                S.op('act', lambda e, pi=pi: e.copy(U[:].rearrange('p h i -> p (h i)'), g.ps[pi][:, :]), [('ps', pi)], ['U'])
                pi = nextps(g)
                for h in range(8):
                    hp, hb = h // 2, h % 2
                    mm_group(S, g.ps[pi][hb * 64:hb * 64 + 64, hp * 128:(hp + 1) * 128], [(AtT[:, h, :], Mf[h % 2][:, h // 2, :])], [kMf[h % 2], 'AtT'], [('ps', pi)])
                S.op('dve', lambda e, pi=pi: e.tensor_copy(f2(P1), g.ps[pi][:, :]), [('ps', pi)], ['P1'])
                if STAGE <= 5:
                    continue
                yield
                for hb in range(2):
                    pe_ = nextps(g)
                    sl = slice(hb * 64, hb * 64 + 64)
                    for hp in range(4):
                        mm_group(S, g.ps[pe_][:, hp * 64:(hp + 1) * 64], [(P1[sl, hp, :], ST[sl, hp, :])], ['P1', 'ST'], [('ps', pe_)])
                    S.op('dve', lambda e, pe_=pe_, hb=hb: e.tensor_tensor(Et[:].rearrange('p (hp hb) i -> p hp hb i', hb=2)[:, :, hb, :],
                         g.ps[pe_][:, 0:256].rearrange('p (hp i) -> p hp i', hp=4), U[:].rearrange('p (hp hb) i -> p hp hb i', hb=2)[:, :, hb, :], ALU.add),
                         [('ps', pe_), 'U'], ['Et'])
                if STAGE <= 6:
                    continue
                py2, pss = [nextps(g), nextps(g)], nextps(g)
                for h in range(8):
                    hp, hb = h // 2, h % 2
                    sl = slice(hb * 64, hb * 64 + 64)
                    mm_group(S, g.ps[pss][sl, hp * 64:(hp + 1) * 64], [(KtT[:, h, :], cvb[:, h * 64:(h + 1) * 64]), (BtT[:, h, :], Et[:, h, :])],
                             ['KtT', 'BtT', 'Et', vk], [('ps', pss)])
                for h in range(8):
                    hp, hb = h // 2, h % 2
                    sl = slice(hb * 64, hb * 64 + 64)
                    mm_group(S, g.ps[py2[hb]][sl, hp * 128:(hp + 1) * 128],
                             [(STb[sbi][sl, hp, :], AR[sl, hp, 128:256]), (Et[:, h, :], AB[h][:, 1, :]), (cvb[:, h * 64:(h + 1) * 64], AB[h][:, 3, :])],
                             [('STb', sbi), 'AR', 'Et', ('AB', h), vk], [('ps', py2[hb])])
                S.op('dve', lambda e, pss=pss: e.tensor_tensor(stt[:].rearrange('p a i -> p (a i)'), g.ps[pss][:, 0:256], ST[:].rearrange('p a i -> p (a i)'), ALU.add),
                     [('ps', pss), 'ST'], ['stt'])
                S.op('dve', lambda e: e.tensor_tensor(ST[:], stt[:], gam[:].broadcast_to([128, 4, 64]), ALU.mult), ['stt', 'gam'], ['ST'])
                sbi = 1 - sbi
                S.op('act', lambda e, sbi=sbi: e.copy(STb[sbi][:], ST[:]), ['ST'], [('STb', sbi)])
                if STAGE <= 7:
                    continue
                if d == 0 and ci == 0:
                    dm = {'d_AR': AR, 'd_AB0': AB[0], 'd_AB1': AB[1], 'd_M0': Mf[0], 'd_U': U, 'd_P1': P1, 'd_Et': Et, 'd_ST': ST, 'd_kt': kt, 'd_bt': bt,
                          'd_AtT': AtT, 'd_G2': G2}
                    for nm, tl in dm.items():
                        if nm in g.dbg:
                            S.dma('sp', g.dbg[nm][:, :], tl[:].rearrange('p a b -> p (a b)'), reads=['AR', ('AB', 0), ('AB', 1), kMf[0], 'U', 'P1', 'Et', 'ST', 'kt', 'bt', 'AtT', 'G2'], writes=['dbg' + nm])
                yb = ysb[ci % 2]
                for hb in range(2):
                    S.op('act', lambda e, yb=yb, hb=hb: e.copy(f2(yb)[hb * 64:hb * 64 + 64, :], g.ps[py2[hb]][hb * 64:hb * 64 + 64, :]), [('ps', py2[hb])], [('ysb', ci % 2)])
                S.dma('sp', fmc(g.yfS if d == 0 else g.ybS, c0), yb[:], reads=[('ysb', ci % 2)], writes=['yfS' if d == 0 else 'ybS'])
                yield

        gens = [scan_dir(0), scan_dir(1)]
        while gens:
            for gen in list(gens):
                try:
                    next(gen)
                except StopIteration:
                    gens.remove(gen)
    with Phase(g) as A:
        bdf = A('bdf2', [128, 128]); rwp = A('rwp2', [128, P_OMKA])
        S.dma('sp', bdf[:], g.cmask[0], writes=['bdf'])
        S.dma('sp', rwp[:], g.rwp[l], writes=['rwp'])
        fmc = lambda ap, c0: ap.rearrange('(k p) t -> p k t', p=128)[:, :, c0:c0 + 128]
        f2 = lambda t: t[:].rearrange('p a b -> p (a b)')
        yfb = [A('yf%d' % i, [128, 4, 128]) for i in range(2)]; ybb = [A('yb%d' % i, [128, 4, 128]) for i in range(2)]
        cbonb = [A('cbon%d' % i, [128, 4, 128], BF16) for i in range(2)]; cgb = [A('cg%d' % i, [128, 4, 128], BF16) for i in range(2)]
        ys = A('ys', [128, 4, 128]); ysq = A('ysq', [128, 4, 128]); gmean = A('gmean', [128, 4, 128]); grs = A('grs', [128, 4, 128]); gt_ = A('gt_', [128, 4, 128])
        yob = [A('yob%d' % i, [128, 4, 128], BF16) for i in range(2)]
        for ci in range(NCK):
            if l == DEPTH - 1 and ci < CTX // 128:
                continue
            c0 = ci * 128
            b = ci % 2
            yf, yb2, cbon, cg = yfb[b], ybb[b], cbonb[b], cgb[b]
            S.dma('sp', yf[:], fmc(g.yfS, c0), reads=['yfS'], writes=[('yf', b)])
            S.dma('sp', yb2[:], fmc(g.ybS, c0), reads=['ybS'], writes=[('yb', b)])
            S.dma('sp', cbon[:], fmc(g.bonS, c0), reads=['bonS'], writes=[('cbon', b)])
            S.dma('sp', cg[:], fmc(g.gS, c0), reads=['gS'], writes=[('cg', b)])
            S.op('dve', lambda e, yf=yf, yb2=yb2: e.tensor_tensor(ys[:], yf[:], yb2[:], ALU.add), [('yf', b), ('yb', b)], ['ys'])
            S.op('act', lambda e: e.activation(ysq[:], ys[:], AF.Square), ['ys'], ['ysq'])
            p1, p2 = nextps(g), nextps(g)
            mm_group(S, g.ps[p1][:, :], [(bdf[:], f2(ys))], ['bdf', 'ys'], [('ps', p1)])
            mm_group(S, g.ps[p2][:, :], [(bdf[:], f2(ysq))], ['bdf', 'ysq'], [('ps', p2)])
            S.op('act', lambda e, p1=p1: e.activation(f2(gmean), g.ps[p1][:, :], AF.Identity, scale=1.0 / 64), [('ps', p1)], ['gmean'])
            S.op('pool', lambda e: e.tensor_tensor(ysq[:], gmean[:], gmean[:], ALU.mult), ['gmean'], ['ysq'])
            S.op('dve', lambda e, p2=p2: e.scalar_tensor_tensor(f2(grs), g.ps[p2][:, :], 1.0 / 64, f2(ysq), ALU.mult, ALU.subtract), [('ps', p2), 'ysq'], ['grs'])
            S.op('act', lambda e: e.activation(grs[:], grs[:], AF.Sqrt, bias=g.epsv[:, 3:4], scale=1.0), ['grs', 'epsv'], ['grs'])
            S.op('dve', lambda e: e.reciprocal(grs[:], grs[:]), ['grs'], ['grs'])
            S.op('pool', lambda e: e.tensor_tensor(gt_[:], ys[:], gmean[:], ALU.subtract), ['ys', 'gmean'], ['gt_'])
            S.op('dve', lambda e: e.tensor_tensor(gt_[:], gt_[:], grs[:], ALU.mult), ['gt_', 'grs'], ['gt_'])
            S.op('pool', lambda e: e.tensor_tensor(gt_[:], gt_[:], rwp[:, P_GG:P_GG + 4].unsqueeze(2).broadcast_to([128, 4, 128]), ALU.mult), ['gt_', 'rwp'], ['gt_'])
            S.op('pool', lambda e: e.tensor_tensor(gt_[:], gt_[:], rwp[:, P_GB:P_GB + 4].unsqueeze(2).broadcast_to([128, 4, 128]), ALU.add), ['gt_', 'rwp'], ['gt_'])
            S.op('dve', lambda e, cbon=cbon: e.tensor_tensor(gt_[:], gt_[:], cbon[:], ALU.add), ['gt_', ('cbon', b)], ['gt_'])
            yo = yob[b]
            S.op('pool', lambda e, yo=yo, cg=cg: e.tensor_tensor(yo[:], gt_[:], cg[:], ALU.mult), ['gt_', ('cg', b)], [('yob', b)])
            S.dma('sp', fmc(g.yrT, c0), yo[:], reads=[('yob', b)], writes=['yrT'])


def phase_merge(g, l):
    S, nc = g.S, g.nc
    last = (l == DEPTH - 1)
    with Phase(g) as A:
        wg = A('wg', [128, NCH, 3072], BF16)
        wp = [A('wp%d' % i, [128, 4, 1024], BF16) for i in range(3)]
        wo = A('wo', [128, NCH, 1024], BF16)
        HL = HLoader(g, A)
        yb = [[A('my%d_%d' % (i, j), [128, 4, 512], BF16) for j in range(2)] for i in range(3)]
        xb = [A('mx%d' % i, [128, NCH, 512]) for i in range(2)]
        sig = [A('msig%d' % i, [128, 512]) for i in range(2)]
        tm = [A('mtm%d' % i, [128, 512]) for i in range(2)]
        macc = A('macc', [128, 512])
        mT = A('mT', [128, NCH, 512], BF16)
        wsrc = g.w_in[l].rearrange('(k p) c -> p k c', p=128)
        for i in range(2):
            S.dma('pool', wg[:, :, i * 1536:(i + 1) * 1536], wsrc[:, :, 4480 + i * 1536:4480 + (i + 1) * 1536], writes=['wg'])
        for i, nm in enumerate(('p_conv', 'p_att', 'p_rwkv')):
            S.dma('pool', wp[i][:], getattr(g, nm)[l].rearrange('(k p) c -> p k c', p=128), writes=[('wp', i)])
        S.dma('pool', wo[:], g.w_out[l].rearrange('(k p) c -> p k c', p=128), writes=['wo'])
        xTv = g.xT.rearrange('(k p) t -> p k t', p=128)
        ysrc = [g.ycT.rearrange('(k p) t -> p k t', p=128), g.yaT.rearrange('(k p) t -> p k t', p=128), g.yrT.rearrange('(k p) t -> p k t', p=128)]
        ynm = ['ycT', 'yaT', 'yrT']
        for bi, (t0, nt) in enumerate(BLKS):
            if last and t0 < CTX:
                continue
            b = bi % 2
            s = 1 if t0 < CTX else 0
            hb, hk = HL.load(bi)
            for i in range(3):
                S.dma('sp', yb[i][b][:, :, :nt], ysrc[i][:, :, t0:t0 + nt], reads=[ynm[i]], writes=[('my', i, b)])
            S.dma('sp', xb[b][:, :, :nt], xTv[:, :, t0:t0 + nt], reads=['xT'], writes=[('mx', b)])
            for oc in range(NCH):
                for i in range(3):
                    pg, pp = nextps(g), nextps(g)
                    c0 = i * 1024 + oc * 128
                    mm_group(S, g.ps[pg][:, :nt], [(wg[:, k, c0:c0 + 128], hb[:, k, :nt]) for k in range(NCH)], ['wg', hk], [('ps', pg)])
                    mm_group(S, g.ps[pp][:, :nt], [(wp[i][:, k, oc * 128:(oc + 1) * 128], yb[i][b][:, k, :nt]) for k in range(4)], [('wp', i), ('my', i, b)], [('ps', pp)])
                    sb_ = i % 2
                    S.op('act', lambda e, pg=pg, sb_=sb_: e.activation(sig[sb_][:, :nt], g.ps[pg][:, :nt], AF.Sigmoid), [('ps', pg)], [('msig', sb_)])
                    if i == 0:
                        S.op('dve', lambda e, pp=pp, sb_=sb_: e.tensor_tensor(macc[:, :nt], g.ps[pp][:, :nt], sig[sb_][:, :nt], ALU.mult), [('ps', pp), ('msig', sb_)], ['macc'])
                    else:
                        S.op('dve', lambda e, pp=pp, sb_=sb_: e.tensor_tensor(tm[sb_][:, :nt], g.ps[pp][:, :nt], sig[sb_][:, :nt], ALU.mult), [('ps', pp), ('msig', sb_)], [('mtm', sb_)])
                        if i == 1:
                            S.op('pool', lambda e, sb_=sb_: e.tensor_tensor(macc[:, :nt], macc[:, :nt], tm[sb_][:, :nt], ALU.add), ['macc', ('mtm', sb_)], ['macc'])
                        else:
                            S.op('pool', lambda e, sb_=sb_, oc=oc: e.tensor_tensor(mT[:, oc, :nt], macc[:, :nt], tm[sb_][:, :nt], ALU.add), ['macc', ('mtm', sb_)], [('mT', oc)])
            for oc in range(NCH):
                po = nextps(g)
                mm_group(S, g.ps[po][:, :nt], [(wo[:, k, oc * 128:(oc + 1) * 128], mT[:, k, :nt]) for k in range(NCH)], ['wo'] + [('mT', k) for k in range(NCH)], [('ps', po)])
                S.op('dve', lambda e, po=po, oc=oc: e.scalar_tensor_tensor(xb[b][:, oc, :nt], g.ps[po][:, :nt], g.modv[:, 2 * 8 + oc, s:s + 1], xb[b][:, oc, :nt], ALU.mult, ALU.add),
                     [('ps', po), 'modv', ('mx', b)], [('mx', b)])
            S.dma('sp', xTv[:, :, t0:t0 + nt], xb[b][:, :, :nt], reads=[('mx', b)], writes=['xT'])


def phase_mlp(g, l):
    S, nc = g.S, g.nc
    last = (l == DEPTH - 1)
    with Phase(g) as A:
        w1 = A('w1', [128, NCH, DFF], BF16)
        w2 = A('w2', [128, DFF // 128, D], BF16)
        xb = [A('fx%d' % i, [128, NCH, 512]) for i in range(1)] * 2
        rs = A('frs', [128, 512]); tmp = [A('ftmp%d' % i, [128, 512]) for i in range(2)]
        h2 = A('fh2', [128, NCH, 512], BF16)
        act = A('fact', [128, DFF // 128, 512], BF16)
        sq = act[:, 0:16, :].bitcast(F32).rearrange('p (a two) b -> p a (two b)', two=2)
        fg = A('fg', [128, NCH])
        osb = A('fosb', [128, D])
        w1src = g.mlp_w1[l].rearrange('(k p) c -> p k c', p=128)
        for i in range(4):
            S.dma('pool', w1[:, :, i * 1024:(i + 1) * 1024], w1src[:, :, i * 1024:(i + 1) * 1024], writes=['w1'])
        w2src = g.mlp_w2[l].rearrange('(k p) c -> p k c', p=128)
        for i in range(4):
            S.dma('pool', w2[:, i * 8:(i + 1) * 8, :], w2src[:, i * 8:(i + 1) * 8, :], writes=['w2'])
        S.dma('sp', fg[:], g.fing[:, :], writes=['fg'])
        xTv = g.xT.rearrange('(k p) t -> p k t', p=128)
        for bi, (t0, nt) in enumerate(BLKS):
            if last and t0 < CTX:
                continue
            b = bi % 2
            s = 1 if t0 < CTX else 0
            x = xb[b]
            xk = ('fx', 0)
            S.dma('sp', x[:, :, :nt], xTv[:, :, t0:t0 + nt], reads=['xT'], writes=[xk])
            rms_stats(g, x, nt, sq, rs, xk, 'fact', 'frs')
            for k in range(NCH):
                tb = k % 2
                S.op('dve', lambda e, k=k, tb=tb: e.tensor_tensor(tmp[tb][:, :nt], x[:, k, :nt], rs[:, :nt], ALU.mult), [xk, 'frs'], [('ftmp', tb)])
                S.op('act', lambda e, k=k, tb=tb: e.activation(h2[:, k, :nt], tmp[tb][:, :nt], AF.Identity,
                     bias=g.modv[:, 3 * 8 + k, s:s + 1], scale=g.gs[:, 1, k, s:s + 1]), [('ftmp', tb), 'modv', 'gs'], ['fh2'])
            for fc in range(DFF // 128):
                pf = nextps(g)
                tb = fc % 2
                mm_group(S, g.ps[pf][:, :nt], [(w1[:, k, fc * 128:(fc + 1) * 128], h2[:, k, :nt]) for k in range(NCH)], ['w1', 'fh2'], [('ps', pf)])
                S.op('dve', lambda e, pf=pf, tb=tb: e.tensor_scalar(tmp[tb][:, :nt], g.ps[pf][:, :nt], 0.0, None, ALU.max), [('ps', pf)], [('ftmp', tb)])
                S.op('act', lambda e, fc=fc, tb=tb: e.activation(act[:, fc, :nt], tmp[tb][:, :nt], AF.Square), [('ftmp', tb)], ['fact'])
            for oc in range(NCH):
                po = nextps(g)
                mm_group(S, g.ps[po][:, :nt], [(w2[:, fc, oc * 128:(oc + 1) * 128], act[:, fc, :nt]) for fc in range(DFF // 128)],
                         ['w2', 'fact'], [('ps', po)])
                S.op('dve', lambda e, po=po, oc=oc: e.scalar_tensor_tensor(x[:, oc, :nt], g.ps[po][:, :nt], g.modv[:, 5 * 8 + oc, s:s + 1], x[:, oc, :nt], ALU.mult, ALU.add),
                     [('ps', po), 'modv', xk], [xk])
            if not last:
                S.dma('sp', xTv[:, :, t0:t0 + nt], x[:, :, :nt], reads=[xk], writes=['xT'])
                continue
            rms_stats(g, x, nt, sq, rs, xk, 'fact', 'frs')
            for k in range(NCH):
                S.op('dve', lambda e, k=k: e.tensor_tensor(sq[:, k, :nt], x[:, k, :nt], rs[:, :nt], ALU.mult), [xk, 'frs', 'fact'], ['fact'])
                S.op('act', lambda e, k=k: e.activation(sq[:, k, :nt], sq[:, k, :nt], AF.Identity, scale=fg[:, k:k + 1]), ['fact', 'fg'], ['fact'])
            for tt in range(nt // 128):
                for half in range(2):
                    pi = nextps(g)

                    def fn(pe, pi=pi, half=half, tt=tt):
                        inst = None
                        for j in range(4):
                            inst = pe.transpose(g.ps[pi][:, j * 128:(j + 1) * 128], sq[:, half * 4 + j, tt * 128:(tt + 1) * 128], g.identf[:])
                        return inst
                    S.op('pe', fn, ['fact', 'identf'], [('ps', pi)])
                    S.op('act', lambda e, pi=pi, half=half: e.copy(osb[:, half * 512:(half + 1) * 512], g.ps[pi][:, :]), [('ps', pi)], ['fosb'])
                r0 = t0 - CTX + tt * 128
                S.dma('sp', g.out[r0:r0 + 128, :], osb[:], reads=['fosb'], writes=['out'])


_CACHE = {}


def kernel(**inputs):
    inp = {k: np.asarray(v) for k, v in inputs.items()}
    if 'nc' not in _CACHE:
        _CACHE['nc'] = build()
    nc = _CACHE['nc']
    shared = host_shared(inp)
    B = inp['x'].shape[0]
    in_maps = [host_inputs(inp, b, shared) for b in range(B)]
    res = run_bass_kernel_spmd(nc, in_maps, core_ids=list(range(B)))
    return np.stack([np.asarray(res.results[b]['out']) for b in range(B)]).astype(np.float32)
```

```python
import math
import numpy as np
import concourse.bass as bass
import concourse.mybir as mybir
from concourse.bass_utils import run_bass_kernel_spmd

F32 = mybir.dt.float32
BF16 = mybir.dt.bfloat16
AF = mybir.ActivationFunctionType
ALU = mybir.AluOpType
AX = mybir.AxisListType

D = 1024
SEQ = 4096
CTX = 256
T = SEQ + CTX
DEPTH = 2
DIN = 7552
DFF = 4096
NCH = D // 128
BLKS = [(0, 256)] + [(256 + 512 * i, 512) for i in range(8)]
NORM_EPS = 1e-6
LN_EPS = 1e-5
SUBLN_EPS = 1e-5
GN_EPS = 64e-5


class Sched:
    def __init__(self, nc, n_dma=24):
        self.nc = nc
        self.eng = dict(pe=nc.tensor, dve=nc.vector, act=nc.scalar, pool=nc.gpsimd, sp=nc.sync)
        self.sem = {e: nc.alloc_semaphore('sem_' + e) for e in self.eng}
        self.cnt = {e: 0 for e in self.eng}
        self.dsem = [nc.alloc_semaphore('dsem%d' % i) for i in range(n_dma)]
        self.dval = [0] * n_dma
        self.drr = 0
        self.seen = {e: {} for e in self.eng}
        self.lastw = {}
        self.readers = {}
        self.nps = 0

    def _semh(self, key):
        return self.sem[key] if isinstance(key, str) else self.dsem[key]

    def _wait(self, e, key, val):
        if self.seen[e].get(key, 0) >= val:
            return
        self.eng[e].wait_ge(self._semh(key), val)
        self.seen[e][key] = val

    def _deps(self, e, reads, writes):
        need = {}
        for r in reads:
            tok = self.lastw.get(r)
            if tok is not None:
                need[tok[0]] = max(need.get(tok[0], 0), tok[1])
        for w in writes:
            tok = self.lastw.get(w)
            if tok is not None:
                need[tok[0]] = max(need.get(tok[0], 0), tok[1])
            for k, v in self.readers.get(w, {}).items():
                need[k] = max(need.get(k, 0), v)
        for k, v in need.items():
            self._wait(e, k, v)

    def _commit(self, tok, reads, writes):
        for w in writes:
            self.lastw[w] = tok
            self.readers[w] = {}
        for r in reads:
            if r in writes:
                continue
            d = self.readers.setdefault(r, {})
            d[tok[0]] = max(d.get(tok[0], 0), tok[1])

    def op(self, e, fn, reads=(), writes=()):
        if e == 'pe':
            self.seen[e]['pe'] = self.cnt['pe']
        self._deps(e, reads, writes)
        inst = fn(self.eng[e])
        self.cnt[e] += 1
        inst.then_inc(self.sem[e], 1)
        self._commit((e, self.cnt[e]), reads, writes)

    def dma(self, e, out, in_, reads=(), writes=(), **kw):
        self._deps(e, reads, writes)
        i = self.drr
        self.drr = (self.drr + 1) % len(self.dsem)
        if self.dval[i] > 0:
            self._wait(e, i, self.dval[i])
        self.dval[i] += 16
        self.eng[e].dma_start(out=out, in_=in_, **kw).then_inc(self.dsem[i], 16)
        self._commit((i, self.dval[i]), reads, writes)

    def barrier(self):
        for e in self.eng:
            for i, v in enumerate(self.dval):
                if v > 0:
                    self._wait(e, i, v)
            for k in self.eng:
                if k != e and self.cnt[k] > 0:
                    self._wait(e, k, self.cnt[k])

    def finish(self, e='sp'):
        for i, v in enumerate(self.dval):
            if v > 0:
                self._wait(e, i, v)
        for k in self.eng:
            if k != e and self.cnt[k] > 0:
                self._wait(e, k, self.cnt[k])


class Ctx:
    pass


class Phase:
    uid = 0

    def __init__(self, g):
        self.g = g
        self.guards = []

    def __enter__(self):
        return self

    def __call__(self, name, shape, dt=F32):
        Phase.uid += 1
        gd = self.g.nc.sbuf_tensor('%s_u%d' % (name, Phase.uid), list(shape), dt)
        t = gd.__enter__()
        self.guards.append(gd)
        return t

    def __exit__(self, *a):
        self.g.S.barrier()
        for gd in reversed(self.guards):
            gd.__exit__(None, None, None)
        return False


def mm_group(S, out, pairs, reads, writes, start=True, stop=True):
    n = len(pairs)

    def fn(pe):
        inst = None
        for i, (l, r) in enumerate(pairs):
            inst = pe.matmul(out, l, r, start=(start and i == 0), stop=(stop and i == n - 1))
        return inst
    S.op('pe', fn, reads, writes)


def build(dbg=(), upto='all'):
    nc = bass.Bass("TRN2", target_bir_lowering=False)
    S = Sched(nc)
    g = Ctx()
    g.nc, g.S = nc, S
    din = lambda name, shape, dt=F32: nc.dram_tensor(name, list(shape), dt, kind="ExternalInput").ap()
    dint = lambda name, shape, dt=F32: nc.dram_tensor(name, list(shape), dt, kind="Internal").ap()
    g.x = din('x', [SEQ, D]); g.ctx = din('ctx', [CTX, D])
    g.cvec = din('cvec', [128, NCH, 2])
    g.mod_w = din('mod_w', [DEPTH, D, 6 * D]); g.modb = din('modb', [DEPTH, 128, 48])
    g.n1g = din('n1g', [DEPTH, 128, NCH]); g.n2g = din('n2g', [DEPTH, 128, NCH]); g.fing = din('fing', [128, NCH])
    g.ident = din('ident', [128, 128])
    g.w_in = din('w_in', [DEPTH, D, DIN])
    g.convw = din('convw', [DEPTH, 128, 4, 31]); g.convp = din('convp', [DEPTH, 128, 3, 4])
    g.wqks = din('wqks', [DEPTH, D, 1024]); g.rope = din('rope', [2, 128, SEQ]); g.cmask = din('cmask', [3, 128, 128])
    g.attp = din('attp', [DEPTH, 4, 64]); g.subg = din('subg', [DEPTH, 128])
    g.rw_w2 = din('rw_w2', [DEPTH, 2, 64, 512]); g.rw_a2 = din('rw_a2', [DEPTH, 2, 64, 512]); g.rw_g2 = din('rw_g2', [DEPTH, 128, 512])
    g.rwp = din('rwp', [DEPTH, 128, P_OMKA]); g.smask = din('smask', [9, 128, 128])
    g.rS = dint('rS', [512, T], BF16); g.kkS = dint('kkS', [512, T], BF16); g.gS = dint('gS', [512, T], BF16); g.bonS = dint('bonS', [512, T], BF16)
    g.kdS = [dint('kdS%d' % d, [512, T], BF16) for d in range(2)]; g.bS = [dint('bS%d' % d, [512, T], BF16) for d in range(2)]
    g.wlS = [dint('wlS%d' % d, [512, T]) for d in range(2)]; g.vT = dint('vT', [T, 512], BF16); g.yfS = dint('yfS', [512, T]); g.ybS = dint('ybS', [512, T])
    g.p_conv = din('p_conv', [DEPTH, 512, D]); g.p_att = din('p_att', [DEPTH, 512, D]); g.p_rwkv = din('p_rwkv', [DEPTH, 512, D])
    g.w_out = din('w_out', [DEPTH, D, D]); g.mlp_w1 = din('mlp_w1', [DEPTH, D, DFF]); g.mlp_w2 = din('mlp_w2', [DEPTH, DFF, D])
    g.out = nc.dram_tensor('out', [SEQ, D], F32, kind="ExternalOutput").ap()
    g.xT = dint('xT', [D, T]); g.hT = dint('hTd', [D, T], BF16)
    g.ycT = dint('ycT', [512, T], BF16); g.yaT = dint('yaT', [512, T], BF16); g.yrT = dint('yrT', [512, T], BF16)
    g.dbg = {}
    for name, shape, dt in dbg:
        g.dbg[name] = nc.dram_tensor('dbg_' + name, list(shape), dt, kind="ExternalOutput").ap()

    sb = lambda name, shape, dt=F32: nc.alloc_sbuf_tensor(name, list(shape), dt)
    g.ps = [nc.alloc_psum_tensor('ps%d' % i, [128, 512], F32) for i in range(8)]
    g.psi = 0

    g.identf = sb('identf', [128, 128]); g.identb = sb('identb', [128, 128], BF16)
    g.onesf = sb('onesf', [128, 128]); g.onesb = sb('onesb', [128, 128], BF16)
    g.epsv = sb('epsv', [128, 4])
    g.cs = sb('cs', [128, NCH, 2]); g.modv = sb('modv', [128, 48, 2]); g.gs = sb('gs', [128, 2, NCH, 2])
    S.dma('sp', g.identf[:], g.ident[:, :], writes=['identf'])
    S.op('dve', lambda e: e.tensor_copy(g.identb[:], g.identf[:]), ['identf'], ['identb'])
    S.op('dve', lambda e: e.memset(g.onesf[:], 1.0), [], ['onesf'])
    S.op('dve', lambda e: e.memset(g.onesb[:], 1.0), [], ['onesb'])
    for i, v in enumerate((NORM_EPS, LN_EPS, SUBLN_EPS, GN_EPS)):
        S.op('dve', lambda e, i=i, v=v: e.memset(g.epsv[:, i:i + 1], v), [], ['epsv'])

    import os
    if os.environ.get('SCAN_LIMIT'):
        g.scan_limit = int(os.environ['SCAN_LIMIT'])
    if upto.startswith('rwonly'):
        g.scan_limit = int(upto[6:] or 0)
        phase_rwkv_scan(g, 0)
        S.finish('sp')
        return nc
    phase_x0(g)
    for l in range(DEPTH):
        phase_mod(g, l)
        phase_h(g, l, 0)
        if upto == 'h':
            break
        phase_conv(g, l)
        if upto == 'conv':
            break
        phase_att(g, l)
        if upto == 'att':
            break
        phase_rwkv_prep(g, l)
        if upto == 'rwprep':
            break
        phase_rwkv_scan(g, l)
        if upto == 'rw':
            break
        phase_merge(g, l)
        phase_mlp(g, l)
        if upto == 'l0':
            break
    for nm in ('ycT', 'yaT', 'yrT', 'hT', 'rS', 'kkS', 'gS', 'bonS', 'vT', 'yfS'):
        if nm in g.dbg:
            S.dma('sp', g.dbg[nm][:, :], getattr(g, nm)[:, :], reads=[nm], writes=['dbg_' + nm])
    for nm, ap in (('kdS0', g.kdS[0]), ('bS0', g.bS[0]), ('wlS0', g.wlS[0])):
        if nm in g.dbg:
            S.dma('sp', g.dbg[nm][:, :], ap[:, :], reads=[nm], writes=['dbg_' + nm])
    S.finish('sp')
    return nc


def nextps(g):
    i = g.psi
    g.psi = (g.psi + 1) % 8
    return i


def phase_x0(g):
    S, nc = g.S, g.nc
    with Phase(g) as A:
        xin = [A('x0in%d' % i, [128, D]) for i in range(2)]
        xst = [A('x0st%d' % i, [128, NCH, 128]) for i in range(2)]
        xTv = g.xT.rearrange('(k p) t -> p k t', p=128)
        for ti in range(T // 128):
            b = ti % 2
            src = g.ctx[ti * 128:(ti + 1) * 128, :] if ti < 2 else g.x[(ti - 2) * 128:(ti - 1) * 128, :]
            S.dma('sp', xin[b][:], src, writes=[('x0in', b)])
            for half in range(2):
                pi = nextps(g)
                ps = g.ps[pi]

                def fn(pe, half=half, ps=ps, b=b):
                    inst = None
                    for j in range(4):
                        k = half * 4 + j
                        inst = pe.transpose(ps[:, j * 128:(j + 1) * 128], xin[b][:, k * 128:(k + 1) * 128], g.identf[:])
                    return inst
                S.op('pe', fn, [('x0in', b), 'identf'], [('ps', pi)])
                dst = xst[b][:, half * 4:(half + 1) * 4, :]
                src_ps = ps[:].rearrange('p (j t) -> p j t', j=4)
                if half == 0:
                    S.op('act', lambda e, dst=dst, s=src_ps: e.copy(dst, s), [('ps', pi)], [('x0st', b, half)])
                else:
                    S.op('dve', lambda e, dst=dst, s=src_ps: e.tensor_copy(dst, s), [('ps', pi)], [('x0st', b, half)])
            S.dma('sp', xTv[:, :, ti * 128:(ti + 1) * 128], xst[b][:], reads=[('x0st', b, 0), ('x0st', b, 1)], writes=['xT'])


def phase_mod(g, l):
    S, nc = g.S, g.nc
    with Phase(g) as A:
        modbs = A('modbs', [128, 48])
        mw = [A('mw%d' % i, [128, NCH, 512]) for i in range(2)]
        ng = A('ng', [128, 2, NCH])
        if l == 0:
            tmp = A('cs_tmp', [128, NCH, 2])
            S.dma('sp', tmp[:], g.cvec[:, :, :], writes=['cs_tmp'])
            S.op('act', lambda e: e.activation(g.cs[:], tmp[:], AF.Sigmoid), ['cs_tmp'], ['cs'])
            S.op('dve', lambda e: e.tensor_tensor(g.cs[:], g.cs[:], tmp[:], ALU.mult), ['cs', 'cs_tmp'], ['cs'])
        S.dma('sp', modbs[:], g.modb[l], writes=['modbs'])
        S.dma('sp', ng[:, 0, :], g.n1g[l], writes=['ng'])
        S.dma('sp', ng[:, 1, :], g.n2g[l], writes=['ng'])
        mwv = g.mod_w[l].rearrange('(k p) c -> p k c', p=128)
        for cg in range(12):
            b = cg % 2
            S.dma('sp', mw[b][:], mwv[:, :, cg * 512:(cg + 1) * 512], writes=[('mw', b)])
            pi = nextps(g)
            ps = g.ps[pi]
            for j in range(4):
                pairs = [(mw[b][:, k, j * 128:(j + 1) * 128], g.cs[:, k, :]) for k in range(NCH)]
                mm_group(S, ps[:, 2 * j:2 * j + 2], pairs, [('mw', b), 'cs'], [('ps', pi)])
            for j in range(4):
                jj = cg * 4 + j
                S.op('dve', lambda e, j=j, jj=jj, ps=ps: e.tensor_scalar(g.modv[:, jj, :], ps[:, 2 * j:2 * j + 2],
                     modbs[:, jj:jj + 1], None, ALU.add), [('ps', pi), 'modbs'], ['modv'])
        for n in range(2):
            sc = g.modv[:, (3 * n + 1) * 8:(3 * n + 2) * 8, :]
            S.op('dve', lambda e, n=n, sc=sc: e.tensor_scalar(g.gs[:, n], sc, 1.0, None, ALU.add), ['modv'], ['gs'])
            S.op('dve', lambda e, n=n: e.tensor_tensor(g.gs[:, n], g.gs[:, n],
                 ng[:, n, :].unsqueeze(2).broadcast_to([128, NCH, 2]), ALU.mult), ['gs', 'ng'], ['gs'])


def rms_stats(g, xb, n, sq, rstd, key_x, key_sq, key_rstd):
    S = g.S
    S.op('act', lambda e: e.activation(sq[:, :, :n], xb[:, :, :n], AF.Square), [key_x], [key_sq])
    pi = nextps(g)
    ps = g.ps[pi]
    mm_group(S, ps[:, :n], [(g.onesf[:], sq[:, k, :n]) for k in range(NCH)], [key_sq, 'onesf'], [('ps', pi)])
    S.op('act', lambda e: e.activation(rstd[:, :n], ps[:, :n], AF.Sqrt, bias=g.epsv[:, 0:1], scale=1.0 / D),
         [('ps', pi), 'epsv'], [key_rstd])
    S.op('dve', lambda e: e.reciprocal(rstd[:, :n], rstd[:, :n]), [key_rstd], [key_rstd])


def phase_h(g, l, n):
    S, nc = g.S, g.nc
    with Phase(g) as A:
        hx = [A('hx%d' % i, [128, NCH, 512]) for i in range(2)]
        hsq = A('hsq', [128, NCH, 512])
        hrs = [A('hrs%d' % i, [128, 512]) for i in range(2)]
        htmp = [A('htmp%d' % i, [128, 512]) for i in range(2)]
        hb = [A('hb%d' % i, [128, NCH, 512], BF16) for i in range(2)]
        xTv = g.xT.rearrange('(k p) t -> p k t', p=128)
        hTv = g.hT.rearrange('(k p) t -> p k t', p=128)
        for bi, (t0, nt) in enumerate(BLKS):
            b = bi % 2
            s = 1 if t0 < CTX else 0
            S.dma('sp', hx[b][:, :, :nt], xTv[:, :, t0:t0 + nt], reads=['xT'], writes=[('hx', b)])
            rms_stats(g, hx[b], nt, hsq, hrs[b], ('hx', b), 'hsq', ('hrs', b))
            for k in range(NCH):
                tb = k % 2
                S.op('dve', lambda e, k=k, tb=tb: e.tensor_tensor(htmp[tb][:, :nt], hx[b][:, k, :nt], hrs[b][:, :nt], ALU.mult),
                     [('hx', b), ('hrs', b)], [('htmp', tb)])
                S.op('act', lambda e, k=k, tb=tb: e.activation(hb[b][:, k, :nt], htmp[tb][:, :nt], AF.Identity,
                     bias=g.modv[:, (3 * n) * 8 + k, s:s + 1], scale=g.gs[:, n, k, s:s + 1]),
                     [('htmp', tb), 'modv', 'gs'], [('hb', b)])
            S.dma('sp', hTv[:, :, t0:t0 + nt], hb[b][:, :, :nt], reads=[('hb', b)], writes=['hT'])


class HLoader:
    def __init__(self, g, A):
        self.g = g
        self.t = [A('hblk%d' % i, [128, NCH, 512], BF16) for i in range(2)]
        self.i = 0

    def load(self, bi):
        g = self.g
        b = self.i
        self.i = (b + 1) % 2
        t0, nt = BLKS[bi]
        hTv = g.hT.rearrange('(k p) t -> p k t', p=128)
        g.S.dma('sp', self.t[b][:, :, :nt], hTv[:, :, t0:t0 + nt], reads=['hT'], writes=[('hblk', b)])
        return self.t[b], ('hblk', b)


def ucol(t):
    return t + 15 if t < CTX else t + 45


def phase_conv(g, l):
    S, nc = g.S, g.nc
    with Phase(g) as A:
        wcv = A('wcv', [128, NCH, 1024], BF16)
        uT = A('uT', [128, 4, T + 60], BF16)
        diag = A('diag', [128, 4, 31, 128], BF16)
        dww = A('dww', [128, 4, 31])
        cvp = A('cvp', [128, 3, 4])
        csg = [A('csg%d' % i, [128, 512]) for i in range(2)]
        cv = A('cv', [128, 4, 512]); cv2 = A('cv2', [128, 4, 512])
        cm = A('cm', [128, 512]); cmsq = A('cmsq', [128, 512]); crs = A('crs', [128, 512])
        ct = [A('ct%d' % i, [128, 512]) for i in range(2)]
        cyb = [A('cyb%d' % i, [128, 4, 512], BF16) for i in range(2)]
        HL = HLoader(g, A)
        S.op('pool', lambda e: e.memset(uT[:], 0.0), [], ['uT'])
        S.dma('pool', wcv[:], g.w_in[l][:, 0:1024].rearrange('(k p) c -> p k c', p=128), writes=['wcv'])
        S.dma('sp', dww[:], g.convw[l], writes=['dww'])
        S.dma('sp', cvp[:], g.convp[l], writes=['cvp'])
        for c in range(4):
            S.op('dve', lambda e, c=c: e.tensor_tensor(diag[:, c], g.identf[:].unsqueeze(1).broadcast_to([128, 31, 128]),
                 dww[:, c, :].unsqueeze(2).broadcast_to([128, 31, 128]), ALU.mult), ['identf', 'dww'], ['diag'])
        for bi, (t0, nt) in enumerate(BLKS):
            hb, hk = HL.load(bi)
            for c in range(4):
                pa, pb = nextps(g), nextps(g)
                mm_group(S, g.ps[pa][:, :nt], [(wcv[:, k, c * 128:(c + 1) * 128], hb[:, k, :nt]) for k in range(NCH)],
                         ['wcv', hk], [('ps', pa)])
                mm_group(S, g.ps[pb][:, :nt], [(wcv[:, k, 512 + c * 128:512 + (c + 1) * 128], hb[:, k, :nt]) for k in range(NCH)],
                         ['wcv', hk], [('ps', pb)])
                sb_ = c % 2
                S.op('act', lambda e, pb=pb, sb_=sb_: e.activation(csg[sb_][:, :nt], g.ps[pb][:, :nt], AF.Sigmoid),
                     [('ps', pb)], [('csg', sb_)])
                S.op('dve', lambda e, pa=pa, sb_=sb_, c=c: e.tensor_tensor(uT[:, c, ucol(t0):ucol(t0) + nt], g.ps[pa][:, :nt],
                     csg[sb_][:, :nt], ALU.mult), [('ps', pa), ('csg', sb_)], ['uT'])
        ycv = g.ycT.rearrange('(k p) t -> p k t', p=128)
        for bi, (t0, nt) in enumerate(BLKS):
            yb = cyb[bi % 2]
            ykey = ('cyb', bi % 2)
            for c in range(4):
                pi = nextps(g)
                base = ucol(t0) - 15
                mm_group(S, g.ps[pi][:, :nt], [(diag[:, c, k, :], uT[:, c, base + k:base + k + nt]) for k in range(31)],
                         ['diag', 'uT'], [('ps', pi)])
                S.op('act', lambda e, pi=pi, c=c: e.activation(cv[:, c, :nt], g.ps[pi][:, :nt], AF.Identity,
                     bias=cvp[:, 0, c:c + 1], scale=1.0), [('ps', pi), 'cvp'], [('cv', c)])
                S.op('act', lambda e, pi=pi, c=c: e.activation(cv2[:, c, :nt], g.ps[pi][:, :nt], AF.Square,
                     bias=cvp[:, 0, c:c + 1], scale=1.0), [('ps', pi), 'cvp'], [('cv2', c)])
            p1, p2 = nextps(g), nextps(g)
            mm_group(S, g.ps[p1][:, :nt], [(g.onesf[:], cv[:, c, :nt]) for c in range(4)], [('cv', c) for c in range(4)] + ['onesf'], [('ps', p1)])
            mm_group(S, g.ps[p2][:, :nt], [(g.onesf[:], cv2[:, c, :nt]) for c in range(4)], [('cv2', c) for c in range(4)] + ['onesf'], [('ps', p2)])
            S.op('act', lambda e: e.activation(cm[:, :nt], g.ps[p1][:, :nt], AF.Identity, scale=1.0 / 512), [('ps', p1)], ['cm'])
            S.op('dve', lambda e: e.tensor_tensor(cmsq[:, :nt], cm[:, :nt], cm[:, :nt], ALU.mult), ['cm'], ['cmsq'])
            S.op('dve', lambda e: e.scalar_tensor_tensor(crs[:, :nt], g.ps[p2][:, :nt], 1.0 / 512, cmsq[:, :nt], ALU.mult, ALU.subtract),
                 [('ps', p2), 'cmsq'], ['crs'])
            S.op('act', lambda e: e.activation(crs[:, :nt], crs[:, :nt], AF.Sqrt, bias=g.epsv[:, 1:2], scale=1.0), ['crs', 'epsv'], ['crs'])
            S.op('dve', lambda e: e.reciprocal(crs[:, :nt], crs[:, :nt]), ['crs'], ['crs'])
            for c in range(4):
                tb = c % 2
                S.op('dve', lambda e, c=c, tb=tb: e.tensor_tensor(ct[tb][:, :nt], cv[:, c, :nt], cm[:, :nt], ALU.subtract),
                     [('cv', c), 'cm'], [('ct', tb)])
                S.op('dve', lambda e, c=c, tb=tb: e.tensor_tensor(ct[tb][:, :nt], ct[tb][:, :nt], crs[:, :nt], ALU.mult),
                     [('ct', tb), 'crs'], [('ct', tb)])
                S.op('act', lambda e, c=c, tb=tb: e.activation(yb[:, c, :nt], ct[tb][:, :nt], AF.Silu,
                     bias=cvp[:, 2, c:c + 1], scale=cvp[:, 1, c:c + 1]), [('ct', tb), 'cvp'], [ykey])
            S.dma('sp', ycv[:, :, t0:t0 + nt], yb[:, :, :nt], reads=[ykey], writes=['ycT'])


def fm(v, nch):
    return np.ascontiguousarray(np.asarray(v, np.float32).reshape(nch, 128).T)


def host_shared(inp):
    m = {}
    m['mod_w'] = np.ascontiguousarray(inp['mod_w'], dtype=np.float32)
    m['modb'] = np.stack([fm(inp['mod_b'][l], 48) for l in range(DEPTH)])
    m['n1g'] = np.stack([fm(inp['norm1_g'][l], NCH) for l in range(DEPTH)])
    m['n2g'] = np.stack([fm(inp['norm2_g'][l], NCH) for l in range(DEPTH)])
    m['fing'] = fm(inp['final_g'], NCH)
    m['ident'] = np.eye(128, dtype=np.float32)
    m['w_in'] = np.ascontiguousarray(inp['w_in'], dtype=np.float32)
    sw = np.arange(1024) ^ 1
    m['wqks'] = np.ascontiguousarray(np.asarray(inp['w_in'])[:, :, 1024:2048][:, :, sw], dtype=np.float32)
    tt = np.arange(SEQ)
    inv = (10000.0 ** (-np.arange(16, dtype=np.float32) / 16)).astype(np.float32)
    ang = np.concatenate([(tt // 64).astype(np.float32)[:, None] * inv, (tt % 64).astype(np.float32)[:, None] * inv], axis=-1)
    pidx = (np.arange(128) % 64) // 2
    cosT = np.cos(ang)[:, pidx].T
    sinT = np.sin(ang)[:, pidx].T * np.where(np.arange(128) % 2 == 0, -1.0, 1.0)[:, None]
    m['rope'] = np.ascontiguousarray(np.stack([cosT, sinT]), dtype=np.float32)
    blk = (np.arange(128) // 64)
    bdm = (blk[:, None] == blk[None, :]).astype(np.float32)
    sel0 = np.repeat((blk == 0).astype(np.float32)[:, None], 128, 1)
    sel1 = np.repeat((blk == 1).astype(np.float32)[:, None], 128, 1)
    m['cmask'] = np.ascontiguousarray(np.stack([bdm, sel0, sel1]))
    m['attp'] = np.ascontiguousarray(np.stack([np.stack([inp[k][l] for k in ('att_lq1', 'att_lk1', 'att_lq2', 'att_lk2')]) for l in range(DEPTH)]), dtype=np.float32)
    m['subg'] = np.ascontiguousarray(inp['att_subln_g'], dtype=np.float32)
    for k in ('p_conv', 'p_att', 'p_rwkv', 'w_out', 'mlp_w1', 'mlp_w2'):
        m[k] = np.ascontiguousarray(inp[k], dtype=np.float32)
    m['rw_w2'] = np.ascontiguousarray(inp['rwkv_w2'], dtype=np.float32)
    m['rw_a2'] = np.ascontiguousarray(inp['rwkv_a2'], dtype=np.float32)
    m['rw_g2'] = np.ascontiguousarray(inp['rwkv_g2'], dtype=np.float32)
    rwp = []
    for l in range(DEPTH):
        cols = [np.asarray(inp['rwkv_shift'][l]).T.reshape(15, 128, 3).transpose(1, 0, 2).reshape(128, 45)]
        cols += [fm(inp['rwkv_w0'][l].reshape(-1), 8), fm(inp['rwkv_a0'][l].reshape(-1), 8)]
        cols += [fm(inp[k][l].reshape(-1), 4) for k in ('rwkv_kk', 'rwkv_ka', 'rwkv_rk', 'rwkv_gn_g', 'rwkv_gn_b')]
        rwp.append(np.concatenate(cols, axis=1))
    m['rwp'] = np.ascontiguousarray(np.stack(rwp), dtype=np.float32)
    ii = np.arange(128)
    lt = (ii[:, None] < ii[None, :]).astype(np.float32); le = (ii[:, None] <= ii[None, :]).astype(np.float32)
    seg = np.repeat((ii != 0).astype(np.float32)[None, :], 128, 0)
    blk = lambda n: (ii[:, None] // n == ii[None, :] // n)
    offm = lambda n: (blk(n) & ~blk(n // 2)).astype(np.float32)
    m['smask'] = np.ascontiguousarray(np.stack([lt, le, lt.T, le.T, seg, blk(16).astype(np.float32), offm(32), offm(64), offm(128)]))
    m['convw'] = np.stack([np.ascontiguousarray(np.asarray(inp['conv_dw_w'][l]).T.reshape(4, 128, 31).transpose(1, 0, 2)) for l in range(DEPTH)])
    m['convp'] = np.stack([np.stack([fm(inp[k][l], 4) for k in ('conv_dw_b', 'conv_ln_g', 'conv_ln_b')], axis=1) for l in range(DEPTH)])
    return m


def host_inputs(inp, b, shared=None):
    m = dict(shared if shared is not None else host_shared(inp))
    m['x'] = np.ascontiguousarray(inp['x'][b], dtype=np.float32)
    m['ctx'] = np.ascontiguousarray(inp['ctx'][b], dtype=np.float32)
    m['cvec'] = np.ascontiguousarray(np.stack([fm(inp['c'][b], NCH), fm(inp['c_ctx'], NCH)], axis=-1))
    return m


def phase_att(g, l):
    S, nc = g.S, g.nc
    lam_init = 0.8 - 0.6 * math.exp(-0.3 * l)
    need_ctx_q = l < DEPTH - 1
    with Phase(g) as A:
        qT = A('qT', [128, 4, T], BF16); kT = A('kT', [128, 4, T], BF16)
        vaug = A('vaug', [128, T // 128, 4, 129], BF16)
        nb = A('nb', [128, 2, 4]); neglam = A('neglam', [128, 1]); gsub = A('gsub', [128, 128])
        A1 = Phase(g)
        wq = A1('wq', [128, NCH, 512], BF16); wqs = A1('wqs', [128, NCH, 512], BF16)
        wk = A1('wk', [128, NCH, 512], BF16); wks = A1('wks', [128, NCH, 512], BF16)
        wv = A1('wv', [128, NCH, 512], BF16)
        cosT = A1('cosT', [128, SEQ]); sinT = A1('sinT', [128, SEQ])
        HL = HLoader(g, A1)
        rt = [A1('rt%d' % i, [128, 512]) for i in range(4)]
        bd = A1('bd', [128, 128], BF16); cmf = A1('cmf', [128, 3, 128])
        stat = A1('stat', [128, 2, 4, len(BLKS)]); stm = A1('stm', [128, 2, 4]); negb = A1('negb', [128, 4])
        lqk = A1('lqk', [128, 4, 64]); lam2 = A1('lam2', [128, 2])

        wsrc = g.w_in[l].rearrange('(k p) c -> p k c', p=128)
        ssrc = g.wqks[l].rearrange('(k p) c -> p k c', p=128)
        S.dma('pool', wq[:], wsrc[:, :, 1024:1536], writes=['wq'])
        S.dma('pool', wk[:], wsrc[:, :, 1536:2048], writes=['wk'])
        S.dma('pool', wv[:], wsrc[:, :, 2048:2560], writes=['wv'])
        S.dma('pool', wqs[:], ssrc[:, :, 0:512], writes=['wqs'])
        S.dma('pool', wks[:], ssrc[:, :, 512:1024], writes=['wks'])
        S.dma('sp', cosT[:], g.rope[0], writes=['cosT'])
        S.dma('sp', sinT[:], g.rope[1], writes=['sinT'])
        S.dma('sp', cmf[:], g.cmask.rearrange('m p c -> p m c'), writes=['cmf'])
        S.op('dve', lambda e: e.tensor_copy(bd[:], cmf[:, 0, :]), ['cmf'], ['bd'])
        S.dma('sp', lqk[:], g.attp[l:l + 1].broadcast_to([128, 4, 64]), writes=['lqk'])
        S.dma('sp', gsub[:], g.subg[l:l + 1, :].broadcast_to([128, 128]), writes=['gsub'])
        S.op('act', lambda e: e.mul(gsub[:], gsub[:], 1.0 - lam_init), ['gsub'], ['gsub'])
        S.op('dve', lambda e: e.tensor_tensor(lqk[:, 0, :], lqk[:, 0, :], lqk[:, 1, :], ALU.mult), ['lqk'], ['lqk'])
        S.op('dve', lambda e: e.tensor_tensor(lqk[:, 2, :], lqk[:, 2, :], lqk[:, 3, :], ALU.mult), ['lqk'], ['lqk'])
        S.op('dve', lambda e: e.reduce_sum(lam2[:, 0:1], lqk[:, 0, :], AX.X), ['lqk'], ['lam2'])
        S.op('dve', lambda e: e.reduce_sum(lam2[:, 1:2], lqk[:, 2, :], AX.X), ['lqk'], ['lam2'])
        S.op('act', lambda e: e.activation(lam2[:], lam2[:], AF.Exp), ['lam2'], ['lam2'])
        S.op('dve', lambda e: e.tensor_tensor(neglam[:], lam2[:, 1:2], lam2[:, 0:1], ALU.subtract), ['lam2'], ['neglam'])
        S.op('dve', lambda e: e.tensor_scalar(neglam[:], neglam[:], -lam_init, None, ALU.add), ['neglam'], ['neglam'])
        S.op('pool', lambda e: e.memset(vaug[:, :, :, 128:129], 1.0), [], ['vaug1'])

        for bi, (t0, nt) in enumerate(BLKS):
            hb, hk = HL.load(bi)
            lat = t0 >= CTX
            tl = t0 - CTX
            for (w, ws, dst, dk, wkey, wskey) in ((wq, wqs, qT, 'qT', 'wq', 'wqs'), (wk, wks, kT, 'kT', 'wk', 'wks')):
                for h in range(4):
                    pa = nextps(g)
                    mm_group(S, g.ps[pa][:, :nt], [(w[:, k, h * 128:(h + 1) * 128], hb[:, k, :nt]) for k in range(NCH)],
                             [wkey, hk], [('ps', pa)])
                    if not lat:
                        S.op('act', lambda e, pa=pa, h=h, dst=dst: e.copy(dst[:, h, t0:t0 + nt], g.ps[pa][:, :nt]), [('ps', pa)], [dk])
                        continue
                    pb = nextps(g)
                    mm_group(S, g.ps[pb][:, :nt], [(ws[:, k, h * 128:(h + 1) * 128], hb[:, k, :nt]) for k in range(NCH)],
                             [wskey, hk], [('ps', pb)])
                    r1, r2 = (0, 1) if h % 2 == 0 else (2, 3)
                    S.op('dve', lambda e, pa=pa, r1=r1: e.tensor_tensor(rt[r1][:, :nt], g.ps[pa][:, :nt], cosT[:, tl:tl + nt], ALU.mult),
                         [('ps', pa), 'cosT'], [('rt', r1)])
                    S.op('dve', lambda e, pb=pb, r2=r2: e.tensor_tensor(rt[r2][:, :nt], g.ps[pb][:, :nt], sinT[:, tl:tl + nt], ALU.mult),
                         [('ps', pb), 'sinT'], [('rt', r2)])
                    S.op('pool', lambda e, r1=r1, r2=r2, h=h, dst=dst: e.tensor_tensor(dst[:, h, t0:t0 + nt], rt[r1][:, :nt], rt[r2][:, :nt], ALU.add),
                         [('rt', r1), ('rt', r2)], [dk])
            for tt in range(nt // 128):
                ti = t0 // 128 + tt
                pv = nextps(g)
                mm_group(S, g.ps[pv][:, :512], [(hb[:, k, tt * 128:(tt + 1) * 128], wv[:, k, :]) for k in range(NCH)],
                         ['wv', hk], [('ps', pv)])
                S.op('act', lambda e, pv=pv, ti=ti: e.copy(vaug[:, ti, :, 0:128], g.ps[pv][:, :].rearrange('p (h d) -> p h d', h=4)),
                     [('ps', pv)], ['vaug'])
        sqb = rt
        for qi, (src, sk) in enumerate(((qT, 'qT'), (kT, 'kT'))):
            for h in range(4):
                for bi, (t0, nt) in enumerate(BLKS):
                    r = (h * len(BLKS) + bi) % 4
                    sq = rt[r][:, 0:256].bitcast(BF16)
                    S.op('act', lambda e, sq=sq, h=h, src=src: e.activation(sq[:, :nt], src[:, h, t0:t0 + nt], AF.Square), [sk], [('rt', r)])
                    pi = nextps(g)
                    mm_group(S, g.ps[pi][:, :nt], [(bd[:], sq[:, :nt])], ['bd', ('rt', r)], [('ps', pi)])
                    S.op('dve', lambda e, pi=pi, h=h, bi=bi, qi=qi: e.reduce_max(stat[:, qi, h, bi:bi + 1], g.ps[pi][:, :nt], AX.X),
                         [('ps', pi)], ['stat'])
        S.op('dve', lambda e: e.reduce_max(stm[:], stat[:], AX.X), ['stat'], ['stm'])
        S.op('dve', lambda e: e.tensor_tensor(negb[:], stm[:, 0, :], stm[:, 1, :], ALU.mult), ['stm'], ['negb'])
        S.op('act', lambda e: e.activation(negb[:], negb[:], AF.Sqrt), ['negb'], ['negb'])
        for c in range(2):
            pi = nextps(g)
            mm_group(S, g.ps[pi][:, 0:4], [(cmf[:, 1 + c, :], negb[:])], ['cmf', 'negb'], [('ps', pi)])
            S.op('act', lambda e, pi=pi, c=c: e.mul(nb[:, c, :], g.ps[pi][:, 0:4], -1.02 * 0.125 / 64.0), [('ps', pi)], ['nb'])

        A1.__exit__(None, None, None)
        pT = [A('pT%d' % i, [128, 512], BF16) for i in range(6)]
        oc = [[A('oc%d_%d' % (c, q), [128, 129]) for q in range(4)] for c in range(2)]
        sm = A('sm', [128, 8]); o0 = A('o0', [128, 128]); aa = A('aa', [128, 128]); junk = A('junk', [128, 128])
        ytok = [A('ytok%d' % q, [128, 512], BF16) for q in range(4)]
        yab = [A('yab%d' % i, [128, 4, 512], BF16) for i in range(2)]
        yav = g.yaT.rearrange('(k p) t -> p k t', p=128)
        pti = [0]
        sbank = [0]

        def attend(q0, nq, kt0, nkt, yslot):
            nqs = nq // 128
            items = [(h, c, kk) for h in range(4) for c in range(2) for kk in range(nkt)]
            DPF = 2
            slots = {}

            def front(i):
                h, c, kk = items[i]
                kt = kt0 + kk
                sb_ = 4 + sbank[0]
                sbank[0] = (sbank[0] + 1) % 4
                mm_group(S, g.ps[sb_][:, :nq], [(kT[64 * c:64 * c + 64, h, kt * 128:(kt + 1) * 128], qT[64 * c:64 * c + 64, h, q0:q0 + nq])],
                         ['kT', 'qT'], [('ps', sb_)])
                pb_ = pti[0]
                pti[0] = (pti[0] + 1) % 6
                S.op('act', lambda e: e.activation(pT[pb_][:, :nq], g.ps[sb_][:, :nq], AF.Exp,
                     bias=nb[:, c, h:h + 1], scale=0.125), [('ps', sb_), 'nb'], [('pT', pb_)])
                slots[i] = pb_

            def back(i):
                h, c, kk = items[i]
                kt = kt0 + kk
                pb_ = slots.pop(i)
                for qs in range(nqs):
                    mm_group(S, g.ps[qs][:, 0:129], [(pT[pb_][:, qs * 128:(qs + 1) * 128], vaug[:, kt, h, :])],
                             [('pT', pb_), 'vaug', 'vaug1'], [('ps', qs)], start=(kk == 0), stop=(kk == nkt - 1))
                if kk != nkt - 1:
                    return
                for qs in range(nqs):
                    S.op('dve', lambda e, qs=qs: e.tensor_copy(oc[c][qs][:], g.ps[qs][:, 0:129]), [('ps', qs)], [('oc', c, qs)])
                if c != 1:
                    return
                for qs in range(nqs):
                    k0, k1 = ('oc', 0, qs), ('oc', 1, qs)
                    S.op('dve', lambda e, qs=qs: e.reciprocal(sm[:, 0:1], oc[0][qs][:, 128:129]), [k0], ['sm0'])
                    S.op('dve', lambda e, qs=qs: e.reciprocal(sm[:, 1:2], oc[1][qs][:, 128:129]), [k1], ['sm1'])
                    S.op('dve', lambda e: e.tensor_tensor(sm[:, 2:3], sm[:, 1:2], neglam[:], ALU.mult), ['sm1', 'neglam'], ['sm2'])
                    S.op('dve', lambda e, qs=qs: e.tensor_scalar(o0[:], oc[0][qs][:, 0:128], sm[:, 0:1], None, ALU.mult), [k0, 'sm0'], ['o0'])
                    S.op('dve', lambda e, qs=qs: e.scalar_tensor_tensor(aa[:], oc[1][qs][:, 0:128], sm[:, 2:3], o0[:], ALU.mult, ALU.add),
                         [k1, 'sm2', 'o0'], ['aa'])
                    S.op('dve', lambda e: e.scalar_tensor_tensor(junk[:], aa[:], 1.0, aa[:], ALU.mult, ALU.mult, accum_out=sm[:, 3:4]), ['aa'], ['junk', 'sm3'])
                    S.op('act', lambda e: e.activation(sm[:, 4:5], sm[:, 3:4], AF.Sqrt, bias=g.epsv[:, 2:3], scale=1.0 / 128), ['sm3', 'epsv'], ['sm4'])
                    S.op('dve', lambda e: e.reciprocal(sm[:, 5:6], sm[:, 4:5]), ['sm4'], ['sm5'])
                    S.op('dve', lambda e, qs=qs: e.scalar_tensor_tensor(ytok[qs][:, h * 128:(h + 1) * 128], aa[:], sm[:, 5:6], gsub[:], ALU.mult, ALU.mult),
                         ['aa', 'sm5', 'gsub'], [('ytok', qs)])

            for i in range(len(items) + DPF):
                if i < len(items):
                    front(i)
                if i - DPF >= 0:
                    back(i - DPF)
            yb = yab[yslot % 2]
            ykey = ('yab', yslot % 2)
            for qs in range(nqs):
                tb_ = 4 + sbank[0]
                sbank[0] = (sbank[0] + 1) % 4
                psb = g.ps[tb_][:, :].bitcast(BF16)

                def fn(pe, qs=qs, psb=psb):
                    inst = None
                    for h in range(4):
                        inst = pe.transpose(psb[:, h * 128:(h + 1) * 128], ytok[qs][:, h * 128:(h + 1) * 128], g.identb[:])
                    return inst
                S.op('pe', fn, [('ytok', qs), 'identb'], [('ps', tb_)])
                S.op('dve', lambda e, qs=qs, psb=psb, yb=yb: e.tensor_copy(yb[:, :, qs * 128:(qs + 1) * 128], psb[:, 0:512].rearrange('p (h q) -> p h q', h=4)),
                     [('ps', tb_)], [ykey])
            S.dma('sp', yav[:, :, q0:q0 + nq], yb[:, :, :nq], reads=[ykey], writes=['yaT'])

        slot = 0
        if need_ctx_q:
            attend(0, CTX, 0, CTX // 128, slot)
            slot += 1
        for qb in range(SEQ // 512):
            attend(CTX + qb * 512, 512, 0, T // 128, slot)
            slot += 1


RW0 = 2560
P_SH, P_W0, P_A0, P_KK, P_KA, P_RK, P_GG, P_GB, P_OMKA, NRWP = 0, 45, 53, 61, 65, 69, 73, 77, 81, 85
DECAY_C = -math.exp(-0.5)


def phase_rwkv_prep(g, l):
    S, nc = g.S, g.nc
    with Phase(g) as A:
        wrw = A('wrw', [128, NCH, 1920], BF16)
        w2b = A('w2b', [128, 512], BF16); a2b = A('a2b', [128, 512], BF16); g2b = A('g2b', [128, 512], BF16)
        rwp = A('rwp', [128, NRWP])
        bdf = A('bdf', [128, 128])
        hbx = [A('hbx%d' % i, [128, NCH, 514], BF16) for i in range(2)]
        zx = [A('zx%d' % i, [128, 514]) for i in range(3)]
        zc = A('zc', [128, 15, 512])
        tw = A('tw', [128, 512], BF16); ab = A('ab', [128, 512], BF16); sg = A('sg', [128, 512], BF16)
        kkt = A('kkt', [128, 4, 512]); kds = A('kds', [128, 4, 512])
        t1 = [A('rt1_%d' % i, [128, 512]) for i in range(3)]
        ob = {n: [A('ob_%s%d' % (n, i), [128, 4, 512], BF16) for i in range(1)] * 2 for n in ('r', 'kk', 'kd0', 'kd1', 'b0', 'b1', 'g', 'bon')}
        owl = {d: [A('owl%d_%d' % (d, i), [128, 4, 512]) for i in range(1)] * 2 for d in range(2)}
        vtile = [A('vtile%d' % i, [128, 512], BF16) for i in range(2)]

        wsrc = g.w_in[l].rearrange('(k p) c -> p k c', p=128)
        S.dma('pool', wrw[:], wsrc[:, :, RW0:RW0 + 1920], writes=['wrw'])
        S.dma('pool', w2b[:], g.rw_w2[l].rearrange('d m c -> (d m) c'), writes=['w2b'])
        S.dma('pool', a2b[:], g.rw_a2[l].rearrange('d m c -> (d m) c'), writes=['a2b'])
        S.dma('pool', g2b[:], g.rw_g2[l], writes=['g2b'])
        S.dma('sp', rwp[:, 0:P_OMKA], g.rwp[l], writes=['rwp'])
        S.dma('sp', bdf[:], g.cmask[0], writes=['bdf'])
        S.op('dve', lambda e: e.tensor_scalar(rwp[:, P_OMKA:P_OMKA + 4], rwp[:, P_KA:P_KA + 4], -1.0, 1.0, ALU.mult, ALU.add), ['rwp'], ['rwp'])
        hTv = g.hT.rearrange('(k p) t -> p k t', p=128)
        fmv = lambda ap: ap.rearrange('(k p) t -> p k t', p=128)
        for bi, (t0, nt) in enumerate(BLKS):
            b = bi % 2
            hb = hbx[b]
            hk = ('hbx', b)
            s0, s1 = (0, CTX) if t0 < CTX else (CTX, T)
            lo, hi = max(s0, t0 - 1), min(s1, t0 + nt + 1)
            if lo == t0:
                S.op('pool', lambda e, hb=hb: e.memset(hb[:, :, 0:1], 0.0), [], [hk])
            if hi == t0 + nt:
                S.op('pool', lambda e, hb=hb: e.memset(hb[:, :, nt + 1:nt + 2], 0.0), [], [hk])
            S.dma('sp', hb[:, :, 1 - (t0 - lo):1 + (hi - t0)], hTv[:, :, lo:hi], reads=['hT'], writes=[hk])
            ph = nextps(g)
            for ch in range(15):
                pm = nextps(g)
                if pm == ph:
                    pm = nextps(g)
                wsl = lambda k, ch=ch: wrw[:, k, ch * 128:(ch + 1) * 128]
                mm_group(S, g.ps[pm][:, :nt], [(wsl(k), hb[:, k, 1:1 + nt]) for k in range(NCH)], ['wrw', hk], [('ps', pm)])
                mm_group(S, g.ps[ph][:, 2 * ch:2 * ch + 1], [(wsl(k), hb[:, k, 0:1]) for k in range(NCH)], ['wrw', hk], [('ps', ph)])
                mm_group(S, g.ps[ph][:, 2 * ch + 1:2 * ch + 2], [(wsl(k), hb[:, k, nt + 1:nt + 2]) for k in range(NCH)], ['wrw', hk], [('ps', ph)])
                z = zx[ch % 3]
                zk = ('zx', ch % 3)
                S.op('act', lambda e, z=z, pm=pm: e.copy(z[:, 1:1 + nt], g.ps[pm][:, :nt]), [('ps', pm)], [zk])
                S.op('act', lambda e, z=z, ch=ch: e.copy(z[:, 0:1], g.ps[ph][:, 2 * ch:2 * ch + 1]), [('ps', ph)], [zk])
                S.op('act', lambda e, z=z, ch=ch: e.copy(z[:, nt + 1:nt + 2], g.ps[ph][:, 2 * ch + 1:2 * ch + 2]), [('ps', ph)], [zk])
                sh = lambda j, ch=ch: rwp[:, P_SH + ch * 3 + j:P_SH + ch * 3 + j + 1]
                S.op('act', lambda e, pm=pm, ch=ch, sh=sh: e.activation(zc[:, ch, :nt], g.ps[pm][:, :nt], AF.Identity, scale=sh(1)), [('ps', pm), 'rwp'], [('zc', ch)])
                tz = t1[ch % 3]
                S.op('pool', lambda e, z=z, tz=tz, sh=sh: e.tensor_scalar(tz[:, :nt], z[:, 0:nt], sh(0), 0.0, ALU.mult, ALU.add), [zk, 'rwp'], [('t1', ch % 3)])
                S.op('dve', lambda e, z=z, ch=ch, sh=sh: e.scalar_tensor_tensor(zc[:, ch, :nt], z[:, 2:nt + 2], sh(2), zc[:, ch, :nt], ALU.mult, ALU.add),
                     [zk, 'rwp', ('zc', ch)], [('zc', ch)])
                S.op('pool', lambda e, tz=tz, ch=ch: e.tensor_tensor(zc[:, ch, :nt], zc[:, ch, :nt], tz[:, :nt], ALU.add), [('zc', ch), ('t1', ch % 3)], [('zc', ch)])
            o = {n: ob[n][0] for n in ob}
            okey = {n: ('ob', n, 0) for n in ob}
            S.op('act', lambda e: e.copy(o['r'][:, :, :nt], zc[:, 0:4, :nt]), [('zc', c) for c in range(4)], [okey['r']])
            S.dma('sp', fmv(g.rS)[:, :, t0:t0 + nt], o['r'][:, :, :nt], reads=[okey['r']], writes=['rS'])
            S.op('act', lambda e: e.activation(tw[:, :nt], zc[:, 12, :nt], AF.Tanh), [('zc', 12)], ['tw'])
            S.op('act', lambda e: e.copy(ab[:, :nt], zc[:, 13, :nt]), [('zc', 13)], ['ab'])
            S.op('act', lambda e: e.activation(sg[:, :nt], zc[:, 14, :nt], AF.Sigmoid), [('zc', 14)], ['sg'])
            for c in range(4):
                ti = c % 3
                S.op('act', lambda e, c=c: e.activation(kkt[:, c, :nt], zc[:, 4 + c, :nt], AF.Identity, scale=rwp[:, P_KK + c:P_KK + c + 1]),
                     [('zc', 4 + c), 'rwp'], [('kkt', c)])
                S.op('act', lambda e, c=c, ti=ti: e.activation(t1[ti][:, :nt], kkt[:, c, :nt], AF.Square), [('kkt', c)], [('t1', ti)])
                pi = nextps(g)
                mm_group(S, g.ps[pi][:, :nt], [(bdf[:], t1[ti][:, :nt])], ['bdf', ('t1', ti)], [('ps', pi)])
                S.op('dve', lambda e, pi=pi, ti=ti: e.tensor_scalar(t1[ti][:, :nt], g.ps[pi][:, :nt], 1e-24, None, ALU.max), [('ps', pi)], [('t1', ti)])
                S.op('act', lambda e, ti=ti: e.activation(t1[ti][:, :nt], t1[ti][:, :nt], AF.Sqrt), [('t1', ti)], [('t1', ti)])
                S.op('dve', lambda e, ti=ti: e.reciprocal(t1[ti][:, :nt], t1[ti][:, :nt]), [('t1', ti)], [('t1', ti)])
                S.op('dve', lambda e, c=c, ti=ti: e.tensor_tensor(kkt[:, c, :nt], kkt[:, c, :nt], t1[ti][:, :nt], ALU.mult), [('kkt', c), ('t1', ti)], [('kkt', c)])
            S.op('act', lambda e: e.copy(o['kk'][:, :, :nt], kkt[:, :, :nt]), [('kkt', c) for c in range(4)], [okey['kk']])
            S.dma('sp', fmv(g.kkS)[:, :, t0:t0 + nt], o['kk'][:, :, :nt], reads=[okey['kk']], writes=['kkS'])
            for d in range(2):
                kdn, bn = 'kd%d' % d, 'b%d' % d
                for c in range(4):
                    pu, pa = nextps(g), nextps(g)
                    mm_group(S, g.ps[pu][:, :nt], [(w2b[64 * d:64 * d + 64, c * 128:(c + 1) * 128], tw[64 * d:64 * d + 64, :nt])], ['w2b', 'tw'], [('ps', pu)])
                    mm_group(S, g.ps[pa][:, :nt], [(a2b[64 * d:64 * d + 64, c * 128:(c + 1) * 128], ab[64 * d:64 * d + 64, :nt])], ['a2b', 'ab'], [('ps', pa)])
                    wl = owl[d][0]
                    wk_ = ('owl', d, 0)
                    S.op('act', lambda e, pu=pu, c=c, d=d, wl=wl: e.activation(wl[:, c, :nt], g.ps[pu][:, :nt], AF.Sigmoid,
                         bias=rwp[:, P_W0 + d * 4 + c:P_W0 + d * 4 + c + 1], scale=1.0), [('ps', pu), 'rwp'], [wk_])
                    S.op('pool', lambda e, c=c, wl=wl: e.tensor_scalar(wl[:, c, :nt], wl[:, c, :nt], DECAY_C, 0.0, ALU.mult, ALU.add), [wk_], [wk_])
                    ta, tb = t1[0], t1[1]
                    S.op('act', lambda e, pa=pa, c=c, d=d: e.activation(ta[:, :nt], g.ps[pa][:, :nt], AF.Sigmoid,
                         bias=rwp[:, P_A0 + d * 4 + c:P_A0 + d * 4 + c + 1], scale=1.0), [('ps', pa), 'rwp'], [('t1', 0)])
                    S.op('pool', lambda e, c=c, bn=bn: e.tensor_tensor(o[bn][:, c, :nt], kkt[:, c, :nt], ta[:, :nt], ALU.mult),
                         [('kkt', c), ('t1', 0)], [okey[bn]])
                    S.op('dve', lambda e, c=c: e.tensor_scalar(tb[:, :nt], ta[:, :nt], rwp[:, P_KA + c:P_KA + c + 1], rwp[:, P_OMKA + c:P_OMKA + c + 1], ALU.mult, ALU.add),
                         [('t1', 0), 'rwp'], [('t1', 1)])
                    S.op('dve', lambda e, c=c: e.tensor_tensor(tb[:, :nt], tb[:, :nt], zc[:, 4 + c, :nt], ALU.mult), [('t1', 1), ('zc', 4 + c)], [('t1', 1)])
                    S.op('act', lambda e, c=c, kdn=kdn: e.copy(o[kdn][:, c, :nt], tb[:, :nt]), [('t1', 1)], [okey[kdn]])
                    if d == 0:
                        S.op('pool', lambda e, c=c: e.tensor_copy(kds[:, c, :nt], tb[:, :nt]), [('t1', 1)], [('kds', c)])
                    else:
                        S.op('pool', lambda e, c=c: e.tensor_tensor(kds[:, c, :nt], kds[:, c, :nt], tb[:, :nt], ALU.add), [('t1', 1), ('kds', c)], [('kds', c)])
                S.dma('sp', fmv(g.wlS[d])[:, :, t0:t0 + nt], owl[d][0][:, :, :nt], reads=[('owl', d, 0)], writes=['wlS%d' % d])
                S.dma('sp', fmv(g.kdS[d])[:, :, t0:t0 + nt], o[kdn][:, :, :nt], reads=[okey[kdn]], writes=['kdS%d' % d])
                S.dma('sp', fmv(g.bS[d])[:, :, t0:t0 + nt], o[bn][:, :, :nt], reads=[okey[bn]], writes=['bS%d' % d])
            for c in range(4):
                pg = nextps(g)
                mm_group(S, g.ps[pg][:, :nt], [(g2b[:, c * 128:(c + 1) * 128], sg[:, :nt])], ['g2b', 'sg'], [('ps', pg)])
                S.op('act', lambda e, pg=pg, c=c: e.copy(o['g'][:, c, :nt], g.ps[pg][:, :nt]), [('ps', pg)], [okey['g']])
            S.dma('sp', fmv(g.gS)[:, :, t0:t0 + nt], o['g'][:, :, :nt], reads=[okey['g']], writes=['gS'])
            for c in range(4):
                tc_ = t1[2]
                S.op('dve', lambda e, c=c: e.scalar_tensor_tensor(tc_[:, :nt], zc[:, c, :nt], rwp[:, P_RK + c:P_RK + c + 1], kds[:, c, :nt], ALU.mult, ALU.mult),
                     [('zc', c), 'rwp', ('kds', c)], [('t1', 2)])
                pi = nextps(g)
                mm_group(S, g.ps[pi][:, :nt], [(bdf[:], tc_[:, :nt])], ['bdf', ('t1', 2)], [('ps', pi)])
                S.op('dve', lambda e, pi=pi, c=c: e.tensor_tensor(o['bon'][:, c, :nt], g.ps[pi][:, :nt], zc[:, 8 + c, :nt], ALU.mult),
                     [('ps', pi), ('zc', 8 + c)], [okey['bon']])
            S.dma('sp', fmv(g.bonS)[:, :, t0:t0 + nt], o['bon'][:, :, :nt], reads=[okey['bon']], writes=['bonS'])
            for tt in range(nt // 128):
                pv = nextps(g)

                def fn(pe, pv=pv, tt=tt):
                    inst = None
                    for c in range(4):
                        inst = pe.transpose(g.ps[pv][:, c * 128:(c + 1) * 128], zc[:, 8 + c, tt * 128:(tt + 1) * 128], g.identf[:])
                    return inst
                S.op('pe', fn, [('zc', 8 + c) for c in range(4)] + ['identf'], [('ps', pv)])
                vb = (t0 // 128 + tt) % 2
                S.op('act', lambda e, pv=pv, vb=vb: e.copy(vtile[vb][:], g.ps[pv][:, :]), [('ps', pv)], [('vtile', vb)])
                S.dma('sp', g.vT[t0 + tt * 128:t0 + (tt + 1) * 128, :], vtile[vb][:], reads=[('vtile', vb)], writes=['vT'])


class TagS:
    GL = {'identb', 'identf', 'rwp', 'bdf', 'smf', 'epsv', 'rS', 'kkS', 'kdS0', 'kdS1', 'bS0', 'bS1', 'wlS0', 'wlS1', 'vT', 'yfS', 'ybS', 'bonS', 'gS', 'yrT'}

    def __init__(self, S, d):
        self.S, self.d = S, d

    def t(self, k):
        if isinstance(k, tuple) and k[0] in ('ps', 'msk', 'm4', 'mnt'):
            return k
        if isinstance(k, str) and (k in self.GL or k.startswith('dbg')):
            return k
        return ('dir%d' % self.d, k)

    def op(self, e, fn, reads=(), writes=()):
        self.S.op(e, fn, [self.t(k) for k in reads], [self.t(k) for k in writes])

    def dma(self, e, out, in_, reads=(), writes=(), **kw):
        self.S.dma(e, out, in_, reads=[self.t(k) for k in reads], writes=[self.t(k) for k in writes], **kw)


def phase_rwkv_scan(g, l):
    S, nc = g.S, g.nc
    import os
    STAGE = int(os.environ.get('SCAN_STAGE', '99'))
    NCK = T // 128
    with Phase(g) as A:
        smf = A('smf', [128, 9, 128])
        msk = [A('msk%d' % i, [128, 128], BF16) for i in range(4)]
        m4 = [A('m4_%d' % d, [128, 4, 128], BF16) for d in range(2)]
        mnt = [A('mnt%d' % d, [128, 128], BF16) for d in range(2)]
        bdf = A('bdf', [128, 128]); rwp = A('rwp', [128, P_OMKA])
        rwp = A('rwp', [128, P_OMKA])
        S.dma('sp', smf[:], g.smask[0:9].rearrange('m p c -> p m c'), writes=['smf'])
        for i in range(4):
            S.op('dve', lambda e, i=i: e.tensor_copy(msk[i][:], smf[:, 5 + i, :]), ['smf'], [('msk', i)])
        S.dma('sp', bdf[:], g.cmask[0], writes=['bdf'])
        S.dma('sp', rwp[:], g.rwp[l], writes=['rwp'])
        for d in range(2):
            for j in range(4):
                S.op('dve', lambda e, d=d, j=j: e.tensor_copy(m4[d][:, j, :], smf[:, 2 * d + (j % 2), :]), ['smf'], [('m4', d)])
        S.op('dve', lambda e: e.tensor_copy(mnt[0][:], smf[:, 2, :]), ['smf'], [('mnt', 0)])
        S.op('dve', lambda e: e.tensor_copy(mnt[1][:], smf[:, 0, :]), ['smf'], [('mnt', 1)])
        fmc = lambda ap, c0: ap.rearrange('(k p) t -> p k t', p=128)[:, :, c0:c0 + 128]
        f2 = lambda t: t[:].rearrange('p a b -> p (a b)')
        segb = smf[:, 4, :].unsqueeze(1).broadcast_to([128, 4, 128])
        it = [0]

        def scan_dir(d, A=A):
            S = TagS(g.S, d)
            A0 = A
            psc = [0]

            def nextps(g_):
                i = 4 * d + psc[0]
                psc[0] = (psc[0] + 1) % 4
                return i
            A = lambda name, shape, dt=F32: A0('%s_d%d' % (name, d), shape, dt)
            NTF = [A('NTF%d' % q, [128, 4, 128], BF16) for q in range(2)]
            NF = [A('NF%d' % q, [128, 4, 128], BF16) for q in range(2)]
            NoT = [[A('NoT%d_%d' % (q, i), [128, 4, 128], BF16) for i in range(3)] for q in range(2)]
            MT = [[A('MT%d_%d' % (q, i), [128, 4, 128], BF16) for i in range(2)] for q in range(2)]
            Wb = [A('Wb%d' % q, [128, 4, 128], BF16) for q in range(2)]
            ld = {n: [A('ld_%s%d' % (n, i), [128, 4, 128], BF16) for i in range(2)] for n in ('r', 'kk', 'kd', 'b')}
            cwl = [A('cwl%d' % i, [128, 4, 128]) for i in range(2)]
            cv = [A('cv%d' % i, [128, 512], BF16) for i in range(2)]
            cum = A('cum', [128, 4, 128]); cumx = A('cumx', [128, 4, 128]); ep = A('ep', [128, 4, 128]); en = A('en', [128, 4, 128])
            AR = A('AR', [128, 4, 256], BF16); kt = A('kt', [128, 4, 128], BF16); bt = A('bt', [128, 4, 128], BF16)
            gam = A('gam', [128, 4, 1])
            AtT = A('AtT', [128, 8, 64], BF16); BtT = A('BtT', [128, 8, 64], BF16); KtT = A('KtT', [128, 8, 64], BF16)
            AB = [A('AB%d' % h, [128, 4, 128], BF16) for h in range(8)]
            X = [[A('X%d_%d' % (q, i), [128, 4, 128], BF16) for i in range(2)] for q in range(2)]
            XT = [[A('XT%d_%d' % (q, i), [128, 4, 128], BF16) for i in range(2)] for q in range(2)]
            M = [[A('M%d_%d' % (q, i), [128, 4, 128], BF16) for i in range(2)] for q in range(2)]
            G2 = A('G2', [128, 8, 64], BF16); U = A('U', [128, 8, 64]); P1 = A('P1', [128, 4, 128]); Et = A('Et', [128, 8, 64], BF16)
            ST = A('ST', [128, 4, 64]); STb = [A('STb%d' % i, [128, 4, 64], BF16) for i in range(2)]; stt = A('stt', [128, 4, 64])
            ysb = [A('ysb%d' % i, [128, 4, 128]) for i in range(2)]
            order = list(range(NCK)) if d == 0 else [1, 0] + list(range(NCK - 1, 1, -1))
            if getattr(g, 'scan_limit', None):
                order = order[:g.scan_limit]
            S.op('pool', lambda e: e.memset(ST[:], 0.0), [], ['ST'])
            sbi = 0
            S.op('pool', lambda e: e.memset(STb[0][:], 0.0), [], [('STb', 0)])
            for ci in order:
                c0 = ci * 128
                b = it[0] % 2
                it[0] += 1
                cr, ckk, ckd, cb = ld['r'][b], ld['kk'][b], ld['kd'][b], ld['b'][b]
                lk = lambda n: ('ld', n, b)
                S.dma('sp', cr[:], fmc(g.rS, c0), reads=['rS'], writes=[lk('r')])
                S.dma('sp', ckk[:], fmc(g.kkS, c0), reads=['kkS'], writes=[lk('kk')])
                S.dma('sp', ckd[:], fmc(g.kdS[d], c0), reads=['kdS%d' % d], writes=[lk('kd')])
                S.dma('sp', cb[:], fmc(g.bS[d], c0), reads=['bS%d' % d], writes=[lk('b')])
                S.dma('sp', cwl[b][:], fmc(g.wlS[d], c0), reads=['wlS%d' % d], writes=[('cwl', b)])
                S.dma('sp', cv[b][:], g.vT[c0:c0 + 128, :], reads=['vT'], writes=[('cv', b)])
                vk = ('cv', b)
                cvb = cv[b]
                yield
                S.op('pool', lambda e: e.tensor_copy(cumx[:], segb), ['smf'], ['cumx'])
                S.op('dve', lambda e, b=b: e.tensor_tensor_scan(f2(cum), f2(cumx), f2(cwl[b]), 0.0, ALU.mult, ALU.add), ['cumx', ('cwl', b)], ['cum'])
                if d == 0:
                    S.op('dve', lambda e, b=b: e.tensor_tensor(cumx[:], cum[:], cwl[b][:], ALU.subtract), ['cum', ('cwl', b)], ['cumx'])
                else:
                    S.op('dve', lambda e: e.tensor_tensor(cumx[:], cum[:, :, 127:128].broadcast_to([128, 4, 128]), cum[:], ALU.subtract), ['cum'], ['cumx'])
                    S.op('dve', lambda e, b=b: e.tensor_tensor(cum[:], cumx[:], cwl[b][:], ALU.add), ['cumx', ('cwl', b)], ['cum'])
                S.op('act', lambda e: e.activation(ep[:], cum[:], AF.Exp), ['cum'], ['ep'])
                S.op('act', lambda e: e.activation(en[:], cum[:], AF.Exp, scale=-1.0), ['cum'], ['en'])
                gcol = 127 if d == 0 else 0
                S.op('act', lambda e: e.copy(gam[:], ep[:, :, gcol:gcol + 1]), ['ep'], ['gam'])
                S.op('dve', lambda e, cr=cr: e.tensor_tensor(AR[:, :, 128:256], cr[:], ep[:], ALU.mult), [lk('r'), 'ep'], ['AR'])
                S.op('act', lambda e: e.activation(ep[:], cumx[:], AF.Exp), ['cumx', 'AR', 'gam'], ['ep'])
                S.op('dve', lambda e, ckk=ckk: e.scalar_tensor_tensor(AR[:, :, 0:128], ckk[:], -1.0, ep[:], ALU.mult, ALU.mult), [lk('kk'), 'ep'], ['AR'])
                S.op('pool', lambda e, ckd=ckd: e.tensor_tensor(kt[:], ckd[:], en[:], ALU.mult), [lk('kd'), 'en'], ['kt'])
                S.op('pool', lambda e, cb=cb: e.tensor_tensor(bt[:], cb[:], en[:], ALU.mult), [lk('b'), 'en'], ['bt'])
                if STAGE <= 1:
                    continue
                yield
                for (srcf, dst, dk, sk) in ((lambda hp: AR[:, hp, 0:128], AtT, 'AtT', 'AR'), (lambda hp: bt[:, hp, :], BtT, 'BtT', 'bt'), (lambda hp: kt[:, hp, :], KtT, 'KtT', 'kt')):
                    pi = nextps(g)
                    psb = g.ps[pi][:, :].bitcast(BF16)

                    def fn(pe, srcf=srcf, psb=psb):
                        inst = None
                        for hp in range(4):
                            inst = pe.transpose(psb[:, hp * 128:(hp + 1) * 128], srcf(hp), g.identb[:])
                        return inst
                    S.op('pe', fn, [sk, 'identb'], [('ps', pi)])
                    S.op('act', lambda e, dst=dst, psb=psb: e.copy(dst[:].rearrange('p h j -> p (h j)'), psb[:, 0:512]), [('ps', pi)], [dk])
                if STAGE <= 2:
                    continue
                yield
                ABH = int(os.environ.get('AB_H', '8')); ABM = int(os.environ.get('AB_MODE', '9'))
                for h in range(ABH):
                    hp, hb = h // 2, h % 2
                    sl = slice(hb * 64, hb * 64 + 64)
                    pi = nextps(g)
                    mm_group(S, g.ps[pi][:, 0:256], [(bt[sl, hp, :], AR[sl, hp, :])], ['bt', 'AR'], [('ps', pi)])
                    if ABM <= 1:
                        continue
                    mm_group(S, g.ps[pi][:, 256:512], [(kt[sl, hp, :], AR[sl, hp, :])], ['kt', 'AR'], [('ps', pi)])
                    if ABM <= 2:
                        continue
                    S.op('dve', lambda e, h=h, pi=pi: e.tensor_tensor(f2(AB[h]), g.ps[pi][:, :], f2(m4[d]), ALU.mult), [('ps', pi), ('m4', d)], [('AB', h)])
                SUB = int(os.environ.get('SCAN_SUB', '9'))
                if SUB <= 0:
                    continue
                b4 = lambda t: t[:].unsqueeze(1).broadcast_to([128, 4, 128])
                for q in range(2):
                    pi = nextps(g)
                    for j in range(4):
                        h = 2 * j + q
                        hp, hb = j, q
                        sl = slice(hb * 64, hb * 64 + 64)
                        mm_group(S, g.ps[pi][:, j * 128:(j + 1) * 128], [(AR[sl, hp, 0:128], bt[sl, hp, :])], ['bt', 'AR'], [('ps', pi)])
                    S.op('dve', lambda e, q=q, pi=pi: e.tensor_tensor(NTF[q][:], g.ps[pi][:, :].rearrange('p (j s) -> p j s', j=4), b4(mnt[d]), ALU.mult),
                         [('ps', pi), ('mnt', d)], [('NTF', q)])
                    for j in range(4):
                        h = 2 * j + q
                        S.op('pool', lambda e, q=q, j=j, h=h: e.tensor_copy(NF[q][:, j, :], AB[h][:, 0, :]), [('AB', h)], [('NF', q)])
                    S.op('pool', lambda e, q=q: e.tensor_tensor(X[q][0][:], NF[q][:], b4(msk[0]), ALU.mult), [('NF', q), ('msk', 0)], [('X', q, 0)])
                    S.op('dve', lambda e, q=q: e.tensor_tensor(XT[q][0][:], NTF[q][:], b4(msk[0]), ALU.mult), [('NTF', q), ('msk', 0)], [('XT', q, 0)])
                    S.op('pool', lambda e, q=q: e.tensor_tensor(M[q][0][:], X[q][0][:], b4(g.identb), ALU.add), [('X', q, 0), 'identb'], [('M', q, 0)])
                    S.op('pool', lambda e, q=q: e.tensor_tensor(MT[q][0][:], XT[q][0][:], b4(g.identb), ALU.add), [('XT', q, 0), 'identb'], [('MT', q, 0)])
                    for i in range(3):
                        S.op('pool', lambda e, q=q, i=i: e.tensor_tensor(NoT[q][i][:], NTF[q][:], b4(msk[1 + i]), ALU.mult), [('NTF', q), ('msk', 1 + i)], [('NoT', q, i)])
                if d == 0 and ci == 0:
                    for nm, tl, kk_ in (('d_XT0', NTF[0], ('NTF', 0)), ('d_X0', NF[0], ('NF', 0)), ('d_Mi', M[0][0], ('M', 0, 0))):
                        if nm in g.dbg:
                            S.dma('sp', g.dbg[nm][:, :], tl[:].rearrange('p a b -> p (a b)'), reads=[kk_], writes=['dbg' + nm])
                yield
                def mm4(q, lh, rh, rk):
                    pi = nextps(g)
                    for j in range(4):
                        mm_group(S, g.ps[pi][:, j * 128:(j + 1) * 128], [(lh[:, j, :], rh[:, j, :])], rk, [('ps', pi)])
                    return pi
                cur = 0
                for k in range(1, 4):
                    nx = 1 - cur
                    for q in range(2):
                        Xp, XTp = X[q][cur], XT[q][cur]
                        Xn, XTn = X[q][nx], XT[q][nx]
                        kX, kXT = ('X', q, cur), ('XT', q, cur)
                        nX, nXT = ('X', q, nx), ('XT', q, nx)
                        pi = mm4(q, Xp, XTp, [kX, kXT])
                        S.op('act', lambda e, XTn=XTn, pi=pi: e.copy(f2(XTn), g.ps[pi][:, :]), [('ps', pi)], [nXT])
                        if k < 3:
                            pi = mm4(q, XTp, Xp, [kX, kXT])
                            S.op('act', lambda e, Xn=Xn, pi=pi: e.copy(f2(Xn), g.ps[pi][:, :]), [('ps', pi)], [nX])
                    yield
                    for q in range(2):
                        Mp, MTp, Mn, MTn, XTn = M[q][cur], MT[q][cur], M[q][nx], MT[q][nx], XT[q][nx]
                        kM, kMT, nM, nMT, nXT = ('M', q, cur), ('MT', q, cur), ('M', q, nx), ('MT', q, nx), ('XT', q, nx)
                        pi = mm4(q, XTn, Mp, [nXT, kM])
                        S.op('dve', lambda e, Mn=Mn, Mp=Mp, pi=pi: e.tensor_tensor(f2(Mn), g.ps[pi][:, :], f2(Mp), ALU.add), [('ps', pi), kM], [nM])
                        pi = mm4(q, Mp, XTn, [nXT, kM])
                        S.op('dve', lambda e, MTn=MTn, MTp=MTp, pi=pi: e.tensor_tensor(f2(MTn), g.ps[pi][:, :], f2(MTp), ALU.add), [('ps', pi), kMT], [nMT])
                    cur = nx
                    yield
                for i in range(3):
                    nx = 1 - cur
                    for q in range(2):
                        pi = mm4(q, NoT[q][i], M[q][cur], [('NoT', q, i), ('M', q, cur)])
                        S.op('act', lambda e, q=q, pi=pi: e.copy(f2(Wb[q]), g.ps[pi][:, :]), [('ps', pi)], [('Wb', q)])
                    yield
                    for q in range(2):
                        Dp, DTp, Dn, DTn = M[q][cur], MT[q][cur], M[q][nx], MT[q][nx]
                        kD, kDT, nD, nDT = ('M', q, cur), ('MT', q, cur), ('M', q, nx), ('MT', q, nx)
                        pi = mm4(q, DTp, Wb[q], [kDT, ('Wb', q)])
                        S.op('dve', lambda e, Dn=Dn, Dp=Dp, pi=pi: e.tensor_tensor(f2(Dn), g.ps[pi][:, :], f2(Dp), ALU.add), [('ps', pi), kD], [nD])
                        if i < 2:
                            pi = mm4(q, Wb[q], DTp, [kDT, ('Wb', q)])
                            S.op('dve', lambda e, DTn=DTn, DTp=DTp, pi=pi: e.tensor_tensor(f2(DTn), g.ps[pi][:, :], f2(DTp), ALU.add), [('ps', pi), kDT], [nDT])
                    cur = nx
                    yield
                Mf = [M[q][cur] for q in range(2)]
                kMf = [('M', q, cur) for q in range(2)]
                if STAGE <= 4:
                    continue
                yield
                pi = nextps(g)
                for h in range(8):
                    mm_group(S, g.ps[pi][:, h * 64:(h + 1) * 64], [(AB[h][:, 2, :], cvb[:, h * 64:(h + 1) * 64])], [('AB', h), vk], [('ps', pi)])
                S.op('act', lambda e, pi=pi: e.copy(G2[:].rearrange('p h i -> p (h i)'), g.ps[pi][:, :]), [('ps', pi)], ['G2'])
                pi = nextps(g)
                for h in range(8):
                    mm_group(S, g.ps[pi][:, h * 64:(h + 1) * 64], [(Mf[h % 2][:, h // 2, :], G2[:, h, :])], [kMf[h % 2], 'G2'], [('ps', pi)])
                S.op('act', lambda e, pi=pi: e.copy(U[:].rearrange('p h i -> p (h i)'), g.ps[pi][:, :]), [('ps', pi)], ['U'])
                pi = nextps(g)
                for h in range(8):
                    hp, hb = h // 2, h % 2
                    mm_group(S, g.ps[pi][hb * 64:hb * 64 + 64, hp * 128:(hp + 1) * 128], [(AtT[:, h, :], Mf[h % 2][:, h // 2, :])], [kMf[h % 2], 'AtT'], [('ps', pi)])
                S.op('dve', lambda e, pi=pi: e.tensor_copy(f2(P1), g.ps[pi][:, :]), [('ps', pi)], ['P1'])
                if STAGE <= 5:
                    continue
                yield
                for hb in range(2):
                    pe_ = nextps(g)
                    sl = slice(hb * 64, hb * 64 + 64)
                    for hp in range(4):
                        mm_group(S, g.ps[pe_][:, hp * 64:(hp + 1) * 64], [(P1[sl, hp, :], ST[sl, hp, :])], ['P1', 'ST'], [('ps', pe_)])
                    S.op('dve', lambda e, pe_=pe_, hb=hb: e.tensor_tensor(Et[:].rearrange('p (hp hb) i -> p hp hb i', hb=2)[:, :, hb, :],
                         g.ps[pe_][:, 0:256].rearrange('p (hp i) -> p hp i', hp=4), U[:].rearrange('p (hp hb) i -> p hp hb i', hb=2)[:, :, hb, :], ALU.add),
                         [('ps', pe_), 'U'], ['Et'])
                if STAGE <= 6:
                    continue
                py2, pss = [nextps(g), nextps(g)], nextps(g)
                for h in range(8):
                    hp, hb = h // 2, h % 2
                    sl = slice(hb * 64, hb * 64 + 64)
                    mm_group(S, g.ps[pss][sl, hp * 64:(hp + 1) * 64], [(KtT[:, h, :], cvb[:, h * 64:(h + 1) * 64]), (BtT[:, h, :], Et[:, h, :])],
                             ['KtT', 'BtT', 'Et', vk], [('ps', pss)])
                for h in range(8):
                    hp, hb = h // 2, h % 2
                    sl = slice(hb * 64, hb * 64 + 64)
                    mm_group(S, g.ps[py2[hb]][sl, hp * 128:(hp + 1) * 128],
                             [(STb[sbi][sl, hp, :], AR[sl, hp, 128:256]), (Et[:, h, :], AB[h][:, 1, :]), (cvb[:, h * 64:(h + 1) * 64], AB[h][:, 3, :])],
                             [('STb', sbi), 'AR', 'Et', ('AB', h), vk], [('ps', py2[hb])])
                S.op('dve', lambda e, pss=pss: e.tensor_tensor(stt[:].rearrange('p a i -> p (a i)'), g.ps[pss][:, 0:256], ST[:].rearrange('p a i -> p (a i)'), ALU.add),
                     [('ps', pss), 'ST'], ['stt'])
                S.op('dve', lambda e: e.tensor_tensor(ST[:], stt[:], gam[:].broadcast_to([128, 4, 64]), ALU.mult), ['stt', 'gam'], ['ST'])
                sbi = 1 - sbi
                S.op('act', lambda e, sbi=sbi: e.copy(STb[sbi][:], ST[:]), ['ST'], [('STb', sbi)])
                if STAGE <= 7:
                    continue
                if d == 0 and ci == 0:
                    dm = {'d_AR': AR, 'd_AB0': AB[0], 'd_AB1': AB[1], 'd_M0': Mf[0], 'd_U': U, 'd_P1': P1, 'd_Et': Et, 'd_ST': ST, 'd_kt': kt, 'd_bt': bt,
                          'd_AtT': AtT, 'd_G2': G2}
                    for nm, tl in dm.items():
                        if nm in g.dbg:
                            S.dma('sp', g.dbg[nm][:, :], tl[:].rearrange('p a b -> p (a b)'), reads=['AR', ('AB', 0), ('AB', 1), kMf[0], 'U', 'P1', 'Et', 'ST', 'kt', 'bt', 'AtT', 'G2'], writes=['dbg' + nm])
                yb = ysb[ci % 2]
                for hb in range(2):
                    S.op('act', lambda e, yb=yb, hb=hb: e.copy(f2(yb)[hb * 64:hb * 64 + 64, :], g.ps[py2[hb]][hb * 64:hb * 64 + 64, :]), [('ps', py2[hb])], [('ysb', ci % 2)])
                S.dma('sp', fmc(g.yfS if d == 0 else g.ybS, c0), yb[:], reads=[('ysb', ci % 2)], writes=['yfS' if d == 0 else 'ybS'])
                yield

        gens = [scan_dir(0), scan_dir(1)]
        while gens:
            for gen in list(gens):
                try:
                    next(gen)
                except StopIteration:
                    gens.remove(gen)
    with Phase(g) as A:
        bdf = A('bdf2', [128, 128]); rwp = A('rwp2', [128, P_OMKA])
        S.dma('sp', bdf[:], g.cmask[0], writes=['bdf'])
        S.dma('sp', rwp[:], g.rwp[l], writes=['rwp'])
        fmc = lambda ap, c0: ap.rearrange('(k p) t -> p k t', p=128)[:, :, c0:c0 + 128]
        f2 = lambda t: t[:].rearrange('p a b -> p (a b)')
        yfb = [A('yf%d' % i, [128, 4, 128]) for i in range(2)]; ybb = [A('yb%d' % i, [128, 4, 128]) for i in range(2)]
        cbonb = [A('cbon%d' % i, [128, 4, 128], BF16) for i in range(2)]; cgb = [A('cg%d' % i, [128, 4, 128], BF16) for i in range(2)]
        ys = A('ys', [128, 4, 128]); ysq = A('ysq', [128, 4, 128]); gmean = A('gmean', [128, 4, 128]); grs = A('grs', [128, 4, 128]); gt_ = A('gt_', [128, 4, 128])
        yob = [A('yob%d' % i, [128, 4, 128], BF16) for i in range(2)]
        for ci in range(NCK):
            if l == DEPTH - 1 and ci < CTX // 128:
                continue
            c0 = ci * 128
            b = ci % 2
            yf, yb2, cbon, cg = yfb[b], ybb[b], cbonb[b], cgb[b]
            S.dma('sp', yf[:], fmc(g.yfS, c0), reads=['yfS'], writes=[('yf', b)])
            S.dma('sp', yb2[:], fmc(g.ybS, c0), reads=['ybS'], writes=[('yb', b)])
            S.dma('sp', cbon[:], fmc(g.bonS, c0), reads=['bonS'], writes=[('cbon', b)])
            S.dma('sp', cg[:], fmc(g.gS, c0), reads=['gS'], writes=[('cg', b)])
            S.op('dve', lambda e, yf=yf, yb2=yb2: e.tensor_tensor(ys[:], yf[:], yb2[:], ALU.add), [('yf', b), ('yb', b)], ['ys'])
            S.op('act', lambda e: e.activation(ysq[:], ys[:], AF.Square), ['ys'], ['ysq'])
            p1, p2 = nextps(g), nextps(g)
            mm_group(S, g.ps[p1][:, :], [(bdf[:], f2(ys))], ['bdf', 'ys'], [('ps', p1)])
            mm_group(S, g.ps[p2][:, :], [(bdf[:], f2(ysq))], ['bdf', 'ysq'], [('ps', p2)])
            S.op('act', lambda e, p1=p1: e.activation(f2(gmean), g.ps[p1][:, :], AF.Identity, scale=1.0 / 64), [('ps', p1)], ['gmean'])
            S.op('pool', lambda e: e.tensor_tensor(ysq[:], gmean[:], gmean[:], ALU.mult), ['gmean'], ['ysq'])
            S.op('dve', lambda e, p2=p2: e.scalar_tensor_tensor(f2(grs), g.ps[p2][:, :], 1.0 / 64, f2(ysq), ALU.mult, ALU.subtract), [('ps', p2), 'ysq'], ['grs'])
            S.op('act', lambda e: e.activation(grs[:], grs[:], AF.Sqrt, bias=g.epsv[:, 3:4], scale=1.0), ['grs', 'epsv'], ['grs'])
            S.op('dve', lambda e: e.reciprocal(grs[:], grs[:]), ['grs'], ['grs'])
            S.op('pool', lambda e: e.tensor_tensor(gt_[:], ys[:], gmean[:], ALU.subtract), ['ys', 'gmean'], ['gt_'])
            S.op('dve', lambda e: e.tensor_tensor(gt_[:], gt_[:], grs[:], ALU.mult), ['gt_', 'grs'], ['gt_'])
            S.op('pool', lambda e: e.tensor_tensor(gt_[:], gt_[:], rwp[:, P_GG:P_GG + 4].unsqueeze(2).broadcast_to([128, 4, 128]), ALU.mult), ['gt_', 'rwp'], ['gt_'])
            S.op('pool', lambda e: e.tensor_tensor(gt_[:], gt_[:], rwp[:, P_GB:P_GB + 4].unsqueeze(2).broadcast_to([128, 4, 128]), ALU.add), ['gt_', 'rwp'], ['gt_'])
            S.op('dve', lambda e, cbon=cbon: e.tensor_tensor(gt_[:], gt_[:], cbon[:], ALU.add), ['gt_', ('cbon', b)], ['gt_'])
            yo = yob[b]
            S.op('pool', lambda e, yo=yo, cg=cg: e.tensor_tensor(yo[:], gt_[:], cg[:], ALU.mult), ['gt_', ('cg', b)], [('yob', b)])
            S.dma('sp', fmc(g.yrT, c0), yo[:], reads=[('yob', b)], writes=['yrT'])


def phase_merge(g, l):
    S, nc = g.S, g.nc
    last = (l == DEPTH - 1)
    with Phase(g) as A:
        wg = A('wg', [128, NCH, 3072], BF16)
        wp = [A('wp%d' % i, [128, 4, 1024], BF16) for i in range(3)]
        wo = A('wo', [128, NCH, 1024], BF16)
        HL = HLoader(g, A)
        yb = [[A('my%d_%d' % (i, j), [128, 4, 512], BF16) for j in range(2)] for i in range(3)]
        xb = [A('mx%d' % i, [128, NCH, 512]) for i in range(2)]
        sig = [A('msig%d' % i, [128, 512]) for i in range(2)]
        tm = [A('mtm%d' % i, [128, 512]) for i in range(2)]
        macc = A('macc', [128, 512])
        mT = A('mT', [128, NCH, 512], BF16)
        wsrc = g.w_in[l].rearrange('(k p) c -> p k c', p=128)
        for i in range(2):
            S.dma('pool', wg[:, :, i * 1536:(i + 1) * 1536], wsrc[:, :, 4480 + i * 1536:4480 + (i + 1) * 1536], writes=['wg'])
        for i, nm in enumerate(('p_conv', 'p_att', 'p_rwkv')):
            S.dma('pool', wp[i][:], getattr(g, nm)[l].rearrange('(k p) c -> p k c', p=128), writes=[('wp', i)])
        S.dma('pool', wo[:], g.w_out[l].rearrange('(k p) c -> p k c', p=128), writes=['wo'])
        xTv = g.xT.rearrange('(k p) t -> p k t', p=128)
        ysrc = [g.ycT.rearrange('(k p) t -> p k t', p=128), g.yaT.rearrange('(k p) t -> p k t', p=128), g.yrT.rearrange('(k p) t -> p k t', p=128)]
        ynm = ['ycT', 'yaT', 'yrT']
        for bi, (t0, nt) in enumerate(BLKS):
            if last and t0 < CTX:
                continue
            b = bi % 2
            s = 1 if t0 < CTX else 0
            hb, hk = HL.load(bi)
            for i in range(3):
                S.dma('sp', yb[i][b][:, :, :nt], ysrc[i][:, :, t0:t0 + nt], reads=[ynm[i]], writes=[('my', i, b)])
            S.dma('sp', xb[b][:, :, :nt], xTv[:, :, t0:t0 + nt], reads=['xT'], writes=[('mx', b)])
            for oc in range(NCH):
                for i in range(3):
                    pg, pp = nextps(g), nextps(g)
                    c0 = i * 1024 + oc * 128
                    mm_group(S, g.ps[pg][:, :nt], [(wg[:, k, c0:c0 + 128], hb[:, k, :nt]) for k in range(NCH)], ['wg', hk], [('ps', pg)])
                    mm_group(S, g.ps[pp][:, :nt], [(wp[i][:, k, oc * 128:(oc + 1) * 128], yb[i][b][:, k, :nt]) for k in range(4)], [('wp', i), ('my', i, b)], [('ps', pp)])
                    sb_ = i % 2
                    S.op('act', lambda e, pg=pg, sb_=sb_: e.activation(sig[sb_][:, :nt], g.ps[pg][:, :nt], AF.Sigmoid), [('ps', pg)], [('msig', sb_)])
                    if i == 0:
                        S.op('dve', lambda e, pp=pp, sb_=sb_: e.tensor_tensor(macc[:, :nt], g.ps[pp][:, :nt], sig[sb_][:, :nt], ALU.mult), [('ps', pp), ('msig', sb_)], ['macc'])
                    else:
                        S.op('dve', lambda e, pp=pp, sb_=sb_: e.tensor_tensor(tm[sb_][:, :nt], g.ps[pp][:, :nt], sig[sb_][:, :nt], ALU.mult), [('ps', pp), ('msig', sb_)], [('mtm', sb_)])
                        if i == 1:
                            S.op('pool', lambda e, sb_=sb_: e.tensor_tensor(macc[:, :nt], macc[:, :nt], tm[sb_][:, :nt], ALU.add), ['macc', ('mtm', sb_)], ['macc'])
                        else:
                            S.op('pool', lambda e, sb_=sb_, oc=oc: e.tensor_tensor(mT[:, oc, :nt], macc[:, :nt], tm[sb_][:, :nt], ALU.add), ['macc', ('mtm', sb_)], [('mT', oc)])
            for oc in range(NCH):
                po = nextps(g)
                mm_group(S, g.ps[po][:, :nt], [(wo[:, k, oc * 128:(oc + 1) * 128], mT[:, k, :nt]) for k in range(NCH)], ['wo'] + [('mT', k) for k in range(NCH)], [('ps', po)])
                S.op('dve', lambda e, po=po, oc=oc: e.scalar_tensor_tensor(xb[b][:, oc, :nt], g.ps[po][:, :nt], g.modv[:, 2 * 8 + oc, s:s + 1], xb[b][:, oc, :nt], ALU.mult, ALU.add),
                     [('ps', po), 'modv', ('mx', b)], [('mx', b)])
            S.dma('sp', xTv[:, :, t0:t0 + nt], xb[b][:, :, :nt], reads=[('mx', b)], writes=['xT'])


def phase_mlp(g, l):
    S, nc = g.S, g.nc
    last = (l == DEPTH - 1)
    with Phase(g) as A:
        w1 = A('w1', [128, NCH, DFF], BF16)
        w2 = A('w2', [128, DFF // 128, D], BF16)
        xb = [A('fx%d' % i, [128, NCH, 512]) for i in range(1)] * 2
        rs = A('frs', [128, 512]); tmp = [A('ftmp%d' % i, [128, 512]) for i in range(2)]
        h2 = A('fh2', [128, NCH, 512], BF16)
        act = A('fact', [128, DFF // 128, 512], BF16)
        sq = act[:, 0:16, :].bitcast(F32).rearrange('p (a two) b -> p a (two b)', two=2)
        fg = A('fg', [128, NCH])
        osb = A('fosb', [128, D])
        w1src = g.mlp_w1[l].rearrange('(k p) c -> p k c', p=128)
        for i in range(4):
            S.dma('pool', w1[:, :, i * 1024:(i + 1) * 1024], w1src[:, :, i * 1024:(i + 1) * 1024], writes=['w1'])
        w2src = g.mlp_w2[l].rearrange('(k p) c -> p k c', p=128)
        for i in range(4):
            S.dma('pool', w2[:, i * 8:(i + 1) * 8, :], w2src[:, i * 8:(i + 1) * 8, :], writes=['w2'])
        S.dma('sp', fg[:], g.fing[:, :], writes=['fg'])
        xTv = g.xT.rearrange('(k p) t -> p k t', p=128)
        for bi, (t0, nt) in enumerate(BLKS):
            if last and t0 < CTX:
                continue
            b = bi % 2
            s = 1 if t0 < CTX else 0
            x = xb[b]
            xk = ('fx', 0)
            S.dma('sp', x[:, :, :nt], xTv[:, :, t0:t0 + nt], reads=['xT'], writes=[xk])
            rms_stats(g, x, nt, sq, rs, xk, 'fact', 'frs')
            for k in range(NCH):
                tb = k % 2
                S.op('dve', lambda e, k=k, tb=tb: e.tensor_tensor(tmp[tb][:, :nt], x[:, k, :nt], rs[:, :nt], ALU.mult), [xk, 'frs'], [('ftmp', tb)])
                S.op('act', lambda e, k=k, tb=tb: e.activation(h2[:, k, :nt], tmp[tb][:, :nt], AF.Identity,
                     bias=g.modv[:, 3 * 8 + k, s:s + 1], scale=g.gs[:, 1, k, s:s + 1]), [('ftmp', tb), 'modv', 'gs'], ['fh2'])
            for fc in range(DFF // 128):
                pf = nextps(g)
                tb = fc % 2
                mm_group(S, g.ps[pf][:, :nt], [(w1[:, k, fc * 128:(fc + 1) * 128], h2[:, k, :nt]) for k in range(NCH)], ['w1', 'fh2'], [('ps', pf)])
                S.op('dve', lambda e, pf=pf, tb=tb: e.tensor_scalar(tmp[tb][:, :nt], g.ps[pf][:, :nt], 0.0, None, ALU.max), [('ps', pf)], [('ftmp', tb)])
                S.op('act', lambda e, fc=fc, tb=tb: e.activation(act[:, fc, :nt], tmp[tb][:, :nt], AF.Square), [('ftmp', tb)], ['fact'])
            for oc in range(NCH):
                po = nextps(g)
                mm_group(S, g.ps[po][:, :nt], [(w2[:, fc, oc * 128:(oc + 1) * 128], act[:, fc, :nt]) for fc in range(DFF // 128)],
                         ['w2', 'fact'], [('ps', po)])
                S.op('dve', lambda e, po=po, oc=oc: e.scalar_tensor_tensor(x[:, oc, :nt], g.ps[po][:, :nt], g.modv[:, 5 * 8 + oc, s:s + 1], x[:, oc, :nt], ALU.mult, ALU.add),
                     [('ps', po), 'modv', xk], [xk])
            if not last:
                S.dma('sp', xTv[:, :, t0:t0 + nt], x[:, :, :nt], reads=[xk], writes=['xT'])
                continue
            rms_stats(g, x, nt, sq, rs, xk, 'fact', 'frs')
            for k in range(NCH):
                S.op('dve', lambda e, k=k: e.tensor_tensor(sq[:, k, :nt], x[:, k, :nt], rs[:, :nt], ALU.mult), [xk, 'frs', 'fact'], ['fact'])
                S.op('act', lambda e, k=k: e.activation(sq[:, k, :nt], sq[:, k, :nt], AF.Identity, scale=fg[:, k:k + 1]), ['fact', 'fg'], ['fact'])
            for tt in range(nt // 128):
                for half in range(2):
                    pi = nextps(g)

                    def fn(pe, pi=pi, half=half, tt=tt):
                        inst = None
                        for j in range(4):
                            inst = pe.transpose(g.ps[pi][:, j * 128:(j + 1) * 128], sq[:, half * 4 + j, tt * 128:(tt + 1) * 128], g.identf[:])
                        return inst
                    S.op('pe', fn, ['fact', 'identf'], [('ps', pi)])
                    S.op('act', lambda e, pi=pi, half=half: e.copy(osb[:, half * 512:(half + 1) * 512], g.ps[pi][:, :]), [('ps', pi)], ['fosb'])
                r0 = t0 - CTX + tt * 128
                S.dma('sp', g.out[r0:r0 + 128, :], osb[:], reads=['fosb'], writes=['out'])


_CACHE = {}


def kernel(**inputs):
    inp = {k: np.asarray(v) for k, v in inputs.items()}
    if 'nc' not in _CACHE:
        _CACHE['nc'] = build()
    nc = _CACHE['nc']
    shared = host_shared(inp)
    B = inp['x'].shape[0]
    in_maps = [host_inputs(inp, b, shared) for b in range(B)]
    res = run_bass_kernel_spmd(nc, in_maps, core_ids=list(range(B)))
    return np.stack([np.asarray(res.results[b]['out']) for b in range(B)]).astype(np.float32)
```

```python
import math
import numpy as np
import concourse.bass as bass
import concourse.mybir as mybir
from concourse.bass_utils import run_bass_kernel_spmd

F32 = mybir.dt.float32
BF16 = mybir.dt.bfloat16
AF = mybir.ActivationFunctionType
ALU = mybir.AluOpType
AX = mybir.AxisListType

D = 1024
SEQ = 4096
CTX = 256
T = SEQ + CTX
DEPTH = 2
DIN = 7552
DFF = 4096
NCH = D // 128
BLKS = [(0, 256)] + [(256 + 512 * i, 512) for i in range(8)]
NORM_EPS = 1e-6
LN_EPS = 1e-5
SUBLN_EPS = 1e-5
GN_EPS = 64e-5


class Sched:
    def __init__(self, nc, n_dma=24):
        self.nc = nc
        self.eng = dict(pe=nc.tensor, dve=nc.vector, act=nc.scalar, pool=nc.gpsimd, sp=nc.sync)
        self.sem = {e: nc.alloc_semaphore('sem_' + e) for e in self.eng}
        self.cnt = {e: 0 for e in self.eng}
        self.dsem = [nc.alloc_semaphore('dsem%d' % i) for i in range(n_dma)]
        self.dval = [0] * n_dma
        self.drr = 0
        self.seen = {e: {} for e in self.eng}
        self.lastw = {}
        self.readers = {}
        self.nps = 0

    def _semh(self, key):
        return self.sem[key] if isinstance(key, str) else self.dsem[key]

    def _wait(self, e, key, val):
        if self.seen[e].get(key, 0) >= val:
            return
        self.eng[e].wait_ge(self._semh(key), val)
        self.seen[e][key] = val

    def _deps(self, e, reads, writes):
        need = {}
        for r in reads:
            tok = self.lastw.get(r)
            if tok is not None:
                need[tok[0]] = max(need.get(tok[0], 0), tok[1])
        for w in writes:
            tok = self.lastw.get(w)
            if tok is not None:
                need[tok[0]] = max(need.get(tok[0], 0), tok[1])
            for k, v in self.readers.get(w, {}).items():
                need[k] = max(need.get(k, 0), v)
        for k, v in need.items():
            self._wait(e, k, v)

    def _commit(self, tok, reads, writes):
        for w in writes:
            self.lastw[w] = tok
            self.readers[w] = {}
        for r in reads:
            if r in writes:
                continue
            d = self.readers.setdefault(r, {})
            d[tok[0]] = max(d.get(tok[0], 0), tok[1])

    def op(self, e, fn, reads=(), writes=()):
        if e == 'pe':
            self.seen[e]['pe'] = self.cnt['pe']
        self._deps(e, reads, writes)
        inst = fn(self.eng[e])
        self.cnt[e] += 1
        inst.then_inc(self.sem[e], 1)
        self._commit((e, self.cnt[e]), reads, writes)

    def dma(self, e, out, in_, reads=(), writes=(), **kw):
        self._deps(e, reads, writes)
        i = self.drr
        self.drr = (self.drr + 1) % len(self.dsem)
        if self.dval[i] > 0:
            self._wait(e, i, self.dval[i])
        self.dval[i] += 16
        self.eng[e].dma_start(out=out, in_=in_, **kw).then_inc(self.dsem[i], 16)
        self._commit((i, self.dval[i]), reads, writes)

    def barrier(self):
        for e in self.eng:
            for i, v in enumerate(self.dval):
                if v > 0:
                    self._wait(e, i, v)
            for k in self.eng:
                if k != e and self.cnt[k] > 0:
                    self._wait(e, k, self.cnt[k])

    def finish(self, e='sp'):
        for i, v in enumerate(self.dval):
            if v > 0:
                self._wait(e, i, v)
        for k in self.eng:
            if k != e and self.cnt[k] > 0:
                self._wait(e, k, self.cnt[k])


class Ctx:
    pass


class Phase:
    uid = 0

    def __init__(self, g):
        self.g = g
        self.guards = []

    def __enter__(self):
        return self

    def __call__(self, name, shape, dt=F32):
        Phase.uid += 1
        gd = self.g.nc.sbuf_tensor('%s_u%d' % (name, Phase.uid), list(shape), dt)
        t = gd.__enter__()
        self.guards.append(gd)
        return t

    def __exit__(self, *a):
        self.g.S.barrier()
        for gd in reversed(self.guards):
            gd.__exit__(None, None, None)
        return False


def mm_group(S, out, pairs, reads, writes, start=True, stop=True):
    n = len(pairs)

    def fn(pe):
        inst = None
        for i, (l, r) in enumerate(pairs):
            inst = pe.matmul(out, l, r, start=(start and i == 0), stop=(stop and i == n - 1))
        return inst
    S.op('pe', fn, reads, writes)


def build(dbg=(), upto='all'):
    nc = bass.Bass("TRN2", target_bir_lowering=False)
    S = Sched(nc)
    g = Ctx()
    g.nc, g.S = nc, S
    din = lambda name, shape, dt=F32: nc.dram_tensor(name, list(shape), dt, kind="ExternalInput").ap()
    dint = lambda name, shape, dt=F32: nc.dram_tensor(name, list(shape), dt, kind="Internal").ap()
    g.x = din('x', [SEQ, D]); g.ctx = din('ctx', [CTX, D])
    g.cvec = din('cvec', [128, NCH, 2])
    g.mod_w = din('mod_w', [DEPTH, D, 6 * D]); g.modb = din('modb', [DEPTH, 128, 48])
    g.n1g = din('n1g', [DEPTH, 128, NCH]); g.n2g = din('n2g', [DEPTH, 128, NCH]); g.fing = din('fing', [128, NCH])
    g.ident = din('ident', [128, 128])
    g.w_in = din('w_in', [DEPTH, D, DIN])
    g.convw = din('convw', [DEPTH, 128, 4, 31]); g.convp = din('convp', [DEPTH, 128, 3, 4])
    g.wqks = din('wqks', [DEPTH, D, 1024]); g.rope = din('rope', [2, 128, SEQ]); g.cmask = din('cmask', [3, 128, 128])
    g.attp = din('attp', [DEPTH, 4, 64]); g.subg = din('subg', [DEPTH, 128])
    g.rw_w2 = din('rw_w2', [DEPTH, 2, 64, 512]); g.rw_a2 = din('rw_a2', [DEPTH, 2, 64, 512]); g.rw_g2 = din('rw_g2', [DEPTH, 128, 512])
    g.rwp = din('rwp', [DEPTH, 128, P_OMKA]); g.smask = din('smask', [9, 128, 128])
    g.rS = dint('rS', [512, T], BF16); g.kkS = dint('kkS', [512, T], BF16); g.gS = dint('gS', [512, T], BF16); g.bonS = dint('bonS', [512, T], BF16)
    g.kdS = [dint('kdS%d' % d, [512, T], BF16) for d in range(2)]; g.bS = [dint('bS%d' % d, [512, T], BF16) for d in range(2)]
    g.wlS = [dint('wlS%d' % d, [512, T]) for d in range(2)]; g.vT = dint('vT', [T, 512], BF16); g.yfS = dint('yfS', [512, T]); g.ybS = dint('ybS', [512, T])
    g.p_conv = din('p_conv', [DEPTH, 512, D]); g.p_att = din('p_att', [DEPTH, 512, D]); g.p_rwkv = din('p_rwkv', [DEPTH, 512, D])
    g.w_out = din('w_out', [DEPTH, D, D]); g.mlp_w1 = din('mlp_w1', [DEPTH, D, DFF]); g.mlp_w2 = din('mlp_w2', [DEPTH, DFF, D])
    g.out = nc.dram_tensor('out', [SEQ, D], F32, kind="ExternalOutput").ap()
    g.xT = dint('xT', [D, T]); g.hT = dint('hTd', [D, T], BF16)
    g.ycT = dint('ycT', [512, T], BF16); g.yaT = dint('yaT', [512, T], BF16); g.yrT = dint('yrT', [512, T], BF16)
    g.dbg = {}
    for name, shape, dt in dbg:
        g.dbg[name] = nc.dram_tensor('dbg_' + name, list(shape), dt, kind="ExternalOutput").ap()

    sb = lambda name, shape, dt=F32: nc.alloc_sbuf_tensor(name, list(shape), dt)
    g.ps = [nc.alloc_psum_tensor('ps%d' % i, [128, 512], F32) for i in range(8)]
    g.psi = 0

    g.identf = sb('identf', [128, 128]); g.identb = sb('identb', [128, 128], BF16)
    g.onesf = sb('onesf', [128, 128]); g.onesb = sb('onesb', [128, 128], BF16)
    g.epsv = sb('epsv', [128, 4])
    g.cs = sb('cs', [128, NCH, 2]); g.modv = sb('modv', [128, 48, 2]); g.gs = sb('gs', [128, 2, NCH, 2])
    S.dma('sp', g.identf[:], g.ident[:, :], writes=['identf'])
    S.op('dve', lambda e: e.tensor_copy(g.identb[:], g.identf[:]), ['identf'], ['identb'])
    S.op('dve', lambda e: e.memset(g.onesf[:], 1.0), [], ['onesf'])
    S.op('dve', lambda e: e.memset(g.onesb[:], 1.0), [], ['onesb'])
    for i, v in enumerate((NORM_EPS, LN_EPS, SUBLN_EPS, GN_EPS)):
        S.op('dve', lambda e, i=i, v=v: e.memset(g.epsv[:, i:i + 1], v), [], ['epsv'])

    import os
    if os.environ.get('SCAN_LIMIT'):
        g.scan_limit = int(os.environ['SCAN_LIMIT'])
    if upto.startswith('rwonly'):
        g.scan_limit = int(upto[6:] or 0)
        phase_rwkv_scan(g, 0)
        S.finish('sp')
        return nc
    phase_x0(g)
    for l in range(DEPTH):
        phase_mod(g, l)
        phase_h(g, l, 0)
        if upto == 'h':
            break
        phase_conv(g, l)
        if upto == 'conv':
            break
        phase_att(g, l)
        if upto == 'att':
            break
        phase_rwkv_prep(g, l)
        if upto == 'rwprep':
            break
        phase_rwkv_scan(g, l)
        if upto == 'rw':
            break
        phase_merge(g, l)
        phase_mlp(g, l)
        if upto == 'l0':
            break
    for nm in ('ycT', 'yaT', 'yrT', 'hT', 'rS', 'kkS', 'gS', 'bonS', 'vT', 'yfS'):
        if nm in g.dbg:
            S.dma('sp', g.dbg[nm][:, :], getattr(g, nm)[:, :], reads=[nm], writes=['dbg_' + nm])
    for nm, ap in (('kdS0', g.kdS[0]), ('bS0', g.bS[0]), ('wlS0', g.wlS[0])):
        if nm in g.dbg:
            S.dma('sp', g.dbg[nm][:, :], ap[:, :], reads=[nm], writes=['dbg_' + nm])
    S.finish('sp')
    return nc


def nextps(g):
    i = g.psi
    g.psi = (g.psi + 1) % 8
    return i


def phase_x0(g):
    S, nc = g.S, g.nc
    with Phase(g) as A:
        xin = [A('x0in%d' % i, [128, D]) for i in range(2)]
        xst = [A('x0st%d' % i, [128, NCH, 128]) for i in range(2)]
        xTv = g.xT.rearrange('(k p) t -> p k t', p=128)
        for ti in range(T // 128):
            b = ti % 2
            src = g.ctx[ti * 128:(ti + 1) * 128, :] if ti < 2 else g.x[(ti - 2) * 128:(ti - 1) * 128, :]
            S.dma('sp', xin[b][:], src, writes=[('x0in', b)])
            for half in range(2):
                pi = nextps(g)
                ps = g.ps[pi]

                def fn(pe, half=half, ps=ps, b=b):
                    inst = None
                    for j in range(4):
                        k = half * 4 + j
                        inst = pe.transpose(ps[:, j * 128:(j + 1) * 128], xin[b][:, k * 128:(k + 1) * 128], g.identf[:])
                    return inst
                S.op('pe', fn, [('x0in', b), 'identf'], [('ps', pi)])
                dst = xst[b][:, half * 4:(half + 1) * 4, :]
                src_ps = ps[:].rearrange('p (j t) -> p j t', j=4)
                if half == 0:
                    S.op('act', lambda e, dst=dst, s=src_ps: e.copy(dst, s), [('ps', pi)], [('x0st', b, half)])
                else:
                    S.op('dve', lambda e, dst=dst, s=src_ps: e.tensor_copy(dst, s), [('ps', pi)], [('x0st', b, half)])
            S.dma('sp', xTv[:, :, ti * 128:(ti + 1) * 128], xst[b][:], reads=[('x0st', b, 0), ('x0st', b, 1)], writes=['xT'])


def phase_mod(g, l):
    S, nc = g.S, g.nc
    with Phase(g) as A:
        modbs = A('modbs', [128, 48])
        mw = [A('mw%d' % i, [128, NCH, 512]) for i in range(2)]
        ng = A('ng', [128, 2, NCH])
        if l == 0:
            tmp = A('cs_tmp', [128, NCH, 2])
            S.dma('sp', tmp[:], g.cvec[:, :, :], writes=['cs_tmp'])
            S.op('act', lambda e: e.activation(g.cs[:], tmp[:], AF.Sigmoid), ['cs_tmp'], ['cs'])
            S.op('dve', lambda e: e.tensor_tensor(g.cs[:], g.cs[:], tmp[:], ALU.mult), ['cs', 'cs_tmp'], ['cs'])
        S.dma('sp', modbs[:], g.modb[l], writes=['modbs'])
        S.dma('sp', ng[:, 0, :], g.n1g[l], writes=['ng'])
        S.dma('sp', ng[:, 1, :], g.n2g[l], writes=['ng'])
        mwv = g.mod_w[l].rearrange('(k p) c -> p k c', p=128)
        for cg in range(12):
            b = cg % 2
            S.dma('sp', mw[b][:], mwv[:, :, cg * 512:(cg + 1) * 512], writes=[('mw', b)])
            pi = nextps(g)
            ps = g.ps[pi]
            for j in range(4):
                pairs = [(mw[b][:, k, j * 128:(j + 1) * 128], g.cs[:, k, :]) for k in range(NCH)]
                mm_group(S, ps[:, 2 * j:2 * j + 2], pairs, [('mw', b), 'cs'], [('ps', pi)])
            for j in range(4):
                jj = cg * 4 + j
                S.op('dve', lambda e, j=j, jj=jj, ps=ps: e.tensor_scalar(g.modv[:, jj, :], ps[:, 2 * j:2 * j + 2],
                     modbs[:, jj:jj + 1], None, ALU.add), [('ps', pi), 'modbs'], ['modv'])
        for n in range(2):
            sc = g.modv[:, (3 * n + 1) * 8:(3 * n + 2) * 8, :]
            S.op('dve', lambda e, n=n, sc=sc: e.tensor_scalar(g.gs[:, n], sc, 1.0, None, ALU.add), ['modv'], ['gs'])
            S.op('dve', lambda e, n=n: e.tensor_tensor(g.gs[:, n], g.gs[:, n],
                 ng[:, n, :].unsqueeze(2).broadcast_to([128, NCH, 2]), ALU.mult), ['gs', 'ng'], ['gs'])


def rms_stats(g, xb, n, sq, rstd, key_x, key_sq, key_rstd):
    S = g.S
    S.op('act', lambda e: e.activation(sq[:, :, :n], xb[:, :, :n], AF.Square), [key_x], [key_sq])
    pi = nextps(g)
    ps = g.ps[pi]
    mm_group(S, ps[:, :n], [(g.onesf[:], sq[:, k, :n]) for k in range(NCH)], [key_sq, 'onesf'], [('ps', pi)])
    S.op('act', lambda e: e.activation(rstd[:, :n], ps[:, :n], AF.Sqrt, bias=g.epsv[:, 0:1], scale=1.0 / D),
         [('ps', pi), 'epsv'], [key_rstd])
    S.op('dve', lambda e: e.reciprocal(rstd[:, :n], rstd[:, :n]), [key_rstd], [key_rstd])


def phase_h(g, l, n):
    S, nc = g.S, g.nc
    with Phase(g) as A:
        hx = [A('hx%d' % i, [128, NCH, 512]) for i in range(2)]
        hsq = A('hsq', [128, NCH, 512])
        hrs = [A('hrs%d' % i, [128, 512]) for i in range(2)]
        htmp = [A('htmp%d' % i, [128, 512]) for i in range(2)]
        hb = [A('hb%d' % i, [128, NCH, 512], BF16) for i in range(2)]
        xTv = g.xT.rearrange('(k p) t -> p k t', p=128)
        hTv = g.hT.rearrange('(k p) t -> p k t', p=128)
        for bi, (t0, nt) in enumerate(BLKS):
            b = bi % 2
            s = 1 if t0 < CTX else 0
            S.dma('sp', hx[b][:, :, :nt], xTv[:, :, t0:t0 + nt], reads=['xT'], writes=[('hx', b)])
            rms_stats(g, hx[b], nt, hsq, hrs[b], ('hx', b), 'hsq', ('hrs', b))
            for k in range(NCH):
                tb = k % 2
                S.op('dve', lambda e, k=k, tb=tb: e.tensor_tensor(htmp[tb][:, :nt], hx[b][:, k, :nt], hrs[b][:, :nt], ALU.mult),
                     [('hx', b), ('hrs', b)], [('htmp', tb)])
                S.op('act', lambda e, k=k, tb=tb: e.activation(hb[b][:, k, :nt], htmp[tb][:, :nt], AF.Identity,
                     bias=g.modv[:, (3 * n) * 8 + k, s:s + 1], scale=g.gs[:, n, k, s:s + 1]),
                     [('htmp', tb), 'modv', 'gs'], [('hb', b)])
            S.dma('sp', hTv[:, :, t0:t0 + nt], hb[b][:, :, :nt], reads=[('hb', b)], writes=['hT'])


class HLoader:
    def __init__(self, g, A):
        self.g = g
        self.t = [A('hblk%d' % i, [128, NCH, 512], BF16) for i in range(2)]
        self.i = 0

    def load(self, bi):
        g = self.g
        b = self.i
        self.i = (b + 1) % 2
        t0, nt = BLKS[bi]
        hTv = g.hT.rearrange('(k p) t -> p k t', p=128)
        g.S.dma('sp', self.t[b][:, :, :nt], hTv[:, :, t0:t0 + nt], reads=['hT'], writes=[('hblk', b)])
        return self.t[b], ('hblk', b)


def ucol(t):
    return t + 15 if t < CTX else t + 45


def phase_conv(g, l):
    S, nc = g.S, g.nc
    with Phase(g) as A:
        wcv = A('wcv', [128, NCH, 1024], BF16)
        uT = A('uT', [128, 4, T + 60], BF16)
        diag = A('diag', [128, 4, 31, 128], BF16)
        dww = A('dww', [128, 4, 31])
        cvp = A('cvp', [128, 3, 4])
        csg = [A('csg%d' % i, [128, 512]) for i in range(2)]
        cv = A('cv', [128, 4, 512]); cv2 = A('cv2', [128, 4, 512])
        cm = A('cm', [128, 512]); cmsq = A('cmsq', [128, 512]); crs = A('crs', [128, 512])
        ct = [A('ct%d' % i, [128, 512]) for i in range(2)]
        cyb = [A('cyb%d' % i, [128, 4, 512], BF16) for i in range(2)]
        HL = HLoader(g, A)
        S.op('pool', lambda e: e.memset(uT[:], 0.0), [], ['uT'])
        S.dma('pool', wcv[:], g.w_in[l][:, 0:1024].rearrange('(k p) c -> p k c', p=128), writes=['wcv'])
        S.dma('sp', dww[:], g.convw[l], writes=['dww'])
        S.dma('sp', cvp[:], g.convp[l], writes=['cvp'])
        for c in range(4):
            S.op('dve', lambda e, c=c: e.tensor_tensor(diag[:, c], g.identf[:].unsqueeze(1).broadcast_to([128, 31, 128]),
                 dww[:, c, :].unsqueeze(2).broadcast_to([128, 31, 128]), ALU.mult), ['identf', 'dww'], ['diag'])
        for bi, (t0, nt) in enumerate(BLKS):
            hb, hk = HL.load(bi)
            for c in range(4):
                pa, pb = nextps(g), nextps(g)
                mm_group(S, g.ps[pa][:, :nt], [(wcv[:, k, c * 128:(c + 1) * 128], hb[:, k, :nt]) for k in range(NCH)],
                         ['wcv', hk], [('ps', pa)])
                mm_group(S, g.ps[pb][:, :nt], [(wcv[:, k, 512 + c * 128:512 + (c + 1) * 128], hb[:, k, :nt]) for k in range(NCH)],
                         ['wcv', hk], [('ps', pb)])
                sb_ = c % 2
                S.op('act', lambda e, pb=pb, sb_=sb_: e.activation(csg[sb_][:, :nt], g.ps[pb][:, :nt], AF.Sigmoid),
                     [('ps', pb)], [('csg', sb_)])
                S.op('dve', lambda e, pa=pa, sb_=sb_, c=c: e.tensor_tensor(uT[:, c, ucol(t0):ucol(t0) + nt], g.ps[pa][:, :nt],
                     csg[sb_][:, :nt], ALU.mult), [('ps', pa), ('csg', sb_)], ['uT'])
        ycv = g.ycT.rearrange('(k p) t -> p k t', p=128)
        for bi, (t0, nt) in enumerate(BLKS):
            yb = cyb[bi % 2]
            ykey = ('cyb', bi % 2)
            for c in range(4):
                pi = nextps(g)
                base = ucol(t0) - 15
                mm_group(S, g.ps[pi][:, :nt], [(diag[:, c, k, :], uT[:, c, base + k:base + k + nt]) for k in range(31)],
                         ['diag', 'uT'], [('ps', pi)])
                S.op('act', lambda e, pi=pi, c=c: e.activation(cv[:, c, :nt], g.ps[pi][:, :nt], AF.Identity,
                     bias=cvp[:, 0, c:c + 1], scale=1.0), [('ps', pi), 'cvp'], [('cv', c)])
                S.op('act', lambda e, pi=pi, c=c: e.activation(cv2[:, c, :nt], g.ps[pi][:, :nt], AF.Square,
                     bias=cvp[:, 0, c:c + 1], scale=1.0), [('ps', pi), 'cvp'], [('cv2', c)])
            p1, p2 = nextps(g), nextps(g)
            mm_group(S, g.ps[p1][:, :nt], [(g.onesf[:], cv[:, c, :nt]) for c in range(4)], [('cv', c) for c in range(4)] + ['onesf'], [('ps', p1)])
            mm_group(S, g.ps[p2][:, :nt], [(g.onesf[:], cv2[:, c, :nt]) for c in range(4)], [('cv2', c) for c in range(4)] + ['onesf'], [('ps', p2)])
            S.op('act', lambda e: e.activation(cm[:, :nt], g.ps[p1][:, :nt], AF.Identity, scale=1.0 / 512), [('ps', p1)], ['cm'])
            S.op('dve', lambda e: e.tensor_tensor(cmsq[:, :nt], cm[:, :nt], cm[:, :nt], ALU.mult), ['cm'], ['cmsq'])
            S.op('dve', lambda e: e.scalar_tensor_tensor(crs[:, :nt], g.ps[p2][:, :nt], 1.0 / 512, cmsq[:, :nt], ALU.mult, ALU.subtract),
                 [('ps', p2), 'cmsq'], ['crs'])
            S.op('act', lambda e: e.activation(crs[:, :nt], crs[:, :nt], AF.Sqrt, bias=g.epsv[:, 1:2], scale=1.0), ['crs', 'epsv'], ['crs'])
            S.op('dve', lambda e: e.reciprocal(crs[:, :nt], crs[:, :nt]), ['crs'], ['crs'])
            for c in range(4):
                tb = c % 2
                S.op('dve', lambda e, c=c, tb=tb: e.tensor_tensor(ct[tb][:, :nt], cv[:, c, :nt], cm[:, :nt], ALU.subtract),
                     [('cv', c), 'cm'], [('ct', tb)])
                S.op('dve', lambda e, c=c, tb=tb: e.tensor_tensor(ct[tb][:, :nt], ct[tb][:, :nt], crs[:, :nt], ALU.mult),
                     [('ct', tb), 'crs'], [('ct', tb)])
                S.op('act', lambda e, c=c, tb=tb: e.activation(yb[:, c, :nt], ct[tb][:, :nt], AF.Silu,
                     bias=cvp[:, 2, c:c + 1], scale=cvp[:, 1, c:c + 1]), [('ct', tb), 'cvp'], [ykey])
            S.dma('sp', ycv[:, :, t0:t0 + nt], yb[:, :, :nt], reads=[ykey], writes=['ycT'])


def fm(v, nch):
    return np.ascontiguousarray(np.asarray(v, np.float32).reshape(nch, 128).T)


def host_shared(inp):
    m = {}
    m['mod_w'] = np.ascontiguousarray(inp['mod_w'], dtype=np.float32)
    m['modb'] = np.stack([fm(inp['mod_b'][l], 48) for l in range(DEPTH)])
    m['n1g'] = np.stack([fm(inp['norm1_g'][l], NCH) for l in range(DEPTH)])
    m['n2g'] = np.stack([fm(inp['norm2_g'][l], NCH) for l in range(DEPTH)])
    m['fing'] = fm(inp['final_g'], NCH)
    m['ident'] = np.eye(128, dtype=np.float32)
    m['w_in'] = np.ascontiguousarray(inp['w_in'], dtype=np.float32)
    sw = np.arange(1024) ^ 1
    m['wqks'] = np.ascontiguousarray(np.asarray(inp['w_in'])[:, :, 1024:2048][:, :, sw], dtype=np.float32)
    tt = np.arange(SEQ)
    inv = (10000.0 ** (-np.arange(16, dtype=np.float32) / 16)).astype(np.float32)
    ang = np.concatenate([(tt // 64).astype(np.float32)[:, None] * inv, (tt % 64).astype(np.float32)[:, None] * inv], axis=-1)
    pidx = (np.arange(128) % 64) // 2
    cosT = np.cos(ang)[:, pidx].T
    sinT = np.sin(ang)[:, pidx].T * np.where(np.arange(128) % 2 == 0, -1.0, 1.0)[:, None]
    m['rope'] = np.ascontiguousarray(np.stack([cosT, sinT]), dtype=np.float32)
    blk = (np.arange(128) // 64)
    bdm = (blk[:, None] == blk[None, :]).astype(np.float32)
    sel0 = np.repeat((blk == 0).astype(np.float32)[:, None], 128, 1)
    sel1 = np.repeat((blk == 1).astype(np.float32)[:, None], 128, 1)
    m['cmask'] = np.ascontiguousarray(np.stack([bdm, sel0, sel1]))
    m['attp'] = np.ascontiguousarray(np.stack([np.stack([inp[k][l] for k in ('att_lq1', 'att_lk1', 'att_lq2', 'att_lk2')]) for l in range(DEPTH)]), dtype=np.float32)
    m['subg'] = np.ascontiguousarray(inp['att_subln_g'], dtype=np.float32)
    for k in ('p_conv', 'p_att', 'p_rwkv', 'w_out', 'mlp_w1', 'mlp_w2'):
        m[k] = np.ascontiguousarray(inp[k], dtype=np.float32)
    m['rw_w2'] = np.ascontiguousarray(inp['rwkv_w2'], dtype=np.float32)
    m['rw_a2'] = np.ascontiguousarray(inp['rwkv_a2'], dtype=np.float32)
    m['rw_g2'] = np.ascontiguousarray(inp['rwkv_g2'], dtype=np.float32)
    rwp = []
    for l in range(DEPTH):
        cols = [np.asarray(inp['rwkv_shift'][l]).T.reshape(15, 128, 3).transpose(1, 0, 2).reshape(128, 45)]
        cols += [fm(inp['rwkv_w0'][l].reshape(-1), 8), fm(inp['rwkv_a0'][l].reshape(-1), 8)]
        cols += [fm(inp[k][l].reshape(-1), 4) for k in ('rwkv_kk', 'rwkv_ka', 'rwkv_rk', 'rwkv_gn_g', 'rwkv_gn_b')]
        rwp.append(np.concatenate(cols, axis=1))
    m['rwp'] = np.ascontiguousarray(np.stack(rwp), dtype=np.float32)
    ii = np.arange(128)
    lt = (ii[:, None] < ii[None, :]).astype(np.float32); le = (ii[:, None] <= ii[None, :]).astype(np.float32)
    seg = np.repeat((ii != 0).astype(np.float32)[None, :], 128, 0)
    blk = lambda n: (ii[:, None] // n == ii[None, :] // n)
    offm = lambda n: (blk(n) & ~blk(n // 2)).astype(np.float32)
    m['smask'] = np.ascontiguousarray(np.stack([lt, le, lt.T, le.T, seg, blk(16).astype(np.float32), offm(32), offm(64), offm(128)]))
    m['convw'] = np.stack([np.ascontiguousarray(np.asarray(inp['conv_dw_w'][l]).T.reshape(4, 128, 31).transpose(1, 0, 2)) for l in range(DEPTH)])
    m['convp'] = np.stack([np.stack([fm(inp[k][l], 4) for k in ('conv_dw_b', 'conv_ln_g', 'conv_ln_b')], axis=1) for l in range(DEPTH)])
    return m


def host_inputs(inp, b, shared=None):
    m = dict(shared if shared is not None else host_shared(inp))
    m['x'] = np.ascontiguousarray(inp['x'][b], dtype=np.float32)
    m['ctx'] = np.ascontiguousarray(inp['ctx'][b], dtype=np.float32)
    m['cvec'] = np.ascontiguousarray(np.stack([fm(inp['c'][b], NCH), fm(inp['c_ctx'], NCH)], axis=-1))
    return m


def phase_att(g, l):
    S, nc = g.S, g.nc
    lam_init = 0.8 - 0.6 * math.exp(-0.3 * l)
    need_ctx_q = l < DEPTH - 1
    with Phase(g) as A:
        qT = A('qT', [128, 4, T], BF16); kT = A('kT', [128, 4, T], BF16)
        vaug = A('vaug', [128, T // 128, 4, 129], BF16)
        nb = A('nb', [128, 2, 4]); neglam = A('neglam', [128, 1]); gsub = A('gsub', [128, 128])
        A1 = Phase(g)
        wq = A1('wq', [128, NCH, 512], BF16); wqs = A1('wqs', [128, NCH, 512], BF16)
        wk = A1('wk', [128, NCH, 512], BF16); wks = A1('wks', [128, NCH, 512], BF16)
        wv = A1('wv', [128, NCH, 512], BF16)
        cosT = A1('cosT', [128, SEQ]); sinT = A1('sinT', [128, SEQ])
        HL = HLoader(g, A1)
        rt = [A1('rt%d' % i, [128, 512]) for i in range(4)]
        bd = A1('bd', [128, 128], BF16); cmf = A1('cmf', [128, 3, 128])
        stat = A1('stat', [128, 2, 4, len(BLKS)]); stm = A1('stm', [128, 2, 4]); negb = A1('negb', [128, 4])
        lqk = A1('lqk', [128, 4, 64]); lam2 = A1('lam2', [128, 2])

        wsrc = g.w_in[l].rearrange('(k p) c -> p k c', p=128)
        ssrc = g.wqks[l].rearrange('(k p) c -> p k c', p=128)
        S.dma('pool', wq[:], wsrc[:, :, 1024:1536], writes=['wq'])
        S.dma('pool', wk[:], wsrc[:, :, 1536:2048], writes=['wk'])
        S.dma('pool', wv[:], wsrc[:, :, 2048:2560], writes=['wv'])
        S.dma('pool', wqs[:], ssrc[:, :, 0:512], writes=['wqs'])
        S.dma('pool', wks[:], ssrc[:, :, 512:1024], writes=['wks'])
        S.dma('sp', cosT[:], g.rope[0], writes=['cosT'])
        S.dma('sp', sinT[:], g.rope[1], writes=['sinT'])
        S.dma('sp', cmf[:], g.cmask.rearrange('m p c -> p m c'), writes=['cmf'])
        S.op('dve', lambda e: e.tensor_copy(bd[:], cmf[:, 0, :]), ['cmf'], ['bd'])
        S.dma('sp', lqk[:], g.attp[l:l + 1].broadcast_to([128, 4, 64]), writes=['lqk'])
        S.dma('sp', gsub[:], g.subg[l:l + 1, :].broadcast_to([128, 128]), writes=['gsub'])
        S.op('act', lambda e: e.mul(gsub[:], gsub[:], 1.0 - lam_init), ['gsub'], ['gsub'])
        S.op('dve', lambda e: e.tensor_tensor(lqk[:, 0, :], lqk[:, 0, :], lqk[:, 1, :], ALU.mult), ['lqk'], ['lqk'])
        S.op('dve', lambda e: e.tensor_tensor(lqk[:, 2, :], lqk[:, 2, :], lqk[:, 3, :], ALU.mult), ['lqk'], ['lqk'])
        S.op('dve', lambda e: e.reduce_sum(lam2[:, 0:1], lqk[:, 0, :], AX.X), ['lqk'], ['lam2'])
        S.op('dve', lambda e: e.reduce_sum(lam2[:, 1:2], lqk[:, 2, :], AX.X), ['lqk'], ['lam2'])
        S.op('act', lambda e: e.activation(lam2[:], lam2[:], AF.Exp), ['lam2'], ['lam2'])
        S.op('dve', lambda e: e.tensor_tensor(neglam[:], lam2[:, 1:2], lam2[:, 0:1], ALU.subtract), ['lam2'], ['neglam'])
        S.op('dve', lambda e: e.tensor_scalar(neglam[:], neglam[:], -lam_init, None, ALU.add), ['neglam'], ['neglam'])
        S.op('pool', lambda e: e.memset(vaug[:, :, :, 128:129], 1.0), [], ['vaug1'])

        for bi, (t0, nt) in enumerate(BLKS):
            hb, hk = HL.load(bi)
            lat = t0 >= CTX
            tl = t0 - CTX
            for (w, ws, dst, dk, wkey, wskey) in ((wq, wqs, qT, 'qT', 'wq', 'wqs'), (wk, wks, kT, 'kT', 'wk', 'wks')):
                for h in range(4):
                    pa = nextps(g)
                    mm_group(S, g.ps[pa][:, :nt], [(w[:, k, h * 128:(h + 1) * 128], hb[:, k, :nt]) for k in range(NCH)],
                             [wkey, hk], [('ps', pa)])
                    if not lat:
                        S.op('act', lambda e, pa=pa, h=h, dst=dst: e.copy(dst[:, h, t0:t0 + nt], g.ps[pa][:, :nt]), [('ps', pa)], [dk])
                        continue
                    pb = nextps(g)
                    mm_group(S, g.ps[pb][:, :nt], [(ws[:, k, h * 128:(h + 1) * 128], hb[:, k, :nt]) for k in range(NCH)],
                             [wskey, hk], [('ps', pb)])
                    r1, r2 = (0, 1) if h % 2 == 0 else (2, 3)
                    S.op('dve', lambda e, pa=pa, r1=r1: e.tensor_tensor(rt[r1][:, :nt], g.ps[pa][:, :nt], cosT[:, tl:tl + nt], ALU.mult),
                         [('ps', pa), 'cosT'], [('rt', r1)])
                    S.op('dve', lambda e, pb=pb, r2=r2: e.tensor_tensor(rt[r2][:, :nt], g.ps[pb][:, :nt], sinT[:, tl:tl + nt], ALU.mult),
                         [('ps', pb), 'sinT'], [('rt', r2)])
                    S.op('pool', lambda e, r1=r1, r2=r2, h=h, dst=dst: e.tensor_tensor(dst[:, h, t0:t0 + nt], rt[r1][:, :nt], rt[r2][:, :nt], ALU.add),
                         [('rt', r1), ('rt', r2)], [dk])
            for tt in range(nt // 128):
                ti = t0 // 128 + tt
                pv = nextps(g)
                mm_group(S, g.ps[pv][:, :512], [(hb[:, k, tt * 128:(tt + 1) * 128], wv[:, k, :]) for k in range(NCH)],
                         ['wv', hk], [('ps', pv)])
                S.op('act', lambda e, pv=pv, ti=ti: e.copy(vaug[:, ti, :, 0:128], g.ps[pv][:, :].rearrange('p (h d) -> p h d', h=4)),
                     [('ps', pv)], ['vaug'])
        sqb = rt
        for qi, (src, sk) in enumerate(((qT, 'qT'), (kT, 'kT'))):
            for h in range(4):
                for bi, (t0, nt) in enumerate(BLKS):
                    r = (h * len(BLKS) + bi) % 4
                    sq = rt[r][:, 0:256].bitcast(BF16)
                    S.op('act', lambda e, sq=sq, h=h, src=src: e.activation(sq[:, :nt], src[:, h, t0:t0 + nt], AF.Square), [sk], [('rt', r)])
                    pi = nextps(g)
                    mm_group(S, g.ps[pi][:, :nt], [(bd[:], sq[:, :nt])], ['bd', ('rt', r)], [('ps', pi)])
                    S.op('dve', lambda e, pi=pi, h=h, bi=bi, qi=qi: e.reduce_max(stat[:, qi, h, bi:bi + 1], g.ps[pi][:, :nt], AX.X),
                         [('ps', pi)], ['stat'])
        S.op('dve', lambda e: e.reduce_max(stm[:], stat[:], AX.X), ['stat'], ['stm'])
        S.op('dve', lambda e: e.tensor_tensor(negb[:], stm[:, 0, :], stm[:, 1, :], ALU.mult), ['stm'], ['negb'])
        S.op('act', lambda e: e.activation(negb[:], negb[:], AF.Sqrt), ['negb'], ['negb'])
        for c in range(2):
            pi = nextps(g)
            mm_group(S, g.ps[pi][:, 0:4], [(cmf[:, 1 + c, :], negb[:])], ['cmf', 'negb'], [('ps', pi)])
            S.op('act', lambda e, pi=pi, c=c: e.mul(nb[:, c, :], g.ps[pi][:, 0:4], -1.02 * 0.125 / 64.0), [('ps', pi)], ['nb'])

        A1.__exit__(None, None, None)
        pT = [A('pT%d' % i, [128, 512], BF16) for i in range(8)]
        accs = [A('acc%d' % i, [128, 512]) for i in range(2)]
        rinv = A('rinv', [128, 512]); on = [A('on%d' % i, [128, 512]) for i in range(2)]
        aa = A('aa', [128, 512]); sq = A('asq', [128, 512]); rstd = A('arstd', [128, 512])
        gsubT = A('gsubT', [128, 1])
        yab = [A('yab%d' % i, [128, 4, 512], BF16) for i in range(2)]
        S.dma('sp', gsubT[:], g.subg[l].rearrange('(p o) -> p o', o=1), writes=['gsubT'])
        S.op('act', lambda e: e.mul(gsubT[:], gsubT[:], 1.0 - lam_init), ['gsubT'], ['gsubT'])
        yav = g.yaT.rearrange('(k p) t -> p k t', p=128)
        pti = [0]
        sbank = [0]

        def sbnext():
            b_ = 3 + sbank[0]
            sbank[0] = (sbank[0] + 1) % 5
            return b_

        def attend(q0, nq, kt0, nkt, yslot):
            items = [(h, c, kk) for h in range(4) for c in range(2) for kk in range(nkt)]
            DPF = 3
            slots = {}
            yb = yab[yslot % 2]
            ykey = ('yab', yslot % 2)

            def front(i):
                h, c, kk = items[i]
                kt = kt0 + kk
                sb_ = sbnext()
                mm_group(S, g.ps[sb_][:, :nq], [(kT[64 * c:64 * c + 64, h, kt * 128:(kt + 1) * 128], qT[64 * c:64 * c + 64, h, q0:q0 + nq])],
                         ['kT', 'qT'], [('ps', sb_)])
                pb_ = pti[0]
                pti[0] = (pti[0] + 1) % 8
                S.op('act', lambda e: e.activation(pT[pb_][:, :nq], g.ps[sb_][:, :nq], AF.Exp,
                     bias=nb[:, c, h:h + 1], scale=0.125), [('ps', sb_), 'nb'], [('pT', pb_)])
                slots[i] = pb_

            def back(i):
                h, c, kk = items[i]
                kt = kt0 + kk
                pb_ = slots.pop(i)
                ob = (2 * h + c) % 3
                mm_group(S, g.ps[ob][:, :nq], [(vaug[:, kt, h, 0:128], pT[pb_][:, :nq])], [('pT', pb_), 'vaug'], [('ps', ob)],
                         start=(kk == 0), stop=(kk == nkt - 1))
                eng = 'dve' if kk % 2 == 0 else 'pool'
                acc, akey = accs[kk % 2], ('acc', kk % 2)
                if kk < 2:
                    S.op(eng, lambda e: e.tensor_copy(acc[:, :nq], pT[pb_][:, :nq]), [('pT', pb_)], [akey])
                else:
                    S.op(eng, lambda e: e.tensor_tensor(acc[:, :nq], acc[:, :nq], pT[pb_][:, :nq], ALU.add), [('pT', pb_), akey], [akey])
                if kk != nkt - 1:
                    return
                S.op('dve', lambda e: e.tensor_tensor(accs[0][:, :nq], accs[0][:, :nq], accs[1][:, :nq], ALU.add), [('acc', 0), ('acc', 1)], [('acc', 0)])
                rb = sbnext()
                mm_group(S, g.ps[rb][:, :nq], [(g.onesf[:], accs[0][:, :nq])], [('acc', 0), 'onesf'], [('ps', rb)])
                S.op('dve', lambda e: e.reciprocal(rinv[:, :nq], g.ps[rb][:, :nq]), [('ps', rb)], ['rinv'])
                S.op('dve', lambda e: e.tensor_tensor(on[c][:, :nq], g.ps[ob][:, :nq], rinv[:, :nq], ALU.mult), [('ps', ob), 'rinv'], [('on', c)])
                if c != 1:
                    return
                S.op('dve', lambda e: e.scalar_tensor_tensor(aa[:, :nq], on[1][:, :nq], neglam[:, 0:1], on[0][:, :nq], ALU.mult, ALU.add),
                     [('on', 0), ('on', 1), 'neglam'], ['aa'])
                S.op('pool', lambda e: e.tensor_tensor(sq[:, :nq], aa[:, :nq], aa[:, :nq], ALU.mult), ['aa'], ['asq'])
                rb2 = sbnext()
                mm_group(S, g.ps[rb2][:, :nq], [(g.onesf[:], sq[:, :nq])], ['asq', 'onesf'], [('ps', rb2)])
                S.op('act', lambda e: e.activation(rstd[:, :nq], g.ps[rb2][:, :nq], AF.Sqrt, bias=g.epsv[:, 2:3], scale=1.0 / 128), [('ps', rb2), 'epsv'], ['arstd'])
                S.op('dve', lambda e: e.reciprocal(rstd[:, :nq], rstd[:, :nq]), ['arstd'], ['arstd'])
                S.op('dve', lambda e: e.scalar_tensor_tensor(yb[:, h, :nq], aa[:, :nq], gsubT[:, 0:1], rstd[:, :nq], ALU.mult, ALU.mult),
                     ['aa', 'arstd', 'gsubT'], [ykey])

            GRP = 3
            for i0 in range(0, len(items) + DPF + GRP, GRP):
                for i in range(i0, i0 + GRP):
                    if i < len(items):
                        front(i)
                for i in range(i0, i0 + GRP):
                    if 0 <= i - DPF < len(items):
                        back(i - DPF)
            S.dma('sp', yav[:, :, q0:q0 + nq], yb[:, :, :nq], reads=[ykey], writes=['yaT'])

        slot = 0
        if need_ctx_q:
            attend(0, CTX, 0, CTX // 128, slot)
            slot += 1
        for qb in range(SEQ // 512):
            attend(CTX + qb * 512, 512, 0, T // 128, slot)
            slot += 1


RW0 = 2560
P_SH, P_W0, P_A0, P_KK, P_KA, P_RK, P_GG, P_GB, P_OMKA, NRWP = 0, 45, 53, 61, 65, 69, 73, 77, 81, 85
DECAY_C = -math.exp(-0.5)


def phase_rwkv_prep(g, l):
    S, nc = g.S, g.nc
    with Phase(g) as A:
        wrw = A('wrw', [128, NCH, 1920], BF16)
        w2b = A('w2b', [128, 512], BF16); a2b = A('a2b', [128, 512], BF16); g2b = A('g2b', [128, 512], BF16)
        rwp = A('rwp', [128, NRWP])
        bdf = A('bdf', [128, 128])
        hbx = [A('hbx%d' % i, [128, NCH, 514], BF16) for i in range(2)]
        zx = [A('zx%d' % i, [128, 514]) for i in range(3)]
        zc = A('zc', [128, 15, 512])
        tw = A('tw', [128, 512], BF16); ab = A('ab', [128, 512], BF16); sg = A('sg', [128, 512], BF16)
        kkt = A('kkt', [128, 4, 512]); kds = A('kds', [128, 4, 512])
        t1 = [A('rt1_%d' % i, [128, 512]) for i in range(3)]
        ob = {n: [A('ob_%s%d' % (n, i), [128, 4, 512], BF16) for i in range(1)] * 2 for n in ('r', 'kk', 'kd0', 'kd1', 'b0', 'b1', 'g', 'bon')}
        owl = {d: [A('owl%d_%d' % (d, i), [128, 4, 512]) for i in range(1)] * 2 for d in range(2)}
        vtile = [A('vtile%d' % i, [128, 512], BF16) for i in range(2)]

        wsrc = g.w_in[l].rearrange('(k p) c -> p k c', p=128)
        S.dma('pool', wrw[:], wsrc[:, :, RW0:RW0 + 1920], writes=['wrw'])
        S.dma('pool', w2b[:], g.rw_w2[l].rearrange('d m c -> (d m) c'), writes=['w2b'])
        S.dma('pool', a2b[:], g.rw_a2[l].rearrange('d m c -> (d m) c'), writes=['a2b'])
        S.dma('pool', g2b[:], g.rw_g2[l], writes=['g2b'])
        S.dma('sp', rwp[:, 0:P_OMKA], g.rwp[l], writes=['rwp'])
        S.dma('sp', bdf[:], g.cmask[0], writes=['bdf'])
        S.op('dve', lambda e: e.tensor_scalar(rwp[:, P_OMKA:P_OMKA + 4], rwp[:, P_KA:P_KA + 4], -1.0, 1.0, ALU.mult, ALU.add), ['rwp'], ['rwp'])
        hTv = g.hT.rearrange('(k p) t -> p k t', p=128)
        fmv = lambda ap: ap.rearrange('(k p) t -> p k t', p=128)
        for bi, (t0, nt) in enumerate(BLKS):
            b = bi % 2
            hb = hbx[b]
            hk = ('hbx', b)
            s0, s1 = (0, CTX) if t0 < CTX else (CTX, T)
            lo, hi = max(s0, t0 - 1), min(s1, t0 + nt + 1)
            if lo == t0:
                S.op('pool', lambda e, hb=hb: e.memset(hb[:, :, 0:1], 0.0), [], [hk])
            if hi == t0 + nt:
                S.op('pool', lambda e, hb=hb: e.memset(hb[:, :, nt + 1:nt + 2], 0.0), [], [hk])
            S.dma('sp', hb[:, :, 1 - (t0 - lo):1 + (hi - t0)], hTv[:, :, lo:hi], reads=['hT'], writes=[hk])
            ph = nextps(g)
            for ch in range(15):
                pm = nextps(g)
                if pm == ph:
                    pm = nextps(g)
                wsl = lambda k, ch=ch: wrw[:, k, ch * 128:(ch + 1) * 128]
                mm_group(S, g.ps[pm][:, :nt], [(wsl(k), hb[:, k, 1:1 + nt]) for k in range(NCH)], ['wrw', hk], [('ps', pm)])
                mm_group(S, g.ps[ph][:, 2 * ch:2 * ch + 1], [(wsl(k), hb[:, k, 0:1]) for k in range(NCH)], ['wrw', hk], [('ps', ph)])
                mm_group(S, g.ps[ph][:, 2 * ch + 1:2 * ch + 2], [(wsl(k), hb[:, k, nt + 1:nt + 2]) for k in range(NCH)], ['wrw', hk], [('ps', ph)])
                z = zx[ch % 3]
                zk = ('zx', ch % 3)
                S.op('act', lambda e, z=z, pm=pm: e.copy(z[:, 1:1 + nt], g.ps[pm][:, :nt]), [('ps', pm)], [zk])
                S.op('act', lambda e, z=z, ch=ch: e.copy(z[:, 0:1], g.ps[ph][:, 2 * ch:2 * ch + 1]), [('ps', ph)], [zk])
                S.op('act', lambda e, z=z, ch=ch: e.copy(z[:, nt + 1:nt + 2], g.ps[ph][:, 2 * ch + 1:2 * ch + 2]), [('ps', ph)], [zk])
                sh = lambda j, ch=ch: rwp[:, P_SH + ch * 3 + j:P_SH + ch * 3 + j + 1]
                S.op('act', lambda e, pm=pm, ch=ch, sh=sh: e.activation(zc[:, ch, :nt], g.ps[pm][:, :nt], AF.Identity, scale=sh(1)), [('ps', pm), 'rwp'], [('zc', ch)])
                tz = t1[ch % 3]
                S.op('pool', lambda e, z=z, tz=tz, sh=sh: e.tensor_scalar(tz[:, :nt], z[:, 0:nt], sh(0), 0.0, ALU.mult, ALU.add), [zk, 'rwp'], [('t1', ch % 3)])
                S.op('dve', lambda e, z=z, ch=ch, sh=sh: e.scalar_tensor_tensor(zc[:, ch, :nt], z[:, 2:nt + 2], sh(2), zc[:, ch, :nt], ALU.mult, ALU.add),
                     [zk, 'rwp', ('zc', ch)], [('zc', ch)])
                S.op('pool', lambda e, tz=tz, ch=ch: e.tensor_tensor(zc[:, ch, :nt], zc[:, ch, :nt], tz[:, :nt], ALU.add), [('zc', ch), ('t1', ch % 3)], [('zc', ch)])
            o = {n: ob[n][0] for n in ob}
            okey = {n: ('ob', n, 0) for n in ob}
            S.op('act', lambda e: e.copy(o['r'][:, :, :nt], zc[:, 0:4, :nt]), [('zc', c) for c in range(4)], [okey['r']])
            S.dma('sp', fmv(g.rS)[:, :, t0:t0 + nt], o['r'][:, :, :nt], reads=[okey['r']], writes=['rS'])
            S.op('act', lambda e: e.activation(tw[:, :nt], zc[:, 12, :nt], AF.Tanh), [('zc', 12)], ['tw'])
            S.op('act', lambda e: e.copy(ab[:, :nt], zc[:, 13, :nt]), [('zc', 13)], ['ab'])
            S.op('act', lambda e: e.activation(sg[:, :nt], zc[:, 14, :nt], AF.Sigmoid), [('zc', 14)], ['sg'])
            for c in range(4):
                ti = c % 3
                S.op('act', lambda e, c=c: e.activation(kkt[:, c, :nt], zc[:, 4 + c, :nt], AF.Identity, scale=rwp[:, P_KK + c:P_KK + c + 1]),
                     [('zc', 4 + c), 'rwp'], [('kkt', c)])
                S.op('act', lambda e, c=c, ti=ti: e.activation(t1[ti][:, :nt], kkt[:, c, :nt], AF.Square), [('kkt', c)], [('t1', ti)])
                pi = nextps(g)
                mm_group(S, g.ps[pi][:, :nt], [(bdf[:], t1[ti][:, :nt])], ['bdf', ('t1', ti)], [('ps', pi)])
                S.op('dve', lambda e, pi=pi, ti=ti: e.tensor_scalar(t1[ti][:, :nt], g.ps[pi][:, :nt], 1e-24, None, ALU.max), [('ps', pi)], [('t1', ti)])
                S.op('act', lambda e, ti=ti: e.activation(t1[ti][:, :nt], t1[ti][:, :nt], AF.Sqrt), [('t1', ti)], [('t1', ti)])
                S.op('dve', lambda e, ti=ti: e.reciprocal(t1[ti][:, :nt], t1[ti][:, :nt]), [('t1', ti)], [('t1', ti)])
                S.op('dve', lambda e, c=c, ti=ti: e.tensor_tensor(kkt[:, c, :nt], kkt[:, c, :nt], t1[ti][:, :nt], ALU.mult), [('kkt', c), ('t1', ti)], [('kkt', c)])
            S.op('act', lambda e: e.copy(o['kk'][:, :, :nt], kkt[:, :, :nt]), [('kkt', c) for c in range(4)], [okey['kk']])
            S.dma('sp', fmv(g.kkS)[:, :, t0:t0 + nt], o['kk'][:, :, :nt], reads=[okey['kk']], writes=['kkS'])
            for d in range(2):
                kdn, bn = 'kd%d' % d, 'b%d' % d
                for c in range(4):
                    pu, pa = nextps(g), nextps(g)
                    mm_group(S, g.ps[pu][:, :nt], [(w2b[64 * d:64 * d + 64, c * 128:(c + 1) * 128], tw[64 * d:64 * d + 64, :nt])], ['w2b', 'tw'], [('ps', pu)])
                    mm_group(S, g.ps[pa][:, :nt], [(a2b[64 * d:64 * d + 64, c * 128:(c + 1) * 128], ab[64 * d:64 * d + 64, :nt])], ['a2b', 'ab'], [('ps', pa)])
                    wl = owl[d][0]
                    wk_ = ('owl', d, 0)
                    S.op('act', lambda e, pu=pu, c=c, d=d, wl=wl: e.activation(wl[:, c, :nt], g.ps[pu][:, :nt], AF.Sigmoid,
                         bias=rwp[:, P_W0 + d * 4 + c:P_W0 + d * 4 + c + 1], scale=1.0), [('ps', pu), 'rwp'], [wk_])
                    S.op('pool', lambda e, c=c, wl=wl: e.tensor_scalar(wl[:, c, :nt], wl[:, c, :nt], DECAY_C, 0.0, ALU.mult, ALU.add), [wk_], [wk_])
                    ta, tb = t1[0], t1[1]
                    S.op('act', lambda e, pa=pa, c=c, d=d: e.activation(ta[:, :nt], g.ps[pa][:, :nt], AF.Sigmoid,
                         bias=rwp[:, P_A0 + d * 4 + c:P_A0 + d * 4 + c + 1], scale=1.0), [('ps', pa), 'rwp'], [('t1', 0)])
                    S.op('pool', lambda e, c=c, bn=bn: e.tensor_tensor(o[bn][:, c, :nt], kkt[:, c, :nt], ta[:, :nt], ALU.mult),
                         [('kkt', c), ('t1', 0)], [okey[bn]])
                    S.op('dve', lambda e, c=c: e.tensor_scalar(tb[:, :nt], ta[:, :nt], rwp[:, P_KA + c:P_KA + c + 1], rwp[:, P_OMKA + c:P_OMKA + c + 1], ALU.mult, ALU.add),
                         [('t1', 0), 'rwp'], [('t1', 1)])
                    S.op('dve', lambda e, c=c: e.tensor_tensor(tb[:, :nt], tb[:, :nt], zc[:, 4 + c, :nt], ALU.mult), [('t1', 1), ('zc', 4 + c)], [('t1', 1)])
                    S.op('act', lambda e, c=c, kdn=kdn: e.copy(o[kdn][:, c, :nt], tb[:, :nt]), [('t1', 1)], [okey[kdn]])
                    if d == 0:
                        S.op('pool', lambda e, c=c: e.tensor_copy(kds[:, c, :nt], tb[:, :nt]), [('t1', 1)], [('kds', c)])
                    else:
                        S.op('pool', lambda e, c=c: e.tensor_tensor(kds[:, c, :nt], kds[:, c, :nt], tb[:, :nt], ALU.add), [('t1', 1), ('kds', c)], [('kds', c)])
                S.dma('sp', fmv(g.wlS[d])[:, :, t0:t0 + nt], owl[d][0][:, :, :nt], reads=[('owl', d, 0)], writes=['wlS%d' % d])
                S.dma('sp', fmv(g.kdS[d])[:, :, t0:t0 + nt], o[kdn][:, :, :nt], reads=[okey[kdn]], writes=['kdS%d' % d])
                S.dma('sp', fmv(g.bS[d])[:, :, t0:t0 + nt], o[bn][:, :, :nt], reads=[okey[bn]], writes=['bS%d' % d])
            for c in range(4):
                pg = nextps(g)
                mm_group(S, g.ps[pg][:, :nt], [(g2b[:, c * 128:(c + 1) * 128], sg[:, :nt])], ['g2b', 'sg'], [('ps', pg)])
                S.op('act', lambda e, pg=pg, c=c: e.copy(o['g'][:, c, :nt], g.ps[pg][:, :nt]), [('ps', pg)], [okey['g']])
            S.dma('sp', fmv(g.gS)[:, :, t0:t0 + nt], o['g'][:, :, :nt], reads=[okey['g']], writes=['gS'])
            for c in range(4):
                tc_ = t1[2]
                S.op('dve', lambda e, c=c: e.scalar_tensor_tensor(tc_[:, :nt], zc[:, c, :nt], rwp[:, P_RK + c:P_RK + c + 1], kds[:, c, :nt], ALU.mult, ALU.mult),
                     [('zc', c), 'rwp', ('kds', c)], [('t1', 2)])
                pi = nextps(g)
                mm_group(S, g.ps[pi][:, :nt], [(bdf[:], tc_[:, :nt])], ['bdf', ('t1', 2)], [('ps', pi)])
                S.op('dve', lambda e, pi=pi, c=c: e.tensor_tensor(o['bon'][:, c, :nt], g.ps[pi][:, :nt], zc[:, 8 + c, :nt], ALU.mult),
                     [('ps', pi), ('zc', 8 + c)], [okey['bon']])
            S.dma('sp', fmv(g.bonS)[:, :, t0:t0 + nt], o['bon'][:, :, :nt], reads=[okey['bon']], writes=['bonS'])
            for tt in range(nt // 128):
                pv = nextps(g)

                def fn(pe, pv=pv, tt=tt):
                    inst = None
                    for c in range(4):
                        inst = pe.transpose(g.ps[pv][:, c * 128:(c + 1) * 128], zc[:, 8 + c, tt * 128:(tt + 1) * 128], g.identf[:])
                    return inst
                S.op('pe', fn, [('zc', 8 + c) for c in range(4)] + ['identf'], [('ps', pv)])
                vb = (t0 // 128 + tt) % 2
                S.op('act', lambda e, pv=pv, vb=vb: e.copy(vtile[vb][:], g.ps[pv][:, :]), [('ps', pv)], [('vtile', vb)])
                S.dma('sp', g.vT[t0 + tt * 128:t0 + (tt + 1) * 128, :], vtile[vb][:], reads=[('vtile', vb)], writes=['vT'])


class TagS:
    GL = {'identb', 'identf', 'rwp', 'bdf', 'smf', 'epsv', 'rS', 'kkS', 'kdS0', 'kdS1', 'bS0', 'bS1', 'wlS0', 'wlS1', 'vT', 'yfS', 'ybS', 'bonS', 'gS', 'yrT'}

    def __init__(self, S, d):
        self.S, self.d = S, d

    def t(self, k):
        if isinstance(k, tuple) and k[0] in ('ps', 'msk', 'm4', 'mnt'):
            return k
        if isinstance(k, str) and (k in self.GL or k.startswith('dbg')):
            return k
        return ('dir%d' % self.d, k)

    def op(self, e, fn, reads=(), writes=()):
        self.S.op(e, fn, [self.t(k) for k in reads], [self.t(k) for k in writes])

    def dma(self, e, out, in_, reads=(), writes=(), **kw):
        self.S.dma(e, out, in_, reads=[self.t(k) for k in reads], writes=[self.t(k) for k in writes], **kw)


def phase_rwkv_scan(g, l):
    S, nc = g.S, g.nc
    import os
    STAGE = int(os.environ.get('SCAN_STAGE', '99'))
    NCK = T // 128
    with Phase(g) as A:
        smf = A('smf', [128, 9, 128])
        msk = [A('msk%d' % i, [128, 128], BF16) for i in range(4)]
        m4 = [A('m4_%d' % d, [128, 4, 128], BF16) for d in range(2)]
        mnt = [A('mnt%d' % d, [128, 128], BF16) for d in range(2)]
        bdf = A('bdf', [128, 128]); rwp = A('rwp', [128, P_OMKA])
        rwp = A('rwp', [128, P_OMKA])
        S.dma('sp', smf[:], g.smask[0:9].rearrange('m p c -> p m c'), writes=['smf'])
        for i in range(4):
            S.op('dve', lambda e, i=i: e.tensor_copy(msk[i][:], smf[:, 5 + i, :]), ['smf'], [('msk', i)])
        S.dma('sp', bdf[:], g.cmask[0], writes=['bdf'])
        S.dma('sp', rwp[:], g.rwp[l], writes=['rwp'])
        for d in range(2):
            for j in range(4):
                S.op('dve', lambda e, d=d, j=j: e.tensor_copy(m4[d][:, j, :], smf[:, 2 * d + (j % 2), :]), ['smf'], [('m4', d)])
        S.op('dve', lambda e: e.tensor_copy(mnt[0][:], smf[:, 2, :]), ['smf'], [('mnt', 0)])
        S.op('dve', lambda e: e.tensor_copy(mnt[1][:], smf[:, 0, :]), ['smf'], [('mnt', 1)])
        fmc = lambda ap, c0: ap.rearrange('(k p) t -> p k t', p=128)[:, :, c0:c0 + 128]
        f2 = lambda t: t[:].rearrange('p a b -> p (a b)')
        segb = smf[:, 4, :].unsqueeze(1).broadcast_to([128, 4, 128])
        it = [0]

        def scan_dir(d, A=A):
            S = TagS(g.S, d)
            A0 = A
            psc = [0]

            def nextps(g_):
                i = 4 * d + psc[0]
                psc[0] = (psc[0] + 1) % 4
                return i
            A = lambda name, shape, dt=F32: A0('%s_d%d' % (name, d), shape, dt)
            NTF = [A('NTF%d' % q, [128, 4, 128], BF16) for q in range(2)]
            NF = [A('NF%d' % q, [128, 4, 128], BF16) for q in range(2)]
            NoT = [[A('NoT%d_%d' % (q, i), [128, 4, 128], BF16) for i in range(3)] for q in range(2)]
            MT = [[A('MT%d_%d' % (q, i), [128, 4, 128], BF16) for i in range(2)] for q in range(2)]
            Wb = [A('Wb%d' % q, [128, 4, 128], BF16) for q in range(2)]
            ld = {n: [A('ld_%s%d' % (n, i), [128, 4, 128], BF16) for i in range(2)] for n in ('r', 'kk', 'kd', 'b')}
            cwl = [A('cwl%d' % i, [128, 4, 128]) for i in range(2)]
            cv = [A('cv%d' % i, [128, 512], BF16) for i in range(2)]
            cum = A('cum', [128, 4, 128]); cumx = A('cumx', [128, 4, 128]); ep = A('ep', [128, 4, 128]); en = A('en', [128, 4, 128])
            AR = A('AR', [128, 4, 256], BF16); kt = A('kt', [128, 4, 128], BF16); bt = A('bt', [128, 4, 128], BF16)
            gam = A('gam', [128, 4, 1])
            AtT = A('AtT', [128, 8, 64], BF16); BtT = A('BtT', [128, 8, 64], BF16); KtT = A('KtT', [128, 8, 64], BF16)
            AB = [A('AB%d' % h, [128, 4, 128], BF16) for h in range(8)]
            X = [[A('X%d_%d' % (q, i), [128, 4, 128], BF16) for i in range(2)] for q in range(2)]
            XT = [[A('XT%d_%d' % (q, i), [128, 4, 128], BF16) for i in range(2)] for q in range(2)]
            M = [[A('M%d_%d' % (q, i), [128, 4, 128], BF16) for i in range(2)] for q in range(2)]
            G2 = A('G2', [128, 8, 64], BF16); U = A('U', [128, 8, 64]); P1 = A('P1', [128, 4, 128]); Et = A('Et', [128, 8, 64], BF16)
            ST = A('ST', [128, 4, 64]); STb = [A('STb%d' % i, [128, 4, 64], BF16) for i in range(2)]; stt = A('stt', [128, 4, 64])
            ysb = [A('ysb%d' % i, [128, 4, 128]) for i in range(2)]
            order = list(range(NCK)) if d == 0 else [1, 0] + list(range(NCK - 1, 1, -1))
            if getattr(g, 'scan_limit', None):
                order = order[:g.scan_limit]
            S.op('pool', lambda e: e.memset(ST[:], 0.0), [], ['ST'])
            sbi = 0
            S.op('pool', lambda e: e.memset(STb[0][:], 0.0), [], [('STb', 0)])
            for ci in order:
                c0 = ci * 128
                b = it[0] % 2
                it[0] += 1
                cr, ckk, ckd, cb = ld['r'][b], ld['kk'][b], ld['kd'][b], ld['b'][b]
                lk = lambda n: ('ld', n, b)
                S.dma('sp', cr[:], fmc(g.rS, c0), reads=['rS'], writes=[lk('r')])
                S.dma('sp', ckk[:], fmc(g.kkS, c0), reads=['kkS'], writes=[lk('kk')])
                S.dma('sp', ckd[:], fmc(g.kdS[d], c0), reads=['kdS%d' % d], writes=[lk('kd')])
                S.dma('sp', cb[:], fmc(g.bS[d], c0), reads=['bS%d' % d], writes=[lk('b')])
                S.dma('sp', cwl[b][:], fmc(g.wlS[d], c0), reads=['wlS%d' % d], writes=[('cwl', b)])
                S.dma('sp', cv[b][:], g.vT[c0:c0 + 128, :], reads=['vT'], writes=[('cv', b)])
                vk = ('cv', b)
                cvb = cv[b]
                yield
                S.op('pool', lambda e: e.tensor_copy(cumx[:], segb), ['smf'], ['cumx'])
                S.op('dve', lambda e, b=b: e.tensor_tensor_scan(f2(cum), f2(cumx), f2(cwl[b]), 0.0, ALU.mult, ALU.add), ['cumx', ('cwl', b)], ['cum'])
                if d == 0:
                    S.op('dve', lambda e, b=b: e.tensor_tensor(cumx[:], cum[:], cwl[b][:], ALU.subtract), ['cum', ('cwl', b)], ['cumx'])
                else:
                    S.op('dve', lambda e: e.tensor_tensor(cumx[:], cum[:, :, 127:128].broadcast_to([128, 4, 128]), cum[:], ALU.subtract), ['cum'], ['cumx'])
                    S.op('dve', lambda e, b=b: e.tensor_tensor(cum[:], cumx[:], cwl[b][:], ALU.add), ['cumx', ('cwl', b)], ['cum'])
                S.op('act', lambda e: e.activation(ep[:], cum[:], AF.Exp), ['cum'], ['ep'])
                S.op('act', lambda e: e.activation(en[:], cum[:], AF.Exp, scale=-1.0), ['cum'], ['en'])
                gcol = 127 if d == 0 else 0
                S.op('act', lambda e: e.copy(gam[:], ep[:, :, gcol:gcol + 1]), ['ep'], ['gam'])
                S.op('dve', lambda e, cr=cr: e.tensor_tensor(AR[:, :, 128:256], cr[:], ep[:], ALU.mult), [lk('r'), 'ep'], ['AR'])
                S.op('act', lambda e: e.activation(ep[:], cumx[:], AF.Exp), ['cumx', 'AR', 'gam'], ['ep'])
                S.op('dve', lambda e, ckk=ckk: e.scalar_tensor_tensor(AR[:, :, 0:128], ckk[:], -1.0, ep[:], ALU.mult, ALU.mult), [lk('kk'), 'ep'], ['AR'])
                S.op('pool', lambda e, ckd=ckd: e.tensor_tensor(kt[:], ckd[:], en[:], ALU.mult), [lk('kd'), 'en'], ['kt'])
                S.op('pool', lambda e, cb=cb: e.tensor_tensor(bt[:], cb[:], en[:], ALU.mult), [lk('b'), 'en'], ['bt'])
                if STAGE <= 1:
                    continue
                yield
                for (srcf, dst, dk, sk) in ((lambda hp: AR[:, hp, 0:128], AtT, 'AtT', 'AR'), (lambda hp: bt[:, hp, :], BtT, 'BtT', 'bt'), (lambda hp: kt[:, hp, :], KtT, 'KtT', 'kt')):
                    pi = nextps(g)
                    psb = g.ps[pi][:, :].bitcast(BF16)

                    def fn(pe, srcf=srcf, psb=psb):
                        inst = None
                        for hp in range(4):
                            inst = pe.transpose(psb[:, hp * 128:(hp + 1) * 128], srcf(hp), g.identb[:])
                        return inst
                    S.op('pe', fn, [sk, 'identb'], [('ps', pi)])
                    S.op('act', lambda e, dst=dst, psb=psb: e.copy(dst[:].rearrange('p h j -> p (h j)'), psb[:, 0:512]), [('ps', pi)], [dk])
                if STAGE <= 2:
                    continue
                yield
                ABH = int(os.environ.get('AB_H', '8')); ABM = int(os.environ.get('AB_MODE', '9'))
                for h in range(ABH):
                    hp, hb = h // 2, h % 2
                    sl = slice(hb * 64, hb * 64 + 64)
                    pi = nextps(g)
                    mm_group(S, g.ps[pi][:, 0:256], [(bt[sl, hp, :], AR[sl, hp, :])], ['bt', 'AR'], [('ps', pi)])
                    if ABM <= 1:
                        continue
                    mm_group(S, g.ps[pi][:, 256:512], [(kt[sl, hp, :], AR[sl, hp, :])], ['kt', 'AR'], [('ps', pi)])
                    if ABM <= 2:
                        continue
                    S.op('dve', lambda e, h=h, pi=pi: e.tensor_tensor(f2(AB[h]), g.ps[pi][:, :], f2(m4[d]), ALU.mult), [('ps', pi), ('m4', d)], [('AB', h)])
                SUB = int(os.environ.get('SCAN_SUB', '9'))
                if SUB <= 0:
                    continue
                b4 = lambda t: t[:].unsqueeze(1).broadcast_to([128, 4, 128])
                for q in range(2):
                    pi = nextps(g)
                    for j in range(4):
                        h = 2 * j + q
                        hp, hb = j, q
                        sl = slice(hb * 64, hb * 64 + 64)
                        mm_group(S, g.ps[pi][:, j * 128:(j + 1) * 128], [(AR[sl, hp, 0:128], bt[sl, hp, :])], ['bt', 'AR'], [('ps', pi)])
                    S.op('dve', lambda e, q=q, pi=pi: e.tensor_tensor(NTF[q][:], g.ps[pi][:, :].rearrange('p (j s) -> p j s', j=4), b4(mnt[d]), ALU.mult),
                         [('ps', pi), ('mnt', d)], [('NTF', q)])
                    for j in range(4):
                        h = 2 * j + q
                        S.op('pool', lambda e, q=q, j=j, h=h: e.tensor_copy(NF[q][:, j, :], AB[h][:, 0, :]), [('AB', h)], [('NF', q)])
                    S.op('pool', lambda e, q=q: e.tensor_tensor(X[q][0][:], NF[q][:], b4(msk[0]), ALU.mult), [('NF', q), ('msk', 0)], [('X', q, 0)])
                    S.op('dve', lambda e, q=q: e.tensor_tensor(XT[q][0][:], NTF[q][:], b4(msk[0]), ALU.mult), [('NTF', q), ('msk', 0)], [('XT', q, 0)])
                    S.op('pool', lambda e, q=q: e.tensor_tensor(M[q][0][:], X[q][0][:], b4(g.identb), ALU.add), [('X', q, 0), 'identb'], [('M', q, 0)])
                    S.op('pool', lambda e, q=q: e.tensor_tensor(MT[q][0][:], XT[q][0][:], b4(g.identb), ALU.add), [('XT', q, 0), 'identb'], [('MT', q, 0)])
                    for i in range(3):
                        S.op('pool', lambda e, q=q, i=i: e.tensor_tensor(NoT[q][i][:], NTF[q][:], b4(msk[1 + i]), ALU.mult), [('NTF', q), ('msk', 1 + i)], [('NoT', q, i)])
                if d == 0 and ci == 0:
                    for nm, tl, kk_ in (('d_XT0', NTF[0], ('NTF', 0)), ('d_X0', NF[0], ('NF', 0)), ('d_Mi', M[0][0], ('M', 0, 0))):
                        if nm in g.dbg:
                            S.dma('sp', g.dbg[nm][:, :], tl[:].rearrange('p a b -> p (a b)'), reads=[kk_], writes=['dbg' + nm])
                yield
                def mm4(q, lh, rh, rk):
                    pi = nextps(g)
                    for j in range(4):
                        mm_group(S, g.ps[pi][:, j * 128:(j + 1) * 128], [(lh[:, j, :], rh[:, j, :])], rk, [('ps', pi)])
                    return pi
                cur = 0
                for k in range(1, 4):
                    nx = 1 - cur
                    for q in range(2):
                        Xp, XTp = X[q][cur], XT[q][cur]
                        Xn, XTn = X[q][nx], XT[q][nx]
                        kX, kXT = ('X', q, cur), ('XT', q, cur)
                        nX, nXT = ('X', q, nx), ('XT', q, nx)
                        pi = mm4(q, Xp, XTp, [kX, kXT])
                        S.op('act', lambda e, XTn=XTn, pi=pi: e.copy(f2(XTn), g.ps[pi][:, :]), [('ps', pi)], [nXT])
                        if k < 3:
                            pi = mm4(q, XTp, Xp, [kX, kXT])
                            S.op('act', lambda e, Xn=Xn, pi=pi: e.copy(f2(Xn), g.ps[pi][:, :]), [('ps', pi)], [nX])
                    yield
                    for q in range(2):
                        Mp, MTp, Mn, MTn, XTn = M[q][cur], MT[q][cur], M[q][nx], MT[q][nx], XT[q][nx]
                        kM, kMT, nM, nMT, nXT = ('M', q, cur), ('MT', q, cur), ('M', q, nx), ('MT', q, nx), ('XT', q, nx)
                        pi = mm4(q, XTn, Mp, [nXT, kM])
                        S.op('dve', lambda e, Mn=Mn, Mp=Mp, pi=pi: e.tensor_tensor(f2(Mn), g.ps[pi][:, :], f2(Mp), ALU.add), [('ps', pi), kM], [nM])
                        pi = mm4(q, Mp, XTn, [nXT, kM])
                        S.op('dve', lambda e, MTn=MTn, MTp=MTp, pi=pi: e.tensor_tensor(f2(MTn), g.ps[pi][:, :], f2(MTp), ALU.add), [('ps', pi), kMT], [nMT])
                    cur = nx
                    yield
                for i in range(3):
                    nx = 1 - cur
                    for q in range(2):
                        pi = mm4(q, NoT[q][i], M[q][cur], [('NoT', q, i), ('M', q, cur)])
                        S.op('act', lambda e, q=q, pi=pi: e.copy(f2(Wb[q]), g.ps[pi][:, :]), [('ps', pi)], [('Wb', q)])
                    yield
                    for q in range(2):
                        Dp, DTp, Dn, DTn = M[q][cur], MT[q][cur], M[q][nx], MT[q][nx]
                        kD, kDT, nD, nDT = ('M', q, cur), ('MT', q, cur), ('M', q, nx), ('MT', q, nx)
                        pi = mm4(q, DTp, Wb[q], [kDT, ('Wb', q)])
                        S.op('dve', lambda e, Dn=Dn, Dp=Dp, pi=pi: e.tensor_tensor(f2(Dn), g.ps[pi][:, :], f2(Dp), ALU.add), [('ps', pi), kD], [nD])
                        if i < 2:
                            pi = mm4(q, Wb[q], DTp, [kDT, ('Wb', q)])
                            S.op('dve', lambda e, DTn=DTn, DTp=DTp, pi=pi: e.tensor_tensor(f2(DTn), g.ps[pi][:, :], f2(DTp), ALU.add), [('ps', pi), kDT], [nDT])
                    cur = nx
                    yield
                Mf = [M[q][cur] for q in range(2)]
                kMf = [('M', q, cur) for q in range(2)]
                if STAGE <= 4:
                    continue
                yield
                pi = nextps(g)
                for h in range(8):
                    mm_group(S, g.ps[pi][:, h * 64:(h + 1) * 64], [(AB[h][:, 2, :], cvb[:, h * 64:(h + 1) * 64])], [('AB', h), vk], [('ps', pi)])
                S.op('act', lambda e, pi=pi: e.copy(G2[:].rearrange('p h i -> p (h i)'), g.ps[pi][:, :]), [('ps', pi)], ['G2'])
                pi = nextps(g)
                for h in range(8):
                    mm_group(S, g.ps[pi][:, h * 64:(h + 1) * 64], [(Mf[h % 2][:, h // 2, :], G2[:, h, :])], [kMf[h % 2], 'G2'], [('ps', pi)])
                S.op('act', lambda e, pi=pi: e.copy(U[:].rearrange('p h i -> p (h i)'), g.ps[pi][:, :]), [('ps', pi)], ['U'])
                pi = nextps(g)
                for h in range(8):
                    hp, hb = h // 2, h % 2
                    mm_group(S, g.ps[pi][hb * 64:hb * 64 + 64, hp * 128:(hp + 1) * 128], [(AtT[:, h, :], Mf[h % 2][:, h // 2, :])], [kMf[h % 2], 'AtT'], [('ps', pi)])
                S.op('dve', lambda e, pi=pi: e.tensor_copy(f2(P1), g.ps[pi][:, :]), [('ps', pi)], ['P1'])
                if STAGE <= 5:
                    continue
                yield
                for hb in range(2):
                    pe_ = nextps(g)
                    sl = slice(hb * 64, hb * 64 + 64)
                    for hp in range(4):
                        mm_group(S, g.ps[pe_][:, hp * 64:(hp + 1) * 64], [(P1[sl, hp, :], ST[sl, hp, :])], ['P1', 'ST'], [('ps', pe_)])
                    S.op('dve', lambda e, pe_=pe_, hb=hb: e.tensor_tensor(Et[:].rearrange('p (hp hb) i -> p hp hb i', hb=2)[:, :, hb, :],
                         g.ps[pe_][:, 0:256].rearrange('p (hp i) -> p hp i', hp=4), U[:].rearrange('p (hp hb) i -> p hp hb i', hb=2)[:, :, hb, :], ALU.add),
                         [('ps', pe_), 'U'], ['Et'])
                if STAGE <= 6:
                    continue
                py2, pss = [nextps(g), nextps(g)], nextps(g)
                for h in range(8):
                    hp, hb = h // 2, h % 2
                    sl = slice(hb * 64, hb * 64 + 64)
                    mm_group(S, g.ps[pss][sl, hp * 64:(hp + 1) * 64], [(KtT[:, h, :], cvb[:, h * 64:(h + 1) * 64]), (BtT[:, h, :], Et[:, h, :])],
                             ['KtT', 'BtT', 'Et', vk], [('ps', pss)])
                for h in range(8):
                    hp, hb = h // 2, h % 2
                    sl = slice(hb * 64, hb * 64 + 64)
                    mm_group(S, g.ps[py2[hb]][sl, hp * 128:(hp + 1) * 128],
                             [(STb[sbi][sl, hp, :], AR[sl, hp, 128:256]), (Et[:, h, :], AB[h][:, 1, :]), (cvb[:, h * 64:(h + 1) * 64], AB[h][:, 3, :])],
                             [('STb', sbi), 'AR', 'Et', ('AB', h), vk], [('ps', py2[hb])])
                S.op('dve', lambda e, pss=pss: e.tensor_tensor(stt[:].rearrange('p a i -> p (a i)'), g.ps[pss][:, 0:256], ST[:].rearrange('p a i -> p (a i)'), ALU.add),
                     [('ps', pss), 'ST'], ['stt'])
                S.op('dve', lambda e: e.tensor_tensor(ST[:], stt[:], gam[:].broadcast_to([128, 4, 64]), ALU.mult), ['stt', 'gam'], ['ST'])
                sbi = 1 - sbi
                S.op('act', lambda e, sbi=sbi: e.copy(STb[sbi][:], ST[:]), ['ST'], [('STb', sbi)])
                if STAGE <= 7:
                    continue
                if d == 0 and ci == 0:
                    dm = {'d_AR': AR, 'd_AB0': AB[0], 'd_AB1': AB[1], 'd_M0': Mf[0], 'd_U': U, 'd_P1': P1, 'd_Et': Et, 'd_ST': ST, 'd_kt': kt, 'd_bt': bt,
                          'd_AtT': AtT, 'd_G2': G2}
                    for nm, tl in dm.items():
                        if nm in g.dbg:
                            S.dma('sp', g.dbg[nm][:, :], tl[:].rearrange('p a b -> p (a b)'), reads=['AR', ('AB', 0), ('AB', 1), kMf[0], 'U', 'P1', 'Et', 'ST', 'kt', 'bt', 'AtT', 'G2'], writes=['dbg' + nm])
                yb = ysb[ci % 2]
                for hb in range(2):
                    S.op('act', lambda e, yb=yb, hb=hb: e.copy(f2(yb)[hb * 64:hb * 64 + 64, :], g.ps[py2[hb]][hb * 64:hb * 64 + 64, :]), [('ps', py2[hb])], [('ysb', ci % 2)])
                S.dma('sp', fmc(g.yfS if d == 0 else g.ybS, c0), yb[:], reads=[('ysb', ci % 2)], writes=['yfS' if d == 0 else 'ybS'])
                yield

        gens = [scan_dir(0), scan_dir(1)]
        while gens:
            for gen in list(gens):
                try:
                    next(gen)
                except StopIteration:
                    gens.remove(gen)
    with Phase(g) as A:
        bdf = A('bdf2', [128, 128]); rwp = A('rwp2', [128, P_OMKA])
        S.dma('sp', bdf[:], g.cmask[0], writes=['bdf'])
        S.dma('sp', rwp[:], g.rwp[l], writes=['rwp'])
        fmc = lambda ap, c0: ap.rearrange('(k p) t -> p k t', p=128)[:, :, c0:c0 + 128]
        f2 = lambda t: t[:].rearrange('p a b -> p (a b)')
        yfb = [A('yf%d' % i, [128, 4, 128]) for i in range(2)]; ybb = [A('yb%d' % i, [128, 4, 128]) for i in range(2)]
        cbonb = [A('cbon%d' % i, [128, 4, 128], BF16) for i in range(2)]; cgb = [A('cg%d' % i, [128, 4, 128], BF16) for i in range(2)]
        ys = A('ys', [128, 4, 128]); ysq = A('ysq', [128, 4, 128]); gmean = A('gmean', [128, 4, 128]); grs = A('grs', [128, 4, 128]); gt_ = A('gt_', [128, 4, 128])
        yob = [A('yob%d' % i, [128, 4, 128], BF16) for i in range(2)]
        for ci in range(NCK):
            if l == DEPTH - 1 and ci < CTX // 128:
                continue
            c0 = ci * 128
            b = ci % 2
            yf, yb2, cbon, cg = yfb[b], ybb[b], cbonb[b], cgb[b]
            S.dma('sp', yf[:], fmc(g.yfS, c0), reads=['yfS'], writes=[('yf', b)])
            S.dma('sp', yb2[:], fmc(g.ybS, c0), reads=['ybS'], writes=[('yb', b)])
            S.dma('sp', cbon[:], fmc(g.bonS, c0), reads=['bonS'], writes=[('cbon', b)])
            S.dma('sp', cg[:], fmc(g.gS, c0), reads=['gS'], writes=[('cg', b)])
            S.op('dve', lambda e, yf=yf, yb2=yb2: e.tensor_tensor(ys[:], yf[:], yb2[:], ALU.add), [('yf', b), ('yb', b)], ['ys'])
            S.op('act', lambda e: e.activation(ysq[:], ys[:], AF.Square), ['ys'], ['ysq'])
            p1, p2 = nextps(g), nextps(g)
            mm_group(S, g.ps[p1][:, :], [(bdf[:], f2(ys))], ['bdf', 'ys'], [('ps', p1)])
            mm_group(S, g.ps[p2][:, :], [(bdf[:], f2(ysq))], ['bdf', 'ysq'], [('ps', p2)])
            S.op('act', lambda e, p1=p1: e.activation(f2(gmean), g.ps[p1][:, :], AF.Identity, scale=1.0 / 64), [('ps', p1)], ['gmean'])
            S.op('pool', lambda e: e.tensor_tensor(ysq[:], gmean[:], gmean[:], ALU.mult), ['gmean'], ['ysq'])
            S.op('dve', lambda e, p2=p2: e.scalar_tensor_tensor(f2(grs), g.ps[p2][:, :], 1.0 / 64, f2(ysq), ALU.mult, ALU.subtract), [('ps', p2), 'ysq'], ['grs'])
            S.op('act', lambda e: e.activation(grs[:], grs[:], AF.Sqrt, bias=g.epsv[:, 3:4], scale=1.0), ['grs', 'epsv'], ['grs'])
            S.op('dve', lambda e: e.reciprocal(grs[:], grs[:]), ['grs'], ['grs'])
            S.op('pool', lambda e: e.tensor_tensor(gt_[:], ys[:], gmean[:], ALU.subtract), ['ys', 'gmean'], ['gt_'])
            S.op('dve', lambda e: e.tensor_tensor(gt_[:], gt_[:], grs[:], ALU.mult), ['gt_', 'grs'], ['gt_'])
            S.op('pool', lambda e: e.tensor_tensor(gt_[:], gt_[:], rwp[:, P_GG:P_GG + 4].unsqueeze(2).broadcast_to([128, 4, 128]), ALU.mult), ['gt_', 'rwp'], ['gt_'])
            S.op('pool', lambda e: e.tensor_tensor(gt_[:], gt_[:], rwp[:, P_GB:P_GB + 4].unsqueeze(2).broadcast_to([128, 4, 128]), ALU.add), ['gt_', 'rwp'], ['gt_'])
            S.op('dve', lambda e, cbon=cbon: e.tensor_tensor(gt_[:], gt_[:], cbon[:], ALU.add), ['gt_', ('cbon', b)], ['gt_'])
            yo = yob[b]
            S.op('pool', lambda e, yo=yo, cg=cg: e.tensor_tensor(yo[:], gt_[:], cg[:], ALU.mult), ['gt_', ('cg', b)], [('yob', b)])
            S.dma('sp', fmc(g.yrT, c0), yo[:], reads=[('yob', b)], writes=['yrT'])


def phase_merge(g, l):
    S, nc = g.S, g.nc
    last = (l == DEPTH - 1)
    with Phase(g) as A:
        wg = A('wg', [128, NCH, 3072], BF16)
        wp = [A('wp%d' % i, [128, 4, 1024], BF16) for i in range(3)]
        wo = A('wo', [128, NCH, 1024], BF16)
        HL = HLoader(g, A)
        yb = [[A('my%d_%d' % (i, j), [128, 4, 512], BF16) for j in range(2)] for i in range(3)]
        xb = [A('mx%d' % i, [128, NCH, 512]) for i in range(2)]
        sig = [A('msig%d' % i, [128, 512]) for i in range(2)]
        tm = [A('mtm%d' % i, [128, 512]) for i in range(2)]
        macc = A('macc', [128, 512])
        mT = A('mT', [128, NCH, 512], BF16)
        wsrc = g.w_in[l].rearrange('(k p) c -> p k c', p=128)
        for i in range(2):
            S.dma('pool', wg[:, :, i * 1536:(i + 1) * 1536], wsrc[:, :, 4480 + i * 1536:4480 + (i + 1) * 1536], writes=['wg'])
        for i, nm in enumerate(('p_conv', 'p_att', 'p_rwkv')):
            S.dma('pool', wp[i][:], getattr(g, nm)[l].rearrange('(k p) c -> p k c', p=128), writes=[('wp', i)])
        S.dma('pool', wo[:], g.w_out[l].rearrange('(k p) c -> p k c', p=128), writes=['wo'])
        xTv = g.xT.rearrange('(k p) t -> p k t', p=128)
        ysrc = [g.ycT.rearrange('(k p) t -> p k t', p=128), g.yaT.rearrange('(k p) t -> p k t', p=128), g.yrT.rearrange('(k p) t -> p k t', p=128)]
        ynm = ['ycT', 'yaT', 'yrT']
        for bi, (t0, nt) in enumerate(BLKS):
            if last and t0 < CTX:
                continue
            b = bi % 2
            s = 1 if t0 < CTX else 0
            hb, hk = HL.load(bi)
            for i in range(3):
                S.dma('sp', yb[i][b][:, :, :nt], ysrc[i][:, :, t0:t0 + nt], reads=[ynm[i]], writes=[('my', i, b)])
            S.dma('sp', xb[b][:, :, :nt], xTv[:, :, t0:t0 + nt], reads=['xT'], writes=[('mx', b)])
            for oc in range(NCH):
                for i in range(3):
                    pg, pp = nextps(g), nextps(g)
                    c0 = i * 1024 + oc * 128
                    mm_group(S, g.ps[pg][:, :nt], [(wg[:, k, c0:c0 + 128], hb[:, k, :nt]) for k in range(NCH)], ['wg', hk], [('ps', pg)])
                    mm_group(S, g.ps[pp][:, :nt], [(wp[i][:, k, oc * 128:(oc + 1) * 128], yb[i][b][:, k, :nt]) for k in range(4)], [('wp', i), ('my', i, b)], [('ps', pp)])
                    sb_ = i % 2
                    S.op('act', lambda e, pg=pg, sb_=sb_: e.activation(sig[sb_][:, :nt], g.ps[pg][:, :nt], AF.Sigmoid), [('ps', pg)], [('msig', sb_)])
                    if i == 0:
                        S.op('dve', lambda e, pp=pp, sb_=sb_: e.tensor_tensor(macc[:, :nt], g.ps[pp][:, :nt], sig[sb_][:, :nt], ALU.mult), [('ps', pp), ('msig', sb_)], ['macc'])
                    else:
                        S.op('dve', lambda e, pp=pp, sb_=sb_: e.tensor_tensor(tm[sb_][:, :nt], g.ps[pp][:, :nt], sig[sb_][:, :nt], ALU.mult), [('ps', pp), ('msig', sb_)], [('mtm', sb_)])
                        if i == 1:
                            S.op('pool', lambda e, sb_=sb_: e.tensor_tensor(macc[:, :nt], macc[:, :nt], tm[sb_][:, :nt], ALU.add), ['macc', ('mtm', sb_)], ['macc'])
                        else:
                            S.op('pool', lambda e, sb_=sb_, oc=oc: e.tensor_tensor(mT[:, oc, :nt], macc[:, :nt], tm[sb_][:, :nt], ALU.add), ['macc', ('mtm', sb_)], [('mT', oc)])
            for oc in range(NCH):
                po = nextps(g)
                mm_group(S, g.ps[po][:, :nt], [(wo[:, k, oc * 128:(oc + 1) * 128], mT[:, k, :nt]) for k in range(NCH)], ['wo'] + [('mT', k) for k in range(NCH)], [('ps', po)])
                S.op('dve', lambda e, po=po, oc=oc: e.scalar_tensor_tensor(xb[b][:, oc, :nt], g.ps[po][:, :nt], g.modv[:, 2 * 8 + oc, s:s + 1], xb[b][:, oc, :nt], ALU.mult, ALU.add),
                     [('ps', po), 'modv', ('mx', b)], [('mx', b)])
            S.dma('sp', xTv[:, :, t0:t0 + nt], xb[b][:, :, :nt], reads=[('mx', b)], writes=['xT'])


def phase_mlp(g, l):
    S, nc = g.S, g.nc
    last = (l == DEPTH - 1)
    with Phase(g) as A:
        w1 = A('w1', [128, NCH, DFF], BF16)
        w2 = A('w2', [128, DFF // 128, D], BF16)
        xb = [A('fx%d' % i, [128, NCH, 512]) for i in range(1)] * 2
        rs = A('frs', [128, 512]); tmp = [A('ftmp%d' % i, [128, 512]) for i in range(2)]
        h2 = A('fh2', [128, NCH, 512], BF16)
        act = A('fact', [128, DFF // 128, 512], BF16)
        sq = act[:, 0:16, :].bitcast(F32).rearrange('p (a two) b -> p a (two b)', two=2)
        fg = A('fg', [128, NCH])
        osb = A('fosb', [128, D])
        w1src = g.mlp_w1[l].rearrange('(k p) c -> p k c', p=128)
        for i in range(4):
            S.dma('pool', w1[:, :, i * 1024:(i + 1) * 1024], w1src[:, :, i * 1024:(i + 1) * 1024], writes=['w1'])
        w2src = g.mlp_w2[l].rearrange('(k p) c -> p k c', p=128)
        for i in range(4):
            S.dma('pool', w2[:, i * 8:(i + 1) * 8, :], w2src[:, i * 8:(i + 1) * 8, :], writes=['w2'])
        S.dma('sp', fg[:], g.fing[:, :], writes=['fg'])
        xTv = g.xT.rearrange('(k p) t -> p k t', p=128)
        for bi, (t0, nt) in enumerate(BLKS):
            if last and t0 < CTX:
                continue
            b = bi % 2
            s = 1 if t0 < CTX else 0
            x = xb[b]
            xk = ('fx', 0)
            S.dma('sp', x[:, :, :nt], xTv[:, :, t0:t0 + nt], reads=['xT'], writes=[xk])
            rms_stats(g, x, nt, sq, rs, xk, 'fact', 'frs')
            for k in range(NCH):
                tb = k % 2
                S.op('dve', lambda e, k=k, tb=tb: e.tensor_tensor(tmp[tb][:, :nt], x[:, k, :nt], rs[:, :nt], ALU.mult), [xk, 'frs'], [('ftmp', tb)])
                S.op('act', lambda e, k=k, tb=tb: e.activation(h2[:, k, :nt], tmp[tb][:, :nt], AF.Identity,
                     bias=g.modv[:, 3 * 8 + k, s:s + 1], scale=g.gs[:, 1, k, s:s + 1]), [('ftmp', tb), 'modv', 'gs'], ['fh2'])
            for fc in range(DFF // 128):
                pf = nextps(g)
                tb = fc % 2
                mm_group(S, g.ps[pf][:, :nt], [(w1[:, k, fc * 128:(fc + 1) * 128], h2[:, k, :nt]) for k in range(NCH)], ['w1', 'fh2'], [('ps', pf)])
                S.op('dve', lambda e, pf=pf, tb=tb: e.tensor_scalar(tmp[tb][:, :nt], g.ps[pf][:, :nt], 0.0, None, ALU.max), [('ps', pf)], [('ftmp', tb)])
                S.op('act', lambda e, fc=fc, tb=tb: e.activation(act[:, fc, :nt], tmp[tb][:, :nt], AF.Square), [('ftmp', tb)], ['fact'])
            for oc in range(NCH):
                po = nextps(g)
                mm_group(S, g.ps[po][:, :nt], [(w2[:, fc, oc * 128:(oc + 1) * 128], act[:, fc, :nt]) for fc in range(DFF // 128)],
                         ['w2', 'fact'], [('ps', po)])
                S.op('dve', lambda e, po=po, oc=oc: e.scalar_tensor_tensor(x[:, oc, :nt], g.ps[po][:, :nt], g.modv[:, 5 * 8 + oc, s:s + 1], x[:, oc, :nt], ALU.mult, ALU.add),
                     [('ps', po), 'modv', xk], [xk])
            if not last:
                S.dma('sp', xTv[:, :, t0:t0 + nt], x[:, :, :nt], reads=[xk], writes=['xT'])
                continue
            rms_stats(g, x, nt, sq, rs, xk, 'fact', 'frs')
            for k in range(NCH):
                S.op('dve', lambda e, k=k: e.tensor_tensor(sq[:, k, :nt], x[:, k, :nt], rs[:, :nt], ALU.mult), [xk, 'frs', 'fact'], ['fact'])
                S.op('act', lambda e, k=k: e.activation(sq[:, k, :nt], sq[:, k, :nt], AF.Identity, scale=fg[:, k:k + 1]), ['fact', 'fg'], ['fact'])
            for tt in range(nt // 128):
                for half in range(2):
                    pi = nextps(g)

                    def fn(pe, pi=pi, half=half, tt=tt):
                        inst = None
                        for j in range(4):
                            inst = pe.transpose(g.ps[pi][:, j * 128:(j + 1) * 128], sq[:, half * 4 + j, tt * 128:(tt + 1) * 128], g.identf[:])
                        return inst
                    S.op('pe', fn, ['fact', 'identf'], [('ps', pi)])
                    S.op('act', lambda e, pi=pi, half=half: e.copy(osb[:, half * 512:(half + 1) * 512], g.ps[pi][:, :]), [('ps', pi)], ['fosb'])
                r0 = t0 - CTX + tt * 128
                S.dma('sp', g.out[r0:r0 + 128, :], osb[:], reads=['fosb'], writes=['out'])


_CACHE = {}


def kernel(**inputs):
    inp = {k: np.asarray(v) for k, v in inputs.items()}
    if 'nc' not in _CACHE:
        _CACHE['nc'] = build()
    nc = _CACHE['nc']
    shared = host_shared(inp)
    B = inp['x'].shape[0]
    in_maps = [host_inputs(inp, b, shared) for b in range(B)]
    res = run_bass_kernel_spmd(nc, in_maps, core_ids=list(range(B)))
    return np.stack([np.asarray(res.results[b]['out']) for b in range(B)]).astype(np.float32)
```

```python
import math
import numpy as np
import concourse.bass as bass
import concourse.mybir as mybir
from concourse.bass_utils import run_bass_kernel_spmd

F32 = mybir.dt.float32
BF16 = mybir.dt.bfloat16
AF = mybir.ActivationFunctionType
ALU = mybir.AluOpType
AX = mybir.AxisListType

D = 1024
SEQ = 4096
CTX = 256
T = SEQ + CTX
DEPTH = 2
DIN = 7552
DFF = 4096
NCH = D // 128
BLKS = [(0, 256)] + [(256 + 512 * i, 512) for i in range(8)]
NORM_EPS = 1e-6
LN_EPS = 1e-5
SUBLN_EPS = 1e-5
GN_EPS = 64e-5


class Sched:
    def __init__(self, nc, n_dma=24):
        self.nc = nc
        self.eng = dict(pe=nc.tensor, dve=nc.vector, act=nc.scalar, pool=nc.gpsimd, sp=nc.sync)
        self.sem = {e: nc.alloc_semaphore('sem_' + e) for e in self.eng}
        self.cnt = {e: 0 for e in self.eng}
        self.dsem = [nc.alloc_semaphore('dsem%d' % i) for i in range(n_dma)]
        self.dval = [0] * n_dma
        self.drr = 0
        self.seen = {e: {} for e in self.eng}
        self.lastw = {}
        self.readers = {}
        self.nps = 0

    def _semh(self, key):
        if isinstance(key, tuple):
            return self.swsem[key]
        return self.sem[key] if isinstance(key, str) else self.dsem[key]

    def _wait(self, e, key, val):
        if self.seen[e].get(key, 0) >= val:
            return
        self.eng[e].wait_ge(self._semh(key), val)
        self.seen[e][key] = val

    def _deps(self, e, reads, writes):
        need = {}
        for r in reads:
            tok = self.lastw.get(r)
            if tok is not None:
                need[tok[0]] = max(need.get(tok[0], 0), tok[1])
        for w in writes:
            tok = self.lastw.get(w)
            if tok is not None:
                need[tok[0]] = max(need.get(tok[0], 0), tok[1])
            for k, v in self.readers.get(w, {}).items():
                need[k] = max(need.get(k, 0), v)
        for k, v in need.items():
            self._wait(e, k, v)

    def _commit(self, tok, reads, writes):
        for w in writes:
            self.lastw[w] = tok
            self.readers[w] = {}
        for r in reads:
            if r in writes:
                continue
            d = self.readers.setdefault(r, {})
            d[tok[0]] = max(d.get(tok[0], 0), tok[1])

    def op(self, e, fn, reads=(), writes=()):
        if e == 'pe':
            self.seen[e]['pe'] = self.cnt['pe']
        self._deps(e, reads, writes)
        inst = fn(self.eng[e])
        self.cnt[e] += 1
        inst.then_inc(self.sem[e], 1)
        self._commit((e, self.cnt[e]), reads, writes)

    def dma(self, e, out, in_, reads=(), writes=(), **kw):
        self._deps(e, reads, writes)
        if e == 'pool':
            if not hasattr(self, 'swsem'):
                self.swsem = {}
            key = ('swd', len(self.swsem))
            self.swsem[key] = self.nc.alloc_semaphore('swd%d' % len(self.swsem))
            self.eng[e].dma_start(out=out, in_=in_, **kw).then_inc(self.swsem[key], 16)
            self._commit((key, 16), reads, writes)
            return
        i = self.drr
        self.drr = (self.drr + 1) % len(self.dsem)
        if self.dval[i] > 0:
            self._wait(e, i, self.dval[i])
        self.dval[i] += 16
        self.eng[e].dma_start(out=out, in_=in_, **kw).then_inc(self.dsem[i], 16)
        self._commit((i, self.dval[i]), reads, writes)

    def barrier(self):
        for e in self.eng:
            for i, v in enumerate(self.dval):
                if v > 0:
                    self._wait(e, i, v)
            for key in getattr(self, 'swsem', {}):
                self._wait(e, key, 16)
            for k in self.eng:
                if k != e and self.cnt[k] > 0:
                    self._wait(e, k, self.cnt[k])

    def finish(self, e='sp'):
        for i, v in enumerate(self.dval):
            if v > 0:
                self._wait(e, i, v)
        for key in getattr(self, 'swsem', {}):
            self._wait(e, key, 16)
        for k in self.eng:
            if k != e and self.cnt[k] > 0:
                self._wait(e, k, self.cnt[k])


class Ctx:
    pass


class Phase:
    uid = 0

    def __init__(self, g):
        self.g = g
        self.guards = []

    def __enter__(self):
        return self

    def __call__(self, name, shape, dt=F32):
        Phase.uid += 1
        gd = self.g.nc.sbuf_tensor('%s_u%d' % (name, Phase.uid), list(shape), dt)
        t = gd.__enter__()
        self.guards.append(gd)
        return t

    def __exit__(self, *a):
        self.g.S.barrier()
        for gd in reversed(self.guards):
            gd.__exit__(None, None, None)
        return False


def mm_group(S, out, pairs, reads, writes, start=True, stop=True):
    n = len(pairs)

    def fn(pe):
        inst = None
        for i, (l, r) in enumerate(pairs):
            inst = pe.matmul(out, l, r, start=(start and i == 0), stop=(stop and i == n - 1))
        return inst
    S.op('pe', fn, reads, writes)


def build(dbg=(), upto='all'):
    nc = bass.Bass("TRN2", target_bir_lowering=False)
    S = Sched(nc)
    g = Ctx()
    g.nc, g.S = nc, S
    din = lambda name, shape, dt=F32: nc.dram_tensor(name, list(shape), dt, kind="ExternalInput").ap()
    dint = lambda name, shape, dt=F32: nc.dram_tensor(name, list(shape), dt, kind="Internal").ap()
    g.x = din('x', [SEQ, D]); g.ctx = din('ctx', [CTX, D])
    g.cvec = din('cvec', [128, NCH, 2])
    g.mod_w = din('mod_w', [DEPTH, D, 6 * D]); g.modb = din('modb', [DEPTH, 128, 48])
    g.n1g = din('n1g', [DEPTH, 128, NCH]); g.n2g = din('n2g', [DEPTH, 128, NCH]); g.fing = din('fing', [128, NCH])
    g.ident = din('ident', [128, 128])
    g.w_in = din('w_in', [DEPTH, D, DIN])
    g.convw = din('convw', [DEPTH, 128, 4, 31]); g.convp = din('convp', [DEPTH, 128, 3, 4])
    g.wqks = din('wqks', [DEPTH, D, 1024]); g.rope = din('rope', [2, 128, SEQ]); g.cmask = din('cmask', [3, 128, 128])
    g.attp = din('attp', [DEPTH, 4, 64]); g.subg = din('subg', [DEPTH, 128])
    g.rw_w2 = din('rw_w2', [DEPTH, 2, 64, 512]); g.rw_a2 = din('rw_a2', [DEPTH, 2, 64, 512]); g.rw_g2 = din('rw_g2', [DEPTH, 128, 512])
    g.rwp = din('rwp', [DEPTH, 128, P_OMKA]); g.smask = din('smask', [9, 128, 128])
    g.rS = dint('rS', [512, T], BF16); g.kkS = dint('kkS', [512, T], BF16); g.gS = dint('gS', [512, T], BF16); g.bonS = dint('bonS', [512, T], BF16)
    g.kdS = [dint('kdS%d' % d, [512, T], BF16) for d in range(2)]; g.bS = [dint('bS%d' % d, [512, T], BF16) for d in range(2)]
    g.wlS = [dint('wlS%d' % d, [512, T]) for d in range(2)]; g.vT = dint('vT', [T, 512], BF16); g.yfS = dint('yfS', [512, T]); g.ybS = dint('ybS', [512, T])
    g.p_conv = din('p_conv', [DEPTH, 512, D]); g.p_att = din('p_att', [DEPTH, 512, D]); g.p_rwkv = din('p_rwkv', [DEPTH, 512, D])
    g.w_out = din('w_out', [DEPTH, D, D]); g.mlp_w1 = din('mlp_w1', [DEPTH, D, DFF]); g.mlp_w2 = din('mlp_w2', [DEPTH, DFF, D])
    g.out = nc.dram_tensor('out', [SEQ, D], F32, kind="ExternalOutput").ap()
    g.xT = dint('xT', [D, T]); g.hT = dint('hTd', [D, T], BF16)
    g.ycT = dint('ycT', [512, T], BF16); g.yaT = dint('yaT', [512, T], BF16); g.yrT = dint('yrT', [512, T], BF16)
    g.dbg = {}
    for name, shape, dt in dbg:
        g.dbg[name] = nc.dram_tensor('dbg_' + name, list(shape), dt, kind="ExternalOutput").ap()

    sb = lambda name, shape, dt=F32: nc.alloc_sbuf_tensor(name, list(shape), dt)
    g.ps = [nc.alloc_psum_tensor('ps%d' % i, [128, 512], F32) for i in range(8)]
    g.psi = 0

    g.identf = sb('identf', [128, 128]); g.identb = sb('identb', [128, 128], BF16)
    g.onesf = sb('onesf', [128, 128]); g.onesb = sb('onesb', [128, 128], BF16)
    g.epsv = sb('epsv', [128, 4])
    g.cs = sb('cs', [128, NCH, 2]); g.modv = sb('modv', [128, 48, 2]); g.gs = sb('gs', [128, 2, NCH, 2])
    S.dma('sp', g.identf[:], g.ident[:, :], writes=['identf'])
    S.op('dve', lambda e: e.tensor_copy(g.identb[:], g.identf[:]), ['identf'], ['identb'])
    S.op('dve', lambda e: e.memset(g.onesf[:], 1.0), [], ['onesf'])
    S.op('dve', lambda e: e.memset(g.onesb[:], 1.0), [], ['onesb'])
    for i, v in enumerate((NORM_EPS, LN_EPS, SUBLN_EPS, GN_EPS)):
        S.op('dve', lambda e, i=i, v=v: e.memset(g.epsv[:, i:i + 1], v), [], ['epsv'])

    import os
    if os.environ.get('SCAN_LIMIT'):
        g.scan_limit = int(os.environ['SCAN_LIMIT'])
    if upto.startswith('rwonly'):
        g.scan_limit = int(upto[6:] or 0)
        phase_rwkv_scan(g, 0)
        S.finish('sp')
        return nc
    phase_x0(g)
    for l in range(DEPTH):
        phase_mod(g, l)
        phase_h(g, l, 0)
        if upto == 'h':
            break
        phase_conv(g, l)
        if upto == 'conv':
            break
        phase_att(g, l)
        if upto == 'att':
            break
        phase_rwkv_prep(g, l)
        if upto == 'rwprep':
            break
        phase_rwkv_scan(g, l)
        if upto == 'rw':
            break
        phase_merge(g, l)
        phase_mlp(g, l)
        if upto == 'l0':
            break
    for nm in ('ycT', 'yaT', 'yrT', 'hT', 'rS', 'kkS', 'gS', 'bonS', 'vT', 'yfS'):
        if nm in g.dbg:
            S.dma('sp', g.dbg[nm][:, :], getattr(g, nm)[:, :], reads=[nm], writes=['dbg_' + nm])
    for nm, ap in (('kdS0', g.kdS[0]), ('bS0', g.bS[0]), ('wlS0', g.wlS[0])):
        if nm in g.dbg:
            S.dma('sp', g.dbg[nm][:, :], ap[:, :], reads=[nm], writes=['dbg_' + nm])
    S.finish('sp')
    return nc


def nextps(g):
    i = g.psi
    g.psi = (g.psi + 1) % 8
    return i


def phase_x0(g):
    S, nc = g.S, g.nc
    with Phase(g) as A:
        xin = [A('x0in%d' % i, [128, D]) for i in range(2)]
        xst = [A('x0st%d' % i, [128, NCH, 128]) for i in range(2)]
        xTv = g.xT.rearrange('(k p) t -> p k t', p=128)
        for ti in range(T // 128):
            b = ti % 2
            src = g.ctx[ti * 128:(ti + 1) * 128, :] if ti < 2 else g.x[(ti - 2) * 128:(ti - 1) * 128, :]
            S.dma('sp', xin[b][:], src, writes=[('x0in', b)])
            for half in range(2):
                pi = nextps(g)
                ps = g.ps[pi]

                def fn(pe, half=half, ps=ps, b=b):
                    inst = None
                    for j in range(4):
                        k = half * 4 + j
                        inst = pe.transpose(ps[:, j * 128:(j + 1) * 128], xin[b][:, k * 128:(k + 1) * 128], g.identf[:])
                    return inst
                S.op('pe', fn, [('x0in', b), 'identf'], [('ps', pi)])
                dst = xst[b][:, half * 4:(half + 1) * 4, :]
                src_ps = ps[:].rearrange('p (j t) -> p j t', j=4)
                if half == 0:
                    S.op('act', lambda e, dst=dst, s=src_ps: e.copy(dst, s), [('ps', pi)], [('x0st', b, half)])
                else:
                    S.op('dve', lambda e, dst=dst, s=src_ps: e.tensor_copy(dst, s), [('ps', pi)], [('x0st', b, half)])
            S.dma('sp', xTv[:, :, ti * 128:(ti + 1) * 128], xst[b][:], reads=[('x0st', b, 0), ('x0st', b, 1)], writes=['xT'])


def phase_mod(g, l):
    S, nc = g.S, g.nc
    with Phase(g) as A:
        modbs = A('modbs', [128, 48])
        mw = [A('mw%d' % i, [128, NCH, 512]) for i in range(2)]
        ng = A('ng', [128, 2, NCH])
        if l == 0:
            tmp = A('cs_tmp', [128, NCH, 2])
            S.dma('sp', tmp[:], g.cvec[:, :, :], writes=['cs_tmp'])
            S.op('act', lambda e: e.activation(g.cs[:], tmp[:], AF.Sigmoid), ['cs_tmp'], ['cs'])
            S.op('dve', lambda e: e.tensor_tensor(g.cs[:], g.cs[:], tmp[:], ALU.mult), ['cs', 'cs_tmp'], ['cs'])
        S.dma('sp', modbs[:], g.modb[l], writes=['modbs'])
        S.dma('sp', ng[:, 0, :], g.n1g[l], writes=['ng'])
        S.dma('sp', ng[:, 1, :], g.n2g[l], writes=['ng'])
        mwv = g.mod_w[l].rearrange('(k p) c -> p k c', p=128)
        for cg in range(12):
            b = cg % 2
            S.dma('sp', mw[b][:], mwv[:, :, cg * 512:(cg + 1) * 512], writes=[('mw', b)])
            pi = nextps(g)
            ps = g.ps[pi]
            for j in range(4):
                pairs = [(mw[b][:, k, j * 128:(j + 1) * 128], g.cs[:, k, :]) for k in range(NCH)]
                mm_group(S, ps[:, 2 * j:2 * j + 2], pairs, [('mw', b), 'cs'], [('ps', pi)])
            for j in range(4):
                jj = cg * 4 + j
                S.op('dve', lambda e, j=j, jj=jj, ps=ps: e.tensor_scalar(g.modv[:, jj, :], ps[:, 2 * j:2 * j + 2],
                     modbs[:, jj:jj + 1], None, ALU.add), [('ps', pi), 'modbs'], ['modv'])
        for n in range(2):
            sc = g.modv[:, (3 * n + 1) * 8:(3 * n + 2) * 8, :]
            S.op('dve', lambda e, n=n, sc=sc: e.tensor_scalar(g.gs[:, n], sc, 1.0, None, ALU.add), ['modv'], ['gs'])
            S.op('dve', lambda e, n=n: e.tensor_tensor(g.gs[:, n], g.gs[:, n],
                 ng[:, n, :].unsqueeze(2).broadcast_to([128, NCH, 2]), ALU.mult), ['gs', 'ng'], ['gs'])


def rms_stats(g, xb, n, sq, rstd, key_x, key_sq, key_rstd):
    S = g.S
    S.op('act', lambda e: e.activation(sq[:, :, :n], xb[:, :, :n], AF.Square), [key_x], [key_sq])
    pi = nextps(g)
    ps = g.ps[pi]
    mm_group(S, ps[:, :n], [(g.onesf[:], sq[:, k, :n]) for k in range(NCH)], [key_sq, 'onesf'], [('ps', pi)])
    S.op('act', lambda e: e.activation(rstd[:, :n], ps[:, :n], AF.Sqrt, bias=g.epsv[:, 0:1], scale=1.0 / D),
         [('ps', pi), 'epsv'], [key_rstd])
    S.op('dve', lambda e: e.reciprocal(rstd[:, :n], rstd[:, :n]), [key_rstd], [key_rstd])


def phase_h(g, l, n):
    S, nc = g.S, g.nc
    with Phase(g) as A:
        hx = [A('hx%d' % i, [128, NCH, 512]) for i in range(2)]
        hsq = A('hsq', [128, NCH, 512])
        hrs = [A('hrs%d' % i, [128, 512]) for i in range(2)]
        htmp = [A('htmp%d' % i, [128, 512]) for i in range(2)]
        hb = [A('hb%d' % i, [128, NCH, 512], BF16) for i in range(2)]
        xTv = g.xT.rearrange('(k p) t -> p k t', p=128)
        hTv = g.hT.rearrange('(k p) t -> p k t', p=128)
        for bi, (t0, nt) in enumerate(BLKS):
            b = bi % 2
            s = 1 if t0 < CTX else 0
            S.dma('sp', hx[b][:, :, :nt], xTv[:, :, t0:t0 + nt], reads=['xT'], writes=[('hx', b)])
            rms_stats(g, hx[b], nt, hsq, hrs[b], ('hx', b), 'hsq', ('hrs', b))
            for k in range(NCH):
                tb = k % 2
                S.op('dve', lambda e, k=k, tb=tb: e.tensor_tensor(htmp[tb][:, :nt], hx[b][:, k, :nt], hrs[b][:, :nt], ALU.mult),
                     [('hx', b), ('hrs', b)], [('htmp', tb)])
                S.op('act', lambda e, k=k, tb=tb: e.activation(hb[b][:, k, :nt], htmp[tb][:, :nt], AF.Identity,
                     bias=g.modv[:, (3 * n) * 8 + k, s:s + 1], scale=g.gs[:, n, k, s:s + 1]),
                     [('htmp', tb), 'modv', 'gs'], [('hb', b)])
            S.dma('sp', hTv[:, :, t0:t0 + nt], hb[b][:, :, :nt], reads=[('hb', b)], writes=['hT'])


class HLoader:
    def __init__(self, g, A):
        self.g = g
        self.t = [A('hblk%d' % i, [128, NCH, 512], BF16) for i in range(2)]
        self.i = 0

    def load(self, bi):
        g = self.g
        b = self.i
        self.i = (b + 1) % 2
        t0, nt = BLKS[bi]
        hTv = g.hT.rearrange('(k p) t -> p k t', p=128)
        g.S.dma('sp', self.t[b][:, :, :nt], hTv[:, :, t0:t0 + nt], reads=['hT'], writes=[('hblk', b)])
        return self.t[b], ('hblk', b)


def ucol(t):
    return t + 15 if t < CTX else t + 45


def phase_conv(g, l):
    S, nc = g.S, g.nc
    with Phase(g) as A:
        wcv = A('wcv', [128, NCH, 1024], BF16)
        uT = A('uT', [128, 4, T + 60], BF16)
        diag = A('diag', [128, 4, 31, 128], BF16)
        dww = A('dww', [128, 4, 31])
        cvp = A('cvp', [128, 3, 4])
        csg = [A('csg%d' % i, [128, 512]) for i in range(2)]
        cv = A('cv', [128, 4, 512]); cv2 = A('cv2', [128, 4, 512])
        cm = A('cm', [128, 512]); cmsq = A('cmsq', [128, 512]); crs = A('crs', [128, 512])
        ct = [A('ct%d' % i, [128, 512]) for i in range(2)]
        cyb = [A('cyb%d' % i, [128, 4, 512], BF16) for i in range(2)]
        HL = HLoader(g, A)
        S.op('pool', lambda e: e.memset(uT[:], 0.0), [], ['uT'])
        S.dma('pool', wcv[:], g.w_in[l][:, 0:1024].rearrange('(k p) c -> p k c', p=128), writes=['wcv'])
        S.dma('sp', dww[:], g.convw[l], writes=['dww'])
        S.dma('sp', cvp[:], g.convp[l], writes=['cvp'])
        for c in range(4):
            S.op('dve', lambda e, c=c: e.tensor_tensor(diag[:, c], g.identf[:].unsqueeze(1).broadcast_to([128, 31, 128]),
                 dww[:, c, :].unsqueeze(2).broadcast_to([128, 31, 128]), ALU.mult), ['identf', 'dww'], ['diag'])
        for bi, (t0, nt) in enumerate(BLKS):
            hb, hk = HL.load(bi)
            for c in range(4):
                pa, pb = nextps(g), nextps(g)
                mm_group(S, g.ps[pa][:, :nt], [(wcv[:, k, c * 128:(c + 1) * 128], hb[:, k, :nt]) for k in range(NCH)],
                         ['wcv', hk], [('ps', pa)])
                mm_group(S, g.ps[pb][:, :nt], [(wcv[:, k, 512 + c * 128:512 + (c + 1) * 128], hb[:, k, :nt]) for k in range(NCH)],
                         ['wcv', hk], [('ps', pb)])
                sb_ = c % 2
                S.op('act', lambda e, pb=pb, sb_=sb_: e.activation(csg[sb_][:, :nt], g.ps[pb][:, :nt], AF.Sigmoid),
                     [('ps', pb)], [('csg', sb_)])
                S.op('dve', lambda e, pa=pa, sb_=sb_, c=c: e.tensor_tensor(uT[:, c, ucol(t0):ucol(t0) + nt], g.ps[pa][:, :nt],
                     csg[sb_][:, :nt], ALU.mult), [('ps', pa), ('csg', sb_)], ['uT'])
        ycv = g.ycT.rearrange('(k p) t -> p k t', p=128)
        for bi, (t0, nt) in enumerate(BLKS):
            yb = cyb[bi % 2]
            ykey = ('cyb', bi % 2)
            for c in range(4):
                pi = nextps(g)
                base = ucol(t0) - 15
                mm_group(S, g.ps[pi][:, :nt], [(diag[:, c, k, :], uT[:, c, base + k:base + k + nt]) for k in range(31)],
                         ['diag', 'uT'], [('ps', pi)])
                S.op('act', lambda e, pi=pi, c=c: e.activation(cv[:, c, :nt], g.ps[pi][:, :nt], AF.Identity,
                     bias=cvp[:, 0, c:c + 1], scale=1.0), [('ps', pi), 'cvp'], [('cv', c)])
                S.op('act', lambda e, pi=pi, c=c: e.activation(cv2[:, c, :nt], g.ps[pi][:, :nt], AF.Square,
                     bias=cvp[:, 0, c:c + 1], scale=1.0), [('ps', pi), 'cvp'], [('cv2', c)])
            p1, p2 = nextps(g), nextps(g)
            mm_group(S, g.ps[p1][:, :nt], [(g.onesf[:], cv[:, c, :nt]) for c in range(4)], [('cv', c) for c in range(4)] + ['onesf'], [('ps', p1)])
            mm_group(S, g.ps[p2][:, :nt], [(g.onesf[:], cv2[:, c, :nt]) for c in range(4)], [('cv2', c) for c in range(4)] + ['onesf'], [('ps', p2)])
            S.op('act', lambda e: e.activation(cm[:, :nt], g.ps[p1][:, :nt], AF.Identity, scale=1.0 / 512), [('ps', p1)], ['cm'])
            S.op('dve', lambda e: e.tensor_tensor(cmsq[:, :nt], cm[:, :nt], cm[:, :nt], ALU.mult), ['cm'], ['cmsq'])
            S.op('dve', lambda e: e.scalar_tensor_tensor(crs[:, :nt], g.ps[p2][:, :nt], 1.0 / 512, cmsq[:, :nt], ALU.mult, ALU.subtract),
                 [('ps', p2), 'cmsq'], ['crs'])
            S.op('act', lambda e: e.activation(crs[:, :nt], crs[:, :nt], AF.Sqrt, bias=g.epsv[:, 1:2], scale=1.0), ['crs', 'epsv'], ['crs'])
            S.op('dve', lambda e: e.reciprocal(crs[:, :nt], crs[:, :nt]), ['crs'], ['crs'])
            for c in range(4):
                tb = c % 2
                S.op('dve', lambda e, c=c, tb=tb: e.tensor_tensor(ct[tb][:, :nt], cv[:, c, :nt], cm[:, :nt], ALU.subtract),
                     [('cv', c), 'cm'], [('ct', tb)])
                S.op('dve', lambda e, c=c, tb=tb: e.tensor_tensor(ct[tb][:, :nt], ct[tb][:, :nt], crs[:, :nt], ALU.mult),
                     [('ct', tb), 'crs'], [('ct', tb)])
                S.op('act', lambda e, c=c, tb=tb: e.activation(yb[:, c, :nt], ct[tb][:, :nt], AF.Silu,
                     bias=cvp[:, 2, c:c + 1], scale=cvp[:, 1, c:c + 1]), [('ct', tb), 'cvp'], [ykey])
            S.dma('sp', ycv[:, :, t0:t0 + nt], yb[:, :, :nt], reads=[ykey], writes=['ycT'])


def fm(v, nch):
    return np.ascontiguousarray(np.asarray(v, np.float32).reshape(nch, 128).T)


def host_shared(inp):
    m = {}
    m['mod_w'] = np.ascontiguousarray(inp['mod_w'], dtype=np.float32)
    m['modb'] = np.stack([fm(inp['mod_b'][l], 48) for l in range(DEPTH)])
    m['n1g'] = np.stack([fm(inp['norm1_g'][l], NCH) for l in range(DEPTH)])
    m['n2g'] = np.stack([fm(inp['norm2_g'][l], NCH) for l in range(DEPTH)])
    m['fing'] = fm(inp['final_g'], NCH)
    m['ident'] = np.eye(128, dtype=np.float32)
    m['w_in'] = np.ascontiguousarray(inp['w_in'], dtype=np.float32)
    sw = np.arange(1024) ^ 1
    m['wqks'] = np.ascontiguousarray(np.asarray(inp['w_in'])[:, :, 1024:2048][:, :, sw], dtype=np.float32)
    tt = np.arange(SEQ)
    inv = (10000.0 ** (-np.arange(16, dtype=np.float32) / 16)).astype(np.float32)
    ang = np.concatenate([(tt // 64).astype(np.float32)[:, None] * inv, (tt % 64).astype(np.float32)[:, None] * inv], axis=-1)
    pidx = (np.arange(128) % 64) // 2
    cosT = np.cos(ang)[:, pidx].T
    sinT = np.sin(ang)[:, pidx].T * np.where(np.arange(128) % 2 == 0, -1.0, 1.0)[:, None]
    m['rope'] = np.ascontiguousarray(np.stack([cosT, sinT]), dtype=np.float32)
    blk = (np.arange(128) // 64)
    bdm = (blk[:, None] == blk[None, :]).astype(np.float32)
    sel0 = np.repeat((blk == 0).astype(np.float32)[:, None], 128, 1)
    sel1 = np.repeat((blk == 1).astype(np.float32)[:, None], 128, 1)
    m['cmask'] = np.ascontiguousarray(np.stack([bdm, sel0, sel1]))
    m['attp'] = np.ascontiguousarray(np.stack([np.stack([inp[k][l] for k in ('att_lq1', 'att_lk1', 'att_lq2', 'att_lk2')]) for l in range(DEPTH)]), dtype=np.float32)
    m['subg'] = np.ascontiguousarray(inp['att_subln_g'], dtype=np.float32)
    for k in ('p_conv', 'p_att', 'p_rwkv', 'w_out', 'mlp_w1', 'mlp_w2'):
        m[k] = np.ascontiguousarray(inp[k], dtype=np.float32)
    m['rw_w2'] = np.ascontiguousarray(inp['rwkv_w2'], dtype=np.float32)
    m['rw_a2'] = np.ascontiguousarray(inp['rwkv_a2'], dtype=np.float32)
    m['rw_g2'] = np.ascontiguousarray(inp['rwkv_g2'], dtype=np.float32)
    rwp = []
    for l in range(DEPTH):
        cols = [np.asarray(inp['rwkv_shift'][l]).T.reshape(15, 128, 3).transpose(1, 0, 2).reshape(128, 45)]
        cols += [fm(inp['rwkv_w0'][l].reshape(-1), 8), fm(inp['rwkv_a0'][l].reshape(-1), 8)]
        cols += [fm(inp[k][l].reshape(-1), 4) for k in ('rwkv_kk', 'rwkv_ka', 'rwkv_rk', 'rwkv_gn_g', 'rwkv_gn_b')]
        rwp.append(np.concatenate(cols, axis=1))
    m['rwp'] = np.ascontiguousarray(np.stack(rwp), dtype=np.float32)
    ii = np.arange(128)
    lt = (ii[:, None] < ii[None, :]).astype(np.float32); le = (ii[:, None] <= ii[None, :]).astype(np.float32)
    seg = np.repeat((ii != 0).astype(np.float32)[None, :], 128, 0)
    blk = lambda n: (ii[:, None] // n == ii[None, :] // n)
    offm = lambda n: (blk(n) & ~blk(n // 2)).astype(np.float32)
    m['smask'] = np.ascontiguousarray(np.stack([lt, le, lt.T, le.T, seg, blk(16).astype(np.float32), offm(32), offm(64), offm(128)]))
    m['convw'] = np.stack([np.ascontiguousarray(np.asarray(inp['conv_dw_w'][l]).T.reshape(4, 128, 31).transpose(1, 0, 2)) for l in range(DEPTH)])
    m['convp'] = np.stack([np.stack([fm(inp[k][l], 4) for k in ('conv_dw_b', 'conv_ln_g', 'conv_ln_b')], axis=1) for l in range(DEPTH)])
    return m


def host_inputs(inp, b, shared=None):
    m = dict(shared if shared is not None else host_shared(inp))
    m['x'] = np.ascontiguousarray(inp['x'][b], dtype=np.float32)
    m['ctx'] = np.ascontiguousarray(inp['ctx'][b], dtype=np.float32)
    m['cvec'] = np.ascontiguousarray(np.stack([fm(inp['c'][b], NCH), fm(inp['c_ctx'], NCH)], axis=-1))
    return m


def phase_att(g, l):
    S, nc = g.S, g.nc
    lam_init = 0.8 - 0.6 * math.exp(-0.3 * l)
    need_ctx_q = l < DEPTH - 1
    with Phase(g) as A:
        qT = A('qT', [128, 4, T], BF16); kT = A('kT', [128, 4, T], BF16)
        vaug = A('vaug', [128, T // 128, 4, 129], BF16)
        nb = A('nb', [128, 2, 4]); neglam = A('neglam', [128, 1]); gsub = A('gsub', [128, 128])
        A1 = Phase(g)
        wq = A1('wq', [128, NCH, 512], BF16); wqs = A1('wqs', [128, NCH, 512], BF16)
        wk = A1('wk', [128, NCH, 512], BF16); wks = A1('wks', [128, NCH, 512], BF16)
        wv = A1('wv', [128, NCH, 512], BF16)
        cosT = A1('cosT', [128, SEQ]); sinT = A1('sinT', [128, SEQ])
        HL = HLoader(g, A1)
        rt = [A1('rt%d' % i, [128, 512]) for i in range(4)]
        bd = A1('bd', [128, 128], BF16); cmf = A1('cmf', [128, 3, 128])
        stat = A1('stat', [128, 2, 4, len(BLKS)]); stm = A1('stm', [128, 2, 4]); negb = A1('negb', [128, 4])
        lqk = A1('lqk', [128, 4, 64]); lam2 = A1('lam2', [128, 2])

        wsrc = g.w_in[l].rearrange('(k p) c -> p k c', p=128)
        ssrc = g.wqks[l].rearrange('(k p) c -> p k c', p=128)
        S.dma('pool', wq[:], wsrc[:, :, 1024:1536], writes=['wq'])
        S.dma('pool', wk[:], wsrc[:, :, 1536:2048], writes=['wk'])
        S.dma('pool', wv[:], wsrc[:, :, 2048:2560], writes=['wv'])
        S.dma('pool', wqs[:], ssrc[:, :, 0:512], writes=['wqs'])
        S.dma('pool', wks[:], ssrc[:, :, 512:1024], writes=['wks'])
        S.dma('sp', cosT[:], g.rope[0], writes=['cosT'])
        S.dma('sp', sinT[:], g.rope[1], writes=['sinT'])
        S.dma('sp', cmf[:], g.cmask.rearrange('m p c -> p m c'), writes=['cmf'])
        S.op('dve', lambda e: e.tensor_copy(bd[:], cmf[:, 0, :]), ['cmf'], ['bd'])
        S.dma('sp', lqk[:], g.attp[l:l + 1].broadcast_to([128, 4, 64]), writes=['lqk'])
        S.dma('sp', gsub[:], g.subg[l:l + 1, :].broadcast_to([128, 128]), writes=['gsub'])
        S.op('act', lambda e: e.mul(gsub[:], gsub[:], 1.0 - lam_init), ['gsub'], ['gsub'])
        S.op('dve', lambda e: e.tensor_tensor(lqk[:, 0, :], lqk[:, 0, :], lqk[:, 1, :], ALU.mult), ['lqk'], ['lqk'])
        S.op('dve', lambda e: e.tensor_tensor(lqk[:, 2, :], lqk[:, 2, :], lqk[:, 3, :], ALU.mult), ['lqk'], ['lqk'])
        S.op('dve', lambda e: e.reduce_sum(lam2[:, 0:1], lqk[:, 0, :], AX.X), ['lqk'], ['lam2'])
        S.op('dve', lambda e: e.reduce_sum(lam2[:, 1:2], lqk[:, 2, :], AX.X), ['lqk'], ['lam2'])
        S.op('act', lambda e: e.activation(lam2[:], lam2[:], AF.Exp), ['lam2'], ['lam2'])
        S.op('dve', lambda e: e.tensor_tensor(neglam[:], lam2[:, 1:2], lam2[:, 0:1], ALU.subtract), ['lam2'], ['neglam'])
        S.op('dve', lambda e: e.tensor_scalar(neglam[:], neglam[:], -lam_init, None, ALU.add), ['neglam'], ['neglam'])
        S.op('pool', lambda e: e.memset(vaug[:, :, :, 128:129], 1.0), [], ['vaug1'])

        for bi, (t0, nt) in enumerate(BLKS):
            hb, hk = HL.load(bi)
            lat = t0 >= CTX
            tl = t0 - CTX
            for (w, ws, dst, dk, wkey, wskey) in ((wq, wqs, qT, 'qT', 'wq', 'wqs'), (wk, wks, kT, 'kT', 'wk', 'wks')):
                for h in range(4):
                    pa = nextps(g)
                    mm_group(S, g.ps[pa][:, :nt], [(w[:, k, h * 128:(h + 1) * 128], hb[:, k, :nt]) for k in range(NCH)],
                             [wkey, hk], [('ps', pa)])
                    if not lat:
                        S.op('act', lambda e, pa=pa, h=h, dst=dst: e.copy(dst[:, h, t0:t0 + nt], g.ps[pa][:, :nt]), [('ps', pa)], [dk])
                        continue
                    pb = nextps(g)
                    mm_group(S, g.ps[pb][:, :nt], [(ws[:, k, h * 128:(h + 1) * 128], hb[:, k, :nt]) for k in range(NCH)],
                             [wskey, hk], [('ps', pb)])
                    r1, r2 = (0, 1) if h % 2 == 0 else (2, 3)
                    S.op('dve', lambda e, pa=pa, r1=r1: e.tensor_tensor(rt[r1][:, :nt], g.ps[pa][:, :nt], cosT[:, tl:tl + nt], ALU.mult),
                         [('ps', pa), 'cosT'], [('rt', r1)])
                    S.op('dve', lambda e, pb=pb, r2=r2: e.tensor_tensor(rt[r2][:, :nt], g.ps[pb][:, :nt], sinT[:, tl:tl + nt], ALU.mult),
                         [('ps', pb), 'sinT'], [('rt', r2)])
                    S.op('pool', lambda e, r1=r1, r2=r2, h=h, dst=dst: e.tensor_tensor(dst[:, h, t0:t0 + nt], rt[r1][:, :nt], rt[r2][:, :nt], ALU.add),
                         [('rt', r1), ('rt', r2)], [dk])
            for tt in range(nt // 128):
                ti = t0 // 128 + tt
                pv = nextps(g)
                mm_group(S, g.ps[pv][:, :512], [(hb[:, k, tt * 128:(tt + 1) * 128], wv[:, k, :]) for k in range(NCH)],
                         ['wv', hk], [('ps', pv)])
                S.op('act', lambda e, pv=pv, ti=ti: e.copy(vaug[:, ti, :, 0:128], g.ps[pv][:, :].rearrange('p (h d) -> p h d', h=4)),
                     [('ps', pv)], ['vaug'])
        sqb = rt
        for qi, (src, sk) in enumerate(((qT, 'qT'), (kT, 'kT'))):
            for h in range(4):
                for bi, (t0, nt) in enumerate(BLKS):
                    r = (h * len(BLKS) + bi) % 4
                    sq = rt[r][:, 0:256].bitcast(BF16)
                    S.op('act', lambda e, sq=sq, h=h, src=src: e.activation(sq[:, :nt], src[:, h, t0:t0 + nt], AF.Square), [sk], [('rt', r)])
                    pi = nextps(g)
                    mm_group(S, g.ps[pi][:, :nt], [(bd[:], sq[:, :nt])], ['bd', ('rt', r)], [('ps', pi)])
                    S.op('dve', lambda e, pi=pi, h=h, bi=bi, qi=qi: e.reduce_max(stat[:, qi, h, bi:bi + 1], g.ps[pi][:, :nt], AX.X),
                         [('ps', pi)], ['stat'])
        S.op('dve', lambda e: e.reduce_max(stm[:], stat[:], AX.X), ['stat'], ['stm'])
        S.op('dve', lambda e: e.tensor_tensor(negb[:], stm[:, 0, :], stm[:, 1, :], ALU.mult), ['stm'], ['negb'])
        S.op('act', lambda e: e.activation(negb[:], negb[:], AF.Sqrt), ['negb'], ['negb'])
        for c in range(2):
            pi = nextps(g)
            mm_group(S, g.ps[pi][:, 0:4], [(cmf[:, 1 + c, :], negb[:])], ['cmf', 'negb'], [('ps', pi)])
            S.op('act', lambda e, pi=pi, c=c: e.mul(nb[:, c, :], g.ps[pi][:, 0:4], -1.02 * 0.125 / 64.0), [('ps', pi)], ['nb'])

        A1.__exit__(None, None, None)
        pT = [A('pT%d' % i, [128, 512], BF16) for i in range(8)]
        accs = [A('acc%d' % i, [128, 512]) for i in range(2)]
        rinv = A('rinv', [128, 512]); on = [A('on%d' % i, [128, 512]) for i in range(2)]
        aa = A('aa', [128, 512]); sq = A('asq', [128, 512]); rstd = A('arstd', [128, 512])
        gsubT = A('gsubT', [128, 1])
        yab = [A('yab%d' % i, [128, 4, 512], BF16) for i in range(2)]
        S.dma('sp', gsubT[:], g.subg[l].rearrange('(p o) -> p o', o=1), writes=['gsubT'])
        S.op('act', lambda e: e.mul(gsubT[:], gsubT[:], 1.0 - lam_init), ['gsubT'], ['gsubT'])
        yav = g.yaT.rearrange('(k p) t -> p k t', p=128)
        pti = [0]
        sbank = [0]

        def sbnext():
            b_ = 3 + sbank[0]
            sbank[0] = (sbank[0] + 1) % 5
            return b_

        def attend(q0, nq, kt0, nkt, yslot):
            items = [(h, c, kk) for h in range(4) for c in range(2) for kk in range(nkt)]
            DPF = 3
            slots = {}
            yb = yab[yslot % 2]
            ykey = ('yab', yslot % 2)

            def front(i):
                h, c, kk = items[i]
                kt = kt0 + kk
                sb_ = sbnext()
                mm_group(S, g.ps[sb_][:, :nq], [(kT[64 * c:64 * c + 64, h, kt * 128:(kt + 1) * 128], qT[64 * c:64 * c + 64, h, q0:q0 + nq])],
                         ['kT', 'qT'], [('ps', sb_)])
                pb_ = pti[0]
                pti[0] = (pti[0] + 1) % 8
                S.op('act', lambda e: e.activation(pT[pb_][:, :nq], g.ps[sb_][:, :nq], AF.Exp,
                     bias=nb[:, c, h:h + 1], scale=0.125), [('ps', sb_), 'nb'], [('pT', pb_)])
                slots[i] = pb_

            def back(i):
                h, c, kk = items[i]
                kt = kt0 + kk
                pb_ = slots.pop(i)
                ob = (2 * h + c) % 3
                mm_group(S, g.ps[ob][:, :nq], [(vaug[:, kt, h, 0:128], pT[pb_][:, :nq])], [('pT', pb_), 'vaug'], [('ps', ob)],
                         start=(kk == 0), stop=(kk == nkt - 1))
                eng = 'dve' if kk % 2 == 0 else 'pool'
                acc, akey = accs[kk % 2], ('acc', kk % 2)
                if kk < 2:
                    S.op(eng, lambda e: e.tensor_copy(acc[:, :nq], pT[pb_][:, :nq]), [('pT', pb_)], [akey])
                else:
                    S.op(eng, lambda e: e.tensor_tensor(acc[:, :nq], acc[:, :nq], pT[pb_][:, :nq], ALU.add), [('pT', pb_), akey], [akey])
                if kk != nkt - 1:
                    return
                S.op('dve', lambda e: e.tensor_tensor(accs[0][:, :nq], accs[0][:, :nq], accs[1][:, :nq], ALU.add), [('acc', 0), ('acc', 1)], [('acc', 0)])
                rb = sbnext()
                mm_group(S, g.ps[rb][:, :nq], [(g.onesf[:], accs[0][:, :nq])], [('acc', 0), 'onesf'], [('ps', rb)])
                S.op('dve', lambda e: e.reciprocal(rinv[:, :nq], g.ps[rb][:, :nq]), [('ps', rb)], ['rinv'])
                S.op('dve', lambda e: e.tensor_tensor(on[c][:, :nq], g.ps[ob][:, :nq], rinv[:, :nq], ALU.mult), [('ps', ob), 'rinv'], [('on', c)])
                if c != 1:
                    return
                S.op('dve', lambda e: e.scalar_tensor_tensor(aa[:, :nq], on[1][:, :nq], neglam[:, 0:1], on[0][:, :nq], ALU.mult, ALU.add),
                     [('on', 0), ('on', 1), 'neglam'], ['aa'])
                S.op('pool', lambda e: e.tensor_tensor(sq[:, :nq], aa[:, :nq], aa[:, :nq], ALU.mult), ['aa'], ['asq'])
                rb2 = sbnext()
                mm_group(S, g.ps[rb2][:, :nq], [(g.onesf[:], sq[:, :nq])], ['asq', 'onesf'], [('ps', rb2)])
                S.op('act', lambda e: e.activation(rstd[:, :nq], g.ps[rb2][:, :nq], AF.Sqrt, bias=g.epsv[:, 2:3], scale=1.0 / 128), [('ps', rb2), 'epsv'], ['arstd'])
                S.op('dve', lambda e: e.reciprocal(rstd[:, :nq], rstd[:, :nq]), ['arstd'], ['arstd'])
                S.op('dve', lambda e: e.scalar_tensor_tensor(yb[:, h, :nq], aa[:, :nq], gsubT[:, 0:1], rstd[:, :nq], ALU.mult, ALU.mult),
                     ['aa', 'arstd', 'gsubT'], [ykey])

            GRP = 3
            for i0 in range(0, len(items) + DPF + GRP, GRP):
                for i in range(i0, i0 + GRP):
                    if i < len(items):
                        front(i)
                for i in range(i0, i0 + GRP):
                    if 0 <= i - DPF < len(items):
                        back(i - DPF)
            S.dma('sp', yav[:, :, q0:q0 + nq], yb[:, :, :nq], reads=[ykey], writes=['yaT'])

        slot = 0
        if need_ctx_q:
            attend(0, CTX, 0, CTX // 128, slot)
            slot += 1
        for qb in range(SEQ // 512):
            attend(CTX + qb * 512, 512, 0, T // 128, slot)
            slot += 1


RW0 = 2560
P_SH, P_W0, P_A0, P_KK, P_KA, P_RK, P_GG, P_GB, P_OMKA, NRWP = 0, 45, 53, 61, 65, 69, 73, 77, 81, 85
DECAY_C = -math.exp(-0.5)


def phase_rwkv_prep(g, l):
    S, nc = g.S, g.nc
    with Phase(g) as A:
        wrw = A('wrw', [128, NCH, 1920], BF16)
        w2b = A('w2b', [128, 512], BF16); a2b = A('a2b', [128, 512], BF16); g2b = A('g2b', [128, 512], BF16)
        rwp = A('rwp', [128, NRWP])
        bdf = A('bdf', [128, 128])
        hbx = [A('hbx%d' % i, [128, NCH, 514], BF16) for i in range(2)]
        zx = [A('zx%d' % i, [128, 514]) for i in range(3)]
        zc = A('zc', [128, 15, 512])
        tw = A('tw', [128, 512], BF16); ab = A('ab', [128, 512], BF16); sg = A('sg', [128, 512], BF16)
        kkt = A('kkt', [128, 4, 512]); kds = A('kds', [128, 4, 512])
        t1 = [A('rt1_%d' % i, [128, 512]) for i in range(3)]
        ob = {n: [A('ob_%s%d' % (n, i), [128, 4, 512], BF16) for i in range(1)] * 2 for n in ('r', 'kk', 'kd0', 'kd1', 'b0', 'b1', 'g', 'bon')}
        owl = {d: [A('owl%d_%d' % (d, i), [128, 4, 512]) for i in range(1)] * 2 for d in range(2)}
        vtile = [A('vtile%d' % i, [128, 512], BF16) for i in range(2)]

        wsrc = g.w_in[l].rearrange('(k p) c -> p k c', p=128)
        S.dma('pool', wrw[:], wsrc[:, :, RW0:RW0 + 1920], writes=['wrw'])
        S.dma('pool', w2b[:], g.rw_w2[l].rearrange('d m c -> (d m) c'), writes=['w2b'])
        S.dma('pool', a2b[:], g.rw_a2[l].rearrange('d m c -> (d m) c'), writes=['a2b'])
        S.dma('pool', g2b[:], g.rw_g2[l], writes=['g2b'])
        S.dma('sp', rwp[:, 0:P_OMKA], g.rwp[l], writes=['rwp'])
        S.dma('sp', bdf[:], g.cmask[0], writes=['bdf'])
        S.op('dve', lambda e: e.tensor_scalar(rwp[:, P_OMKA:P_OMKA + 4], rwp[:, P_KA:P_KA + 4], -1.0, 1.0, ALU.mult, ALU.add), ['rwp'], ['rwp'])
        hTv = g.hT.rearrange('(k p) t -> p k t', p=128)
        fmv = lambda ap: ap.rearrange('(k p) t -> p k t', p=128)
        for bi, (t0, nt) in enumerate(BLKS):
            b = bi % 2
            hb = hbx[b]
            hk = ('hbx', b)
            s0, s1 = (0, CTX) if t0 < CTX else (CTX, T)
            lo, hi = max(s0, t0 - 1), min(s1, t0 + nt + 1)
            if lo == t0:
                S.op('pool', lambda e, hb=hb: e.memset(hb[:, :, 0:1], 0.0), [], [hk])
            if hi == t0 + nt:
                S.op('pool', lambda e, hb=hb: e.memset(hb[:, :, nt + 1:nt + 2], 0.0), [], [hk])
            S.dma('sp', hb[:, :, 1 - (t0 - lo):1 + (hi - t0)], hTv[:, :, lo:hi], reads=['hT'], writes=[hk])
            ph = nextps(g)
            for ch in range(15):
                pm = nextps(g)
                if pm == ph:
                    pm = nextps(g)
                wsl = lambda k, ch=ch: wrw[:, k, ch * 128:(ch + 1) * 128]
                mm_group(S, g.ps[pm][:, :nt], [(wsl(k), hb[:, k, 1:1 + nt]) for k in range(NCH)], ['wrw', hk], [('ps', pm)])
                mm_group(S, g.ps[ph][:, 2 * ch:2 * ch + 1], [(wsl(k), hb[:, k, 0:1]) for k in range(NCH)], ['wrw', hk], [('ps', ph)])
                mm_group(S, g.ps[ph][:, 2 * ch + 1:2 * ch + 2], [(wsl(k), hb[:, k, nt + 1:nt + 2]) for k in range(NCH)], ['wrw', hk], [('ps', ph)])
                z = zx[ch % 3]
                zk = ('zx', ch % 3)
                S.op('act', lambda e, z=z, pm=pm: e.copy(z[:, 1:1 + nt], g.ps[pm][:, :nt]), [('ps', pm)], [zk])
                S.op('act', lambda e, z=z, ch=ch: e.copy(z[:, 0:1], g.ps[ph][:, 2 * ch:2 * ch + 1]), [('ps', ph)], [zk])
                S.op('act', lambda e, z=z, ch=ch: e.copy(z[:, nt + 1:nt + 2], g.ps[ph][:, 2 * ch + 1:2 * ch + 2]), [('ps', ph)], [zk])
                sh = lambda j, ch=ch: rwp[:, P_SH + ch * 3 + j:P_SH + ch * 3 + j + 1]
                S.op('act', lambda e, pm=pm, ch=ch, sh=sh: e.activation(zc[:, ch, :nt], g.ps[pm][:, :nt], AF.Identity, scale=sh(1)), [('ps', pm), 'rwp'], [('zc', ch)])
                tz = t1[ch % 3]
                S.op('pool', lambda e, z=z, tz=tz, sh=sh: e.tensor_scalar(tz[:, :nt], z[:, 0:nt], sh(0), 0.0, ALU.mult, ALU.add), [zk, 'rwp'], [('t1', ch % 3)])
                S.op('dve', lambda e, z=z, ch=ch, sh=sh: e.scalar_tensor_tensor(zc[:, ch, :nt], z[:, 2:nt + 2], sh(2), zc[:, ch, :nt], ALU.mult, ALU.add),
                     [zk, 'rwp', ('zc', ch)], [('zc', ch)])
                S.op('pool', lambda e, tz=tz, ch=ch: e.tensor_tensor(zc[:, ch, :nt], zc[:, ch, :nt], tz[:, :nt], ALU.add), [('zc', ch), ('t1', ch % 3)], [('zc', ch)])
            o = {n: ob[n][0] for n in ob}
            okey = {n: ('ob', n, 0) for n in ob}
            S.op('act', lambda e: e.copy(o['r'][:, :, :nt], zc[:, 0:4, :nt]), [('zc', c) for c in range(4)], [okey['r']])
            S.dma('sp', fmv(g.rS)[:, :, t0:t0 + nt], o['r'][:, :, :nt], reads=[okey['r']], writes=['rS'])
            S.op('act', lambda e: e.activation(tw[:, :nt], zc[:, 12, :nt], AF.Tanh), [('zc', 12)], ['tw'])
            S.op('act', lambda e: e.copy(ab[:, :nt], zc[:, 13, :nt]), [('zc', 13)], ['ab'])
            S.op('act', lambda e: e.activation(sg[:, :nt], zc[:, 14, :nt], AF.Sigmoid), [('zc', 14)], ['sg'])
            for c in range(4):
                ti = c % 3
                S.op('act', lambda e, c=c: e.activation(kkt[:, c, :nt], zc[:, 4 + c, :nt], AF.Identity, scale=rwp[:, P_KK + c:P_KK + c + 1]),
                     [('zc', 4 + c), 'rwp'], [('kkt', c)])
                S.op('act', lambda e, c=c, ti=ti: e.activation(t1[ti][:, :nt], kkt[:, c, :nt], AF.Square), [('kkt', c)], [('t1', ti)])
                pi = nextps(g)
                mm_group(S, g.ps[pi][:, :nt], [(bdf[:], t1[ti][:, :nt])], ['bdf', ('t1', ti)], [('ps', pi)])
                S.op('dve', lambda e, pi=pi, ti=ti: e.tensor_scalar(t1[ti][:, :nt], g.ps[pi][:, :nt], 1e-24, None, ALU.max), [('ps', pi)], [('t1', ti)])
                S.op('act', lambda e, ti=ti: e.activation(t1[ti][:, :nt], t1[ti][:, :nt], AF.Sqrt), [('t1', ti)], [('t1', ti)])
                S.op('dve', lambda e, ti=ti: e.reciprocal(t1[ti][:, :nt], t1[ti][:, :nt]), [('t1', ti)], [('t1', ti)])
                S.op('dve', lambda e, c=c, ti=ti: e.tensor_tensor(kkt[:, c, :nt], kkt[:, c, :nt], t1[ti][:, :nt], ALU.mult), [('kkt', c), ('t1', ti)], [('kkt', c)])
            S.op('act', lambda e: e.copy(o['kk'][:, :, :nt], kkt[:, :, :nt]), [('kkt', c) for c in range(4)], [okey['kk']])
            S.dma('sp', fmv(g.kkS)[:, :, t0:t0 + nt], o['kk'][:, :, :nt], reads=[okey['kk']], writes=['kkS'])
            for d in range(2):
                kdn, bn = 'kd%d' % d, 'b%d' % d
                for c in range(4):
                    pu, pa = nextps(g), nextps(g)
                    mm_group(S, g.ps[pu][:, :nt], [(w2b[64 * d:64 * d + 64, c * 128:(c + 1) * 128], tw[64 * d:64 * d + 64, :nt])], ['w2b', 'tw'], [('ps', pu)])
                    mm_group(S, g.ps[pa][:, :nt], [(a2b[64 * d:64 * d + 64, c * 128:(c + 1) * 128], ab[64 * d:64 * d + 64, :nt])], ['a2b', 'ab'], [('ps', pa)])
                    wl = owl[d][0]
                    wk_ = ('owl', d, 0)
                    S.op('act', lambda e, pu=pu, c=c, d=d, wl=wl: e.activation(wl[:, c, :nt], g.ps[pu][:, :nt], AF.Sigmoid,
                         bias=rwp[:, P_W0 + d * 4 + c:P_W0 + d * 4 + c + 1], scale=1.0), [('ps', pu), 'rwp'], [wk_])
                    S.op('pool', lambda e, c=c, wl=wl: e.tensor_scalar(wl[:, c, :nt], wl[:, c, :nt], DECAY_C, 0.0, ALU.mult, ALU.add), [wk_], [wk_])
                    ta, tb = t1[0], t1[1]
                    S.op('act', lambda e, pa=pa, c=c, d=d: e.activation(ta[:, :nt], g.ps[pa][:, :nt], AF.Sigmoid,
                         bias=rwp[:, P_A0 + d * 4 + c:P_A0 + d * 4 + c + 1], scale=1.0), [('ps', pa), 'rwp'], [('t1', 0)])
                    S.op('pool', lambda e, c=c, bn=bn: e.tensor_tensor(o[bn][:, c, :nt], kkt[:, c, :nt], ta[:, :nt], ALU.mult),
                         [('kkt', c), ('t1', 0)], [okey[bn]])
                    S.op('dve', lambda e, c=c: e.tensor_scalar(tb[:, :nt], ta[:, :nt], rwp[:, P_KA + c:P_KA + c + 1], rwp[:, P_OMKA + c:P_OMKA + c + 1], ALU.mult, ALU.add),
                         [('t1', 0), 'rwp'], [('t1', 1)])
                    S.op('dve', lambda e, c=c: e.tensor_tensor(tb[:, :nt], tb[:, :nt], zc[:, 4 + c, :nt], ALU.mult), [('t1', 1), ('zc', 4 + c)], [('t1', 1)])
                    S.op('act', lambda e, c=c, kdn=kdn: e.copy(o[kdn][:, c, :nt], tb[:, :nt]), [('t1', 1)], [okey[kdn]])
                    if d == 0:
                        S.op('pool', lambda e, c=c: e.tensor_copy(kds[:, c, :nt], tb[:, :nt]), [('t1', 1)], [('kds', c)])
                    else:
                        S.op('pool', lambda e, c=c: e.tensor_tensor(kds[:, c, :nt], kds[:, c, :nt], tb[:, :nt], ALU.add), [('t1', 1), ('kds', c)], [('kds', c)])
                S.dma('sp', fmv(g.wlS[d])[:, :, t0:t0 + nt], owl[d][0][:, :, :nt], reads=[('owl', d, 0)], writes=['wlS%d' % d])
                S.dma('sp', fmv(g.kdS[d])[:, :, t0:t0 + nt], o[kdn][:, :, :nt], reads=[okey[kdn]], writes=['kdS%d' % d])
                S.dma('sp', fmv(g.bS[d])[:, :, t0:t0 + nt], o[bn][:, :, :nt], reads=[okey[bn]], writes=['bS%d' % d])
            for c in range(4):
                pg = nextps(g)
                mm_group(S, g.ps[pg][:, :nt], [(g2b[:, c * 128:(c + 1) * 128], sg[:, :nt])], ['g2b', 'sg'], [('ps', pg)])
                S.op('act', lambda e, pg=pg, c=c: e.copy(o['g'][:, c, :nt], g.ps[pg][:, :nt]), [('ps', pg)], [okey['g']])
            S.dma('sp', fmv(g.gS)[:, :, t0:t0 + nt], o['g'][:, :, :nt], reads=[okey['g']], writes=['gS'])
            for c in range(4):
                tc_ = t1[2]
                S.op('dve', lambda e, c=c: e.scalar_tensor_tensor(tc_[:, :nt], zc[:, c, :nt], rwp[:, P_RK + c:P_RK + c + 1], kds[:, c, :nt], ALU.mult, ALU.mult),
                     [('zc', c), 'rwp', ('kds', c)], [('t1', 2)])
                pi = nextps(g)
                mm_group(S, g.ps[pi][:, :nt], [(bdf[:], tc_[:, :nt])], ['bdf', ('t1', 2)], [('ps', pi)])
                S.op('dve', lambda e, pi=pi, c=c: e.tensor_tensor(o['bon'][:, c, :nt], g.ps[pi][:, :nt], zc[:, 8 + c, :nt], ALU.mult),
                     [('ps', pi), ('zc', 8 + c)], [okey['bon']])
            S.dma('sp', fmv(g.bonS)[:, :, t0:t0 + nt], o['bon'][:, :, :nt], reads=[okey['bon']], writes=['bonS'])
            for tt in range(nt // 128):
                pv = nextps(g)

                def fn(pe, pv=pv, tt=tt):
                    inst = None
                    for c in range(4):
                        inst = pe.transpose(g.ps[pv][:, c * 128:(c + 1) * 128], zc[:, 8 + c, tt * 128:(tt + 1) * 128], g.identf[:])
                    return inst
                S.op('pe', fn, [('zc', 8 + c) for c in range(4)] + ['identf'], [('ps', pv)])
                vb = (t0 // 128 + tt) % 2
                S.op('act', lambda e, pv=pv, vb=vb: e.copy(vtile[vb][:], g.ps[pv][:, :]), [('ps', pv)], [('vtile', vb)])
                S.dma('sp', g.vT[t0 + tt * 128:t0 + (tt + 1) * 128, :], vtile[vb][:], reads=[('vtile', vb)], writes=['vT'])


class TagS:
    GL = {'identb', 'identf', 'rwp', 'bdf', 'smf', 'epsv', 'rS', 'kkS', 'kdS0', 'kdS1', 'bS0', 'bS1', 'wlS0', 'wlS1', 'vT', 'yfS', 'ybS', 'bonS', 'gS', 'yrT'}

    def __init__(self, S, d):
        self.S, self.d = S, d

    def t(self, k):
        if isinstance(k, tuple) and k[0] in ('ps', 'msk', 'm4', 'mnt'):
            return k
        if isinstance(k, str) and (k in self.GL or k.startswith('dbg')):
            return k
        return ('dir%d' % self.d, k)

    def op(self, e, fn, reads=(), writes=()):
        self.S.op(e, fn, [self.t(k) for k in reads], [self.t(k) for k in writes])

    def dma(self, e, out, in_, reads=(), writes=(), **kw):
        self.S.dma(e, out, in_, reads=[self.t(k) for k in reads], writes=[self.t(k) for k in writes], **kw)


def phase_rwkv_scan(g, l):
    S, nc = g.S, g.nc
    import os
    STAGE = int(os.environ.get('SCAN_STAGE', '99'))
    NCK = T // 128
    with Phase(g) as A:
        smf = A('smf', [128, 9, 128])
        msk = [A('msk%d' % i, [128, 128], BF16) for i in range(4)]
        m4 = [A('m4_%d' % d, [128, 4, 128], BF16) for d in range(2)]
        mnt = [A('mnt%d' % d, [128, 128], BF16) for d in range(2)]
        bdf = A('bdf', [128, 128]); rwp = A('rwp', [128, P_OMKA])
        rwp = A('rwp', [128, P_OMKA])
        S.dma('sp', smf[:], g.smask[0:9].rearrange('m p c -> p m c'), writes=['smf'])
        for i in range(4):
            S.op('dve', lambda e, i=i: e.tensor_copy(msk[i][:], smf[:, 5 + i, :]), ['smf'], [('msk', i)])
        S.dma('sp', bdf[:], g.cmask[0], writes=['bdf'])
        S.dma('sp', rwp[:], g.rwp[l], writes=['rwp'])
        for d in range(2):
            for j in range(4):
                S.op('dve', lambda e, d=d, j=j: e.tensor_copy(m4[d][:, j, :], smf[:, 2 * d + (j % 2), :]), ['smf'], [('m4', d)])
        S.op('dve', lambda e: e.tensor_copy(mnt[0][:], smf[:, 2, :]), ['smf'], [('mnt', 0)])
        S.op('dve', lambda e: e.tensor_copy(mnt[1][:], smf[:, 0, :]), ['smf'], [('mnt', 1)])
        fmc = lambda ap, c0: ap.rearrange('(k p) t -> p k t', p=128)[:, :, c0:c0 + 128]
        f2 = lambda t: t[:].rearrange('p a b -> p (a b)')
        segb = smf[:, 4, :].unsqueeze(1).broadcast_to([128, 4, 128])
        it = [0]

        def scan_dir(d, A=A):
            S = TagS(g.S, d)
            A0 = A
            psc = [0]

            def nextps(g_):
                i = 4 * d + psc[0]
                psc[0] = (psc[0] + 1) % 4
                return i
            A = lambda name, shape, dt=F32: A0('%s_d%d' % (name, d), shape, dt)
            NTF = [A('NTF%d' % q, [128, 4, 128], BF16) for q in range(2)]
            NF = [A('NF%d' % q, [128, 4, 128], BF16) for q in range(2)]
            NoT = [[A('NoT%d_%d' % (q, i), [128, 4, 128], BF16) for i in range(3)] for q in range(2)]
            MT = [[A('MT%d_%d' % (q, i), [128, 4, 128], BF16) for i in range(2)] for q in range(2)]
            Wb = [A('Wb%d' % q, [128, 4, 128], BF16) for q in range(2)]
            ld = {n: [A('ld_%s%d' % (n, i), [128, 4, 128], BF16) for i in range(2)] for n in ('r', 'kk', 'kd', 'b')}
            cwl = [A('cwl%d' % i, [128, 4, 128]) for i in range(2)]
            cv = [A('cv%d' % i, [128, 512], BF16) for i in range(2)]
            cum = A('cum', [128, 4, 128]); cumx = A('cumx', [128, 4, 128]); ep = A('ep', [128, 4, 128]); en = A('en', [128, 4, 128])
            AR = A('AR', [128, 4, 256], BF16); kt = A('kt', [128, 4, 128], BF16); bt = A('bt', [128, 4, 128], BF16)
            gam = A('gam', [128, 4, 1])
            AtT = A('AtT', [128, 8, 64], BF16); BtT = A('BtT', [128, 8, 64], BF16); KtT = A('KtT', [128, 8, 64], BF16)
            AB = [A('AB%d' % h, [128, 4, 128], BF16) for h in range(8)]
            X = [[A('X%d_%d' % (q, i), [128, 4, 128], BF16) for i in range(2)] for q in range(2)]
            XT = [[A('XT%d_%d' % (q, i), [128, 4, 128], BF16) for i in range(2)] for q in range(2)]
            M = [[A('M%d_%d' % (q, i), [128, 4, 128], BF16) for i in range(2)] for q in range(2)]
            G2 = A('G2', [128, 8, 64], BF16); U = A('U', [128, 8, 64]); P1 = A('P1', [128, 4, 128]); Et = A('Et', [128, 8, 64], BF16)
            ST = A('ST', [128, 4, 64]); STb = [A('STb%d' % i, [128, 4, 64], BF16) for i in range(2)]; stt = A('stt', [128, 4, 64])
            ysb = [A('ysb%d' % i, [128, 4, 128]) for i in range(2)]
            order = list(range(NCK)) if d == 0 else [1, 0] + list(range(NCK - 1, 1, -1))
            if getattr(g, 'scan_limit', None):
                order = order[:g.scan_limit]
            S.op('pool', lambda e: e.memset(ST[:], 0.0), [], ['ST'])
            sbi = 0
            S.op('pool', lambda e: e.memset(STb[0][:], 0.0), [], [('STb', 0)])
            for ci in order:
                c0 = ci * 128
                b = it[0] % 2
                it[0] += 1
                cr, ckk, ckd, cb = ld['r'][b], ld['kk'][b], ld['kd'][b], ld['b'][b]
                lk = lambda n: ('ld', n, b)
                S.dma('sp', cr[:], fmc(g.rS, c0), reads=['rS'], writes=[lk('r')])
                S.dma('sp', ckk[:], fmc(g.kkS, c0), reads=['kkS'], writes=[lk('kk')])
                S.dma('sp', ckd[:], fmc(g.kdS[d], c0), reads=['kdS%d' % d], writes=[lk('kd')])
                S.dma('sp', cb[:], fmc(g.bS[d], c0), reads=['bS%d' % d], writes=[lk('b')])
                S.dma('sp', cwl[b][:], fmc(g.wlS[d], c0), reads=['wlS%d' % d], writes=[('cwl', b)])
                S.dma('sp', cv[b][:], g.vT[c0:c0 + 128, :], reads=['vT'], writes=[('cv', b)])
                vk = ('cv', b)
                cvb = cv[b]
                yield
                S.op('pool', lambda e: e.tensor_copy(cumx[:], segb), ['smf'], ['cumx'])
                S.op('dve', lambda e, b=b: e.tensor_tensor_scan(f2(cum), f2(cumx), f2(cwl[b]), 0.0, ALU.mult, ALU.add), ['cumx', ('cwl', b)], ['cum'])
                if d == 0:
                    S.op('dve', lambda e, b=b: e.tensor_tensor(cumx[:], cum[:], cwl[b][:], ALU.subtract), ['cum', ('cwl', b)], ['cumx'])
                else:
                    S.op('dve', lambda e: e.tensor_tensor(cumx[:], cum[:, :, 127:128].broadcast_to([128, 4, 128]), cum[:], ALU.subtract), ['cum'], ['cumx'])
                    S.op('dve', lambda e, b=b: e.tensor_tensor(cum[:], cumx[:], cwl[b][:], ALU.add), ['cumx', ('cwl', b)], ['cum'])
                S.op('act', lambda e: e.activation(ep[:], cum[:], AF.Exp), ['cum'], ['ep'])
                S.op('act', lambda e: e.activation(en[:], cum[:], AF.Exp, scale=-1.0), ['cum'], ['en'])
                gcol = 127 if d == 0 else 0
                S.op('act', lambda e: e.copy(gam[:], ep[:, :, gcol:gcol + 1]), ['ep'], ['gam'])
                S.op('dve', lambda e, cr=cr: e.tensor_tensor(AR[:, :, 128:256], cr[:], ep[:], ALU.mult), [lk('r'), 'ep'], ['AR'])
                S.op('act', lambda e: e.activation(ep[:], cumx[:], AF.Exp), ['cumx', 'AR', 'gam'], ['ep'])
                S.op('dve', lambda e, ckk=ckk: e.scalar_tensor_tensor(AR[:, :, 0:128], ckk[:], -1.0, ep[:], ALU.mult, ALU.mult), [lk('kk'), 'ep'], ['AR'])
                S.op('pool', lambda e, ckd=ckd: e.tensor_tensor(kt[:], ckd[:], en[:], ALU.mult), [lk('kd'), 'en'], ['kt'])
                S.op('pool', lambda e, cb=cb: e.tensor_tensor(bt[:], cb[:], en[:], ALU.mult), [lk('b'), 'en'], ['bt'])
                if STAGE <= 1:
                    continue
                yield
                for (srcf, dst, dk, sk) in ((lambda hp: AR[:, hp, 0:128], AtT, 'AtT', 'AR'), (lambda hp: bt[:, hp, :], BtT, 'BtT', 'bt'), (lambda hp: kt[:, hp, :], KtT, 'KtT', 'kt')):
                    pi = nextps(g)
                    psb = g.ps[pi][:, :].bitcast(BF16)

                    def fn(pe, srcf=srcf, psb=psb):
                        inst = None
                        for hp in range(4):
                            inst = pe.transpose(psb[:, hp * 128:(hp + 1) * 128], srcf(hp), g.identb[:])
                        return inst
                    S.op('pe', fn, [sk, 'identb'], [('ps', pi)])
                    S.op('act', lambda e, dst=dst, psb=psb: e.copy(dst[:].rearrange('p h j -> p (h j)'), psb[:, 0:512]), [('ps', pi)], [dk])
                if STAGE <= 2:
                    continue
                yield
                ABH = int(os.environ.get('AB_H', '8')); ABM = int(os.environ.get('AB_MODE', '9'))
                for h in range(ABH):
                    hp, hb = h // 2, h % 2
                    sl = slice(hb * 64, hb * 64 + 64)
                    pi = nextps(g)
                    mm_group(S, g.ps[pi][:, 0:256], [(bt[sl, hp, :], AR[sl, hp, :])], ['bt', 'AR'], [('ps', pi)])
                    if ABM <= 1:
                        continue
                    mm_group(S, g.ps[pi][:, 256:512], [(kt[sl, hp, :], AR[sl, hp, :])], ['kt', 'AR'], [('ps', pi)])
                    if ABM <= 2:
                        continue
                    S.op('dve', lambda e, h=h, pi=pi: e.tensor_tensor(f2(AB[h]), g.ps[pi][:, :], f2(m4[d]), ALU.mult), [('ps', pi), ('m4', d)], [('AB', h)])
                SUB = int(os.environ.get('SCAN_SUB', '9'))
                if SUB <= 0:
                    continue
                b4 = lambda t: t[:].unsqueeze(1).broadcast_to([128, 4, 128])
                for q in range(2):
                    pi = nextps(g)
                    for j in range(4):
                        h = 2 * j + q
                        hp, hb = j, q
                        sl = slice(hb * 64, hb * 64 + 64)
                        mm_group(S, g.ps[pi][:, j * 128:(j + 1) * 128], [(AR[sl, hp, 0:128], bt[sl, hp, :])], ['bt', 'AR'], [('ps', pi)])
                    S.op('dve', lambda e, q=q, pi=pi: e.tensor_tensor(NTF[q][:], g.ps[pi][:, :].rearrange('p (j s) -> p j s', j=4), b4(mnt[d]), ALU.mult),
                         [('ps', pi), ('mnt', d)], [('NTF', q)])
                    for j in range(4):
                        h = 2 * j + q
                        S.op('pool', lambda e, q=q, j=j, h=h: e.tensor_copy(NF[q][:, j, :], AB[h][:, 0, :]), [('AB', h)], [('NF', q)])
                    S.op('pool', lambda e, q=q: e.tensor_tensor(X[q][0][:], NF[q][:], b4(msk[0]), ALU.mult), [('NF', q), ('msk', 0)], [('X', q, 0)])
                    S.op('dve', lambda e, q=q: e.tensor_tensor(XT[q][0][:], NTF[q][:], b4(msk[0]), ALU.mult), [('NTF', q), ('msk', 0)], [('XT', q, 0)])
                    S.op('pool', lambda e, q=q: e.tensor_tensor(M[q][0][:], X[q][0][:], b4(g.identb), ALU.add), [('X', q, 0), 'identb'], [('M', q, 0)])
                    S.op('pool', lambda e, q=q: e.tensor_tensor(MT[q][0][:], XT[q][0][:], b4(g.identb), ALU.add), [('XT', q, 0), 'identb'], [('MT', q, 0)])
                    for i in range(3):
                        S.op('pool', lambda e, q=q, i=i: e.tensor_tensor(NoT[q][i][:], NTF[q][:], b4(msk[1 + i]), ALU.mult), [('NTF', q), ('msk', 1 + i)], [('NoT', q, i)])
                if d == 0 and ci == 0:
                    for nm, tl, kk_ in (('d_XT0', NTF[0], ('NTF', 0)), ('d_X0', NF[0], ('NF', 0)), ('d_Mi', M[0][0], ('M', 0, 0))):
                        if nm in g.dbg:
                            S.dma('sp', g.dbg[nm][:, :], tl[:].rearrange('p a b -> p (a b)'), reads=[kk_], writes=['dbg' + nm])
                yield
                def mm4(q, lh, rh, rk):
                    pi = nextps(g)
                    for j in range(4):
                        mm_group(S, g.ps[pi][:, j * 128:(j + 1) * 128], [(lh[:, j, :], rh[:, j, :])], rk, [('ps', pi)])
                    return pi
                cur = 0
                for k in range(1, 4):
                    nx = 1 - cur
                    for q in range(2):
                        Xp, XTp = X[q][cur], XT[q][cur]
                        Xn, XTn = X[q][nx], XT[q][nx]
                        kX, kXT = ('X', q, cur), ('XT', q, cur)
                        nX, nXT = ('X', q, nx), ('XT', q, nx)
                        pi = mm4(q, Xp, XTp, [kX, kXT])
                        S.op('act', lambda e, XTn=XTn, pi=pi: e.copy(f2(XTn), g.ps[pi][:, :]), [('ps', pi)], [nXT])
                        if k < 3:
                            pi = mm4(q, XTp, Xp, [kX, kXT])
                            S.op('act', lambda e, Xn=Xn, pi=pi: e.copy(f2(Xn), g.ps[pi][:, :]), [('ps', pi)], [nX])
                    yield
                    for q in range(2):
                        Mp, MTp, Mn, MTn, XTn = M[q][cur], MT[q][cur], M[q][nx], MT[q][nx], XT[q][nx]
                        kM, kMT, nM, nMT, nXT = ('M', q, cur), ('MT', q, cur), ('M', q, nx), ('MT', q, nx), ('XT', q, nx)
                        pi = mm4(q, XTn, Mp, [nXT, kM])
                        S.op('dve', lambda e, Mn=Mn, Mp=Mp, pi=pi: e.tensor_tensor(f2(Mn), g.ps[pi][:, :], f2(Mp), ALU.add), [('ps', pi), kM], [nM])
                        pi = mm4(q, Mp, XTn, [nXT, kM])
                        S.op('dve', lambda e, MTn=MTn, MTp=MTp, pi=pi: e.tensor_tensor(f2(MTn), g.ps[pi][:, :], f2(MTp), ALU.add), [('ps', pi), kMT], [nMT])
                    cur = nx
                    yield
                for i in range(3):
                    nx = 1 - cur
                    for q in range(2):
                        pi = mm4(q, NoT[q][i], M[q][cur], [('NoT', q, i), ('M', q, cur)])
                        S.op('act', lambda e, q=q, pi=pi: e.copy(f2(Wb[q]), g.ps[pi][:, :]), [('ps', pi)], [('Wb', q)])
                    yield
                    for q in range(2):
                        Dp, DTp, Dn, DTn = M[q][cur], MT[q][cur], M[q][nx], MT[q][nx]
                        kD, kDT, nD, nDT = ('M', q, cur), ('MT', q, cur), ('M', q, nx), ('MT', q, nx)
                        pi = mm4(q, DTp, Wb[q], [kDT, ('Wb', q)])
                        S.op('dve', lambda e, Dn=Dn, Dp=Dp, pi=pi: e.tensor_tensor(f2(Dn), g.ps[pi][:, :], f2(Dp), ALU.add), [('ps', pi), kD], [nD])
                        if i < 2:
                            pi = mm4(q, Wb[q], DTp, [kDT, ('Wb', q)])
                            S.op('dve', lambda e, DTn=DTn, DTp=DTp, pi=pi: e.tensor_tensor(f2(DTn), g.ps[pi][:, :], f2(DTp), ALU.add), [('ps', pi), kDT], [nDT])
                    cur = nx
                    yield
                Mf = [M[q][cur] for q in range(2)]
                kMf = [('M', q, cur) for q in range(2)]
                if STAGE <= 4:
                    continue
                yield
                pi = nextps(g)
                for h in range(8):
                    mm_group(S, g.ps[pi][:, h * 64:(h + 1) * 64], [(AB[h][:, 2, :], cvb[:, h * 64:(h + 1) * 64])], [('AB', h), vk], [('ps', pi)])
                S.op('act', lambda e, pi=pi: e.copy(G2[:].rearrange('p h i -> p (h i)'), g.ps[pi][:, :]), [('ps', pi)], ['G2'])
                pi = nextps(g)
                for h in range(8):
                    mm_group(S, g.ps[pi][:, h * 64:(h + 1) * 64], [(Mf[h % 2][:, h // 2, :], G2[:, h, :])], [kMf[h % 2], 'G2'], [('ps', pi)])
                S.op('act', lambda e, pi=pi: e.copy(U[:].rearrange('p h i -> p (h i)'), g.ps[pi][:, :]), [('ps', pi)], ['U'])
                pi = nextps(g)
                for h in range(8):
                    hp, hb = h // 2, h % 2
                    mm_group(S, g.ps[pi][hb * 64:hb * 64 + 64, hp * 128:(hp + 1) * 128], [(AtT[:, h, :], Mf[h % 2][:, h // 2, :])], [kMf[h % 2], 'AtT'], [('ps', pi)])
                S.op('dve', lambda e, pi=pi: e.tensor_copy(f2(P1), g.ps[pi][:, :]), [('ps', pi)], ['P1'])
                if STAGE <= 5:
                    continue
                yield
                for hb in range(2):
                    pe_ = nextps(g)
                    sl = slice(hb * 64, hb * 64 + 64)
                    for hp in range(4):
                        mm_group(S, g.ps[pe_][:, hp * 64:(hp + 1) * 64], [(P1[sl, hp, :], ST[sl, hp, :])], ['P1', 'ST'], [('ps', pe_)])
                    S.op('dve', lambda e, pe_=pe_, hb=hb: e.tensor_tensor(Et[:].rearrange('p (hp hb) i -> p hp hb i', hb=2)[:, :, hb, :],
                         g.ps[pe_][:, 0:256].rearrange('p (hp i) -> p hp i', hp=4), U[:].rearrange('p (hp hb) i -> p hp hb i', hb=2)[:, :, hb, :], ALU.add),
                         [('ps', pe_), 'U'], ['Et'])
                if STAGE <= 6:
                    continue
                py2, pss = [nextps(g), nextps(g)], nextps(g)
                for h in range(8):
                    hp, hb = h // 2, h % 2
                    sl = slice(hb * 64, hb * 64 + 64)
                    mm_group(S, g.ps[pss][sl, hp * 64:(hp + 1) * 64], [(KtT[:, h, :], cvb[:, h * 64:(h + 1) * 64]), (BtT[:, h, :], Et[:, h, :])],
                             ['KtT', 'BtT', 'Et', vk], [('ps', pss)])
                for h in range(8):
                    hp, hb = h // 2, h % 2
                    sl = slice(hb * 64, hb * 64 + 64)
                    mm_group(S, g.ps[py2[hb]][sl, hp * 128:(hp + 1) * 128],
                             [(STb[sbi][sl, hp, :], AR[sl, hp, 128:256]), (Et[:, h, :], AB[h][:, 1, :]), (cvb[:, h * 64:(h + 1) * 64], AB[h][:, 3, :])],
                             [('STb', sbi), 'AR', 'Et', ('AB', h), vk], [('ps', py2[hb])])
                S.op('dve', lambda e, pss=pss: e.tensor_tensor(stt[:].rearrange('p a i -> p (a i)'), g.ps[pss][:, 0:256], ST[:].rearrange('p a i -> p (a i)'), ALU.add),
                     [('ps', pss), 'ST'], ['stt'])
                S.op('dve', lambda e: e.tensor_tensor(ST[:], stt[:], gam[:].broadcast_to([128, 4, 64]), ALU.mult), ['stt', 'gam'], ['ST'])
                sbi = 1 - sbi
                S.op('act', lambda e, sbi=sbi: e.copy(STb[sbi][:], ST[:]), ['ST'], [('STb', sbi)])
                if STAGE <= 7:
                    continue
                if d == 0 and ci == 0:
                    dm = {'d_AR': AR, 'd_AB0': AB[0], 'd_AB1': AB[1], 'd_M0': Mf[0], 'd_U': U, 'd_P1': P1, 'd_Et': Et, 'd_ST': ST, 'd_kt': kt, 'd_bt': bt,
                          'd_AtT': AtT, 'd_G2': G2}
                    for nm, tl in dm.items():
                        if nm in g.dbg:
                            S.dma('sp', g.dbg[nm][:, :], tl[:].rearrange('p a b -> p (a b)'), reads=['AR', ('AB', 0), ('AB', 1), kMf[0], 'U', 'P1', 'Et', 'ST', 'kt', 'bt', 'AtT', 'G2'], writes=['dbg' + nm])
                yb = ysb[ci % 2]
                for hb in range(2):
                    S.op('act', lambda e, yb=yb, hb=hb: e.copy(f2(yb)[hb * 64:hb * 64 + 64, :], g.ps[py2[hb]][hb * 64:hb * 64 + 64, :]), [('ps', py2[hb])], [('ysb', ci % 2)])
                S.dma('sp', fmc(g.yfS if d == 0 else g.ybS, c0), yb[:], reads=[('ysb', ci % 2)], writes=['yfS' if d == 0 else 'ybS'])
                yield

        gens = [scan_dir(0), scan_dir(1)]
        while gens:
            for gen in list(gens):
                try:
                    next(gen)
                except StopIteration:
                    gens.remove(gen)
    with Phase(g) as A:
        bdf = A('bdf2', [128, 128]); rwp = A('rwp2', [128, P_OMKA])
        S.dma('sp', bdf[:], g.cmask[0], writes=['bdf'])
        S.dma('sp', rwp[:], g.rwp[l], writes=['rwp'])
        fmc = lambda ap, c0: ap.rearrange('(k p) t -> p k t', p=128)[:, :, c0:c0 + 128]
        f2 = lambda t: t[:].rearrange('p a b -> p (a b)')
        yfb = [A('yf%d' % i, [128, 4, 128]) for i in range(2)]; ybb = [A('yb%d' % i, [128, 4, 128]) for i in range(2)]
        cbonb = [A('cbon%d' % i, [128, 4, 128], BF16) for i in range(2)]; cgb = [A('cg%d' % i, [128, 4, 128], BF16) for i in range(2)]
        ys = A('ys', [128, 4, 128]); ysq = A('ysq', [128, 4, 128]); gmean = A('gmean', [128, 4, 128]); grs = A('grs', [128, 4, 128]); gt_ = A('gt_', [128, 4, 128])
        yob = [A('yob%d' % i, [128, 4, 128], BF16) for i in range(2)]
        for ci in range(NCK):
            if l == DEPTH - 1 and ci < CTX // 128:
                continue
            c0 = ci * 128
            b = ci % 2
            yf, yb2, cbon, cg = yfb[b], ybb[b], cbonb[b], cgb[b]
            S.dma('sp', yf[:], fmc(g.yfS, c0), reads=['yfS'], writes=[('yf', b)])
            S.dma('sp', yb2[:], fmc(g.ybS, c0), reads=['ybS'], writes=[('yb', b)])
            S.dma('sp', cbon[:], fmc(g.bonS, c0), reads=['bonS'], writes=[('cbon', b)])
            S.dma('sp', cg[:], fmc(g.gS, c0), reads=['gS'], writes=[('cg', b)])
            S.op('dve', lambda e, yf=yf, yb2=yb2: e.tensor_tensor(ys[:], yf[:], yb2[:], ALU.add), [('yf', b), ('yb', b)], ['ys'])
            S.op('act', lambda e: e.activation(ysq[:], ys[:], AF.Square), ['ys'], ['ysq'])
            p1, p2 = nextps(g), nextps(g)
            mm_group(S, g.ps[p1][:, :], [(bdf[:], f2(ys))], ['bdf', 'ys'], [('ps', p1)])
            mm_group(S, g.ps[p2][:, :], [(bdf[:], f2(ysq))], ['bdf', 'ysq'], [('ps', p2)])
            S.op('act', lambda e, p1=p1: e.activation(f2(gmean), g.ps[p1][:, :], AF.Identity, scale=1.0 / 64), [('ps', p1)], ['gmean'])
            S.op('pool', lambda e: e.tensor_tensor(ysq[:], gmean[:], gmean[:], ALU.mult), ['gmean'], ['ysq'])
            S.op('dve', lambda e, p2=p2: e.scalar_tensor_tensor(f2(grs), g.ps[p2][:, :], 1.0 / 64, f2(ysq), ALU.mult, ALU.subtract), [('ps', p2), 'ysq'], ['grs'])
            S.op('act', lambda e: e.activation(grs[:], grs[:], AF.Sqrt, bias=g.epsv[:, 3:4], scale=1.0), ['grs', 'epsv'], ['grs'])
            S.op('dve', lambda e: e.reciprocal(grs[:], grs[:]), ['grs'], ['grs'])
            S.op('pool', lambda e: e.tensor_tensor(gt_[:], ys[:], gmean[:], ALU.subtract), ['ys', 'gmean'], ['gt_'])
            S.op('dve', lambda e: e.tensor_tensor(gt_[:], gt_[:], grs[:], ALU.mult), ['gt_', 'grs'], ['gt_'])
            S.op('pool', lambda e: e.tensor_tensor(gt_[:], gt_[:], rwp[:, P_GG:P_GG + 4].unsqueeze(2).broadcast_to([128, 4, 128]), ALU.mult), ['gt_', 'rwp'], ['gt_'])
            S.op('pool', lambda e: e.tensor_tensor(gt_[:], gt_[:], rwp[:, P_GB:P_GB + 4].unsqueeze(2).broadcast_to([128, 4, 128]), ALU.add), ['gt_', 'rwp'], ['gt_'])
            S.op('dve', lambda e, cbon=cbon: e.tensor_tensor(gt_[:], gt_[:], cbon[:], ALU.add), ['gt_', ('cbon', b)], ['gt_'])
            yo = yob[b]
            S.op('pool', lambda e, yo=yo, cg=cg: e.tensor_tensor(yo[:], gt_[:], cg[:], ALU.mult), ['gt_', ('cg', b)], [('yob', b)])
            S.dma('sp', fmc(g.yrT, c0), yo[:], reads=[('yob', b)], writes=['yrT'])


def phase_merge(g, l):
    S, nc = g.S, g.nc
    last = (l == DEPTH - 1)
    with Phase(g) as A:
        wg = A('wg', [128, NCH, 3072], BF16)
        wp = [A('wp%d' % i, [128, 4, 1024], BF16) for i in range(3)]
        wo = A('wo', [128, NCH, 1024], BF16)
        HL = HLoader(g, A)
        yb = [[A('my%d_%d' % (i, j), [128, 4, 512], BF16) for j in range(2)] for i in range(3)]
        xb = [A('mx%d' % i, [128, NCH, 512]) for i in range(2)]
        sig = [A('msig%d' % i, [128, 512]) for i in range(2)]
        tm = [A('mtm%d' % i, [128, 512]) for i in range(2)]
        macc = A('macc', [128, 512])
        mT = A('mT', [128, NCH, 512], BF16)
        wsrc = g.w_in[l].rearrange('(k p) c -> p k c', p=128)
        for i in range(2):
            S.dma('pool', wg[:, :, i * 1536:(i + 1) * 1536], wsrc[:, :, 4480 + i * 1536:4480 + (i + 1) * 1536], writes=['wg'])
        for i, nm in enumerate(('p_conv', 'p_att', 'p_rwkv')):
            S.dma('pool', wp[i][:], getattr(g, nm)[l].rearrange('(k p) c -> p k c', p=128), writes=[('wp', i)])
        S.dma('pool', wo[:], g.w_out[l].rearrange('(k p) c -> p k c', p=128), writes=['wo'])
        xTv = g.xT.rearrange('(k p) t -> p k t', p=128)
        ysrc = [g.ycT.rearrange('(k p) t -> p k t', p=128), g.yaT.rearrange('(k p) t -> p k t', p=128), g.yrT.rearrange('(k p) t -> p k t', p=128)]
        ynm = ['ycT', 'yaT', 'yrT']
        for bi, (t0, nt) in enumerate(BLKS):
            if last and t0 < CTX:
                continue
            b = bi % 2
            s = 1 if t0 < CTX else 0
            hb, hk = HL.load(bi)
            for i in range(3):
                S.dma('sp', yb[i][b][:, :, :nt], ysrc[i][:, :, t0:t0 + nt], reads=[ynm[i]], writes=[('my', i, b)])
            S.dma('sp', xb[b][:, :, :nt], xTv[:, :, t0:t0 + nt], reads=['xT'], writes=[('mx', b)])
            for oc in range(NCH):
                for i in range(3):
                    pg, pp = nextps(g), nextps(g)
                    c0 = i * 1024 + oc * 128
                    mm_group(S, g.ps[pg][:, :nt], [(wg[:, k, c0:c0 + 128], hb[:, k, :nt]) for k in range(NCH)], ['wg', hk], [('ps', pg)])
                    mm_group(S, g.ps[pp][:, :nt], [(wp[i][:, k, oc * 128:(oc + 1) * 128], yb[i][b][:, k, :nt]) for k in range(4)], [('wp', i), ('my', i, b)], [('ps', pp)])
                    sb_ = i % 2
                    S.op('act', lambda e, pg=pg, sb_=sb_: e.activation(sig[sb_][:, :nt], g.ps[pg][:, :nt], AF.Sigmoid), [('ps', pg)], [('msig', sb_)])
                    if i == 0:
                        S.op('dve', lambda e, pp=pp, sb_=sb_: e.tensor_tensor(macc[:, :nt], g.ps[pp][:, :nt], sig[sb_][:, :nt], ALU.mult), [('ps', pp), ('msig', sb_)], ['macc'])
                    else:
                        S.op('dve', lambda e, pp=pp, sb_=sb_: e.tensor_tensor(tm[sb_][:, :nt], g.ps[pp][:, :nt], sig[sb_][:, :nt], ALU.mult), [('ps', pp), ('msig', sb_)], [('mtm', sb_)])
                        if i == 1:
                            S.op('pool', lambda e, sb_=sb_: e.tensor_tensor(macc[:, :nt], macc[:, :nt], tm[sb_][:, :nt], ALU.add), ['macc', ('mtm', sb_)], ['macc'])
                        else:
                            S.op('pool', lambda e, sb_=sb_, oc=oc: e.tensor_tensor(mT[:, oc, :nt], macc[:, :nt], tm[sb_][:, :nt], ALU.add), ['macc', ('mtm', sb_)], [('mT', oc)])
            for oc in range(NCH):
                po = nextps(g)
                mm_group(S, g.ps[po][:, :nt], [(wo[:, k, oc * 128:(oc + 1) * 128], mT[:, k, :nt]) for k in range(NCH)], ['wo'] + [('mT', k) for k in range(NCH)], [('ps', po)])
                S.op('dve', lambda e, po=po, oc=oc: e.scalar_tensor_tensor(xb[b][:, oc, :nt], g.ps[po][:, :nt], g.modv[:, 2 * 8 + oc, s:s + 1], xb[b][:, oc, :nt], ALU.mult, ALU.add),
                     [('ps', po), 'modv', ('mx', b)], [('mx', b)])
            S.dma('sp', xTv[:, :, t0:t0 + nt], xb[b][:, :, :nt], reads=[('mx', b)], writes=['xT'])


def phase_mlp(g, l):
    S, nc = g.S, g.nc
    last = (l == DEPTH - 1)
    with Phase(g) as A:
        w1 = A('w1', [128, NCH, DFF], BF16)
        w2 = A('w2', [128, DFF // 128, D], BF16)
        xb = [A('fx%d' % i, [128, NCH, 512]) for i in range(1)] * 2
        rs = A('frs', [128, 512]); tmp = [A('ftmp%d' % i, [128, 512]) for i in range(2)]
        h2 = A('fh2', [128, NCH, 512], BF16)
        act = A('fact', [128, DFF // 128, 512], BF16)
        sq = act[:, 0:16, :].bitcast(F32).rearrange('p (a two) b -> p a (two b)', two=2)
        fg = A('fg', [128, NCH])
        osb = A('fosb', [128, D])
        w1src = g.mlp_w1[l].rearrange('(k p) c -> p k c', p=128)
        for i in range(4):
            S.dma('pool', w1[:, :, i * 1024:(i + 1) * 1024], w1src[:, :, i * 1024:(i + 1) * 1024], writes=['w1'])
        w2src = g.mlp_w2[l].rearrange('(k p) c -> p k c', p=128)
        for i in range(4):
            S.dma('pool', w2[:, i * 8:(i + 1) * 8, :], w2src[:, i * 8:(i + 1) * 8, :], writes=['w2'])
        S.dma('sp', fg[:], g.fing[:, :], writes=['fg'])
        xTv = g.xT.rearrange('(k p) t -> p k t', p=128)
        for bi, (t0, nt) in enumerate(BLKS):
            if last and t0 < CTX:
                continue
            b = bi % 2
            s = 1 if t0 < CTX else 0
            x = xb[b]
            xk = ('fx', 0)
            S.dma('sp', x[:, :, :nt], xTv[:, :, t0:t0 + nt], reads=['xT'], writes=[xk])
            rms_stats(g, x, nt, sq, rs, xk, 'fact', 'frs')
            for k in range(NCH):
                tb = k % 2
                S.op('dve', lambda e, k=k, tb=tb: e.tensor_tensor(tmp[tb][:, :nt], x[:, k, :nt], rs[:, :nt], ALU.mult), [xk, 'frs'], [('ftmp', tb)])
                S.op('act', lambda e, k=k, tb=tb: e.activation(h2[:, k, :nt], tmp[tb][:, :nt], AF.Identity,
                     bias=g.modv[:, 3 * 8 + k, s:s + 1], scale=g.gs[:, 1, k, s:s + 1]), [('ftmp', tb), 'modv', 'gs'], ['fh2'])
            for fc in range(DFF // 128):
                pf = nextps(g)
                tb = fc % 2
                mm_group(S, g.ps[pf][:, :nt], [(w1[:, k, fc * 128:(fc + 1) * 128], h2[:, k, :nt]) for k in range(NCH)], ['w1', 'fh2'], [('ps', pf)])
                S.op('dve', lambda e, pf=pf, tb=tb: e.tensor_scalar(tmp[tb][:, :nt], g.ps[pf][:, :nt], 0.0, None, ALU.max), [('ps', pf)], [('ftmp', tb)])
                S.op('act', lambda e, fc=fc, tb=tb: e.activation(act[:, fc, :nt], tmp[tb][:, :nt], AF.Square), [('ftmp', tb)], ['fact'])
            for oc in range(NCH):
                po = nextps(g)
                mm_group(S, g.ps[po][:, :nt], [(w2[:, fc, oc * 128:(oc + 1) * 128], act[:, fc, :nt]) for fc in range(DFF // 128)],
                         ['w2', 'fact'], [('ps', po)])
                S.op('dve', lambda e, po=po, oc=oc: e.scalar_tensor_tensor(x[:, oc, :nt], g.ps[po][:, :nt], g.modv[:, 5 * 8 + oc, s:s + 1], x[:, oc, :nt], ALU.mult, ALU.add),
                     [('ps', po), 'modv', xk], [xk])
            if not last:
                S.dma('sp', xTv[:, :, t0:t0 + nt], x[:, :, :nt], reads=[xk], writes=['xT'])
                continue
            rms_stats(g, x, nt, sq, rs, xk, 'fact', 'frs')
            for k in range(NCH):
                S.op('dve', lambda e, k=k: e.tensor_tensor(sq[:, k, :nt], x[:, k, :nt], rs[:, :nt], ALU.mult), [xk, 'frs', 'fact'], ['fact'])
                S.op('act', lambda e, k=k: e.activation(sq[:, k, :nt], sq[:, k, :nt], AF.Identity, scale=fg[:, k:k + 1]), ['fact', 'fg'], ['fact'])
            for tt in range(nt // 128):
                for half in range(2):
                    pi = nextps(g)

                    def fn(pe, pi=pi, half=half, tt=tt):
                        inst = None
                        for j in range(4):
                            inst = pe.transpose(g.ps[pi][:, j * 128:(j + 1) * 128], sq[:, half * 4 + j, tt * 128:(tt + 1) * 128], g.identf[:])
                        return inst
                    S.op('pe', fn, ['fact', 'identf'], [('ps', pi)])
                    S.op('act', lambda e, pi=pi, half=half: e.copy(osb[:, half * 512:(half + 1) * 512], g.ps[pi][:, :]), [('ps', pi)], ['fosb'])
                r0 = t0 - CTX + tt * 128
                S.dma('sp', g.out[r0:r0 + 128, :], osb[:], reads=['fosb'], writes=['out'])


_CACHE = {}


def kernel(**inputs):
    inp = {k: np.asarray(v) for k, v in inputs.items()}
    if 'nc' not in _CACHE:
        _CACHE['nc'] = build()
    nc = _CACHE['nc']
    shared = host_shared(inp)
    B = inp['x'].shape[0]
    in_maps = [host_inputs(inp, b, shared) for b in range(B)]
    res = run_bass_kernel_spmd(nc, in_maps, core_ids=list(range(B)))
    return np.stack([np.asarray(res.results[b]['out']) for b in range(B)]).astype(np.float32)
```
